# Optimizing a Trainium2 kernel written in Bass

```python
import math
import jax, jax.numpy as jnp
from jax import lax
import numpy as np

D_MODEL = 1024
BATCH = 4
SEQ = 8192
DEPTH = 2
DEC_BATCH = 8
DEC_SEQ = 32
PAST_LEN = 2048

CHUNK = 64
N_MEM = 256
Q_BLOCK = 128
EPS = 1e-6
N_EVEN = (DEPTH + 1) // 2
N_ODD = DEPTH // 2
H_A = 4
DK_A = 128
DV_A = 128
W_A = H_A * DV_A
H_B = 4
DK_B = 64
DV_B = 2 * DK_B
W_B = H_B * DV_B
H_X = 4
DH_X = 128
W_X = H_X * DH_X
W_C = D_MODEL
CONV_W = 31
ALIBI_SLOPES = tuple(2.0 ** (-8.0 * (h + 1) / H_B) for h in range(H_B))
SPLIT_EVEN = (H_A * DK_A, H_A * DK_A, W_A, W_A, W_A, H_A, H_A, 2 * H_B * DK_B, 2 * H_B * DK_B, W_B, W_B, W_X, W_X)
N_IN_EVEN = sum(SPLIT_EVEN)
SPLIT_ODD = (W_C, W_C, W_C, W_X, W_X)
N_IN_ODD = sum(SPLIT_ODD)

kernel_name = 'hybrid_mlstm_diffattn_conformer_stream_step'


def lambda_init(layer):
    return 0.8 - 0.6 * math.exp(-0.3 * layer)


def rmsnorm(x, g):
    xf = x.astype(jnp.float32)
    y = xf * lax.rsqrt(jnp.mean(xf * xf, axis=-1, keepdims=True) + EPS)
    return (y * g.astype(jnp.float32)).astype(x.dtype)


def layernorm(x, g, b):
    xf = x.astype(jnp.float32)
    mu = jnp.mean(xf, axis=-1, keepdims=True)
    var = jnp.mean(jnp.square(xf - mu), axis=-1, keepdims=True)
    y = (xf - mu) * lax.rsqrt(var + EPS) * g.astype(jnp.float32) + b.astype(jnp.float32)
    return y.astype(x.dtype)


def split_cols(y, sizes):
    idx = [int(i) for i in np.cumsum(sizes)[:-1]]
    return jnp.split(y, idx, axis=-1)


def mlstm_chunk(state, inp):
    C0, n0, m0 = state
    q, k, v, ig, lf = inp
    L = q.shape[1]
    b = jnp.cumsum(lf, axis=1).transpose(0, 2, 1)
    it = ig.transpose(0, 2, 1)
    causal = jnp.tril(jnp.ones((L, L), dtype=bool))
    logw = jnp.where(causal, b[..., :, None] - b[..., None, :] + it[..., None, :], -jnp.inf)
    g = b + m0[..., None]
    m = jnp.maximum(g, jnp.max(logw, axis=-1))
    w_intra = jnp.exp(logw - m[..., None])
    w_inter = jnp.exp(g - m)
    s = jnp.einsum('blhd,bshd->bhls', q, k) * w_intra
    num = w_inter[..., None] * jnp.einsum('blhd,bhde->bhle', q, C0) + jnp.einsum('bhls,bshe->bhle', s, v)
    den = w_inter * jnp.einsum('blhd,bhd->bhl', q, n0) + jnp.sum(s, axis=-1)
    h = num / jnp.maximum(jnp.abs(den), jnp.exp(-m))[..., None]
    m_last = m[..., -1]
    decay = jnp.exp(g[..., -1] - m_last)
    w_end = jnp.exp(b[..., -1:] - b + it - m_last[..., None])
    C1 = decay[..., None, None] * C0 + jnp.einsum('bhs,bshd,bshe->bhde', w_end, k, v)
    n1 = decay[..., None] * n0 + jnp.einsum('bhs,bshd->bhd', w_end, k)
    return (C1, n1, m_last), h.transpose(0, 2, 1, 3)


def diff_attn_core(q, k, v, qpos, kpos, lam):
    s = jnp.einsum('bqhcd,bkhcd->bhcqk', q, k).astype(jnp.float32) * (DK_B ** -0.5)
    slopes = jnp.array(ALIBI_SLOPES, dtype=jnp.float32)
    dist = jnp.abs(qpos[:, None] - kpos[None, :]).astype(jnp.float32)
    s = s - slopes[:, None, None, None] * dist
    visible = (kpos // CHUNK)[None, :] <= (qpos // CHUNK)[:, None]
    s = jnp.where(visible, s, -jnp.inf)
    p = jax.nn.softmax(s, axis=-1)
    w = p[:, :, 0] - lam * p[:, :, 1]
    return jnp.einsum('bhqk,bkhd->bqhd', w.astype(v.dtype), v)


def diff_attn_prompt(q, k, v, lam):
    bsz, S = q.shape[0], q.shape[1]
    nb = S // Q_BLOCK
    qb = q.reshape(bsz, nb, Q_BLOCK, H_B, 2, DK_B).swapaxes(0, 1)
    kpos = jnp.arange(S)

    def block(args):
        qblk, j = args
        qpos = j * Q_BLOCK + jnp.arange(Q_BLOCK)
        return diff_attn_core(qblk, k, v, qpos, kpos, lam)

    o = lax.map(block, (qb, jnp.arange(nb)))
    return o.swapaxes(0, 1).reshape(bsz, S, H_B, DV_B)


def memory_kv(mem, g, w_kv, kg):
    bsz, M = mem.shape[0], mem.shape[1]
    kv = rmsnorm(mem, g) @ w_kv
    k, v = jnp.split(kv, 2, axis=-1)
    k = rmsnorm(k.reshape(bsz, M, H_X, DH_X), kg)
    return k, v.reshape(bsz, M, H_X, DH_X)


def cross_attn(q, mem_k, mem_v):
    bsz, L = q.shape[0], q.shape[1]
    s = jnp.einsum('blhd,bmhd->bhlm', q, mem_k.astype(q.dtype)).astype(jnp.float32) * (DH_X ** -0.5)
    p = jax.nn.softmax(s, axis=-1).astype(mem_v.dtype)
    return jnp.einsum('bhlm,bmhd->blhd', p, mem_v).reshape(bsz, L, W_X)


def even_layer(x, mem_k, mem_v, hist, norm_g, w_in, b_ig, b_fg, mlstm_g, qn_g, kn_g,
               lq1, lk1, lq2, lk2, subln_g, w_out, xq_g, lam_init):
    f32 = jnp.float32
    bsz, L = x.shape[0], x.shape[1]
    y = rmsnorm(x, norm_g) @ w_in
    aq, ak, av, ao, az, ai, af, bq, bk, bv, bz, xq, xz = split_cols(y, SPLIT_EVEN)
    qa = aq.reshape(bsz, L, H_A, DK_A).astype(f32)
    ka = ak.reshape(bsz, L, H_A, DK_A).astype(f32) * (DK_A ** -0.5)
    va = av.reshape(bsz, L, H_A, DV_A).astype(f32)
    ig = ai.astype(f32) + b_ig.astype(f32)
    lf = jax.nn.log_sigmoid(af.astype(f32) + b_fg.astype(f32))
    qb = rmsnorm(bq.reshape(bsz, L, H_B, 2, DK_B), qn_g)
    kb = rmsnorm(bk.reshape(bsz, L, H_B, 2, DK_B), kn_g)
    vb = bv.reshape(bsz, L, H_B, DV_B)
    lam = (jnp.exp(jnp.sum(lq1.astype(f32) * lk1.astype(f32)))
           - jnp.exp(jnp.sum(lq2.astype(f32) * lk2.astype(f32))) + lam_init)
    if hist is None:
        state0 = (jnp.zeros((bsz, H_A, DK_A, DV_A), f32), jnp.zeros((bsz, H_A, DK_A), f32),
                  jnp.zeros((bsz, H_A), f32))
        n_chunks = L // CHUNK

        def to_chunks(a):
            return a.reshape(bsz, n_chunks, CHUNK, *a.shape[2:]).swapaxes(0, 1)

        state1, ha = lax.scan(mlstm_chunk, state0, tuple(to_chunks(a) for a in (qa, ka, va, ig, lf)))
        ha = ha.swapaxes(0, 1).reshape(bsz, L, H_A, DV_A)
        ob = diff_attn_prompt(qb, kb, vb, lam)
    else:
        k_past, v_past, C0, n0, m0 = hist
        state1, ha = mlstm_chunk((C0.astype(f32), n0.astype(f32), m0.astype(f32)), (qa, ka, va, ig, lf))
        P = k_past.shape[1]
        k_all = jnp.concatenate([k_past.reshape(bsz, P, H_B, 2, DK_B).astype(kb.dtype), kb], axis=1)
        v_all = jnp.concatenate([v_past.astype(vb.dtype), vb], axis=1)
        ob = diff_attn_core(qb, k_all, v_all, P + jnp.arange(L), jnp.arange(P + L), lam)
    oa = rmsnorm(ha, mlstm_g) * jax.nn.sigmoid(ao.astype(f32)).reshape(bsz, L, H_A, DV_A)
    ob = rmsnorm(ob, subln_g) * (1.0 - lam_init)
    ox = cross_attn(rmsnorm(xq.reshape(bsz, L, H_X, DH_X), xq_g), mem_k, mem_v)
    mixed = jnp.concatenate([
        oa.reshape(bsz, L, W_A).astype(x.dtype) * jax.nn.silu(az),
        ob.reshape(bsz, L, W_B) * jax.nn.silu(bz),
        ox.astype(x.dtype) * jax.nn.silu(xz)], axis=-1)
    C1, n1, m1 = state1
    return x + mixed @ w_out, kb.reshape(bsz, L, H_B, 2 * DK_B), vb, C1, n1, m1


def odd_layer(x, mem_k, mem_v, conv_hist, norm_g, w_in, conv_w, conv_b, ln_g, ln_b, w_out, xq_g):
    bsz, L = x.shape[0], x.shape[1]
    y = rmsnorm(x, norm_g) @ w_in
    cu, cg, cz, xq, xz = split_cols(y, SPLIT_ODD)
    u = cu * jax.nn.sigmoid(cg)
    if conv_hist is None:
        u_pad = jnp.pad(u, ((0, 0), (CONV_W - 1, 0), (0, 0)))
    else:
        u_pad = jnp.concatenate([conv_hist.astype(u.dtype), u], axis=1)
    c = lax.conv_general_dilated(u_pad, conv_w.astype(u.dtype)[:, None, :], (1,), 'VALID',
                                 dimension_numbers=('NWC', 'WIO', 'NWC'),
                                 feature_group_count=W_C) + conv_b
    c = jax.nn.silu(layernorm(c, ln_g, ln_b))
    ox = cross_attn(rmsnorm(xq.reshape(bsz, L, H_X, DH_X), xq_g), mem_k, mem_v)
    mixed = jnp.concatenate([c * jax.nn.silu(cz), ox.astype(x.dtype) * jax.nn.silu(xz)], axis=-1)
    return x + mixed @ w_out, u_pad[:, -(CONV_W - 1):]


def setup_inputs(seed: int = 0) -> dict:
    key = jax.random.key(seed)
    keys = iter(jax.random.split(key, 48))

    def nrm(shape, scale):
        return jax.random.normal(next(keys), shape, jnp.float32) * scale

    def gain(shape):
        return 1.0 + nrm(shape, 0.02)

    d_in_a = W_A + W_B + W_X
    d_in_c = W_C + W_X
    return {
        'x_prompt': nrm((BATCH, SEQ, D_MODEL), 1.0),
        'x_sample': nrm((DEC_BATCH, DEC_SEQ, D_MODEL), 1.0),
        'mem_prompt': nrm((BATCH, N_MEM, D_MODEL), 1.0),
        'cache_xk': nrm((DEPTH, DEC_BATCH, N_MEM, H_X, DH_X), 1.0),
        'cache_xv': nrm((DEPTH, DEC_BATCH, N_MEM, H_X, DH_X), 1.0),
        'cache_k': nrm((N_EVEN, DEC_BATCH, PAST_LEN, H_B, 2 * DK_B), 1.0),
        'cache_v': nrm((N_EVEN, DEC_BATCH, PAST_LEN, H_B, DV_B), 1.0),
        'state_C': nrm((N_EVEN, DEC_BATCH, H_A, DK_A, DV_A), 0.5),
        'state_n': nrm((N_EVEN, DEC_BATCH, H_A, DK_A), 0.3),
        'state_m': nrm((N_EVEN, DEC_BATCH, H_A), 0.5),
        'state_conv': nrm((N_ODD, DEC_BATCH, CONV_W - 1, W_C), 0.5),
        'norm_g': gain((DEPTH, D_MODEL)),
        'w_in_a': nrm((N_EVEN, D_MODEL, N_IN_EVEN), D_MODEL ** -0.5),
        'b_ig': nrm((N_EVEN, H_A), 0.1),
        'b_fg': jnp.linspace(3.0, 6.0, H_A, dtype=jnp.float32)[None, :] + nrm((N_EVEN, H_A), 0.1),
        'mlstm_norm_g': gain((N_EVEN, H_A, DV_A)),
        'qn_g': gain((N_EVEN, DK_B)),
        'kn_g': gain((N_EVEN, DK_B)),
        'lam_q1': nrm((N_EVEN, DK_B), 0.1),
        'lam_k1': nrm((N_EVEN, DK_B), 0.1),
        'lam_q2': nrm((N_EVEN, DK_B), 0.1),
        'lam_k2': nrm((N_EVEN, DK_B), 0.1),
        'subln_g': gain((N_EVEN, DV_B)),
        'w_out_a': nrm((N_EVEN, d_in_a, D_MODEL), d_in_a ** -0.5),
        'w_in_c': nrm((N_ODD, D_MODEL, N_IN_ODD), D_MODEL ** -0.5),
        'conv_w': nrm((N_ODD, CONV_W, W_C), CONV_W ** -0.5),
        'conv_b': nrm((N_ODD, W_C), 0.02),
        'conv_ln_g': gain((N_ODD, W_C)),
        'conv_ln_b': nrm((N_ODD, W_C), 0.02),
        'w_out_c': nrm((N_ODD, d_in_c, D_MODEL), d_in_c ** -0.5),
        'mem_norm_g': gain((DEPTH, D_MODEL)),
        'w_mem_kv': nrm((DEPTH, D_MODEL, 2 * W_X), D_MODEL ** -0.5),
        'xq_norm_g': gain((DEPTH, DH_X)),
        'xk_norm_g': gain((DEPTH, DH_X)),
    }


def reference(x_prompt, x_sample, mem_prompt, cache_xk, cache_xv, cache_k, cache_v, state_C, state_n,
              state_m, state_conv, norm_g, w_in_a, b_ig, b_fg, mlstm_norm_g, qn_g, kn_g, lam_q1, lam_k1,
              lam_q2, lam_k2, subln_g, w_out_a, w_in_c, conv_w, conv_b, conv_ln_g, conv_ln_b, w_out_c,
              mem_norm_g, w_mem_kv, xq_norm_g, xk_norm_g):
    yp, ys = x_prompt, x_sample
    p_xk, p_xv, p_k, p_v, p_C, p_n, p_m, p_conv = [], [], [], [], [], [], [], []
    s_k, s_v, s_C, s_n, s_m, s_conv = [], [], [], [], [], []
    for layer in range(DEPTH):
        mk, mv = memory_kv(mem_prompt, mem_norm_g[layer], w_mem_kv[layer], xk_norm_g[layer])
        p_xk.append(mk)
        p_xv.append(mv)
        if layer % 2 == 0:
            e = layer // 2
            wts = (norm_g[layer], w_in_a[e], b_ig[e], b_fg[e], mlstm_norm_g[e], qn_g[e], kn_g[e],
                   lam_q1[e], lam_k1[e], lam_q2[e], lam_k2[e], subln_g[e], w_out_a[e], xq_norm_g[layer],
                   lambda_init(layer))
            yp, k_new, v_new, C1, n1, m1 = even_layer(yp, mk, mv, None, *wts)
            p_k.append(k_new)
            p_v.append(v_new)
            p_C.append(C1)
            p_n.append(n1)
            p_m.append(m1)
            hist = (cache_k[e], cache_v[e], state_C[e], state_n[e], state_m[e])
            ys, k_new, v_new, C1, n1, m1 = even_layer(ys, cache_xk[layer], cache_xv[layer], hist, *wts)
            s_k.append(k_new)
            s_v.append(v_new)
            s_C.append(C1)
            s_n.append(n1)
            s_m.append(m1)
        else:
            o = layer // 2
            wts = (norm_g[layer], w_in_c[o], conv_w[o], conv_b[o], conv_ln_g[o], conv_ln_b[o], w_out_c[o],
                   xq_norm_g[layer])
            yp, cv = odd_layer(yp, mk, mv, None, *wts)
            p_conv.append(cv)
            ys, cv = odd_layer(ys, cache_xk[layer], cache_xv[layer], state_conv[o], *wts)
            s_conv.append(cv)
    new_p_xk = jnp.stack(p_xk)
    new_p_xv = jnp.stack(p_xv)
    new_p_k = jnp.stack(p_k)
    new_p_v = jnp.stack(p_v)
    new_p_C = jnp.stack(p_C)
    new_p_n = jnp.stack(p_n)
    new_p_m = jnp.stack(p_m)
    new_p_conv = jnp.stack(p_conv)
    new_s_k = jnp.stack(s_k)
    new_s_v = jnp.stack(s_v)
    new_s_C = jnp.stack(s_C)
    new_s_n = jnp.stack(s_n)
    new_s_m = jnp.stack(s_m)
    new_s_conv = jnp.stack(s_conv)
    return (yp, ys, new_p_xk, new_p_xv, new_p_k, new_p_v, new_p_C, new_p_n, new_p_m, new_p_conv,
            new_s_k, new_s_v, new_s_C, new_s_n, new_s_m, new_s_conv)
```

```python
import math
import numpy as np
import concourse.bass as bass
import concourse.mybir as mybir
from concourse.bass_utils import run_bass_kernel_spmd

F32 = mybir.dt.float32
BF16 = mybir.dt.bfloat16
AF = mybir.ActivationFunctionType
ALU = mybir.AluOpType
AX = mybir.AxisListType

D = 1024
EPS = 1e-6
SLOPES = [2.0 ** (-8.0 * (h + 1) / 4) for h in range(4)]
NSUBH = [2, 1, 1, 1]
NDELTA = 72
DOFF = 3
PMASK = -200.0
LAM_INIT0 = 0.8 - 0.6 * math.exp(0.0)


class Buf:
    __slots__ = ("name", "last_w", "readers", "excl")

    def __init__(self, name="", excl=False):
        self.name = name
        self.last_w = None
        self.readers = []
        self.excl = excl


class Op:
    __slots__ = ("eng", "fn", "deps", "signal", "is_dma", "sem", "semval", "vc", "gi", "desc")


class Sched:
    def __init__(self, nc, n_dma_sems=10):
        self.nc = nc
        self.ops = {e: [] for e in ("pe", "act", "dve", "pool", "sp")}
        self.all = []
        self.n_dma_sems = n_dma_sems
        self.dma_rr = {q: 0 for q in ("sp", "act", "pool")}
        self.dma_last = {}

    def op(self, eng, fn, reads=(), writes=(), dma=False):
        import os as _os
        if len(self.all) >= int(_os.environ.get("MAXOPS", "100000000")):
            return None
        o = Op()
        o.eng = eng
        o.fn = fn
        o.is_dma = dma
        o.signal = False
        o.sem = None
        o.semval = 0
        o.gi = len(self.all)
        deps = set()
        reads = list(reads)
        writes = list(writes)
        for b in reads:
            if b.excl and b not in writes:
                writes.append(b)
        for b in reads:
            if b.last_w is not None:
                deps.add(b.last_w)
        for b in writes:
            if b.last_w is not None:
                deps.add(b.last_w)
            for r in b.readers:
                deps.add(r)
        if dma:
            key = (eng, self.dma_rr[eng] % self.n_dma_sems)
            self.dma_rr[eng] += 1
            prev = self.dma_last.get(key)
            if prev is not None:
                deps.add(prev)
            self.dma_last[key] = o
            o.sem = key
        deps.discard(o)
        o.deps = deps
        for b in reads:
            b.readers.append(o)
        for b in writes:
            b.last_w = o
            b.readers = []
        self.ops[eng].append(o)
        self.all.append(o)
        return o

    def emit(self, final_waits=()):
        nc = self.nc

        def pe_pe(o, d):
            return (not d.is_dma) and d.eng == "pe" and o.eng == "pe" and (not o.is_dma)

        for o in self.all:
            for d in o.deps:
                if d.is_dma or pe_pe(o, d):
                    continue
                d.signal = True
        for o in final_waits:
            if not o.is_dma:
                o.signal = True
        cnt = {}
        for o in self.all:
            if o.is_dma:
                cnt[o.sem] = cnt.get(o.sem, 0) + 16
                o.semval = cnt[o.sem]
            elif o.signal:
                o.sem = ("c", o.eng)
                cnt[o.sem] = cnt.get(o.sem, 0) + 1
                o.semval = cnt[o.sem]
        known = {e: {} for e in self.ops}
        plans = {}
        for o in self.all:
            kn = known[o.eng]
            waits = {}
            for d in o.deps:
                if pe_pe(o, d):
                    continue
                if kn.get(d.sem, 0) >= d.semval:
                    continue
                if waits.get(d.sem, 0) < d.semval:
                    waits[d.sem] = d.semval
            for d in o.deps:
                if pe_pe(o, d):
                    continue
                for s, v in d.vc.items():
                    if kn.get(s, 0) < v:
                        kn[s] = v
            for s in list(waits):
                others = 0
                for d in o.deps:
                    if pe_pe(o, d) or d.sem == s:
                        continue
                    others = max(others, d.vc.get(s, 0))
                if others >= waits[s]:
                    pass
            plans[o.gi] = waits
            vc = dict(kn)
            if o.sem is not None and vc.get(o.sem, 0) < o.semval:
                vc[o.sem] = o.semval
            o.vc = vc
        sems = {}
        for key in cnt:
            sems[key] = nc.alloc_semaphore("s_" + "_".join(str(k) for k in key))
        fw = [(o.sem, o.semval) for o in final_waits]
        self.n_waits = sum(len(p) for p in plans.values())
        self.plans = plans

        def run_engine(ename):
            def body(eng):
                for o in self.ops[ename]:
                    for s, v in plans[o.gi].items():
                        eng.wait_ge(sems[s], v)
                    ins = o.fn(eng)
                    if o.is_dma:
                        ins.then_inc(sems[o.sem], 16)
                    elif o.signal:
                        ins.then_inc(sems[o.sem], 1)
                if ename == "sp":
                    done = {}
                    for s, v in fw:
                        done[s] = max(done.get(s, 0), v)
                    for s, v in done.items():
                        eng.wait_ge(sems[s], v)
            return body

        with nc.Block() as block:
            block.tensor(run_engine("pe"))
            block.scalar(run_engine("act"))
            block.vector(run_engine("dve"))
            block.gpsimd(run_engine("pool"))
            block.sync(run_engine("sp"))


class Tl:
    def __init__(self, t, name):
        self.t = t
        self.b = Buf(name)

    def __getitem__(self, k):
        return self.t[k]


class Prog:
    def __init__(self, cfg):
        self.cfg = cfg
        self.nc = bass.Bass("TRN2", target_bir_lowering=False)
        self.S = Sched(self.nc)
        self.outs = []
        self.uid = 0
        self.din = {}
        self.dout = {}

    def sb(self, name, shape, dt=F32):
        return Tl(self.nc.alloc_sbuf_tensor(name, list(shape), dt), name)

    def ps(self, name, shape, dt=F32):
        t = Tl(self.nc.alloc_psum_tensor(name, list(shape), dt), name)
        t.b.excl = True
        return t

    def inp(self, name, shape):
        a = self.nc.dram_tensor(name, list(shape), F32, kind="ExternalInput").ap()
        self.din[name] = a
        return a

    def outp(self, name, shape):
        a = self.nc.dram_tensor(name, list(shape), F32, kind="ExternalOutput").ap()
        self.dout[name] = a
        return a

    @staticmethod
    def _b(xs):
        return [x.b if hasattr(x, "b") else x for x in xs]

    def I(self, eng, meth, *a, r=(), w=(), **kw):
        o = self.S.op(eng, lambda e: getattr(e, meth)(*a, **kw), self._b(r), self._b(w))
        if o is not None:
            o.desc = (eng, meth, str(kw.get("out", a[0] if a else ""))[:90], str(kw.get("func", kw.get("op", kw.get("op0", "")))))
        return o

    def dma(self, q, out, in_, r=(), w=(), final=False, **kw):
        o = self.S.op(q, lambda e: e.dma_start(out=out, in_=in_, **kw), self._b(r), self._b(w), dma=True)
        if o is not None:
            o.desc = (q, "dma", str(out)[:90], "")
        if final and o is not None:
            self.outs.append(o)
        return o

    def mm(self, out, lhsT, rhs, start, stop, r, w, **kw):
        return self.I("pe", "matmul", out, lhsT=lhsT, rhs=rhs, start=start, stop=stop, r=r, w=w, **kw)

    def tr(self, out, in_, ident, r, w):
        return self.I("pe", "transpose", out=out, in_=in_, identity=ident, r=r, w=w)


def bc(ap, shape):
    return ap.broadcast_to(list(shape))


def build(cfg):
    P = Prog(cfg)
    nc = P.nc
    I, dma, mm, tr = P.I, P.dma, P.mm, P.tr
    N_OWN = 128 * sum(cfg["own"])
    N_PRE = 128 * sum(n for n, _ in cfg["pre"])
    NB_OWN = N_OWN // 128
    NB_PRE = N_PRE // 128
    do_smp = cfg.get("sample", True)
    do_l1 = cfg.get("l1", True)

    x_own = P.inp("x_own", [N_OWN, D])
    x_pre = P.inp("x_pre", [max(N_PRE, 128), D])
    x_smp = P.inp("x_smp", [128, D])
    mem = P.inp("mem", [256, D])
    c_xk = P.inp("c_xk", [2, 256, 512])
    c_xv = P.inp("c_xv", [2, 256, 512])
    c_k = P.inp("c_k", [2048, 512])
    c_v = P.inp("c_v", [2048, 512])
    st_C = P.inp("st_C", [4, 128, 128])
    st_n = P.inp("st_n", [4, 128])
    st_m = P.inp("st_m", [4, 1])
    st_conv = P.inp("st_conv", [30, D])
    norm_g = P.inp("norm_g", [2, D])
    w_in_a = P.inp("w_in_a", [D, 5640])
    b_ig = P.inp("b_ig", [4, 1])
    b_fg = P.inp("b_fg", [4, 1])
    mlstm_g = P.inp("mlstm_g", [512])
    qn_g = P.inp("qn_g", [512])
    kn_g = P.inp("kn_g", [512])
    lamv = P.inp("lamv", [4, 64])
    subln_g = P.inp("subln_g", [512])
    w_out_a = P.inp("w_out_a", [1536, D])
    w_in_c = P.inp("w_in_c", [D, 4096])
    conv_wT = P.inp("conv_wT", [D, 31])
    conv_b = P.inp("conv_b", [D])
    ln_g = P.inp("ln_g", [D])
    ln_b = P.inp("ln_b", [D])
    w_out_c = P.inp("w_out_c", [1536, D])
    mem_norm_g = P.inp("mem_norm_g", [2, D])
    w_mem_kv = P.inp("w_mem_kv", [2, D, 1024])
    xq_g = P.inp("xq_g", [2, 512])
    xk_g = P.inp("xk_g", [2, 512])
    c_ident = P.inp("c_ident", [128, 128])
    c_bias = P.inp("c_bias", [128, 2, 4 * NDELTA])
    c_dmask = P.inp("c_dmask", [128, 2, 4, 128])
    c_amask = P.inp("c_amask", [128, 128])
    c_hsel = P.inp("c_hsel", [4, 4, 128])
    c_flag = P.inp("c_flag", [4, 1])

    y_own = P.outp("y_own", [N_OWN, D])
    y_smp = P.outp("y_smp", [32, D])
    o_pxk = P.outp("o_pxk", [2, 256, 512])
    o_pxv = P.outp("o_pxv", [2, 256, 512])
    o_pk = P.outp("o_pk", [N_OWN, 512])
    o_pv = P.outp("o_pv", [N_OWN, 512])
    o_pC = P.outp("o_pC", [4, 128, 128])
    o_pn = P.outp("o_pn", [4, 128])
    o_pm = P.outp("o_pm", [4, 1])
    o_pconv = P.outp("o_pconv", [30, D])
    o_sk = P.outp("o_sk", [32, 512])
    o_sv = P.outp("o_sv", [32, 512])
    o_sC = P.outp("o_sC", [4, 128, 128])
    o_sn = P.outp("o_sn", [4, 128])
    o_sm = P.outp("o_sm", [4, 1])
    o_sconv = P.outp("o_sconv", [30, D])

    wsrc32 = {"w_in_a": (w_in_a, [D, 5640]), "w_out_a": (w_out_a, [1536, D]), "w_in_c": (w_in_c, [D, 4096]),
              "w_out_c": (w_out_c, [1536, D]), "w_mem0": (w_mem_kv[0], [D, 1024]), "w_mem1": (w_mem_kv[1], [D, 1024])}
    wbf = {}
    wbf_b = {}
    for nm, (src32, shp) in wsrc32.items():
        wbf[nm] = nc.dram_tensor("bf_" + nm, shp, BF16, kind="Internal").ap()
        wbf_b[nm] = [Buf(f"wb_{nm}{i}") for i in range(shp[0] // 128)]

    WA_GROUPS = [(512, 1536), (2560, 2568), (3080, 4104), (0, 512), (1536, 2560), (2568, 3080), (4104, 5640)]
    wa_b = [[Buf(f"wa_{g}_{i}") for i in range(8)] for g in range(len(WA_GROUPS))]

    def convert_weights(names):
        for nm in names:
            src32, shp = wsrc32[nm]
            if nm == "w_in_a":
                continue
            for i in range(shp[0] // 128):
                dma("pool", wbf[nm][i * 128:(i + 1) * 128, :], src32[i * 128:(i + 1) * 128, :], w=[wbf_b[nm][i]])

    def convert_wa(groups):
        src32 = wsrc32["w_in_a"][0]
        for g in groups:
            c0, c1 = WA_GROUPS[g]
            for i in range(8):
                dma("pool", wbf["w_in_a"][i * 128:(i + 1) * 128, c0:c1], src32[i * 128:(i + 1) * 128, c0:c1], w=[wa_b[g][i]])

    def wdeps(nm, c0, c1):
        if nm != "w_in_a":
            return wbf_b[nm]
        out = []
        for g, (g0, g1) in enumerate(WA_GROUPS):
            if g0 < c1 and c0 < g1:
                out += wa_b[g]
        return out

    NBLK = max(NB_PRE + NB_OWN, 16)
    NHALF = (NBLK + 15) // 16
    scrK = nc.dram_tensor("scrK", [4, NHALF, 128, 2048], BF16, kind="Internal").ap()
    scrV = nc.dram_tensor("scrV", [4, NHALF, 128, 16 * 130], BF16, kind="Internal").ap()
    scr_b = [Buf(f"scr{i}") for i in range(NHALF * 16)]

    pb = [P.ps(f"pb{i}", [128, 512], F32) for i in range(7)]
    ptb_all = nc.alloc_psum_tensor("ptb_all", [128, 1024], BF16)

    class PV_:
        def __init__(self, name, ap):
            self.ap = ap
            self.b = Buf(name)

        def __getitem__(self, k):
            return self.ap[k]
    ptb = [PV_("ptb0", ptb_all[:, 0:512]), PV_("ptb1", pb[5][:, 0:256].bitcast(BF16))]
    ptb[0].b.excl = True
    ptb[1].b = pb[5].b
    identf = P.sb("identf", [128, 128])
    identb = P.sb("identb", [128, 128], BF16)
    onesf = P.sb("onesf", [128, 128])
    epsc = P.sb("epsc", [128, 1])
    lnk = P.sb("lnk", [128, 1])
    ng = P.sb("ng", [128, 2, 8])
    mng = P.sb("mng", [128, 2, 8])
    g_mlstm = P.sb("g_mlstm", [128, 512])
    g_qn = P.sb("g_qn", [128, 512])
    g_kn = P.sb("g_kn", [128, 512])
    g_subln = P.sb("g_subln", [128, 512])
    g_xq = P.sb("g_xq", [128, 2, 512])
    g_xk = P.sb("g_xk", [128, 2, 512])
    lam_t = P.sb("lam_t", [128, 4, 64])
    lam_s = P.sb("lam_s", [128, 4])
    nlam = P.sb("nlam", [128, 1])
    big = P.sb("big", [4, 1])
    bfg = P.sb("bfg", [4, 1])
    flag = P.sb("flag", [4, 1])
    biasT = P.sb("biasT", [128, 2, 4 * NDELTA])
    dmask = P.sb("dmask", [128, 2, 4, 128])
    amask = P.sb("amask", [128, 128])
    hsel = P.sb("hsel", [4, 4, 128])
    wgate = P.sb("wgate", [128, 8, 8], BF16)
    cw = P.sb("cw", [128, 8, 31])
    cb = P.sb("cb", [128, 8])
    lg = P.sb("lg", [128, 8])
    lb = P.sb("lb", [128, 8])
    dummy = P.sb("dummyt", [128, 2])

    NW = 4
    wring = [P.sb(f"wring{i}", [128, 4096], BF16) for i in range(NW)]
    wctr = [0]
    NXS = 3
    xs = [P.sb(f"xs{i}", [128, D]) for i in range(NXS)]
    y0 = P.sb("y0", [128, 4, D])
    y0b = [Buf(f"y0_{j}") for j in range(4)]
    xnb = P.sb("xnb", [128, 4, D], BF16)
    xnT = P.sb("xnT", [128, 8, 512], BF16)
    sqs = P.sb("sqs", [128, D])
    st1 = P.sb("st1", [128, 8])
    st2 = P.sb("st2", [128, 8])
    tk0 = P.sb("tk0", [128, 512])
    tk1 = P.sb("tk1", [128, 512])
    tk2 = P.sb("tk2", [128, 512])
    tkrot = [tk0, tk1, tk2]
    tkrc = [0]
    tkb = P.sb("tkb", [128, 512], BF16)
    tkbs = [tkb, P.sb("tkb2", [128, 512], BF16)]
    tkc = [0]
    mixT = P.sb("mixT", [128, 12, 512], BF16)
    PT = [[P.sb(f"PT{i}{m}", [128, 512], BF16) for m in range(2)] for i in range(2)]
    pctr = [0]
    ob = P.sb("ob", [128, 4, 512])
    xqT = P.sb("xqT", [128, 4, 512], BF16)
    mKT = P.sb("mKT", [128, 2, 4, 256], BF16)
    mV = P.sb("mV", [128, 2, 2, 4, 130], BF16)
    e1 = P.sb("e1", [128, 512])
    Cst = P.sb("Cst", [128, 4, 130])
    Cbf = P.sb("Cbf", [128, 4, 130], BF16)
    mst = P.sb("mst", [4, 1])
    gtok = P.sb("gtok", [128, 4, 8])
    cA = P.sb("cA", [4, 8])
    cB = P.sb("cB", [4, 8])
    cM = P.sb("cM", [4, 9])
    cX = P.sb("cX", [4, 8])
    cD = P.sb("cD", [4, 8])
    cDx = P.sb("cDx", [4, 8, 4])
    decb = P.sb("decb", [128, 8, 4])
    den4 = P.sb("den4", [128, 4])
    den4s = [den4, P.sb("den4b", [128, 4])]
    rl = P.sb("rl", [128, 2])
    rl2 = [P.sb("rl2a", [128, 2]), P.sb("rl2b", [128, 2])]
    rl8s = [P.sb("rl8a", [128, 8]), P.sb("rl8b", [128, 8])]
    rl8c = [0]
    xoc = [0]
    halo = P.sb("halo", [128, 8, 30])

    ARW = 12288
    arena = nc.alloc_sbuf_tensor("arena", [128, ARW], F32)
    arena_tls = []

    class View:
        def __init__(self, name, ap):
            self.ap = ap
            self.b = Buf(name)
            arena_tls.append(self)

        def __getitem__(self, k):
            return self.ap[k]

    aoff = [0]

    def av(name, words, dt, pat=None, part=None, **kw):
        a = arena[:, aoff[0]:aoff[0] + words] if part is None else arena[0:part, aoff[0]:aoff[0] + words]
        aoff[0] += words
        assert aoff[0] <= ARW, (name, aoff[0])
        if dt == BF16:
            a = a.bitcast(BF16)
        if pat is not None:
            a = a.rearrange(pat, **kw)
        return View(name, a)

    aoff[0] = 0
    aqT = av("aqT", 1024, BF16, "p (h t) -> p h t", h=4)
    akT = av("akT", 1024, BF16, "p (h t) -> p h t", h=4)
    ga = av("ga", 2048, F32, "p (j c) -> p j c", j=4)
    Stil = av("Stil", 256, BF16, "p (h t) -> p h t", h=4)
    hraw = av("hraw", 512, F32)
    kw = av("kw", 1024, BF16, "p (j h d) -> p j h d", j=4, h=4)
    avx = av("avx", 1040, BF16, "p (j h e) -> p j h e", j=4, h=4)
    gI = av("gI", 512, F32, part=4)
    gF = av("gF", 512, F32, part=4)
    gB = av("gB", 512, F32, part=4)
    gA = av("gA", 512, F32, part=4)
    gZ = av("gZ", 512, F32, part=4)
    gWe = av("gWe", 512, F32, part=4)
    gTh = av("gTh", 512, F32, part=4)
    hraws = [hraw, av("hraw2", 512, F32)]
    aoff[0] = 0
    KTo = av("KTo", 1024, BF16, "p (h t) -> p h t", h=4)
    Vo = av("Vo", 1040, BF16, "p (j h e) -> p j h e", j=4, h=4)
    assert aoff[0] <= 4864
    bqT = av("bqT", 1024, BF16, "p (h t) -> p h t", h=4)
    rK = [av(f"rK{i}", 1024, BF16) for i in range(2)]
    rV = [av(f"rV{i}", 1040, BF16, "p (b e) -> p b e", b=16) for i in range(2)]
    rctr = [0]
    aoff[0] = 0
    uT = av("uT", 2168, BF16, "p (c t) -> p c t", c=8)
    dgr = [av(f"dg{i}", 64, BF16) for i in range(8)]
    dgc = [0]
    cacc = av("cacc", 4096, F32, "p (c t) -> p c t", c=8)
    szT = av("szT", 2048, BF16, "p (c t) -> p c t", c=8)
    e2 = av("e2", 512, F32)
    lmean = av("lmean", 512, F32)
    lrstd = av("lrstd", 512, F32)

    def switch(phase):
        flush()
        I("pool", "memset", dummy[:, 0:1], 0.0, w=[dummy] + arena_tls)
        if phase in ("A", "AB"):
            I("pool", "memset", avx[:, :, :, 128:130], 1.0, w=[avx])
        if phase in ("B", "AB"):
            I("pool", "memset", Vo[:, :, :, 128:130], 1.0, w=[Vo])
        if phase == "B":
            for r_ in rV:
                I("pool", "memset", r_[:, :, 128:130], 1.0, w=[r_])

    dma("sp", identf[:], c_ident, w=[identf])
    I("dve", "tensor_copy", out=identb[:], in_=identf[:], r=[identf], w=[identb])
    I("pool", "memset", onesf[:], 1.0, w=[onesf])
    I("pool", "memset", epsc[:], EPS, w=[epsc])
    I("pool", "memset", lnk[:], math.log(128.0 ** -0.5), w=[lnk])
    for l in range(2):
        dma("sp", ng[:, l, :], norm_g[l].rearrange("(c p) -> p c", p=128), w=[ng], allow_slow_non_contiguous=True)
        dma("sp", mng[:, l, :], mem_norm_g[l].rearrange("(c p) -> p c", p=128), w=[mng], allow_slow_non_contiguous=True)
        dma("sp", g_xq[:, l, :], xq_g[l].partition_broadcast(128), w=[g_xq])
        dma("sp", g_xk[:, l, :], xk_g[l].partition_broadcast(128), w=[g_xk])
    dma("sp", g_mlstm[:], mlstm_g.partition_broadcast(128), w=[g_mlstm])
    dma("sp", g_qn[:], qn_g.partition_broadcast(128), w=[g_qn])
    dma("sp", g_kn[:], kn_g.partition_broadcast(128), w=[g_kn])
    dma("sp", g_subln[:], subln_g.partition_broadcast(128), w=[g_subln])
    dma("sp", lam_t[:].rearrange("p a d -> p (a d)"), lamv.rearrange("a d -> (a d)").partition_broadcast(128), w=[lam_t])
    dma("sp", big[:], b_ig, w=[big])
    dma("sp", bfg[:], b_fg, w=[bfg])
    dma("sp", flag[:], c_flag, w=[flag])
    dma("sp", biasT[:], c_bias, w=[biasT])
    dma("sp", dmask[:], c_dmask, w=[dmask])
    dma("sp", amask[:], c_amask, w=[amask])
    dma("sp", hsel[:], c_hsel, w=[hsel])
    convert_weights(["w_mem0", "w_mem1"])
    convert_wa([0, 1, 2])
    convert_wa([3, 4, 5, 6])
    convert_weights(["w_out_a", "w_in_c", "w_out_c"])
    dma("pool", wgate[:], wbf["w_in_a"].rearrange("(c p) n -> p c n", p=128)[:, :, 2560:2568], r=wdeps("w_in_a", 2560, 2568), w=[wgate])
    dma("sp", cw[:], conv_wT.rearrange("(c p) j -> p c j", p=128), w=[cw])
    dma("sp", cb[:], conv_b.rearrange("(c p) -> p c", p=128), w=[cb], allow_slow_non_contiguous=True)
    dma("sp", lg[:], ln_g.rearrange("(c p) -> p c", p=128), w=[lg], allow_slow_non_contiguous=True)
    dma("sp", lb[:], ln_b.rearrange("(c p) -> p c", p=128), w=[lb], allow_slow_non_contiguous=True)
    I("dve", "tensor_scalar", out=g_subln[:], in0=g_subln[:], scalar1=1.0 - LAM_INIT0, scalar2=None, op0=ALU.mult, r=[g_subln], w=[g_subln])
    I("dve", "tensor_scalar", out=bfg[:], in0=bfg[:], scalar1=-1.0, scalar2=None, op0=ALU.mult, r=[bfg], w=[bfg])
    I("dve", "tensor_tensor", out=lam_t[:, 0, :], in0=lam_t[:, 0, :], in1=lam_t[:, 1, :], op=ALU.mult, r=[lam_t], w=[lam_t])
    I("dve", "tensor_tensor", out=lam_t[:, 2, :], in0=lam_t[:, 2, :], in1=lam_t[:, 3, :], op=ALU.mult, r=[lam_t], w=[lam_t])
    I("dve", "tensor_reduce", out=lam_s[:, 0:1], in_=lam_t[:, 0, :], axis=AX.X, op=ALU.add, r=[lam_t], w=[lam_s])
    I("dve", "tensor_reduce", out=lam_s[:, 1:2], in_=lam_t[:, 2, :], axis=AX.X, op=ALU.add, r=[lam_t], w=[lam_s])
    I("act", "activation", out=lam_s[:, 2:4], in_=lam_s[:, 0:2], func=AF.Exp, r=[lam_s], w=[lam_s])
    I("dve", "tensor_tensor", out=nlam[:], in0=lam_s[:, 3:4], in1=lam_s[:, 2:3], op=ALU.subtract, r=[lam_s], w=[nlam])
    I("dve", "tensor_scalar", out=nlam[:], in0=nlam[:], scalar1=-LAM_INIT0, scalar2=None, op0=ALU.add, r=[nlam], w=[nlam])
    I("pool", "memset", mV[:], 1.0, w=[mV])
    I("pool", "memset", halo[:], 0.0, w=[halo])

    def load_w(nm, c0, ncols, nkc):
        wt = wring[wctr[0] % NW]
        wctr[0] += 1
        v = wt[:, 0:nkc * ncols].rearrange("p (c n) -> p c n", c=nkc)
        dma("pool", v, wbf[nm].rearrange("(c p) n -> p c n", p=128)[:, :, c0:c0 + ncols], r=wdeps(nm, c0, c0 + ncols), w=[wt])
        return (wt, v)

    def make_xnT(nsb, gcol, src_fn, stats_done=False):
        flush()
        if not stats_done:
            xn_stats(nsb, src_fn)
        xn_T(nsb, gcol)

    def xn_stats(nsb, src_fn):
        for j in range(nsb):
            sap, stl = src_fn(j)
            I("act", "activation", out=sqs[:], in_=sap, func=AF.Square, accum_out=st1[:, j:j + 1], r=[stl], w=[sqs, st1])
            I("act", "activation", out=st2[:, j:j + 1], in_=st1[:, j:j + 1], func=AF.Ln, scale=1.0 / D, bias=epsc[:], r=[st1, epsc], w=[st2])
            I("act", "activation", out=st2[:, j:j + 1], in_=st2[:, j:j + 1], func=AF.Exp, scale=-0.5, r=[st2], w=[st2])
            I("dve", "tensor_scalar", out=xnb[:, j, :], in0=sap, scalar1=st2[:, j:j + 1], scalar2=None, op0=ALU.mult, r=[stl, st2], w=[xnb])
            if hasattr(src_fn, "release"):
                src_fn.release(j)

    def xn_T(nsb, gcol):
        N = 128 * nsb
        for kc in range(8):
            pt = ptb[kc % 2]
            for j in range(nsb):
                tr(pt[:, j * 128:(j + 1) * 128], xnb[:, j, kc * 128:(kc + 1) * 128], identb[:], r=[xnb, identb], w=[pt])
            if kc % 2 == 0:
                I("dve", "tensor_scalar", out=xnT[:, kc, 0:N], in0=pt[:, 0:N], scalar1=gcol[:, kc:kc + 1], scalar2=None, op0=ALU.mult, r=[pt, ng, mng], w=[xnT])
            else:
                I("act", "activation", out=xnT[:, kc, 0:N], in_=pt[:, 0:N], func=AF.Copy, scale=gcol[:, kc:kc + 1], r=[pt, ng, mng], w=[xnT])

    xplan = []
    for l_ in range(2 if do_l1 else 1):
        xplan += [(mem, 0), (mem, 128)]
    blk_ = 0
    for (n_, m_) in cfg["pre"]:
        xplan += [(x_pre, (blk_ + k_) * 128) for k_ in range(n_)]
        blk_ += n_
    blk_ = 0
    for n_ in cfg["own"]:
        xplan += [(x_own, (blk_ + k_) * 128) for k_ in range(n_)]
        blk_ += n_
    if do_smp:
        xplan += [(x_smp, 0)]
    xstate = {"issued": 0, "next": 0}

    def x_issue():
        i = xstate["issued"]
        if i < len(xplan):
            src_, row_ = xplan[i]
            t = xs[i % NXS]
            dma("sp", t[:], src_[row_:row_ + 128, :], w=[t])
            xstate["issued"] = i + 1

    def dram_src(src, row0):
        def fn(j):
            i = xstate["next"]
            assert xplan[i][0] is src and xplan[i][1] == row0 + j * 128, (i, row0, j)
            while xstate["issued"] <= i:
                x_issue()
            xstate["next"] = i + 1
            t = xs[i % NXS]
            return t[:], t
        fn.release = lambda j: x_issue()
        return fn

    def proj_tok(w, j, ncols, out_ps, nkc=8, coff=0):
        wt, wv = w
        for kc in range(nkc):
            mm(out_ps[:, 0:ncols], xnT[:, kc, j * 128:(j + 1) * 128], wv[:, kc, coff:coff + ncols], kc == 0, kc == nkc - 1, r=[xnT, wt], w=[out_ps])

    def proj_feat(w, c, N, out_ps, nkc=8):
        wt, wv = w
        for kc in range(nkc):
            mm(out_ps[:, 0:N], wv[:, kc, c * 128:(c + 1) * 128], xnT[:, kc, 0:N], kc == 0, kc == nkc - 1, r=[xnT, wt], w=[out_ps])

    def group_norm(src_ap, src_tl, ng_, gs, gain_ap, gain_tl, dst_ap, dst_tl):
        n = ng_ * gs
        if ng_ <= 4:
            for g_ in range(ng_):
                I("act", "activation", out=sqs[:, g_ * gs:(g_ + 1) * gs], in_=src_ap[:, g_ * gs:(g_ + 1) * gs], func=AF.Square, accum_out=st1[:, g_:g_ + 1],
                  r=[src_tl], w=[sqs, st1])
        else:
            I("act", "activation", out=sqs[:, 0:n], in_=src_ap, func=AF.Square, r=[src_tl], w=[sqs])
            I("dve", "tensor_reduce", out=st1[:, 0:ng_], in_=sqs[:, 0:n].rearrange("p (g d) -> p g d", g=ng_), axis=AX.X, op=ALU.add, r=[sqs], w=[st1])
        I("act", "activation", out=st2[:, 0:ng_], in_=st1[:, 0:ng_], func=AF.Ln, scale=1.0 / gs, bias=epsc[:], r=[st1, epsc], w=[st2])
        I("act", "activation", out=st2[:, 0:ng_], in_=st2[:, 0:ng_], func=AF.Exp, scale=-0.5, r=[st2], w=[st2])
        I("dve", "tensor_tensor", out=dst_ap.rearrange("p (g d) -> p g d", g=ng_), in0=src_ap.rearrange("p (g d) -> p g d", g=ng_),
          in1=bc(st2[:, 0:ng_].unsqueeze(2), [128, ng_, gs]), op=ALU.mult, r=[src_tl, st2], w=[dst_tl])
        I("dve", "tensor_tensor", out=dst_ap, in0=dst_ap, in1=gain_ap, op=ALU.mult, r=[dst_tl, gain_tl], w=[dst_tl])

    def silu_gate(ps_tl, ncols, dst_ap, dst_tl, mul_ap, mul_tl):
        I("act", "activation", out=e1[:, 0:ncols], in_=ps_tl[:, 0:ncols], func=AF.Silu, r=[ps_tl], w=[e1])
        I("dve", "tensor_tensor", out=dst_ap, in0=e1[:, 0:ncols], in1=mul_ap, op=ALU.mult, r=[e1, mul_tl], w=[dst_tl])

    pending = []

    def flush(keep=0):
        n = len(pending) - keep
        if n <= 0:
            return
        fs = pending[:n]
        del pending[:n]
        for f_ in fs:
            f_()

    def tok_to_T(src_ap, src_tl, dstT, j, dst_tl):
        flush(keep=1)
        tb = tkbs[tkc[0] % 2]
        tkc[0] += 1
        I("dve", "tensor_copy", out=tb[:], in_=src_ap, r=[src_tl], w=[tb])

        def later():
            pt = ptb[j % 2]
            for c in range(4):
                tr(pt[:, c * 128:(c + 1) * 128], tb[:, c * 128:(c + 1) * 128], identb[:], r=[tb, identb], w=[pt])
            I("act", "activation", out=dstT, in_=pt[:, 0:512].rearrange("p (c t) -> p c t", c=4), func=AF.Copy, r=[pt], w=[dst_tl])
        pending.append(later)

    def to_mixT(src_ap, src_tl, j, c0):
        tok_to_T(src_ap, src_tl, mixT[:, c0:c0 + 4, j * 128:(j + 1) * 128], j, mixT)

    def mem_kv(l):
        make_xnT(2, mng[:, l, :], dram_src(mem, 0))
        wk = load_w(f"w_mem{l}", 0, 512, 8)
        wv = load_w(f"w_mem{l}", 512, 512, 8)
        for j in range(2):
            proj_tok(wk, j, 512, pb[0])
            group_norm(pb[0][:, :], pb[0], 4, 128, g_xk[:, l, :], g_xk, tk0[:], tk0)
            dma("sp", o_pxk[l, j * 128:(j + 1) * 128, :], tk0[:], r=[tk0], final=True)
            tok_to_T(tk0[:], tk0, mKT[:, l, :, j * 128:(j + 1) * 128], j, mKT)
            proj_tok(wv, j, 512, pb[1])
            I("dve", "tensor_copy", out=tk1[:], in_=pb[1][:, :], r=[pb[1]], w=[tk1])
            I("act", "activation", out=mV[:, l, j, :, 0:128], in_=pb[1][:, :].rearrange("p (h d) -> p h d", h=4), func=AF.Copy, r=[pb[1]], w=[mV])
            dma("sp", o_pxv[l, j * 128:(j + 1) * 128, :], tk1[:], r=[tk1], final=True)

    def smp_mem_kv():
        for l in range(2):
            for j in range(2):
                dma("sp", tk0[:], c_xk[l, j * 128:(j + 1) * 128, :], w=[tk0])
                tok_to_T(tk0[:], tk0, mKT[:, l, :, j * 128:(j + 1) * 128], j, mKT)
                dma("pool", mV[:, l, j, :, 0:128], c_xv[l, j * 128:(j + 1) * 128, :].rearrange("p (h d) -> p h d", h=4), w=[mV])

    def x_q(l, nsb, wsrc, cq):
        wq = load_w(wsrc, cq, 512, 8)
        for j in range(nsb):
            tkq = tkrot[tkrc[0] % 3]
            tkrc[0] += 1
            proj_tok(wq, j, 512, pb[j % 4])
            group_norm(pb[j % 4][:, :], pb[j % 4], 4, 128, g_xq[:, l, :], g_xq, tkq[:], tkq)
            tok_to_T(tkq[:], tkq, xqT[:, :, j * 128:(j + 1) * 128], j, xqT)

    def x_branch(l, nsb, wsrc, cq, cz, c0, q_done=False):
        N = 128 * nsb
        if not q_done:
            x_q(l, nsb, wsrc, cq)
        wz = load_w(wsrc, cz, 512, 8)
        flush()
        def x_scores(h):
            pts = PT[pctr[0] % 2]
            pctr[0] += 1
            for mb in range(2):
                sp = pb[2 * (h % 2) + mb]
                mm(sp[:, 0:N], mKT[:, l, h, mb * 128:(mb + 1) * 128], xqT[:, h, 0:N], True, True, r=[mKT, xqT], w=[sp])
                I("act", "activation", out=pts[mb][:, 0:N], in_=sp[:, 0:N], func=AF.Exp, scale=128.0 ** -0.5, r=[sp], w=[pts[mb]])
            return pts

        def x_pv(h, pts):
            for j in range(nsb):
                oa = pb[4 + (xoc[0] % 3)]
                xoc[0] += 1
                for mb in range(2):
                    mm(oa[:, 0:129], pts[mb][:, j * 128:(j + 1) * 128], mV[:, l, mb, h, 0:129], mb == 0, mb == 1, r=[pts[mb], mV], w=[oa])
                rr = rl2[xoc[0] % 2]
                I("dve", "reciprocal", out=rr[:, 0:1], in_=oa[:, 128:129], r=[oa], w=[rr])
                I("dve", "tensor_scalar", out=ob[:, j, h * 128:(h + 1) * 128], in0=oa[:, 0:128], scalar1=rr[:, 0:1], scalar2=None, op0=ALU.mult, r=[oa, rr], w=[ob])

        prev_ = None
        for h in range(4):
            pts_h = x_scores(h)
            if prev_ is not None:
                x_pv(*prev_)
            prev_ = (h, pts_h)
        x_pv(*prev_)
        for j in range(nsb):
            proj_tok(wz, j, 512, pb[j % 4])
            silu_gate(pb[j % 4], 512, tk0[:], tk0, ob[:, j, :], ob)
            to_mixT(tk0[:], tk0, j, c0)

    def a_gates(nsb, Lc, nch_per_sb):
        flush()
        N = 128 * nsb
        nch = nsb * nch_per_sb
        CS = 128 // nch_per_sb
        for kc in range(8):
            mm(pb[0][0:4, 0:N], wgate[:, kc, 0:4], xnT[:, kc, 0:N], kc == 0, kc == 7, r=[xnT, wgate], w=[pb[0]])
        for kc in range(8):
            mm(pb[1][0:4, 0:N], wgate[:, kc, 4:8], xnT[:, kc, 0:N], kc == 0, kc == 7, r=[xnT, wgate], w=[pb[1]])
        I("dve", "tensor_scalar", out=gI[:, 0:N], in0=pb[0][0:4, 0:N], scalar1=big[:, 0:1], scalar2=None, op0=ALU.add, r=[pb[0], big], w=[gI])
        I("act", "activation", out=gF[:, 0:N], in_=pb[1][0:4, 0:N], func=AF.Exp, scale=-1.0, bias=bfg[:, 0:1], r=[pb[1], bfg], w=[gF])
        I("act", "activation", out=gF[:, 0:N], in_=gF[:, 0:N], func=AF.Ln, scale=1.0, bias=onesf[0:4, 0:1], r=[gF, onesf], w=[gF])
        I("pool", "memset", gZ[:, 0:N], 0.0, w=[gZ])
        I("pool", "memset", gB[:, 0:N], 0.0, w=[gB])
        for c in range(nch):
            sl = slice(c * CS, c * CS + Lc)
            I("dve", "tensor_tensor_scan", out=gB[:, sl], data0=gF[:, sl], data1=gZ[:, sl], initial=0.0, op0=ALU.add, op1=ALU.add, r=[gF, gZ], w=[gB])
        I("dve", "tensor_tensor", out=gA[:, 0:N], in0=gI[:, 0:N], in1=gB[:, 0:N], op=ALU.add, r=[gI, gB], w=[gA])
        gA3 = gA[:, 0:N].rearrange("p (c t) -> p c t", t=CS)
        gB3 = gB[:, 0:N].rearrange("p (c t) -> p c t", t=CS)
        I("dve", "tensor_reduce", out=cA[:, 0:nch], in_=gA3[:, :, 0:Lc], axis=AX.X, op=ALU.max, r=[gA], w=[cA])
        I("dve", "tensor_scalar", out=cB[:, 0:nch], in0=gB3[:, :, Lc - 1], scalar1=-1.0, scalar2=None, op0=ALU.mult, r=[gB], w=[cB])
        I("dve", "tensor_copy", out=cM[:, 0:1], in_=mst[:], r=[mst], w=[cM])
        I("dve", "tensor_tensor_scan", out=cM[:, 1:nch + 1], data0=cA[:, 0:nch], data1=cB[:, 0:nch], initial=mst[:, 0:1], op0=ALU.max, op1=ALU.add, r=[cA, cB, mst], w=[cM])
        I("dve", "tensor_copy", out=mst[:], in_=cM[:, nch:nch + 1], r=[cM], w=[mst])
        I("dve", "tensor_tensor", out=cX[:, 0:nch], in0=cM[:, 0:nch], in1=cA[:, 0:nch], op=ALU.max, r=[cM, cA], w=[cX])
        I("dve", "tensor_tensor", out=cD[:, 0:nch], in0=cM[:, 0:nch], in1=cX[:, 0:nch], op=ALU.subtract, r=[cM, cX], w=[cD])
        I("act", "activation", out=cD[:, 0:nch], in_=cD[:, 0:nch], func=AF.Exp, r=[cD], w=[cD])
        Xb = bc(cX[:, 0:nch].unsqueeze(2), [4, nch, CS])
        I("dve", "tensor_tensor", out=gA3, in0=gA3, in1=Xb, op=ALU.subtract, r=[gA, cX], w=[gA])
        I("act", "activation", out=gWe[:, 0:N], in_=gA[:, 0:N], func=AF.Exp, bias=lnk[0:4, 0:1], r=[gA, lnk], w=[gWe])
        I("dve", "tensor_tensor", out=gI[:, 0:N].rearrange("p (c t) -> p c t", t=CS), in0=gB3, in1=Xb, op=ALU.subtract, r=[gB, cX], w=[gI])
        I("act", "activation", out=gTh[:, 0:N], in_=gI[:, 0:N], func=AF.Exp, r=[gI], w=[gTh])
        for j in range(nsb):
            tr(pb[2][:, 0:4], gWe[:, j * 128:(j + 1) * 128], identf[0:4, 0:4], r=[gWe, identf], w=[pb[2]])
            tr(pb[2][:, 4:8], gTh[:, j * 128:(j + 1) * 128], identf[0:4, 0:4], r=[gTh, identf], w=[pb[2]])
            I("dve", "tensor_copy", out=gtok[:, j, :], in_=pb[2][:, 0:8], r=[pb[2]], w=[gtok])
        I("dve", "tensor_tensor", out=cDx[:, 0:nch, :], in0=bc(cD[:, 0:nch].unsqueeze(2), [4, nch, 4]),
          in1=hsel[:, :, 0:nch].rearrange("k h c -> k c h"), op=ALU.mult, r=[cD, hsel], w=[cDx])
        mm(pb[2][:, 0:nch * 4], onesf[0:4, :], cDx[:, 0:nch, :].rearrange("k c h -> k (c h)"), True, True, r=[onesf, cDx], w=[pb[2]])
        I("dve", "tensor_copy", out=decb[:, 0:nch, :], in_=pb[2][:, 0:nch * 4].rearrange("p (c h) -> p c h", h=4), r=[pb[2]], w=[decb])

    def evac(h, out_ap, in_ap, r, w):
        if h % 2:
            I("act", "activation", out=out_ap, in_=in_ap, func=AF.Copy, r=r, w=w)
        else:
            I("dve", "tensor_copy", out=out_ap, in_=in_ap, r=r, w=w)

    def a_branch(nsb, mode, wsrc, Lc=64, nch_per_sb=2, state_later=False):
        N = 128 * nsb
        full = mode != "pre"
        CS = 128 // nch_per_sb
        if full:
            wq = load_w(wsrc, 0, 512, 8)
            for h in range(4):
                proj_feat(wq, h, N, pb[h % 4])
                evac(h, aqT[:, h, 0:N], pb[h % 4][:, 0:N], [pb[h % 4]], [aqT])
        wk = load_w(wsrc, 512, 512, 8)
        wv = load_w(wsrc, 1024, 512, 8)
        a_gates(nsb, Lc, nch_per_sb)
        for j in range(nsb):
            proj_tok(wv, j, 512, pb[j % 4])
            I("act", "activation", out=avx[:, j, :, 0:128], in_=pb[j % 4][:, :].rearrange("p (h d) -> p h d", h=4), func=AF.Copy, r=[pb[j % 4]], w=[avx])
        if full:
            for h in range(4):
                proj_feat(wk, h, N, pb[h % 4])
                evac(h, akT[:, h, 0:N], pb[h % 4][:, 0:N], [pb[h % 4]], [akT])
        for j in range(nsb):
            proj_tok(wk, j, 512, pb[j % 4])
            I("dve", "tensor_tensor", out=kw[:, j, :, :], in0=pb[j % 4][:, :].rearrange("p (h d) -> p h d", h=4),
              in1=bc(gtok[:, j, 0:4].unsqueeze(2), [128, 4, 128]), op=ALU.mult, r=[pb[j % 4], gtok], w=[kw])
        if full:
            wo = load_w(wsrc, 1536, 512, 8)
            for j in range(nsb):
                proj_tok(wo, j, 512, pb[j % 4])
                I("act", "activation", out=ga[:, j, :], in_=pb[j % 4][:, :], func=AF.Sigmoid, r=[pb[j % 4]], w=[ga])
            wz = load_w(wsrc, 2048, 512, 8)
            for j in range(nsb):
                proj_tok(wz, j, 512, pb[j % 4])
                silu_gate(pb[j % 4], 512, ga[:, j, :], ga, ga[:, j, :], ga)
        if state_later:
            def later_state():
                k_ = 0
                for j in range(nsb):
                    for cc in range(nch_per_sb):
                        rows = slice(cc * CS, cc * CS + CS)
                        c = j * nch_per_sb + cc
                        ups = (pb[6], pb[3]) if k_ % 2 == 0 else (pb[4], pb[5])
                        k_ += 1
                        I("dve", "tensor_tensor", out=Cst[:], in0=Cst[:], in1=bc(decb[:, c, :].unsqueeze(2), [128, 4, 130]), op=ALU.mult, r=[Cst, decb], w=[Cst])
                        for h in range(4):
                            up = ups[h // 2]
                            hh = h % 2
                            mm(up[:, hh * 129:(hh + 1) * 129], kw[rows, j, h, :], avx[rows, j, h, 0:129], True, True, r=[kw, avx], w=[up])
                        for hp in range(2):
                            I("dve", "tensor_tensor", out=Cst[:, 2 * hp:2 * hp + 2, 0:129], in0=Cst[:, 2 * hp:2 * hp + 2, 0:129],
                              in1=ups[hp][:, 0:258].rearrange("p (h e) -> p h e", h=2), op=ALU.add, r=[Cst, ups[hp]], w=[Cst])
            return later_state
        prev_epi = [None]
        for j in range(nsb):
            accs = (pb[4], pb[5]) if j % 2 == 0 else (pb[1], pb[2])
            if full:
                for h in range(4):
                    mm(pb[0][:, h * 128:(h + 1) * 128], akT[:, h, j * 128:(j + 1) * 128], aqT[:, h, j * 128:(j + 1) * 128], True, True, r=[akT, aqT], w=[pb[0]])
                for h in range(4):
                    I("dve", "scalar_tensor_tensor", out=Stil[:, h, :], in0=pb[0][:, h * 128:(h + 1) * 128], scalar=gtok[:, j, h:h + 1], in1=amask[:],
                      op0=ALU.mult, op1=ALU.mult, r=[pb[0], gtok, amask], w=[Stil])
            for cc in range(nch_per_sb):
                rows = slice(cc * CS, cc * CS + CS)
                c = j * nch_per_sb + cc
                t0 = j * 128 + cc * CS
                if full:
                    I("dve", "tensor_tensor", out=Cbf[:], in0=Cst[:], in1=bc(decb[:, c, :].unsqueeze(2), [128, 4, 130]), op=ALU.mult, r=[Cst, decb], w=[Cbf])
                I("dve", "tensor_tensor", out=Cst[:], in0=Cst[:], in1=bc(decb[:, c, :].unsqueeze(2), [128, 4, 130]), op=ALU.mult, r=[Cst, decb], w=[Cst])
                if full:
                    for h in range(4):
                        ac = accs[h // 2]
                        hh = h % 2
                        mm(ac[rows, hh * 129:(hh + 1) * 129], aqT[:, h, t0:t0 + CS], Cbf[:, h, 0:129], True, False, r=[aqT, Cbf], w=[ac])
                        mm(ac[rows, hh * 129:(hh + 1) * 129], Stil[:, h, cc * CS:cc * CS + CS], avx[:, j, h, 0:129], False, True, r=[Stil, avx], w=[ac])
                ups = (pb[6], pb[3])
                for h in range(4):
                    up = ups[h // 2]
                    hh = h % 2
                    mm(up[:, hh * 129:(hh + 1) * 129], kw[rows, j, h, :], avx[rows, j, h, 0:129], True, True, r=[kw, avx], w=[up])
                for hp in range(2):
                    I("dve", "tensor_tensor", out=Cst[:, 2 * hp:2 * hp + 2, 0:129], in0=Cst[:, 2 * hp:2 * hp + 2, 0:129],
                      in1=ups[hp][:, 0:258].rearrange("p (h e) -> p h e", h=2), op=ALU.add, r=[Cst, ups[hp]], w=[Cst])
            if full:
                d4 = den4s[j % 2]
                hr = hraws[j % 2]
                for hp in range(2):
                    ac = accs[hp]
                    I("act", "activation", out=d4[:, 2 * hp:2 * hp + 2], in_=ac[:, 0:258].rearrange("p (h e) -> p h e", h=2)[:, :, 128], func=AF.Abs, r=[ac], w=[d4])

                def epi(j=j, d4=d4, hr=hr, accs=accs):
                    I("dve", "tensor_tensor", out=d4[:], in0=d4[:], in1=gtok[:, j, 4:8], op=ALU.max, r=[d4, gtok], w=[d4])
                    I("dve", "reciprocal", out=d4[:], in_=d4[:], r=[d4], w=[d4])
                    for hp in range(2):
                        ac = accs[hp]
                        I("dve", "tensor_tensor", out=hr[:, hp * 256:(hp + 1) * 256].rearrange("p (h e) -> p h e", h=2),
                          in0=ac[:, 0:258].rearrange("p (h e) -> p h e", h=2)[:, :, 0:128],
                          in1=bc(d4[:, 2 * hp:2 * hp + 2].unsqueeze(2), [128, 2, 128]), op=ALU.mult, r=[ac, d4], w=[hr])
                    group_norm(hr[:], hr, 4, 128, g_mlstm[:], g_mlstm, tk0[:], tk0)
                    I("dve", "tensor_tensor", out=tk0[:], in0=tk0[:], in1=ga[:, j, :], op=ALU.mult, r=[tk0, ga], w=[tk0])
                    to_mixT(tk0[:], tk0, j, 0)
                if prev_epi[0] is not None:
                    prev_epi[0]()
                prev_epi[0] = epi
        if full and prev_epi[0] is not None:
            prev_epi[0]()
            prev_epi[0] = None

    def scr_write(j, blk):
        flush()
        hf, bi = blk // 16, blk % 16
        dma("sp", scrK[:, hf, :, bi * 128:(bi + 1) * 128].rearrange("h p k -> p h k"), KTo[:, :, j * 128:(j + 1) * 128], r=[KTo], w=[scr_b[blk]])
        dma("sp", scrV[:, hf, :, bi * 130:(bi + 1) * 130].rearrange("h p e -> p h e"), Vo[:, j, :, :], r=[Vo], w=[scr_b[blk]])

    def b_kv(nsb, wsrc, blk0, out_row0, smp=False):
        wk = load_w(wsrc, 3080, 512, 8)
        for j in range(nsb):
            proj_tok(wk, j, 512, pb[j % 4])
            tkr = tk0 if j % 2 == 0 else tk2
            group_norm(pb[j % 4][:, :], pb[j % 4], 8, 64, g_kn[:], g_kn, tkr[:], tkr)
            if out_row0 is not None:
                if smp:
                    dma("sp", o_sk[0:32, :], tkr[0:32, :], r=[tkr], final=True)
                else:
                    dma("sp", o_pk[out_row0 + j * 128:out_row0 + (j + 1) * 128, :], tkr[:], r=[tkr], final=True)
            tok_to_T(tkr[:], tkr, KTo[:, :, j * 128:(j + 1) * 128], j, KTo)
        wv = load_w(wsrc, 3592, 512, 8)
        for j in range(nsb):
            proj_tok(wv, j, 512, pb[j % 4])
            if out_row0 is not None:
                I("dve", "tensor_copy", out=tk1[:], in_=pb[j % 4][:, :], r=[pb[j % 4]], w=[tk1])
                if smp:
                    dma("sp", o_sv[0:32, :], tk1[0:32, :], r=[tk1], final=True)
                else:
                    dma("sp", o_pv[out_row0 + j * 128:out_row0 + (j + 1) * 128, :], tk1[:], r=[tk1], final=True)
            I("act", "activation", out=Vo[:, j, :, 0:128], in_=pb[j % 4][:, :].rearrange("p (h d) -> p h d", h=4), func=AF.Copy, r=[pb[j % 4]], w=[Vo])
        if blk0 is not None:
            for j in range(nsb):
                scr_write(j, blk0 + j)

    def b_attn(nsb, wsrc, tile_blk, n_masked, diag_kind):
        N = 128 * nsb
        wq = load_w(wsrc, 2568, 512, 8)
        for j in range(nsb):
            proj_tok(wq, j, 512, pb[j % 4])
            group_norm(pb[j % 4][:, :], pb[j % 4], 8, 64, g_qn[:], g_qn, tk0[:], tk0)
            tok_to_T(tk0[:], tk0, bqT[:, :, j * 128:(j + 1) * 128], j, bqT)
        wz = load_w(wsrc, 4104, 512, 8)
        flush()
        items = []
        for h in range(4):
            nsub = NSUBH[h] if nsb == 4 else nsb
            gsz = nsb // nsub
            blocks = []
            ring_loads = []
            for hf in range((tile_blk + 15) // 16):
                nb_h = min(16, tile_blk - hf * 16)
                ring_loads.append((hf, nb_h))
            items.append(("head", h, nsub, gsz, ring_loads))

        def emit_qk(it):
            (h, nsub, gsz, tbl, kap, vap, kdep, vdep, eb, dj, bidx, pts, sps) = it
            q0 = 0 if dj is None else dj
            for m in range(2):
                mm(sps[m][:, q0 * 128:N], kap[m * 64:(m + 1) * 64, :], bqT[m * 64:(m + 1) * 64, h, q0 * 128:N], True, True, r=[kdep, bqT], w=[sps[m]])
            if dj is not None:
                for m in range(2):
                    I("dve", "tensor_tensor", out=sps[m][:, dj * 128:(dj + 1) * 128], in0=sps[m][:, dj * 128:(dj + 1) * 128],
                      in1=dmask[:, diag_kind, h, :], op=ALU.add, r=[sps[m], dmask], w=[sps[m]])
            for m in range(2):
                for g in range(nsub):
                    qa, qe = max(g * gsz, q0), (g + 1) * gsz
                    if qe <= qa:
                        continue
                    delta = (tile_blk + g * gsz) - eb
                    col = h * NDELTA + delta + DOFF
                    I("act", "activation", out=pts[m][:, qa * 128:qe * 128], in_=sps[m][:, qa * 128:qe * 128], func=AF.Exp, scale=0.125,
                      bias=biasT[:, tbl, col:col + 1], r=[sps[m], biasT], w=[pts[m]])

        def emit_pv(it):
            (h, nsub, gsz, tbl, kap, vap, kdep, vdep, eb, dj, bidx, pts, sps) = it
            q0 = 0 if dj is None else dj
            for qs in range(q0, nsb):
                for m in range(2):
                    rg = qs * 2 + m
                    oa = pb[4 + rg // 3]
                    o0 = (rg % 3) * 129
                    last = (dj is not None and dj == qs)
                    mm(oa[:, o0:o0 + 129], pts[m][:, qs * 128:(qs + 1) * 128], vap, (bidx == 0 and rg % 3 == 0), last, r=[pts[m], vdep], w=[oa],
                       skip_group_check=True)

        def emit_epilogue(h):
            nreg = 2 * nsb
            rl8 = rl8s[rl8c[0] % 2]
            rl8c[0] += 1
            for bk in range((nreg + 2) // 3):
                nr = min(3, nreg - 3 * bk)
                oa = pb[4 + bk]
                I("dve", "reciprocal", out=rl8[:, 3 * bk:3 * bk + nr], in_=oa[:, 0:nr * 129].rearrange("p (r e) -> p r e", e=129)[:, :, 128], r=[oa], w=[rl8])
            I("dve", "tensor_tensor", out=rl8[:, 0:nreg].rearrange("p (q m) -> p q m", m=2)[:, :, 1], in0=rl8[:, 0:nreg].rearrange("p (q m) -> p q m", m=2)[:, :, 1],
              in1=bc(nlam[:, 0:1], [128, nsb]), op=ALU.mult, r=[rl8, nlam], w=[rl8])
            for qs in range(nsb):
                r0, r1 = qs * 2, qs * 2 + 1
                oa0, oa1 = pb[4 + r0 // 3], pb[4 + r1 // 3]
                a0, a1 = (r0 % 3) * 129, (r1 % 3) * 129
                I("dve", "tensor_scalar", out=ob[:, qs, h * 128:(h + 1) * 128], in0=oa0[:, a0:a0 + 128], scalar1=rl8[:, r0:r0 + 1], scalar2=None, op0=ALU.mult, r=[oa0, rl8], w=[ob])
                I("dve", "scalar_tensor_tensor", out=ob[:, qs, h * 128:(h + 1) * 128], in0=oa1[:, a1:a1 + 128], scalar=rl8[:, r1:r1 + 1], in1=ob[:, qs, h * 128:(h + 1) * 128],
                  op0=ALU.mult, op1=ALU.add, r=[oa1, rl8, ob], w=[ob])

        prev = None
        for (_, h, nsub, gsz, ring_loads) in items:
            blocks = []
            first_eb = 0
            for eb_ in range(tile_blk):
                if SLOPES[h] * (128 * (tile_blk - eb_) - 127) > 56.0:
                    first_eb = eb_ + 1
            for (hf, nb_h) in ring_loads:
                b0 = max(0, first_eb - hf * 16)
                if b0 >= nb_h:
                    continue
                slot = rctr[0] % 2
                rctr[0] += 1
                deps = [scr_b[hf * 16 + bi] for bi in range(b0, nb_h)]
                dma("sp", rK[slot][:, b0 * 128:nb_h * 128], scrK[h, hf, :, b0 * 128:nb_h * 128], r=deps, w=[rK[slot]])
                dma("sp", rV[slot][:, b0:nb_h, :], scrV[h, hf, :, b0 * 130:nb_h * 130].rearrange("p (b e) -> p b e", e=130), r=deps, w=[rV[slot]])
                for bi in range(b0, nb_h):
                    eb = hf * 16 + bi
                    blocks.append((1 if eb < n_masked else 0, rK[slot][:, bi * 128:(bi + 1) * 128], rV[slot][:, bi, 0:129], rK[slot], rV[slot], eb, None))
            for dj in range(nsb):
                blocks.append((0, KTo[:, h, dj * 128:(dj + 1) * 128], Vo[:, dj, h, 0:129], KTo, Vo, tile_blk + dj, dj))
            for bidx, (tbl, kap, vap, kdep, vdep, eb, dj) in enumerate(blocks):
                pts = PT[pctr[0] % 2]
                sps = [pb[(pctr[0] % 2) * 2 + m] for m in range(2)]
                pctr[0] += 1
                it = (h, nsub, gsz, tbl, kap, vap, kdep, vdep, eb, dj, bidx, pts, sps)
                emit_qk(it)
                if prev is not None:
                    emit_pv(prev)
                    if prev[0] != h:
                        emit_epilogue(prev[0])
                prev = it
        emit_pv(prev)
        emit_epilogue(prev[0])
        for j in range(nsb):
            group_norm(ob[:, j, :], ob, 4, 128, g_subln[:], g_subln, tk1[:], tk1)
            proj_tok(wz, j, 512, pb[j % 4])
            silu_gate(pb[j % 4], 512, tk0[:], tk0, tk1[:], tk1)
            to_mixT(tk0[:], tk0, j, 4)

    def out_proj(nsb, wsrc, res_fn, dst_fn):
        flush()
        wos = [[load_w(wsrc, cg * 512 + hh * 256, 256, 12) for hh in range(2)] for cg in range(2)]
        k_ = 0
        for j in range(nsb):
            for cg in range(2):
                woh = wos[cg]
                ps_ = pb[k_ % 4]
                k_ += 1
                rap, rtl = res_fn(j, cg)
                for hh in range(2):
                    wt, wv = woh[hh]
                    for kc in range(12):
                        mm(ps_[:, hh * 256:(hh + 1) * 256], mixT[:, kc, j * 128:(j + 1) * 128], wv[:, kc, :], kc == 0, kc == 11, r=[mixT, wt], w=[ps_])
                dap, dtl = dst_fn(j, cg)
                I("dve", "tensor_tensor", out=dap, in0=ps_[:, :], in1=rap, op=ALU.add, r=[ps_, rtl], w=[dtl])
                dst_done(j, cg)

    dst_done_fn = [lambda j, cg: None]

    def dst_done(j, cg):
        dst_done_fn[0](j, cg)

    def conv_out(dst, t0, t1):
        flush()
        j = (t1 - 1) // 128
        for c in range(8):
            tr(pb[4 + c // 4][:, (c % 4) * 128:(c % 4 + 1) * 128], cacc[:, c, 0:128], identf[:], r=[cacc, identf], w=[pb[4 + c // 4]])
        for hp in range(2):
            I("dve", "tensor_copy", out=sqs[:, hp * 512:(hp + 1) * 512], in_=pb[4 + hp][:, :], r=[pb[4 + hp]], w=[sqs])
        r0 = t0 - j * 128
        dma("sp", dst, sqs[r0:r0 + 30, :], r=[sqs], final=True)

    def layer1(nsb, y_dst, nrows, halo_only=False, smp=False, pconv=False, before_out=None):
        N = 128 * nsb
        make_xnT(nsb, ng[:, 1, :], lambda j: (y0[:, j, :], y0b[j]))
        switch("L")
        I("dve", "tensor_copy", out=uT[:, :, 0:30], in_=halo[:], r=[halo], w=[uT])
        for c2 in range(4):
            wu = load_w("w_in_c", c2 * 256, 256, 8)
            wg = load_w("w_in_c", 1024 + c2 * 256, 256, 8)
            for cc in range(2):
                c = c2 * 2 + cc
                pu, pg = pb[2 * (c % 2)], pb[2 * (c % 2) + 1]
                proj_feat(wu, cc, N, pu)
                proj_feat(wg, cc, N, pg)
                I("act", "activation", out=e2[:, 0:N], in_=pg[:, 0:N], func=AF.Sigmoid, r=[pg], w=[e2])
                I("dve", "tensor_tensor", out=uT[:, c, 30:30 + N], in0=pu[:, 0:N], in1=e2[:, 0:N], op=ALU.mult, r=[pu, e2], w=[uT])
                if smp or pconv:
                    I("dve", "tensor_tensor", out=cacc[:, c, 0:128], in0=pu[:, N - 128:N], in1=e2[:, N - 128:N], op=ALU.mult, r=[pu, e2], w=[cacc])
        if smp:
            conv_out(o_sconv, 2, 32)
        if pconv:
            conv_out(o_pconv, N - 30, N)
        I("dve", "tensor_copy", out=halo[:], in_=uT[:, :, N:N + 30], r=[uT], w=[halo])
        if halo_only:
            return
        x_q(1, nsb, "w_in_c", 3072)
        for c in range(8):
            psc = pb[c % 4]
            for jt in range(31):
                dg = dgr[dgc[0] % 8]
                dgc[0] += 1
                I("dve", "tensor_scalar", out=dg[:, :], in0=identb[:], scalar1=cw[:, c, jt:jt + 1], scalar2=None, op0=ALU.mult, r=[identb, cw], w=[dg])
                mm(psc[:, 0:N], dg[:, :], uT[:, c, jt:jt + N], jt == 0, jt == 30, r=[dg, uT], w=[psc])
            I("act", "activation", out=cacc[:, c, 0:N], in_=psc[:, 0:N], func=AF.Identity, bias=cb[:, c:c + 1], r=[psc, cb], w=[cacc])
        for c in range(8):
            mm(pb[2][:, 0:N], onesf[:], cacc[:, c, 0:N], c == 0, c == 7, r=[onesf, cacc], w=[pb[2]])
        for c in range(8):
            I("act", "activation", out=sqs[:, 0:N], in_=cacc[:, c, 0:N], func=AF.Square, r=[cacc], w=[sqs])
            mm(pb[3][:, 0:N], onesf[:], sqs[:, 0:N], c == 0, c == 7, r=[onesf, sqs], w=[pb[3]])
        I("dve", "tensor_scalar", out=lmean[:, 0:N], in0=pb[2][:, 0:N], scalar1=1.0 / D, scalar2=None, op0=ALU.mult, r=[pb[2]], w=[lmean])
        I("dve", "tensor_tensor", out=e1[:, 0:N], in0=lmean[:, 0:N], in1=lmean[:, 0:N], op=ALU.mult, r=[lmean], w=[e1])
        I("dve", "scalar_tensor_tensor", out=e1[:, 0:N], in0=pb[3][:, 0:N], scalar=1.0 / D, in1=e1[:, 0:N], op0=ALU.mult, op1=ALU.subtract, r=[pb[3], e1], w=[e1])
        I("act", "activation", out=lrstd[:, 0:N], in_=e1[:, 0:N], func=AF.Ln, bias=epsc[:], r=[e1, epsc], w=[lrstd])
        I("act", "activation", out=lrstd[:, 0:N], in_=lrstd[:, 0:N], func=AF.Exp, scale=-0.5, r=[lrstd], w=[lrstd])
        wzs = [load_w("w_in_c", 2048, 512, 8), load_w("w_in_c", 2560, 512, 8)]
        for c in range(8):
            proj_feat(wzs[c // 4], c % 4, N, pb[c % 4])
            I("act", "activation", out=szT[:, c, 0:N], in_=pb[c % 4][:, 0:N], func=AF.Silu, r=[pb[c % 4]], w=[szT])
            I("dve", "tensor_tensor", out=cacc[:, c, 0:N], in0=cacc[:, c, 0:N], in1=lmean[:, 0:N], op=ALU.subtract, r=[cacc, lmean], w=[cacc])
            I("dve", "tensor_tensor", out=cacc[:, c, 0:N], in0=cacc[:, c, 0:N], in1=lrstd[:, 0:N], op=ALU.mult, r=[cacc, lrstd], w=[cacc])
            I("act", "activation", out=e1[:, 0:N], in_=cacc[:, c, 0:N], func=AF.Silu, scale=lg[:, c:c + 1], bias=lb[:, c:c + 1], r=[cacc, lg, lb], w=[e1])
            I("dve", "tensor_tensor", out=mixT[:, c, 0:N], in0=e1[:, 0:N], in1=szT[:, c, 0:N], op=ALU.mult, r=[e1, szT], w=[mixT])
        x_branch(1, nsb, "w_in_c", 3072, 3584, 8, q_done=True)

        cur = [None]

        def dst_fn(j, cg):
            cur[0] = tkrot[tkrc[0] % 3]
            tkrc[0] += 1
            return cur[0][:], cur[0]

        def done(j, cg):
            dma("sp", y_dst[j * 128:j * 128 + nrows, cg * 512:(cg + 1) * 512], cur[0][0:nrows, :], r=[cur[0]], final=True)
        dst_done_fn[0] = done
        if before_out is not None:
            before_out()
        out_proj(nsb, "w_out_c", lambda j, cg: (y0[:, j, cg * 512:(cg + 1) * 512], y0b[j]), dst_fn)
        dst_done_fn[0] = lambda j, cg: None

    class _Stop(Exception):
        pass

    def stage(n):
        if cfg.get("stage") == n:
            raise _Stop()

    try:
        for _ in range(NXS):
            x_issue()
        stage(1)
        mem_kv(0)
        stage(2)
        if do_l1:
            mem_kv(1)
        stage(3)
        I("dve", "memset", Cst[:], 0.0, w=[Cst])
        I("dve", "memset", mst[:], 0.0, w=[mst])

        def res_from(src, row0):
            def fn(j, cg):
                t_ = tkrot[tkrc[0] % 3]
                tkrc[0] += 1
                dma("sp", t_[:], src[row0 + j * 128:row0 + (j + 1) * 128, cg * 512:(cg + 1) * 512], w=[t_])
                return t_[:], t_
            return fn

        def full_tile(src, row0, nsb, tile_blk, n_masked, blk0, out_row0, y_dst, halo_only=False, pconv=False, stats_done=False, before_out=None):
            make_xnT(nsb, ng[:, 0, :], None if stats_done else dram_src(src, row0), stats_done=stats_done)
            switch("A")
            a_branch(nsb, "full", "w_in_a")
            switch("B")
            b_kv(nsb, "w_in_a", blk0, out_row0)
            x_q(0, nsb, "w_in_a", 4616)
            b_attn(nsb, "w_in_a", tile_blk, n_masked, 0)
            x_branch(0, nsb, "w_in_a", 4616, 5128, 8, q_done=True)
            out_proj(nsb, "w_out_a", res_from(src, row0), lambda j, cg: (y0[:, j, cg * 512:(cg + 1) * 512], y0b[j]))
            if do_l1:
                layer1(nsb, y_dst, 128, halo_only=halo_only, pconv=pconv, before_out=before_out)

        blk = 0
        pre_started = [False]
        for (nsb, mode) in cfg["pre"]:
            if mode == "pre":
                make_xnT(nsb, ng[:, 0, :], dram_src(x_pre, blk * 128))
                if not pre_started[0]:
                    switch("AB")
                    pre_started[0] = True
                st_later = a_branch(nsb, "pre", "w_in_a", state_later=True)
                b_kv(nsb, "w_in_a", blk, None)
                st_later()
                stage(4)
            else:
                assert nsb == 1
                full_tile(x_pre, blk * 128, 1, blk, NB_PRE, blk, None, None, halo_only=True)
                stage(5)
            blk += nsb
        if N_PRE > 0:
            I("dve", "tensor_tensor", out=mst[:], in0=mst[:], in1=flag[:], op=ALU.mult, r=[mst, flag], w=[mst])
        ob_ = 0
        nown = len(cfg["own"])
        for ti, nsb in enumerate(cfg["own"]):
            r0 = ob_ * 128
            last = ti == nown - 1
            hook = None
            if (not last) and do_l1:
                nsb2 = cfg["own"][ti + 1]
                r2 = (ob_ + nsb) * 128
                hook = (lambda nsb2=nsb2, r2=r2: xn_stats(nsb2, dram_src(x_own, r2)))
            full_tile(x_own, r0, nsb, NB_PRE + ob_, NB_PRE, (NB_PRE + ob_) if not last else None, r0, y_own[r0:r0 + 128 * nsb, :], pconv=last,
                      stats_done=(ti > 0 and do_l1), before_out=hook)
            ob_ += nsb
        dma("sp", o_pC.rearrange("h d e -> d h e"), Cst[:, :, 0:128], r=[Cst], final=True)
        dma("sp", o_pn.rearrange("h d -> d h"), Cst[:, :, 128], r=[Cst], final=True, allow_slow_non_contiguous=True)
        dma("sp", o_pm, mst[:], r=[mst], final=True)
        stage(6)

        if do_smp:
            smp_mem_kv()
            switch("B")
            for kb in range(16):
                tks = tkrot[kb % 3]
                dma("act", tks[:], c_k[kb * 128:(kb + 1) * 128, :], w=[tks])
                tok_to_T(tks[:], tks, KTo[:, :, (kb % 4) * 128:(kb % 4 + 1) * 128], kb, KTo)
                dma("pool", Vo[:, kb % 4, :, 0:128], c_v[kb * 128:(kb + 1) * 128, :].rearrange("p (h d) -> p h d", h=4), w=[Vo])
                scr_write(kb % 4, kb)
            dma("sp", Cst[:, :, 0:128], st_C.rearrange("h d e -> d h e"), w=[Cst])
            dma("sp", Cst[:, :, 128], st_n.rearrange("h d -> d h"), w=[Cst], allow_slow_non_contiguous=True)
            dma("sp", mst[:], st_m, w=[mst])
            I("pool", "memset", sqs[:], 0.0, w=[sqs])
            dma("sp", sqs[0:30, :], st_conv, w=[sqs])
            for c in range(8):
                tr(pb[4 + c // 4][:, (c % 4) * 128:(c % 4 + 1) * 128], sqs[:, c * 128:(c + 1) * 128], identf[:], r=[sqs, identf], w=[pb[4 + c // 4]])
            for hp in range(2):
                I("dve", "tensor_copy", out=halo[:, hp * 4:(hp + 1) * 4, :], in_=pb[4 + hp][:, :].rearrange("p (c t) -> p c t", c=4)[:, :, 0:30], r=[pb[4 + hp]], w=[halo])
            make_xnT(1, ng[:, 0, :], dram_src(x_smp, 0))
            switch("A")
            a_branch(1, "full", "w_in_a", Lc=32, nch_per_sb=1)
            dma("sp", o_sC.rearrange("h d e -> d h e"), Cst[:, :, 0:128], r=[Cst], final=True)
            dma("sp", o_sn.rearrange("h d -> d h"), Cst[:, :, 128], r=[Cst], final=True, allow_slow_non_contiguous=True)
            dma("sp", o_sm, mst[:], r=[mst], final=True)
            switch("B")
            b_kv(1, "w_in_a", None, 0, smp=True)
            b_attn(1, "w_in_a", 16, 0, 1)
            x_branch(0, 1, "w_in_a", 4616, 5128, 8)
            out_proj(1, "w_out_a", res_from(x_smp, 0), lambda j, cg: (y0[:, j, cg * 512:(cg + 1) * 512], y0b[j]))
            if do_l1:
                layer1(1, y_smp, 32, smp=True)

    except _Stop:
        pass
    flush()
    P.S.emit(final_waits=P.outs)
    return P


def host_consts(has_prefix):
    ident = np.eye(128, dtype=np.float32)
    a = np.arange(128, dtype=np.float64)
    bias = np.zeros((128, 2, 4 * NDELTA), np.float32)
    for h in range(4):
        for dl in range(-DOFF, NDELTA - DOFF):
            v = SLOPES[h] * (a - 128.0 * dl)
            bias[:, 0, h * NDELTA + dl + DOFF] = v
            bias[:, 1, h * NDELTA + dl + DOFF] = v + (0.0 if has_prefix else PMASK)
    ka = np.arange(128)[:, None]
    qc = np.arange(128)[None, :]
    dm = np.zeros((128, 2, 4, 128), np.float32)
    for h in range(4):
        corr = np.where(ka > qc, -16.0 * SLOPES[h] * (ka - qc), 0.0)
        vis = (ka // 64) <= (qc // 64)
        dm[:, 0, h, :] = np.where(vis, corr, -1.0e5)
        vis_s = (ka < 32) & (qc < 128)
        dm[:, 1, h, :] = np.where(vis_s, corr, -1.0e5)
    am = ((ka <= qc) & ((ka // 64) == (qc // 64))).astype(np.float32)
    hsel = np.zeros((4, 4, 128), np.float32)
    for h in range(4):
        hsel[h, h, :] = 1.0
    flag = np.full((4, 1), 1.0 if has_prefix else 0.0, np.float32)
    return dict(c_ident=ident, c_bias=bias, c_dmask=dm, c_amask=am, c_hsel=hsel, c_flag=flag)


FULL_CFG = dict(own=[4] * 8, pre=[(4, "pre")] * 7 + [(3, "pre"), (1, "halo")], sample=True, l1=True)
_CACHE = {}


def core_inputs(c, cfg, inp):
    f = lambda a: np.ascontiguousarray(a, dtype=np.float32)
    b, half = c // 2, c % 2
    n_own = 128 * sum(cfg["own"])
    n_pre = 128 * sum(n for n, _ in cfg["pre"])
    xp = inp["x_prompt"][b]
    d = {}
    d["x_own"] = f(xp[half * n_own:(half + 1) * n_own])
    if half == 1:
        d["x_pre"] = f(xp[n_own - n_pre:n_own])
    else:
        d["x_pre"] = np.zeros((max(n_pre, 128), D), np.float32)
    xs = np.zeros((128, D), np.float32)
    xs[:32] = inp["x_sample"][c]
    d["x_smp"] = xs
    d["mem"] = f(inp["mem_prompt"][b])
    d["c_xk"] = f(inp["cache_xk"][:, c].reshape(2, 256, 512))
    d["c_xv"] = f(inp["cache_xv"][:, c].reshape(2, 256, 512))
    d["c_k"] = f(inp["cache_k"][0, c].reshape(2048, 512))
    d["c_v"] = f(inp["cache_v"][0, c].reshape(2048, 512))
    d["st_C"] = f(inp["state_C"][0, c])
    d["st_n"] = f(inp["state_n"][0, c])
    d["st_m"] = f(inp["state_m"][0, c].reshape(4, 1))
    d["st_conv"] = f(inp["state_conv"][0, c])
    d["norm_g"] = f(inp["norm_g"])
    d["w_in_a"] = f(inp["w_in_a"][0])
    d["b_ig"] = f(inp["b_ig"][0].reshape(4, 1))
    d["b_fg"] = f(inp["b_fg"][0].reshape(4, 1))
    d["mlstm_g"] = f(inp["mlstm_norm_g"][0].reshape(512))
    d["qn_g"] = f(np.tile(inp["qn_g"][0], 8))
    d["kn_g"] = f(np.tile(inp["kn_g"][0], 8))
    d["lamv"] = f(np.stack([inp["lam_q1"][0], inp["lam_k1"][0], inp["lam_q2"][0], inp["lam_k2"][0]]))
    d["subln_g"] = f(np.tile(inp["subln_g"][0], 4))
    d["w_out_a"] = f(inp["w_out_a"][0])
    d["w_in_c"] = f(inp["w_in_c"][0])
    d["conv_wT"] = f(inp["conv_w"][0].T)
    d["conv_b"] = f(inp["conv_b"][0])
    d["ln_g"] = f(inp["conv_ln_g"][0])
    d["ln_b"] = f(inp["conv_ln_b"][0])
    d["w_out_c"] = f(inp["w_out_c"][0])
    d["mem_norm_g"] = f(inp["mem_norm_g"])
    d["w_mem_kv"] = f(inp["w_mem_kv"])
    d["xq_g"] = f(np.stack([np.tile(inp["xq_norm_g"][l], 4) for l in range(2)]))
    d["xk_g"] = f(np.stack([np.tile(inp["xk_norm_g"][l], 4) for l in range(2)]))
    d.update(host_consts(half == 1))
    return d


def kernel(**inp):
    inp = {k: np.asarray(v) for k, v in inp.items()}
    cfg = FULL_CFG
    if "prog" not in _CACHE:
        _CACHE["prog"] = build(cfg)
    P = _CACHE["prog"]
    in_maps = [core_inputs(c, cfg, inp) for c in range(8)]
    res = run_bass_kernel_spmd(P.nc, in_maps, core_ids=list(range(8))).results
    B, SEQ = 4, 8192
    yp = np.zeros((B, SEQ, D), np.float32)
    pk = np.zeros((1, B, SEQ, 4, 128), np.float32)
    pv = np.zeros((1, B, SEQ, 4, 128), np.float32)
    pxk = np.zeros((2, B, 256, 4, 128), np.float32)
    pxv = np.zeros((2, B, 256, 4, 128), np.float32)
    pC = np.zeros((1, B, 4, 128, 128), np.float32)
    pn = np.zeros((1, B, 4, 128), np.float32)
    pm = np.zeros((1, B, 4), np.float32)
    pconv = np.zeros((1, B, 30, D), np.float32)
    ys = np.zeros((8, 32, D), np.float32)
    sk = np.zeros((1, 8, 32, 4, 128), np.float32)
    sv = np.zeros((1, 8, 32, 4, 128), np.float32)
    sC = np.zeros((1, 8, 4, 128, 128), np.float32)
    sn = np.zeros((1, 8, 4, 128), np.float32)
    sm = np.zeros((1, 8, 4), np.float32)
    sconv = np.zeros((1, 8, 30, D), np.float32)
    for c in range(8):
        r = res[c]
        b, half = c // 2, c % 2
        sl = slice(half * 4096, (half + 1) * 4096)
        yp[b, sl] = r["y_own"]
        pk[0, b, sl] = r["o_pk"].reshape(4096, 4, 128)
        pv[0, b, sl] = r["o_pv"].reshape(4096, 4, 128)
        if half == 1:
            pxk[:, b] = r["o_pxk"].reshape(2, 256, 4, 128)
            pxv[:, b] = r["o_pxv"].reshape(2, 256, 4, 128)
            pC[0, b] = r["o_pC"]
            pn[0, b] = r["o_pn"]
            pm[0, b] = r["o_pm"].reshape(4)
            pconv[0, b] = r["o_pconv"]
        ys[c] = r["y_smp"]
        sk[0, c] = r["o_sk"].reshape(32, 4, 128)
        sv[0, c] = r["o_sv"].reshape(32, 4, 128)
        sC[0, c] = r["o_sC"]
        sn[0, c] = r["o_sn"]
        sm[0, c] = r["o_sm"].reshape(4)
        sconv[0, c] = r["o_sconv"]
    return (yp, ys, pxk, pxv, pk, pv, pC, pn, pm, pconv, sk, sv, sC, sn, sm, sconv)
```

```python
import math
import numpy as np
import concourse.bass as bass
import concourse.mybir as mybir
from concourse.bass_utils import run_bass_kernel_spmd

F32 = mybir.dt.float32
BF16 = mybir.dt.bfloat16
AF = mybir.ActivationFunctionType
ALU = mybir.AluOpType
AX = mybir.AxisListType

D = 1024
EPS = 1e-6
SLOPES = [2.0 ** (-8.0 * (h + 1) / 4) for h in range(4)]
NSUBH = [2, 1, 1, 1]
NDELTA = 72
DOFF = 3
PMASK = -200.0
LAM_INIT0 = 0.8 - 0.6 * math.exp(0.0)


class Buf:
    __slots__ = ("name", "last_w", "readers", "excl")

    def __init__(self, name="", excl=False):
        self.name = name
        self.last_w = None
        self.readers = []
        self.excl = excl


class Op:
    __slots__ = ("eng", "fn", "deps", "signal", "is_dma", "sem", "semval", "vc", "gi", "desc")


class Sched:
    def __init__(self, nc, n_dma_sems=10):
        self.nc = nc
        self.ops = {e: [] for e in ("pe", "act", "dve", "pool", "sp")}
        self.all = []
        self.n_dma_sems = n_dma_sems
        self.dma_rr = {q: 0 for q in ("sp", "act", "pool")}
        self.dma_last = {}

    def op(self, eng, fn, reads=(), writes=(), dma=False):
        import os as _os
        if len(self.all) >= int(_os.environ.get("MAXOPS", "100000000")):
            return None
        o = Op()
        o.eng = eng
        o.fn = fn
        o.is_dma = dma
        o.signal = False
        o.sem = None
        o.semval = 0
        o.gi = len(self.all)
        deps = set()
        reads = list(reads)
        writes = list(writes)
        for b in reads:
            if b.excl and b not in writes:
                writes.append(b)
        for b in reads:
            if b.last_w is not None:
                deps.add(b.last_w)
        for b in writes:
            if b.last_w is not None:
                deps.add(b.last_w)
            for r in b.readers:
                deps.add(r)
        if dma:
            key = (eng, self.dma_rr[eng] % self.n_dma_sems)
            self.dma_rr[eng] += 1
            prev = self.dma_last.get(key)
            if prev is not None:
                deps.add(prev)
            self.dma_last[key] = o
            o.sem = key
        deps.discard(o)
        o.deps = deps
        for b in reads:
            b.readers.append(o)
        for b in writes:
            b.last_w = o
            b.readers = []
        self.ops[eng].append(o)
        self.all.append(o)
        return o

    def emit(self, final_waits=()):
        nc = self.nc

        def pe_pe(o, d):
            return (not d.is_dma) and d.eng == "pe" and o.eng == "pe" and (not o.is_dma)

        for o in self.all:
            for d in o.deps:
                if d.is_dma or pe_pe(o, d):
                    continue
                d.signal = True
        for o in final_waits:
            if not o.is_dma:
                o.signal = True
        cnt = {}
        for o in self.all:
            if o.is_dma:
                cnt[o.sem] = cnt.get(o.sem, 0) + 16
                o.semval = cnt[o.sem]
            elif o.signal:
                o.sem = ("c", o.eng)
                cnt[o.sem] = cnt.get(o.sem, 0) + 1
                o.semval = cnt[o.sem]
        known = {e: {} for e in self.ops}
        plans = {}
        for o in self.all:
            kn = known[o.eng]
            waits = {}
            for d in o.deps:
                if pe_pe(o, d):
                    continue
                if kn.get(d.sem, 0) >= d.semval:
                    continue
                if waits.get(d.sem, 0) < d.semval:
                    waits[d.sem] = d.semval
            for d in o.deps:
                if pe_pe(o, d):
                    continue
                for s, v in d.vc.items():
                    if kn.get(s, 0) < v:
                        kn[s] = v
            for s in list(waits):
                others = 0
                for d in o.deps:
                    if pe_pe(o, d) or d.sem == s:
                        continue
                    others = max(others, d.vc.get(s, 0))
                if others >= waits[s]:
                    pass
            plans[o.gi] = waits
            vc = dict(kn)
            if o.sem is not None and vc.get(o.sem, 0) < o.semval:
                vc[o.sem] = o.semval
            o.vc = vc
        sems = {}
        for key in cnt:
            sems[key] = nc.alloc_semaphore("s_" + "_".join(str(k) for k in key))
        fw = [(o.sem, o.semval) for o in final_waits]
        self.n_waits = sum(len(p) for p in plans.values())
        self.plans = plans

        def run_engine(ename):
            def body(eng):
                for o in self.ops[ename]:
                    wl = list(plans[o.gi].items())
                    for s, v in wl[:-1]:
                        eng.wait_ge(sems[s], v)
                    ins = o.fn(eng)
                    if wl:
                        ins = ins._wait_ge(sems[wl[-1][0]], wl[-1][1])
                    if o.is_dma:
                        ins.then_inc(sems[o.sem], 16)
                    elif o.signal:
                        ins.then_inc(sems[o.sem], 1)
                if ename == "sp":
                    done = {}
                    for s, v in fw:
                        done[s] = max(done.get(s, 0), v)
                    for s, v in done.items():
                        eng.wait_ge(sems[s], v)
            return body

        with nc.Block() as block:
            block.tensor(run_engine("pe"))
            block.scalar(run_engine("act"))
            block.vector(run_engine("dve"))
            block.gpsimd(run_engine("pool"))
            block.sync(run_engine("sp"))


class Tl:
    def __init__(self, t, name):
        self.t = t
        self.b = Buf(name)

    def __getitem__(self, k):
        return self.t[k]


class Prog:
    def __init__(self, cfg):
        self.cfg = cfg
        self.nc = bass.Bass("TRN2", target_bir_lowering=False)
        self.S = Sched(self.nc)
        self.outs = []
        self.uid = 0
        self.din = {}
        self.dout = {}

    def sb(self, name, shape, dt=F32):
        return Tl(self.nc.alloc_sbuf_tensor(name, list(shape), dt), name)

    def ps(self, name, shape, dt=F32):
        t = Tl(self.nc.alloc_psum_tensor(name, list(shape), dt), name)
        t.b.excl = True
        return t

    def inp(self, name, shape):
        a = self.nc.dram_tensor(name, list(shape), F32, kind="ExternalInput").ap()
        self.din[name] = a
        return a

    def outp(self, name, shape):
        a = self.nc.dram_tensor(name, list(shape), F32, kind="ExternalOutput").ap()
        self.dout[name] = a
        return a

    @staticmethod
    def _b(xs):
        return [x.b if hasattr(x, "b") else x for x in xs]

    def I(self, eng, meth, *a, r=(), w=(), **kw):
        o = self.S.op(eng, lambda e: getattr(e, meth)(*a, **kw), self._b(r), self._b(w))
        if o is not None:
            o.desc = (eng, meth, str(kw.get("out", a[0] if a else ""))[:90], str(kw.get("func", kw.get("op", kw.get("op0", "")))))
        return o

    def dma(self, q, out, in_, r=(), w=(), final=False, **kw):
        o = self.S.op(q, lambda e: e.dma_start(out=out, in_=in_, **kw), self._b(r), self._b(w), dma=True)
        if o is not None:
            o.desc = (q, "dma", str(out)[:90], "")
        if final and o is not None:
            self.outs.append(o)
        return o

    def mm(self, out, lhsT, rhs, start, stop, r, w, **kw):
        return self.I("pe", "matmul", out, lhsT=lhsT, rhs=rhs, start=start, stop=stop, r=r, w=w, **kw)

    def tr(self, out, in_, ident, r, w):
        return self.I("pe", "transpose", out=out, in_=in_, identity=ident, r=r, w=w)


def bc(ap, shape):
    return ap.broadcast_to(list(shape))


def build(cfg):
    P = Prog(cfg)
    nc = P.nc
    I, dma, mm, tr = P.I, P.dma, P.mm, P.tr
    N_OWN = 128 * sum(cfg["own"])
    N_PRE = 128 * sum(n for n, _ in cfg["pre"])
    NB_OWN = N_OWN // 128
    NB_PRE = N_PRE // 128
    do_smp = cfg.get("sample", True)
    do_l1 = cfg.get("l1", True)

    x_own = P.inp("x_own", [N_OWN, D])
    x_pre = P.inp("x_pre", [max(N_PRE, 128), D])
    x_smp = P.inp("x_smp", [128, D])
    mem = P.inp("mem", [256, D])
    c_xk = P.inp("c_xk", [2, 256, 512])
    c_xv = P.inp("c_xv", [2, 256, 512])
    c_k = P.inp("c_k", [2048, 512])
    c_v = P.inp("c_v", [2048, 512])
    st_C = P.inp("st_C", [4, 128, 128])
    st_n = P.inp("st_n", [4, 128])
    st_m = P.inp("st_m", [4, 1])
    st_conv = P.inp("st_conv", [30, D])
    norm_g = P.inp("norm_g", [2, D])
    w_in_a = P.inp("w_in_a", [D, 5640])
    b_ig = P.inp("b_ig", [4, 1])
    b_fg = P.inp("b_fg", [4, 1])
    mlstm_g = P.inp("mlstm_g", [512])
    qn_g = P.inp("qn_g", [512])
    kn_g = P.inp("kn_g", [512])
    lamv = P.inp("lamv", [4, 64])
    subln_g = P.inp("subln_g", [512])
    w_out_a = P.inp("w_out_a", [1536, D])
    w_in_c = P.inp("w_in_c", [D, 4096])
    conv_wT = P.inp("conv_wT", [D, 31])
    conv_b = P.inp("conv_b", [D])
    ln_g = P.inp("ln_g", [D])
    ln_b = P.inp("ln_b", [D])
    w_out_c = P.inp("w_out_c", [1536, D])
    mem_norm_g = P.inp("mem_norm_g", [2, D])
    w_mem_kv = P.inp("w_mem_kv", [2, D, 1024])
    xq_g = P.inp("xq_g", [2, 512])
    xk_g = P.inp("xk_g", [2, 512])
    c_ident = P.inp("c_ident", [128, 128])
    c_bias = P.inp("c_bias", [128, 2, 4 * NDELTA])
    c_dmask = P.inp("c_dmask", [128, 2, 4, 128])
    c_amask = P.inp("c_amask", [128, 128])
    c_hsel = P.inp("c_hsel", [4, 4, 128])
    c_flag = P.inp("c_flag", [4, 1])

    y_own = P.outp("y_own", [N_OWN, D])
    y_smp = P.outp("y_smp", [32, D])
    o_pxk = P.outp("o_pxk", [2, 256, 512])
    o_pxv = P.outp("o_pxv", [2, 256, 512])
    o_pk = P.outp("o_pk", [N_OWN, 512])
    o_pv = P.outp("o_pv", [N_OWN, 512])
    o_pC = P.outp("o_pC", [4, 128, 128])
    o_pn = P.outp("o_pn", [4, 128])
    o_pm = P.outp("o_pm", [4, 1])
    o_pconv = P.outp("o_pconv", [30, D])
    o_sk = P.outp("o_sk", [32, 512])
    o_sv = P.outp("o_sv", [32, 512])
    o_sC = P.outp("o_sC", [4, 128, 128])
    o_sn = P.outp("o_sn", [4, 128])
    o_sm = P.outp("o_sm", [4, 1])
    o_sconv = P.outp("o_sconv", [30, D])

    wsrc32 = {"w_in_a": (w_in_a, [D, 5640]), "w_out_a": (w_out_a, [1536, D]), "w_in_c": (w_in_c, [D, 4096]),
              "w_out_c": (w_out_c, [1536, D]), "w_mem0": (w_mem_kv[0], [D, 1024]), "w_mem1": (w_mem_kv[1], [D, 1024])}
    wbf = {}
    wbf_b = {}
    for nm, (src32, shp) in wsrc32.items():
        wbf[nm] = nc.dram_tensor("bf_" + nm, shp, BF16, kind="Internal").ap()
        wbf_b[nm] = [Buf(f"wb_{nm}{i}") for i in range(shp[0] // 128)]

    def convert_weights(names):
        for nm in names:
            src32, shp = wsrc32[nm]
            for i in range(shp[0] // 128):
                dma("pool", wbf[nm][i * 128:(i + 1) * 128, :], src32[i * 128:(i + 1) * 128, :], w=[wbf_b[nm][i]])

    NBLK = max(NB_PRE + NB_OWN, 16)
    NHALF = (NBLK + 15) // 16
    scrK = nc.dram_tensor("scrK", [4, NHALF, 128, 2048], BF16, kind="Internal").ap()
    scrV = nc.dram_tensor("scrV", [4, NHALF, 128, 16 * 130], BF16, kind="Internal").ap()
    scr_b = [Buf(f"scr{i}") for i in range(NHALF * 16)]

    pb = [P.ps(f"pb{i}", [128, 512], F32) for i in range(7)]
    ptb_all = nc.alloc_psum_tensor("ptb_all", [128, 1024], BF16)

    class PV_:
        def __init__(self, name, ap):
            self.ap = ap
            self.b = Buf(name)

        def __getitem__(self, k):
            return self.ap[k]
    ptb = [PV_("ptb0", ptb_all[:, 0:512]), PV_("ptb1", pb[5][:, 0:256].bitcast(BF16))]
    ptb[0].b.excl = True
    ptb[1].b = pb[5].b
    identf = P.sb("identf", [128, 128])
    identb = P.sb("identb", [128, 128], BF16)
    onesf = P.sb("onesf", [128, 128])
    epsc = P.sb("epsc", [128, 1])
    lnk = P.sb("lnk", [128, 1])
    ng = P.sb("ng", [128, 2, 8])
    mng = P.sb("mng", [128, 2, 8])
    g_mlstm = P.sb("g_mlstm", [128, 512])
    g_qn = P.sb("g_qn", [128, 512])
    g_kn = P.sb("g_kn", [128, 512])
    g_subln = P.sb("g_subln", [128, 512])
    g_xq = P.sb("g_xq", [128, 2, 512])
    g_xk = P.sb("g_xk", [128, 2, 512])
    lam_t = P.sb("lam_t", [128, 4, 64])
    lam_s = P.sb("lam_s", [128, 4])
    nlam = P.sb("nlam", [128, 1])
    big = P.sb("big", [4, 1])
    bfg = P.sb("bfg", [4, 1])
    flag = P.sb("flag", [4, 1])
    biasT = P.sb("biasT", [128, 2, 4 * NDELTA])
    dmask = P.sb("dmask", [128, 2, 4, 128])
    amask = P.sb("amask", [128, 128])
    hsel = P.sb("hsel", [4, 4, 128])
    wgate = P.sb("wgate", [128, 8, 8], BF16)
    cw = P.sb("cw", [128, 8, 31])
    cb = P.sb("cb", [128, 8])
    lg = P.sb("lg", [128, 8])
    lb = P.sb("lb", [128, 8])
    dummy = P.sb("dummyt", [128, 2])

    NW = 4
    wring = [P.sb(f"wring{i}", [128, 4096], BF16) for i in range(NW)]
    wctr = [0]
    NXS = 3
    xs = [P.sb(f"xs{i}", [128, D]) for i in range(NXS)]
    y0 = P.sb("y0", [128, 4, D])
    y0b = [Buf(f"y0_{j}") for j in range(4)]
    xnb = P.sb("xnb", [128, 4, D], BF16)
    xnT = P.sb("xnT", [128, 8, 512], BF16)
    sqs = P.sb("sqs", [128, D])
    st1 = P.sb("st1", [128, 8])
    st2 = P.sb("st2", [128, 8])
    tk0 = P.sb("tk0", [128, 512])
    tk1 = P.sb("tk1", [128, 512])
    tk2 = P.sb("tk2", [128, 512])
    tkrot = [tk0, tk1, tk2]
    tkrc = [0]
    tkb = P.sb("tkb", [128, 512], BF16)
    tkbs = [tkb, P.sb("tkb2", [128, 512], BF16)]
    tkc = [0]
    mixT = P.sb("mixT", [128, 12, 512], BF16)
    PT = [[P.sb(f"PT{i}{m}", [128, 512], BF16) for m in range(2)] for i in range(2)]
    pctr = [0]
    ob = P.sb("ob", [128, 4, 512])
    xqT = P.sb("xqT", [128, 4, 512], BF16)
    mKT = P.sb("mKT", [128, 2, 4, 256], BF16)
    mV = P.sb("mV", [128, 2, 2, 4, 130], BF16)
    e1 = P.sb("e1", [128, 512])
    Cst = P.sb("Cst", [128, 4, 130])
    Cbf = P.sb("Cbf", [128, 4, 130], BF16)
    mst = P.sb("mst", [4, 1])
    gtok = P.sb("gtok", [128, 4, 8])
    cA = P.sb("cA", [4, 8])
    cB = P.sb("cB", [4, 8])
    cM = P.sb("cM", [4, 9])
    cX = P.sb("cX", [4, 8])
    cD = P.sb("cD", [4, 8])
    cDx = P.sb("cDx", [4, 8, 4])
    decb = P.sb("decb", [128, 8, 4])
    den4 = P.sb("den4", [128, 4])
    rl = P.sb("rl", [128, 2])
    rl2 = [P.sb("rl2a", [128, 2]), P.sb("rl2b", [128, 2])]
    rl8s = [P.sb("rl8a", [128, 8]), P.sb("rl8b", [128, 8])]
    rl8c = [0]
    xoc = [0]
    halo = P.sb("halo", [128, 8, 30])

    ARW = 12288
    arena = nc.alloc_sbuf_tensor("arena", [128, ARW], F32)
    arena_tls = []

    class View:
        def __init__(self, name, ap):
            self.ap = ap
            self.b = Buf(name)
            arena_tls.append(self)

        def __getitem__(self, k):
            return self.ap[k]

    aoff = [0]

    def av(name, words, dt, pat=None, part=None, **kw):
        a = arena[:, aoff[0]:aoff[0] + words] if part is None else arena[0:part, aoff[0]:aoff[0] + words]
        aoff[0] += words
        assert aoff[0] <= ARW, (name, aoff[0])
        if dt == BF16:
            a = a.bitcast(BF16)
        if pat is not None:
            a = a.rearrange(pat, **kw)
        return View(name, a)

    aoff[0] = 0
    aqT = av("aqT", 1024, BF16, "p (h t) -> p h t", h=4)
    akT = av("akT", 1024, BF16, "p (h t) -> p h t", h=4)
    ga = av("ga", 2048, F32, "p (j c) -> p j c", j=4)
    Stil = av("Stil", 256, BF16, "p (h t) -> p h t", h=4)
    hraw = av("hraw", 512, F32)
    kw = av("kw", 1024, BF16, "p (j h d) -> p j h d", j=4, h=4)
    avx = av("avx", 1040, BF16, "p (j h e) -> p j h e", j=4, h=4)
    gI = av("gI", 512, F32, part=4)
    gF = av("gF", 512, F32, part=4)
    gB = av("gB", 512, F32, part=4)
    gA = av("gA", 512, F32, part=4)
    gZ = av("gZ", 512, F32, part=4)
    gWe = av("gWe", 512, F32, part=4)
    gTh = av("gTh", 512, F32, part=4)
    aoff[0] = 0
    KTo = av("KTo", 1024, BF16, "p (h t) -> p h t", h=4)
    Vo = av("Vo", 1040, BF16, "p (j h e) -> p j h e", j=4, h=4)
    assert aoff[0] <= 4864
    bqT = av("bqT", 1024, BF16, "p (h t) -> p h t", h=4)
    rK = [av(f"rK{i}", 1024, BF16) for i in range(2)]
    rV = [av(f"rV{i}", 1040, BF16, "p (b e) -> p b e", b=16) for i in range(2)]
    rctr = [0]
    aoff[0] = 0
    uT = av("uT", 2168, BF16, "p (c t) -> p c t", c=8)
    dgr = [av(f"dg{i}", 64, BF16) for i in range(8)]
    dgc = [0]
    cacc = av("cacc", 4096, F32, "p (c t) -> p c t", c=8)
    szT = av("szT", 2048, BF16, "p (c t) -> p c t", c=8)
    e2 = av("e2", 512, F32)
    lmean = av("lmean", 512, F32)
    lrstd = av("lrstd", 512, F32)

    def switch(phase):
        flush()
        I("pool", "memset", dummy[:, 0:1], 0.0, w=[dummy] + arena_tls)
        if phase in ("A", "AB"):
            I("pool", "memset", avx[:, :, :, 128:130], 1.0, w=[avx])
        if phase in ("B", "AB"):
            I("pool", "memset", Vo[:, :, :, 128:130], 1.0, w=[Vo])
        if phase == "B":
            for r_ in rV:
                I("pool", "memset", r_[:, :, 128:130], 1.0, w=[r_])

    dma("sp", identf[:], c_ident, w=[identf])
    I("dve", "tensor_copy", out=identb[:], in_=identf[:], r=[identf], w=[identb])
    I("pool", "memset", onesf[:], 1.0, w=[onesf])
    I("pool", "memset", epsc[:], EPS, w=[epsc])
    I("pool", "memset", lnk[:], math.log(128.0 ** -0.5), w=[lnk])
    for l in range(2):
        dma("sp", ng[:, l, :], norm_g[l].rearrange("(c p) -> p c", p=128), w=[ng], allow_slow_non_contiguous=True)
        dma("sp", mng[:, l, :], mem_norm_g[l].rearrange("(c p) -> p c", p=128), w=[mng], allow_slow_non_contiguous=True)
        dma("sp", g_xq[:, l, :], xq_g[l].partition_broadcast(128), w=[g_xq])
        dma("sp", g_xk[:, l, :], xk_g[l].partition_broadcast(128), w=[g_xk])
    dma("sp", g_mlstm[:], mlstm_g.partition_broadcast(128), w=[g_mlstm])
    dma("sp", g_qn[:], qn_g.partition_broadcast(128), w=[g_qn])
    dma("sp", g_kn[:], kn_g.partition_broadcast(128), w=[g_kn])
    dma("sp", g_subln[:], subln_g.partition_broadcast(128), w=[g_subln])
    dma("sp", lam_t[:].rearrange("p a d -> p (a d)"), lamv.rearrange("a d -> (a d)").partition_broadcast(128), w=[lam_t])
    dma("sp", big[:], b_ig, w=[big])
    dma("sp", bfg[:], b_fg, w=[bfg])
    dma("sp", flag[:], c_flag, w=[flag])
    dma("sp", biasT[:], c_bias, w=[biasT])
    dma("sp", dmask[:], c_dmask, w=[dmask])
    dma("sp", amask[:], c_amask, w=[amask])
    dma("sp", hsel[:], c_hsel, w=[hsel])
    convert_weights(["w_mem0", "w_in_a", "w_out_a", "w_mem1", "w_in_c", "w_out_c"])
    dma("pool", wgate[:], wbf["w_in_a"].rearrange("(c p) n -> p c n", p=128)[:, :, 2560:2568], r=wbf_b["w_in_a"], w=[wgate])
    dma("sp", cw[:], conv_wT.rearrange("(c p) j -> p c j", p=128), w=[cw])
    dma("sp", cb[:], conv_b.rearrange("(c p) -> p c", p=128), w=[cb], allow_slow_non_contiguous=True)
    dma("sp", lg[:], ln_g.rearrange("(c p) -> p c", p=128), w=[lg], allow_slow_non_contiguous=True)
    dma("sp", lb[:], ln_b.rearrange("(c p) -> p c", p=128), w=[lb], allow_slow_non_contiguous=True)
    I("dve", "tensor_scalar", out=g_subln[:], in0=g_subln[:], scalar1=1.0 - LAM_INIT0, scalar2=None, op0=ALU.mult, r=[g_subln], w=[g_subln])
    I("dve", "tensor_scalar", out=bfg[:], in0=bfg[:], scalar1=-1.0, scalar2=None, op0=ALU.mult, r=[bfg], w=[bfg])
    I("dve", "tensor_tensor", out=lam_t[:, 0, :], in0=lam_t[:, 0, :], in1=lam_t[:, 1, :], op=ALU.mult, r=[lam_t], w=[lam_t])
    I("dve", "tensor_tensor", out=lam_t[:, 2, :], in0=lam_t[:, 2, :], in1=lam_t[:, 3, :], op=ALU.mult, r=[lam_t], w=[lam_t])
    I("dve", "tensor_reduce", out=lam_s[:, 0:1], in_=lam_t[:, 0, :], axis=AX.X, op=ALU.add, r=[lam_t], w=[lam_s])
    I("dve", "tensor_reduce", out=lam_s[:, 1:2], in_=lam_t[:, 2, :], axis=AX.X, op=ALU.add, r=[lam_t], w=[lam_s])
    I("act", "activation", out=lam_s[:, 2:4], in_=lam_s[:, 0:2], func=AF.Exp, r=[lam_s], w=[lam_s])
    I("dve", "tensor_tensor", out=nlam[:], in0=lam_s[:, 3:4], in1=lam_s[:, 2:3], op=ALU.subtract, r=[lam_s], w=[nlam])
    I("dve", "tensor_scalar", out=nlam[:], in0=nlam[:], scalar1=-LAM_INIT0, scalar2=None, op0=ALU.add, r=[nlam], w=[nlam])
    I("pool", "memset", mV[:], 1.0, w=[mV])
    I("pool", "memset", halo[:], 0.0, w=[halo])

    def load_w(nm, c0, ncols, nkc):
        wt = wring[wctr[0] % NW]
        wctr[0] += 1
        v = wt[:, 0:nkc * ncols].rearrange("p (c n) -> p c n", c=nkc)
        dma("pool", v, wbf[nm].rearrange("(c p) n -> p c n", p=128)[:, :, c0:c0 + ncols], r=wbf_b[nm], w=[wt])
        return (wt, v)

    def make_xnT(nsb, gcol, src_fn, stats_done=False):
        flush()
        if not stats_done:
            xn_stats(nsb, src_fn)
        xn_T(nsb, gcol)

    def xn_stats(nsb, src_fn):
        for j in range(nsb):
            sap, stl = src_fn(j)
            I("act", "activation", out=sqs[:], in_=sap, func=AF.Square, accum_out=st1[:, j:j + 1], r=[stl], w=[sqs, st1])
            I("act", "activation", out=st2[:, j:j + 1], in_=st1[:, j:j + 1], func=AF.Ln, scale=1.0 / D, bias=epsc[:], r=[st1, epsc], w=[st2])
            I("act", "activation", out=st2[:, j:j + 1], in_=st2[:, j:j + 1], func=AF.Exp, scale=-0.5, r=[st2], w=[st2])
            I("dve", "tensor_scalar", out=xnb[:, j, :], in0=sap, scalar1=st2[:, j:j + 1], scalar2=None, op0=ALU.mult, r=[stl, st2], w=[xnb])
            if hasattr(src_fn, "release"):
                src_fn.release(j)

    def xn_T(nsb, gcol):
        N = 128 * nsb
        for kc in range(8):
            pt = ptb[kc % 2]
            for j in range(nsb):
                tr(pt[:, j * 128:(j + 1) * 128], xnb[:, j, kc * 128:(kc + 1) * 128], identb[:], r=[xnb, identb], w=[pt])
            if kc % 2 == 0:
                I("dve", "tensor_scalar", out=xnT[:, kc, 0:N], in0=pt[:, 0:N], scalar1=gcol[:, kc:kc + 1], scalar2=None, op0=ALU.mult, r=[pt, ng, mng], w=[xnT])
            else:
                I("act", "activation", out=xnT[:, kc, 0:N], in_=pt[:, 0:N], func=AF.Copy, scale=gcol[:, kc:kc + 1], r=[pt, ng, mng], w=[xnT])

    xplan = []
    for l_ in range(2 if do_l1 else 1):
        xplan += [(mem, 0), (mem, 128)]
    blk_ = 0
    for (n_, m_) in cfg["pre"]:
        xplan += [(x_pre, (blk_ + k_) * 128) for k_ in range(n_)]
        blk_ += n_
    blk_ = 0
    for n_ in cfg["own"]:
        xplan += [(x_own, (blk_ + k_) * 128) for k_ in range(n_)]
        blk_ += n_
    if do_smp:
        xplan += [(x_smp, 0)]
    xstate = {"issued": 0, "next": 0}

    def x_issue():
        i = xstate["issued"]
        if i < len(xplan):
            src_, row_ = xplan[i]
            t = xs[i % NXS]
            dma("sp", t[:], src_[row_:row_ + 128, :], w=[t])
            xstate["issued"] = i + 1

    def dram_src(src, row0):
        def fn(j):
            i = xstate["next"]
            assert xplan[i][0] is src and xplan[i][1] == row0 + j * 128, (i, row0, j)
            while xstate["issued"] <= i:
                x_issue()
            xstate["next"] = i + 1
            t = xs[i % NXS]
            return t[:], t
        fn.release = lambda j: x_issue()
        return fn

    def proj_tok(w, j, ncols, out_ps, nkc=8, coff=0):
        wt, wv = w
        for kc in range(nkc):
            mm(out_ps[:, 0:ncols], xnT[:, kc, j * 128:(j + 1) * 128], wv[:, kc, coff:coff + ncols], kc == 0, kc == nkc - 1, r=[xnT, wt], w=[out_ps])

    def proj_feat(w, c, N, out_ps, nkc=8):
        wt, wv = w
        for kc in range(nkc):
            mm(out_ps[:, 0:N], wv[:, kc, c * 128:(c + 1) * 128], xnT[:, kc, 0:N], kc == 0, kc == nkc - 1, r=[xnT, wt], w=[out_ps])

    def group_norm(src_ap, src_tl, ng_, gs, gain_ap, gain_tl, dst_ap, dst_tl):
        n = ng_ * gs
        I("act", "activation", out=sqs[:, 0:n], in_=src_ap, func=AF.Square, r=[src_tl], w=[sqs])
        I("dve", "tensor_reduce", out=st1[:, 0:ng_], in_=sqs[:, 0:n].rearrange("p (g d) -> p g d", g=ng_), axis=AX.X, op=ALU.add, r=[sqs], w=[st1])
        I("act", "activation", out=st2[:, 0:ng_], in_=st1[:, 0:ng_], func=AF.Ln, scale=1.0 / gs, bias=epsc[:], r=[st1, epsc], w=[st2])
        I("act", "activation", out=st2[:, 0:ng_], in_=st2[:, 0:ng_], func=AF.Exp, scale=-0.5, r=[st2], w=[st2])
        I("dve", "tensor_tensor", out=dst_ap.rearrange("p (g d) -> p g d", g=ng_), in0=src_ap.rearrange("p (g d) -> p g d", g=ng_),
          in1=bc(st2[:, 0:ng_].unsqueeze(2), [128, ng_, gs]), op=ALU.mult, r=[src_tl, st2], w=[dst_tl])
        I("dve", "tensor_tensor", out=dst_ap, in0=dst_ap, in1=gain_ap, op=ALU.mult, r=[dst_tl, gain_tl], w=[dst_tl])

    def silu_gate(ps_tl, ncols, dst_ap, dst_tl, mul_ap, mul_tl):
        I("act", "activation", out=e1[:, 0:ncols], in_=ps_tl[:, 0:ncols], func=AF.Silu, r=[ps_tl], w=[e1])
        I("dve", "tensor_tensor", out=dst_ap, in0=e1[:, 0:ncols], in1=mul_ap, op=ALU.mult, r=[e1, mul_tl], w=[dst_tl])

    pending = []

    def flush(keep=0):
        n = len(pending) - keep
        if n <= 0:
            return
        fs = pending[:n]
        del pending[:n]
        for f_ in fs:
            f_()

    def tok_to_T(src_ap, src_tl, dstT, j, dst_tl):
        flush(keep=1)
        tb = tkbs[tkc[0] % 2]
        tkc[0] += 1
        I("dve", "tensor_copy", out=tb[:], in_=src_ap, r=[src_tl], w=[tb])

        def later():
            pt = ptb[j % 2]
            for c in range(4):
                tr(pt[:, c * 128:(c + 1) * 128], tb[:, c * 128:(c + 1) * 128], identb[:], r=[tb, identb], w=[pt])
            I("act", "activation", out=dstT, in_=pt[:, 0:512].rearrange("p (c t) -> p c t", c=4), func=AF.Copy, r=[pt], w=[dst_tl])
        pending.append(later)

    def to_mixT(src_ap, src_tl, j, c0):
        tok_to_T(src_ap, src_tl, mixT[:, c0:c0 + 4, j * 128:(j + 1) * 128], j, mixT)

    def mem_kv(l):
        make_xnT(2, mng[:, l, :], dram_src(mem, 0))
        wk = load_w(f"w_mem{l}", 0, 512, 8)
        wv = load_w(f"w_mem{l}", 512, 512, 8)
        for j in range(2):
            proj_tok(wk, j, 512, pb[0])
            group_norm(pb[0][:, :], pb[0], 4, 128, g_xk[:, l, :], g_xk, tk0[:], tk0)
            dma("sp", o_pxk[l, j * 128:(j + 1) * 128, :], tk0[:], r=[tk0], final=True)
            tok_to_T(tk0[:], tk0, mKT[:, l, :, j * 128:(j + 1) * 128], j, mKT)
            proj_tok(wv, j, 512, pb[1])
            I("dve", "tensor_copy", out=tk1[:], in_=pb[1][:, :], r=[pb[1]], w=[tk1])
            I("act", "activation", out=mV[:, l, j, :, 0:128], in_=pb[1][:, :].rearrange("p (h d) -> p h d", h=4), func=AF.Copy, r=[pb[1]], w=[mV])
            dma("sp", o_pxv[l, j * 128:(j + 1) * 128, :], tk1[:], r=[tk1], final=True)

    def smp_mem_kv():
        for l in range(2):
            for j in range(2):
                dma("sp", tk0[:], c_xk[l, j * 128:(j + 1) * 128, :], w=[tk0])
                tok_to_T(tk0[:], tk0, mKT[:, l, :, j * 128:(j + 1) * 128], j, mKT)
                dma("pool", mV[:, l, j, :, 0:128], c_xv[l, j * 128:(j + 1) * 128, :].rearrange("p (h d) -> p h d", h=4), w=[mV])

    def x_q(l, nsb, wsrc, cq):
        wq = load_w(wsrc, cq, 512, 8)
        for j in range(nsb):
            tkq = tkrot[tkrc[0] % 3]
            tkrc[0] += 1
            proj_tok(wq, j, 512, pb[j % 4])
            group_norm(pb[j % 4][:, :], pb[j % 4], 4, 128, g_xq[:, l, :], g_xq, tkq[:], tkq)
            tok_to_T(tkq[:], tkq, xqT[:, :, j * 128:(j + 1) * 128], j, xqT)

    def x_branch(l, nsb, wsrc, cq, cz, c0, q_done=False):
        N = 128 * nsb
        if not q_done:
            x_q(l, nsb, wsrc, cq)
        wz = load_w(wsrc, cz, 512, 8)
        flush()
        def x_scores(h):
            pts = PT[pctr[0] % 2]
            pctr[0] += 1
            for mb in range(2):
                sp = pb[2 * (h % 2) + mb]
                mm(sp[:, 0:N], mKT[:, l, h, mb * 128:(mb + 1) * 128], xqT[:, h, 0:N], True, True, r=[mKT, xqT], w=[sp])
                I("act", "activation", out=pts[mb][:, 0:N], in_=sp[:, 0:N], func=AF.Exp, scale=128.0 ** -0.5, r=[sp], w=[pts[mb]])
            return pts

        def x_pv(h, pts):
            for j in range(nsb):
                oa = pb[4 + (xoc[0] % 3)]
                xoc[0] += 1
                for mb in range(2):
                    mm(oa[:, 0:129], pts[mb][:, j * 128:(j + 1) * 128], mV[:, l, mb, h, 0:129], mb == 0, mb == 1, r=[pts[mb], mV], w=[oa])
                rr = rl2[xoc[0] % 2]
                I("dve", "reciprocal", out=rr[:, 0:1], in_=oa[:, 128:129], r=[oa], w=[rr])
                I("dve", "tensor_scalar", out=ob[:, j, h * 128:(h + 1) * 128], in0=oa[:, 0:128], scalar1=rr[:, 0:1], scalar2=None, op0=ALU.mult, r=[oa, rr], w=[ob])

        prev_ = None
        for h in range(4):
            pts_h = x_scores(h)
            if prev_ is not None:
                x_pv(*prev_)
            prev_ = (h, pts_h)
        x_pv(*prev_)
        for j in range(nsb):
            proj_tok(wz, j, 512, pb[j % 4])
            silu_gate(pb[j % 4], 512, tk0[:], tk0, ob[:, j, :], ob)
            to_mixT(tk0[:], tk0, j, c0)

    def a_gates(nsb, Lc, nch_per_sb):
        flush()
        N = 128 * nsb
        nch = nsb * nch_per_sb
        CS = 128 // nch_per_sb
        for kc in range(8):
            mm(pb[0][0:4, 0:N], wgate[:, kc, 0:4], xnT[:, kc, 0:N], kc == 0, kc == 7, r=[xnT, wgate], w=[pb[0]])
        for kc in range(8):
            mm(pb[1][0:4, 0:N], wgate[:, kc, 4:8], xnT[:, kc, 0:N], kc == 0, kc == 7, r=[xnT, wgate], w=[pb[1]])
        I("dve", "tensor_scalar", out=gI[:, 0:N], in0=pb[0][0:4, 0:N], scalar1=big[:, 0:1], scalar2=None, op0=ALU.add, r=[pb[0], big], w=[gI])
        I("act", "activation", out=gF[:, 0:N], in_=pb[1][0:4, 0:N], func=AF.Exp, scale=-1.0, bias=bfg[:, 0:1], r=[pb[1], bfg], w=[gF])
        I("act", "activation", out=gF[:, 0:N], in_=gF[:, 0:N], func=AF.Ln, scale=1.0, bias=onesf[0:4, 0:1], r=[gF, onesf], w=[gF])
        I("pool", "memset", gZ[:, 0:N], 0.0, w=[gZ])
        I("pool", "memset", gB[:, 0:N], 0.0, w=[gB])
        for c in range(nch):
            sl = slice(c * CS, c * CS + Lc)
            I("dve", "tensor_tensor_scan", out=gB[:, sl], data0=gF[:, sl], data1=gZ[:, sl], initial=0.0, op0=ALU.add, op1=ALU.add, r=[gF, gZ], w=[gB])
        I("dve", "tensor_tensor", out=gA[:, 0:N], in0=gI[:, 0:N], in1=gB[:, 0:N], op=ALU.add, r=[gI, gB], w=[gA])
        gA3 = gA[:, 0:N].rearrange("p (c t) -> p c t", t=CS)
        gB3 = gB[:, 0:N].rearrange("p (c t) -> p c t", t=CS)
        I("dve", "tensor_reduce", out=cA[:, 0:nch], in_=gA3[:, :, 0:Lc], axis=AX.X, op=ALU.max, r=[gA], w=[cA])
        I("dve", "tensor_scalar", out=cB[:, 0:nch], in0=gB3[:, :, Lc - 1], scalar1=-1.0, scalar2=None, op0=ALU.mult, r=[gB], w=[cB])
        I("dve", "tensor_copy", out=cM[:, 0:1], in_=mst[:], r=[mst], w=[cM])
        I("dve", "tensor_tensor_scan", out=cM[:, 1:nch + 1], data0=cA[:, 0:nch], data1=cB[:, 0:nch], initial=mst[:, 0:1], op0=ALU.max, op1=ALU.add, r=[cA, cB, mst], w=[cM])
        I("dve", "tensor_copy", out=mst[:], in_=cM[:, nch:nch + 1], r=[cM], w=[mst])
        I("dve", "tensor_tensor", out=cX[:, 0:nch], in0=cM[:, 0:nch], in1=cA[:, 0:nch], op=ALU.max, r=[cM, cA], w=[cX])
        I("dve", "tensor_tensor", out=cD[:, 0:nch], in0=cM[:, 0:nch], in1=cX[:, 0:nch], op=ALU.subtract, r=[cM, cX], w=[cD])
        I("act", "activation", out=cD[:, 0:nch], in_=cD[:, 0:nch], func=AF.Exp, r=[cD], w=[cD])
        Xb = bc(cX[:, 0:nch].unsqueeze(2), [4, nch, CS])
        I("dve", "tensor_tensor", out=gA3, in0=gA3, in1=Xb, op=ALU.subtract, r=[gA, cX], w=[gA])
        I("act", "activation", out=gWe[:, 0:N], in_=gA[:, 0:N], func=AF.Exp, bias=lnk[0:4, 0:1], r=[gA, lnk], w=[gWe])
        I("dve", "tensor_tensor", out=gI[:, 0:N].rearrange("p (c t) -> p c t", t=CS), in0=gB3, in1=Xb, op=ALU.subtract, r=[gB, cX], w=[gI])
        I("act", "activation", out=gTh[:, 0:N], in_=gI[:, 0:N], func=AF.Exp, r=[gI], w=[gTh])
        for j in range(nsb):
            tr(pb[2][:, 0:4], gWe[:, j * 128:(j + 1) * 128], identf[0:4, 0:4], r=[gWe, identf], w=[pb[2]])
            tr(pb[2][:, 4:8], gTh[:, j * 128:(j + 1) * 128], identf[0:4, 0:4], r=[gTh, identf], w=[pb[2]])
            I("dve", "tensor_copy", out=gtok[:, j, :], in_=pb[2][:, 0:8], r=[pb[2]], w=[gtok])
        I("dve", "tensor_tensor", out=cDx[:, 0:nch, :], in0=bc(cD[:, 0:nch].unsqueeze(2), [4, nch, 4]),
          in1=hsel[:, :, 0:nch].rearrange("k h c -> k c h"), op=ALU.mult, r=[cD, hsel], w=[cDx])
        mm(pb[2][:, 0:nch * 4], onesf[0:4, :], cDx[:, 0:nch, :].rearrange("k c h -> k (c h)"), True, True, r=[onesf, cDx], w=[pb[2]])
        I("dve", "tensor_copy", out=decb[:, 0:nch, :], in_=pb[2][:, 0:nch * 4].rearrange("p (c h) -> p c h", h=4), r=[pb[2]], w=[decb])

    def evac(h, out_ap, in_ap, r, w):
        if h % 2:
            I("act", "activation", out=out_ap, in_=in_ap, func=AF.Copy, r=r, w=w)
        else:
            I("dve", "tensor_copy", out=out_ap, in_=in_ap, r=r, w=w)

    def a_branch(nsb, mode, wsrc, Lc=64, nch_per_sb=2, state_later=False):
        N = 128 * nsb
        full = mode != "pre"
        CS = 128 // nch_per_sb
        if full:
            wq = load_w(wsrc, 0, 512, 8)
            for h in range(4):
                proj_feat(wq, h, N, pb[h % 4])
                evac(h, aqT[:, h, 0:N], pb[h % 4][:, 0:N], [pb[h % 4]], [aqT])
        wk = load_w(wsrc, 512, 512, 8)
        a_gates(nsb, Lc, nch_per_sb)
        if full:
            for h in range(4):
                proj_feat(wk, h, N, pb[h % 4])
                evac(h, akT[:, h, 0:N], pb[h % 4][:, 0:N], [pb[h % 4]], [akT])
        for j in range(nsb):
            proj_tok(wk, j, 512, pb[j % 4])
            I("dve", "tensor_tensor", out=kw[:, j, :, :], in0=pb[j % 4][:, :].rearrange("p (h d) -> p h d", h=4),
              in1=bc(gtok[:, j, 0:4].unsqueeze(2), [128, 4, 128]), op=ALU.mult, r=[pb[j % 4], gtok], w=[kw])
        wv = load_w(wsrc, 1024, 512, 8)
        for j in range(nsb):
            proj_tok(wv, j, 512, pb[j % 4])
            I("act", "activation", out=avx[:, j, :, 0:128], in_=pb[j % 4][:, :].rearrange("p (h d) -> p h d", h=4), func=AF.Copy, r=[pb[j % 4]], w=[avx])
        if full:
            wo = load_w(wsrc, 1536, 512, 8)
            for j in range(nsb):
                proj_tok(wo, j, 512, pb[j % 4])
                I("act", "activation", out=ga[:, j, :], in_=pb[j % 4][:, :], func=AF.Sigmoid, r=[pb[j % 4]], w=[ga])
            wz = load_w(wsrc, 2048, 512, 8)
            for j in range(nsb):
                proj_tok(wz, j, 512, pb[j % 4])
                silu_gate(pb[j % 4], 512, ga[:, j, :], ga, ga[:, j, :], ga)
        if state_later:
            def later_state():
                k_ = 0
                for j in range(nsb):
                    for cc in range(nch_per_sb):
                        rows = slice(cc * CS, cc * CS + CS)
                        c = j * nch_per_sb + cc
                        ups = (pb[6], pb[3]) if k_ % 2 == 0 else (pb[4], pb[5])
                        k_ += 1
                        I("dve", "tensor_tensor", out=Cst[:], in0=Cst[:], in1=bc(decb[:, c, :].unsqueeze(2), [128, 4, 130]), op=ALU.mult, r=[Cst, decb], w=[Cst])
                        for h in range(4):
                            up = ups[h // 2]
                            hh = h % 2
                            mm(up[:, hh * 129:(hh + 1) * 129], kw[rows, j, h, :], avx[rows, j, h, 0:129], True, True, r=[kw, avx], w=[up])
                        for hp in range(2):
                            I("dve", "tensor_tensor", out=Cst[:, 2 * hp:2 * hp + 2, 0:129], in0=Cst[:, 2 * hp:2 * hp + 2, 0:129],
                              in1=ups[hp][:, 0:258].rearrange("p (h e) -> p h e", h=2), op=ALU.add, r=[Cst, ups[hp]], w=[Cst])
            return later_state
        for j in range(nsb):
            if full:
                for h in range(4):
                    mm(pb[0][:, h * 128:(h + 1) * 128], akT[:, h, j * 128:(j + 1) * 128], aqT[:, h, j * 128:(j + 1) * 128], True, True, r=[akT, aqT], w=[pb[0]])
                for h in range(4):
                    I("dve", "scalar_tensor_tensor", out=Stil[:, h, :], in0=pb[0][:, h * 128:(h + 1) * 128], scalar=gtok[:, j, h:h + 1], in1=amask[:],
                      op0=ALU.mult, op1=ALU.mult, r=[pb[0], gtok, amask], w=[Stil])
            for cc in range(nch_per_sb):
                rows = slice(cc * CS, cc * CS + CS)
                c = j * nch_per_sb + cc
                t0 = j * 128 + cc * CS
                if full:
                    I("dve", "tensor_tensor", out=Cbf[:], in0=Cst[:], in1=bc(decb[:, c, :].unsqueeze(2), [128, 4, 130]), op=ALU.mult, r=[Cst, decb], w=[Cbf])
                I("dve", "tensor_tensor", out=Cst[:], in0=Cst[:], in1=bc(decb[:, c, :].unsqueeze(2), [128, 4, 130]), op=ALU.mult, r=[Cst, decb], w=[Cst])
                if full:
                    for h in range(4):
                        ac = pb[4] if h < 2 else pb[5]
                        hh = h % 2
                        mm(ac[rows, hh * 129:(hh + 1) * 129], aqT[:, h, t0:t0 + CS], Cbf[:, h, 0:129], True, False, r=[aqT, Cbf], w=[ac])
                        mm(ac[rows, hh * 129:(hh + 1) * 129], Stil[:, h, cc * CS:cc * CS + CS], avx[:, j, h, 0:129], False, True, r=[Stil, avx], w=[ac])
                ups = (pb[6], pb[3]) if c % 2 == 0 else (pb[1], pb[2])
                for h in range(4):
                    up = ups[h // 2]
                    hh = h % 2
                    mm(up[:, hh * 129:(hh + 1) * 129], kw[rows, j, h, :], avx[rows, j, h, 0:129], True, True, r=[kw, avx], w=[up])
                for hp in range(2):
                    I("dve", "tensor_tensor", out=Cst[:, 2 * hp:2 * hp + 2, 0:129], in0=Cst[:, 2 * hp:2 * hp + 2, 0:129],
                      in1=ups[hp][:, 0:258].rearrange("p (h e) -> p h e", h=2), op=ALU.add, r=[Cst, ups[hp]], w=[Cst])
            if full:
                for hp in range(2):
                    ac = pb[4 + hp]
                    I("act", "activation", out=den4[:, 2 * hp:2 * hp + 2], in_=ac[:, 0:258].rearrange("p (h e) -> p h e", h=2)[:, :, 128], func=AF.Abs, r=[ac], w=[den4])
                I("dve", "tensor_tensor", out=den4[:], in0=den4[:], in1=gtok[:, j, 4:8], op=ALU.max, r=[den4, gtok], w=[den4])
                I("dve", "reciprocal", out=den4[:], in_=den4[:], r=[den4], w=[den4])
                for hp in range(2):
                    ac = pb[4 + hp]
                    I("dve", "tensor_tensor", out=hraw[:, hp * 256:(hp + 1) * 256].rearrange("p (h e) -> p h e", h=2),
                      in0=ac[:, 0:258].rearrange("p (h e) -> p h e", h=2)[:, :, 0:128],
                      in1=bc(den4[:, 2 * hp:2 * hp + 2].unsqueeze(2), [128, 2, 128]), op=ALU.mult, r=[ac, den4], w=[hraw])
                group_norm(hraw[:], hraw, 4, 128, g_mlstm[:], g_mlstm, tk0[:], tk0)
                I("dve", "tensor_tensor", out=tk0[:], in0=tk0[:], in1=ga[:, j, :], op=ALU.mult, r=[tk0, ga], w=[tk0])
                to_mixT(tk0[:], tk0, j, 0)

    def scr_write(j, blk):
        flush()
        hf, bi = blk // 16, blk % 16
        dma("sp", scrK[:, hf, :, bi * 128:(bi + 1) * 128].rearrange("h p k -> p h k"), KTo[:, :, j * 128:(j + 1) * 128], r=[KTo], w=[scr_b[blk]])
        dma("sp", scrV[:, hf, :, bi * 130:(bi + 1) * 130].rearrange("h p e -> p h e"), Vo[:, j, :, :], r=[Vo], w=[scr_b[blk]])

    def b_kv(nsb, wsrc, blk0, out_row0, smp=False):
        wk = load_w(wsrc, 3080, 512, 8)
        for j in range(nsb):
            proj_tok(wk, j, 512, pb[j % 4])
            tkr = tk0 if j % 2 == 0 else tk2
            group_norm(pb[j % 4][:, :], pb[j % 4], 8, 64, g_kn[:], g_kn, tkr[:], tkr)
            if out_row0 is not None:
                if smp:
                    dma("sp", o_sk[0:32, :], tkr[0:32, :], r=[tkr], final=True)
                else:
                    dma("sp", o_pk[out_row0 + j * 128:out_row0 + (j + 1) * 128, :], tkr[:], r=[tkr], final=True)
            tok_to_T(tkr[:], tkr, KTo[:, :, j * 128:(j + 1) * 128], j, KTo)
        wv = load_w(wsrc, 3592, 512, 8)
        for j in range(nsb):
            proj_tok(wv, j, 512, pb[j % 4])
            if out_row0 is not None:
                I("dve", "tensor_copy", out=tk1[:], in_=pb[j % 4][:, :], r=[pb[j % 4]], w=[tk1])
                if smp:
                    dma("sp", o_sv[0:32, :], tk1[0:32, :], r=[tk1], final=True)
                else:
                    dma("sp", o_pv[out_row0 + j * 128:out_row0 + (j + 1) * 128, :], tk1[:], r=[tk1], final=True)
            I("act", "activation", out=Vo[:, j, :, 0:128], in_=pb[j % 4][:, :].rearrange("p (h d) -> p h d", h=4), func=AF.Copy, r=[pb[j % 4]], w=[Vo])
        if blk0 is not None:
            for j in range(nsb):
                scr_write(j, blk0 + j)

    def b_attn(nsb, wsrc, tile_blk, n_masked, diag_kind):
        N = 128 * nsb
        wq = load_w(wsrc, 2568, 512, 8)
        for j in range(nsb):
            proj_tok(wq, j, 512, pb[j % 4])
            group_norm(pb[j % 4][:, :], pb[j % 4], 8, 64, g_qn[:], g_qn, tk0[:], tk0)
            tok_to_T(tk0[:], tk0, bqT[:, :, j * 128:(j + 1) * 128], j, bqT)
        wz = load_w(wsrc, 4104, 512, 8)
        flush()
        items = []
        for h in range(4):
            nsub = NSUBH[h] if nsb == 4 else nsb
            gsz = nsb // nsub
            blocks = []
            ring_loads = []
            for hf in range((tile_blk + 15) // 16):
                nb_h = min(16, tile_blk - hf * 16)
                ring_loads.append((hf, nb_h))
            items.append(("head", h, nsub, gsz, ring_loads))

        def emit_qk(it):
            (h, nsub, gsz, tbl, kap, vap, kdep, vdep, eb, dj, bidx, pts, sps) = it
            q0 = 0 if dj is None else dj
            for m in range(2):
                mm(sps[m][:, q0 * 128:N], kap[m * 64:(m + 1) * 64, :], bqT[m * 64:(m + 1) * 64, h, q0 * 128:N], True, True, r=[kdep, bqT], w=[sps[m]])
            if dj is not None:
                for m in range(2):
                    I("dve", "tensor_tensor", out=sps[m][:, dj * 128:(dj + 1) * 128], in0=sps[m][:, dj * 128:(dj + 1) * 128],
                      in1=dmask[:, diag_kind, h, :], op=ALU.add, r=[sps[m], dmask], w=[sps[m]])
            for m in range(2):
                for g in range(nsub):
                    qa, qe = max(g * gsz, q0), (g + 1) * gsz
                    if qe <= qa:
                        continue
                    delta = (tile_blk + g * gsz) - eb
                    col = h * NDELTA + delta + DOFF
                    I("act", "activation", out=pts[m][:, qa * 128:qe * 128], in_=sps[m][:, qa * 128:qe * 128], func=AF.Exp, scale=0.125,
                      bias=biasT[:, tbl, col:col + 1], r=[sps[m], biasT], w=[pts[m]])

        def emit_pv(it):
            (h, nsub, gsz, tbl, kap, vap, kdep, vdep, eb, dj, bidx, pts, sps) = it
            q0 = 0 if dj is None else dj
            for qs in range(q0, nsb):
                for m in range(2):
                    rg = qs * 2 + m
                    oa = pb[4 + rg // 3]
                    o0 = (rg % 3) * 129
                    last = (dj is not None and dj == qs)
                    mm(oa[:, o0:o0 + 129], pts[m][:, qs * 128:(qs + 1) * 128], vap, (bidx == 0 and rg % 3 == 0), last, r=[pts[m], vdep], w=[oa],
                       skip_group_check=True)

        def emit_epilogue(h):
            nreg = 2 * nsb
            rl8 = rl8s[rl8c[0] % 2]
            rl8c[0] += 1
            for bk in range((nreg + 2) // 3):
                nr = min(3, nreg - 3 * bk)
                oa = pb[4 + bk]
                I("dve", "reciprocal", out=rl8[:, 3 * bk:3 * bk + nr], in_=oa[:, 0:nr * 129].rearrange("p (r e) -> p r e", e=129)[:, :, 128], r=[oa], w=[rl8])
            I("dve", "tensor_tensor", out=rl8[:, 0:nreg].rearrange("p (q m) -> p q m", m=2)[:, :, 1], in0=rl8[:, 0:nreg].rearrange("p (q m) -> p q m", m=2)[:, :, 1],
              in1=bc(nlam[:, 0:1], [128, nsb]), op=ALU.mult, r=[rl8, nlam], w=[rl8])
            for qs in range(nsb):
                r0, r1 = qs * 2, qs * 2 + 1
                oa0, oa1 = pb[4 + r0 // 3], pb[4 + r1 // 3]
                a0, a1 = (r0 % 3) * 129, (r1 % 3) * 129
                I("dve", "tensor_scalar", out=ob[:, qs, h * 128:(h + 1) * 128], in0=oa0[:, a0:a0 + 128], scalar1=rl8[:, r0:r0 + 1], scalar2=None, op0=ALU.mult, r=[oa0, rl8], w=[ob])
                I("dve", "scalar_tensor_tensor", out=ob[:, qs, h * 128:(h + 1) * 128], in0=oa1[:, a1:a1 + 128], scalar=rl8[:, r1:r1 + 1], in1=ob[:, qs, h * 128:(h + 1) * 128],
                  op0=ALU.mult, op1=ALU.add, r=[oa1, rl8, ob], w=[ob])

        prev = None
        for (_, h, nsub, gsz, ring_loads) in items:
            blocks = []
            first_eb = 0
            for eb_ in range(tile_blk):
                if SLOPES[h] * (128 * (tile_blk - eb_) - 127) > 56.0:
                    first_eb = eb_ + 1
            for (hf, nb_h) in ring_loads:
                b0 = max(0, first_eb - hf * 16)
                if b0 >= nb_h:
                    continue
                slot = rctr[0] % 2
                rctr[0] += 1
                deps = [scr_b[hf * 16 + bi] for bi in range(b0, nb_h)]
                dma("sp", rK[slot][:, b0 * 128:nb_h * 128], scrK[h, hf, :, b0 * 128:nb_h * 128], r=deps, w=[rK[slot]])
                dma("sp", rV[slot][:, b0:nb_h, :], scrV[h, hf, :, b0 * 130:nb_h * 130].rearrange("p (b e) -> p b e", e=130), r=deps, w=[rV[slot]])
                for bi in range(b0, nb_h):
                    eb = hf * 16 + bi
                    blocks.append((1 if eb < n_masked else 0, rK[slot][:, bi * 128:(bi + 1) * 128], rV[slot][:, bi, 0:129], rK[slot], rV[slot], eb, None))
            for dj in range(nsb):
                blocks.append((0, KTo[:, h, dj * 128:(dj + 1) * 128], Vo[:, dj, h, 0:129], KTo, Vo, tile_blk + dj, dj))
            for bidx, (tbl, kap, vap, kdep, vdep, eb, dj) in enumerate(blocks):
                pts = PT[pctr[0] % 2]
                sps = [pb[(pctr[0] % 2) * 2 + m] for m in range(2)]
                pctr[0] += 1
                it = (h, nsub, gsz, tbl, kap, vap, kdep, vdep, eb, dj, bidx, pts, sps)
                emit_qk(it)
                if prev is not None:
                    emit_pv(prev)
                    if prev[0] != h:
                        emit_epilogue(prev[0])
                prev = it
        emit_pv(prev)
        emit_epilogue(prev[0])
        for j in range(nsb):
            group_norm(ob[:, j, :], ob, 4, 128, g_subln[:], g_subln, tk1[:], tk1)
            proj_tok(wz, j, 512, pb[j % 4])
            silu_gate(pb[j % 4], 512, tk0[:], tk0, tk1[:], tk1)
            to_mixT(tk0[:], tk0, j, 4)

    def out_proj(nsb, wsrc, res_fn, dst_fn):
        flush()
        wos = [[load_w(wsrc, cg * 512 + hh * 256, 256, 12) for hh in range(2)] for cg in range(2)]
        k_ = 0
        for j in range(nsb):
            for cg in range(2):
                woh = wos[cg]
                ps_ = pb[k_ % 4]
                k_ += 1
                rap, rtl = res_fn(j, cg)
                for hh in range(2):
                    wt, wv = woh[hh]
                    for kc in range(12):
                        mm(ps_[:, hh * 256:(hh + 1) * 256], mixT[:, kc, j * 128:(j + 1) * 128], wv[:, kc, :], kc == 0, kc == 11, r=[mixT, wt], w=[ps_])
                dap, dtl = dst_fn(j, cg)
                I("dve", "tensor_tensor", out=dap, in0=ps_[:, :], in1=rap, op=ALU.add, r=[ps_, rtl], w=[dtl])
                dst_done(j, cg)

    dst_done_fn = [lambda j, cg: None]

    def dst_done(j, cg):
        dst_done_fn[0](j, cg)

    def conv_out(dst, t0, t1):
        flush()
        j = (t1 - 1) // 128
        for c in range(8):
            tr(pb[4 + c // 4][:, (c % 4) * 128:(c % 4 + 1) * 128], cacc[:, c, 0:128], identf[:], r=[cacc, identf], w=[pb[4 + c // 4]])
        for hp in range(2):
            I("dve", "tensor_copy", out=sqs[:, hp * 512:(hp + 1) * 512], in_=pb[4 + hp][:, :], r=[pb[4 + hp]], w=[sqs])
        r0 = t0 - j * 128
        dma("sp", dst, sqs[r0:r0 + 30, :], r=[sqs], final=True)

    def layer1(nsb, y_dst, nrows, halo_only=False, smp=False, pconv=False, before_out=None):
        N = 128 * nsb
        make_xnT(nsb, ng[:, 1, :], lambda j: (y0[:, j, :], y0b[j]))
        switch("L")
        I("dve", "tensor_copy", out=uT[:, :, 0:30], in_=halo[:], r=[halo], w=[uT])
        for c2 in range(4):
            wu = load_w("w_in_c", c2 * 256, 256, 8)
            wg = load_w("w_in_c", 1024 + c2 * 256, 256, 8)
            for cc in range(2):
                c = c2 * 2 + cc
                pu, pg = pb[2 * (c % 2)], pb[2 * (c % 2) + 1]
                proj_feat(wu, cc, N, pu)
                proj_feat(wg, cc, N, pg)
                I("act", "activation", out=e2[:, 0:N], in_=pg[:, 0:N], func=AF.Sigmoid, r=[pg], w=[e2])
                I("dve", "tensor_tensor", out=uT[:, c, 30:30 + N], in0=pu[:, 0:N], in1=e2[:, 0:N], op=ALU.mult, r=[pu, e2], w=[uT])
                if smp or pconv:
                    I("dve", "tensor_tensor", out=cacc[:, c, 0:128], in0=pu[:, N - 128:N], in1=e2[:, N - 128:N], op=ALU.mult, r=[pu, e2], w=[cacc])
        if smp:
            conv_out(o_sconv, 2, 32)
        if pconv:
            conv_out(o_pconv, N - 30, N)
        I("dve", "tensor_copy", out=halo[:], in_=uT[:, :, N:N + 30], r=[uT], w=[halo])
        if halo_only:
            return
        x_q(1, nsb, "w_in_c", 3072)
        for c in range(8):
            psc = pb[c % 4]
            for jt in range(31):
                dg = dgr[dgc[0] % 8]
                dgc[0] += 1
                I("dve", "tensor_scalar", out=dg[:, :], in0=identb[:], scalar1=cw[:, c, jt:jt + 1], scalar2=None, op0=ALU.mult, r=[identb, cw], w=[dg])
                mm(psc[:, 0:N], dg[:, :], uT[:, c, jt:jt + N], jt == 0, jt == 30, r=[dg, uT], w=[psc])
            I("act", "activation", out=cacc[:, c, 0:N], in_=psc[:, 0:N], func=AF.Identity, bias=cb[:, c:c + 1], r=[psc, cb], w=[cacc])
        for c in range(8):
            mm(pb[2][:, 0:N], onesf[:], cacc[:, c, 0:N], c == 0, c == 7, r=[onesf, cacc], w=[pb[2]])
        for c in range(8):
            I("act", "activation", out=sqs[:, 0:N], in_=cacc[:, c, 0:N], func=AF.Square, r=[cacc], w=[sqs])
            mm(pb[3][:, 0:N], onesf[:], sqs[:, 0:N], c == 0, c == 7, r=[onesf, sqs], w=[pb[3]])
        I("dve", "tensor_scalar", out=lmean[:, 0:N], in0=pb[2][:, 0:N], scalar1=1.0 / D, scalar2=None, op0=ALU.mult, r=[pb[2]], w=[lmean])
        I("dve", "tensor_tensor", out=e1[:, 0:N], in0=lmean[:, 0:N], in1=lmean[:, 0:N], op=ALU.mult, r=[lmean], w=[e1])
        I("dve", "scalar_tensor_tensor", out=e1[:, 0:N], in0=pb[3][:, 0:N], scalar=1.0 / D, in1=e1[:, 0:N], op0=ALU.mult, op1=ALU.subtract, r=[pb[3], e1], w=[e1])
        I("act", "activation", out=lrstd[:, 0:N], in_=e1[:, 0:N], func=AF.Ln, bias=epsc[:], r=[e1, epsc], w=[lrstd])
        I("act", "activation", out=lrstd[:, 0:N], in_=lrstd[:, 0:N], func=AF.Exp, scale=-0.5, r=[lrstd], w=[lrstd])
        wzs = [load_w("w_in_c", 2048, 512, 8), load_w("w_in_c", 2560, 512, 8)]
        for c in range(8):
            proj_feat(wzs[c // 4], c % 4, N, pb[c % 4])
            I("act", "activation", out=szT[:, c, 0:N], in_=pb[c % 4][:, 0:N], func=AF.Silu, r=[pb[c % 4]], w=[szT])
            I("dve", "tensor_tensor", out=cacc[:, c, 0:N], in0=cacc[:, c, 0:N], in1=lmean[:, 0:N], op=ALU.subtract, r=[cacc, lmean], w=[cacc])
            I("dve", "tensor_tensor", out=cacc[:, c, 0:N], in0=cacc[:, c, 0:N], in1=lrstd[:, 0:N], op=ALU.mult, r=[cacc, lrstd], w=[cacc])
            I("act", "activation", out=e1[:, 0:N], in_=cacc[:, c, 0:N], func=AF.Silu, scale=lg[:, c:c + 1], bias=lb[:, c:c + 1], r=[cacc, lg, lb], w=[e1])
            I("dve", "tensor_tensor", out=mixT[:, c, 0:N], in0=e1[:, 0:N], in1=szT[:, c, 0:N], op=ALU.mult, r=[e1, szT], w=[mixT])
        x_branch(1, nsb, "w_in_c", 3072, 3584, 8, q_done=True)

        cur = [None]

        def dst_fn(j, cg):
            cur[0] = tkrot[tkrc[0] % 3]
            tkrc[0] += 1
            return cur[0][:], cur[0]

        def done(j, cg):
            dma("sp", y_dst[j * 128:j * 128 + nrows, cg * 512:(cg + 1) * 512], cur[0][0:nrows, :], r=[cur[0]], final=True)
        dst_done_fn[0] = done
        if before_out is not None:
            before_out()
        out_proj(nsb, "w_out_c", lambda j, cg: (y0[:, j, cg * 512:(cg + 1) * 512], y0b[j]), dst_fn)
        dst_done_fn[0] = lambda j, cg: None

    class _Stop(Exception):
        pass

    def stage(n):
        if cfg.get("stage") == n:
            raise _Stop()

    try:
        for _ in range(NXS):
            x_issue()
        stage(1)
        mem_kv(0)
        stage(2)
        if do_l1:
            mem_kv(1)
        stage(3)
        I("dve", "memset", Cst[:], 0.0, w=[Cst])
        I("dve", "memset", mst[:], 0.0, w=[mst])

        def res_from(src, row0):
            def fn(j, cg):
                t_ = tkrot[tkrc[0] % 3]
                tkrc[0] += 1
                dma("sp", t_[:], src[row0 + j * 128:row0 + (j + 1) * 128, cg * 512:(cg + 1) * 512], w=[t_])
                return t_[:], t_
            return fn

        def full_tile(src, row0, nsb, tile_blk, n_masked, blk0, out_row0, y_dst, halo_only=False, pconv=False, stats_done=False, before_out=None):
            make_xnT(nsb, ng[:, 0, :], None if stats_done else dram_src(src, row0), stats_done=stats_done)
            switch("A")
            a_branch(nsb, "full", "w_in_a")
            switch("B")
            b_kv(nsb, "w_in_a", blk0, out_row0)
            x_q(0, nsb, "w_in_a", 4616)
            b_attn(nsb, "w_in_a", tile_blk, n_masked, 0)
            x_branch(0, nsb, "w_in_a", 4616, 5128, 8, q_done=True)
            out_proj(nsb, "w_out_a", res_from(src, row0), lambda j, cg: (y0[:, j, cg * 512:(cg + 1) * 512], y0b[j]))
            if do_l1:
                layer1(nsb, y_dst, 128, halo_only=halo_only, pconv=pconv, before_out=before_out)

        blk = 0
        pre_started = [False]
        for (nsb, mode) in cfg["pre"]:
            if mode == "pre":
                make_xnT(nsb, ng[:, 0, :], dram_src(x_pre, blk * 128))
                if not pre_started[0]:
                    switch("AB")
                    pre_started[0] = True
                st_later = a_branch(nsb, "pre", "w_in_a", state_later=True)
                b_kv(nsb, "w_in_a", blk, None)
                st_later()
                stage(4)
            else:
                assert nsb == 1
                full_tile(x_pre, blk * 128, 1, blk, NB_PRE, blk, None, None, halo_only=True)
                stage(5)
            blk += nsb
        if N_PRE > 0:
            I("dve", "tensor_tensor", out=mst[:], in0=mst[:], in1=flag[:], op=ALU.mult, r=[mst, flag], w=[mst])
        ob_ = 0
        nown = len(cfg["own"])
        for ti, nsb in enumerate(cfg["own"]):
            r0 = ob_ * 128
            last = ti == nown - 1
            hook = None
            if (not last) and do_l1:
                nsb2 = cfg["own"][ti + 1]
                r2 = (ob_ + nsb) * 128
                hook = (lambda nsb2=nsb2, r2=r2: xn_stats(nsb2, dram_src(x_own, r2)))
            full_tile(x_own, r0, nsb, NB_PRE + ob_, NB_PRE, (NB_PRE + ob_) if not last else None, r0, y_own[r0:r0 + 128 * nsb, :], pconv=last,
                      stats_done=(ti > 0 and do_l1), before_out=hook)
            ob_ += nsb
        dma("sp", o_pC.rearrange("h d e -> d h e"), Cst[:, :, 0:128], r=[Cst], final=True)
        dma("sp", o_pn.rearrange("h d -> d h"), Cst[:, :, 128], r=[Cst], final=True, allow_slow_non_contiguous=True)
        dma("sp", o_pm, mst[:], r=[mst], final=True)
        stage(6)

        if do_smp:
            smp_mem_kv()
            switch("B")
            for kb in range(16):
                tks = tkrot[kb % 3]
                dma("act", tks[:], c_k[kb * 128:(kb + 1) * 128, :], w=[tks])
                tok_to_T(tks[:], tks, KTo[:, :, (kb % 4) * 128:(kb % 4 + 1) * 128], kb, KTo)
                dma("pool", Vo[:, kb % 4, :, 0:128], c_v[kb * 128:(kb + 1) * 128, :].rearrange("p (h d) -> p h d", h=4), w=[Vo])
                scr_write(kb % 4, kb)
            dma("sp", Cst[:, :, 0:128], st_C.rearrange("h d e -> d h e"), w=[Cst])
            dma("sp", Cst[:, :, 128], st_n.rearrange("h d -> d h"), w=[Cst], allow_slow_non_contiguous=True)
            dma("sp", mst[:], st_m, w=[mst])
            I("pool", "memset", sqs[:], 0.0, w=[sqs])
            dma("sp", sqs[0:30, :], st_conv, w=[sqs])
            for c in range(8):
                tr(pb[4 + c // 4][:, (c % 4) * 128:(c % 4 + 1) * 128], sqs[:, c * 128:(c + 1) * 128], identf[:], r=[sqs, identf], w=[pb[4 + c // 4]])
            for hp in range(2):
                I("dve", "tensor_copy", out=halo[:, hp * 4:(hp + 1) * 4, :], in_=pb[4 + hp][:, :].rearrange("p (c t) -> p c t", c=4)[:, :, 0:30], r=[pb[4 + hp]], w=[halo])
            make_xnT(1, ng[:, 0, :], dram_src(x_smp, 0))
            switch("A")
            a_branch(1, "full", "w_in_a", Lc=32, nch_per_sb=1)
            dma("sp", o_sC.rearrange("h d e -> d h e"), Cst[:, :, 0:128], r=[Cst], final=True)
            dma("sp", o_sn.rearrange("h d -> d h"), Cst[:, :, 128], r=[Cst], final=True, allow_slow_non_contiguous=True)
            dma("sp", o_sm, mst[:], r=[mst], final=True)
            switch("B")
            b_kv(1, "w_in_a", None, 0, smp=True)
            b_attn(1, "w_in_a", 16, 0, 1)
            x_branch(0, 1, "w_in_a", 4616, 5128, 8)
            out_proj(1, "w_out_a", res_from(x_smp, 0), lambda j, cg: (y0[:, j, cg * 512:(cg + 1) * 512], y0b[j]))
            if do_l1:
                layer1(1, y_smp, 32, smp=True)

    except _Stop:
        pass
    flush()
    P.S.emit(final_waits=P.outs)
    return P


def host_consts(has_prefix):
    ident = np.eye(128, dtype=np.float32)
    a = np.arange(128, dtype=np.float64)
    bias = np.zeros((128, 2, 4 * NDELTA), np.float32)
    for h in range(4):
        for dl in range(-DOFF, NDELTA - DOFF):
            v = SLOPES[h] * (a - 128.0 * dl)
            bias[:, 0, h * NDELTA + dl + DOFF] = v
            bias[:, 1, h * NDELTA + dl + DOFF] = v + (0.0 if has_prefix else PMASK)
    ka = np.arange(128)[:, None]
    qc = np.arange(128)[None, :]
    dm = np.zeros((128, 2, 4, 128), np.float32)
    for h in range(4):
        corr = np.where(ka > qc, -16.0 * SLOPES[h] * (ka - qc), 0.0)
        vis = (ka // 64) <= (qc // 64)
        dm[:, 0, h, :] = np.where(vis, corr, -1.0e5)
        vis_s = (ka < 32) & (qc < 128)
        dm[:, 1, h, :] = np.where(vis_s, corr, -1.0e5)
    am = ((ka <= qc) & ((ka // 64) == (qc // 64))).astype(np.float32)
    hsel = np.zeros((4, 4, 128), np.float32)
    for h in range(4):
        hsel[h, h, :] = 1.0
    flag = np.full((4, 1), 1.0 if has_prefix else 0.0, np.float32)
    return dict(c_ident=ident, c_bias=bias, c_dmask=dm, c_amask=am, c_hsel=hsel, c_flag=flag)


FULL_CFG = dict(own=[4] * 8, pre=[(4, "pre")] * 7 + [(3, "pre"), (1, "halo")], sample=True, l1=True)
_CACHE = {}


def core_inputs(c, cfg, inp):
    f = lambda a: np.ascontiguousarray(a, dtype=np.float32)
    b, half = c // 2, c % 2
    n_own = 128 * sum(cfg["own"])
    n_pre = 128 * sum(n for n, _ in cfg["pre"])
    xp = inp["x_prompt"][b]
    d = {}
    d["x_own"] = f(xp[half * n_own:(half + 1) * n_own])
    if half == 1:
        d["x_pre"] = f(xp[n_own - n_pre:n_own])
    else:
        d["x_pre"] = np.zeros((max(n_pre, 128), D), np.float32)
    xs = np.zeros((128, D), np.float32)
    xs[:32] = inp["x_sample"][c]
    d["x_smp"] = xs
    d["mem"] = f(inp["mem_prompt"][b])
    d["c_xk"] = f(inp["cache_xk"][:, c].reshape(2, 256, 512))
    d["c_xv"] = f(inp["cache_xv"][:, c].reshape(2, 256, 512))
    d["c_k"] = f(inp["cache_k"][0, c].reshape(2048, 512))
    d["c_v"] = f(inp["cache_v"][0, c].reshape(2048, 512))
    d["st_C"] = f(inp["state_C"][0, c])
    d["st_n"] = f(inp["state_n"][0, c])
    d["st_m"] = f(inp["state_m"][0, c].reshape(4, 1))
    d["st_conv"] = f(inp["state_conv"][0, c])
    d["norm_g"] = f(inp["norm_g"])
    d["w_in_a"] = f(inp["w_in_a"][0])
    d["b_ig"] = f(inp["b_ig"][0].reshape(4, 1))
    d["b_fg"] = f(inp["b_fg"][0].reshape(4, 1))
    d["mlstm_g"] = f(inp["mlstm_norm_g"][0].reshape(512))
    d["qn_g"] = f(np.tile(inp["qn_g"][0], 8))
    d["kn_g"] = f(np.tile(inp["kn_g"][0], 8))
    d["lamv"] = f(np.stack([inp["lam_q1"][0], inp["lam_k1"][0], inp["lam_q2"][0], inp["lam_k2"][0]]))
    d["subln_g"] = f(np.tile(inp["subln_g"][0], 4))
    d["w_out_a"] = f(inp["w_out_a"][0])
    d["w_in_c"] = f(inp["w_in_c"][0])
    d["conv_wT"] = f(inp["conv_w"][0].T)
    d["conv_b"] = f(inp["conv_b"][0])
    d["ln_g"] = f(inp["conv_ln_g"][0])
    d["ln_b"] = f(inp["conv_ln_b"][0])
    d["w_out_c"] = f(inp["w_out_c"][0])
    d["mem_norm_g"] = f(inp["mem_norm_g"])
    d["w_mem_kv"] = f(inp["w_mem_kv"])
    d["xq_g"] = f(np.stack([np.tile(inp["xq_norm_g"][l], 4) for l in range(2)]))
    d["xk_g"] = f(np.stack([np.tile(inp["xk_norm_g"][l], 4) for l in range(2)]))
    d.update(host_consts(half == 1))
    return d


def kernel(**inp):
    inp = {k: np.asarray(v) for k, v in inp.items()}
    cfg = FULL_CFG
    if "prog" not in _CACHE:
        _CACHE["prog"] = build(cfg)
    P = _CACHE["prog"]
    in_maps = [core_inputs(c, cfg, inp) for c in range(8)]
    res = run_bass_kernel_spmd(P.nc, in_maps, core_ids=list(range(8))).results
    B, SEQ = 4, 8192
    yp = np.zeros((B, SEQ, D), np.float32)
    pk = np.zeros((1, B, SEQ, 4, 128), np.float32)
    pv = np.zeros((1, B, SEQ, 4, 128), np.float32)
    pxk = np.zeros((2, B, 256, 4, 128), np.float32)
    pxv = np.zeros((2, B, 256, 4, 128), np.float32)
    pC = np.zeros((1, B, 4, 128, 128), np.float32)
    pn = np.zeros((1, B, 4, 128), np.float32)
    pm = np.zeros((1, B, 4), np.float32)
    pconv = np.zeros((1, B, 30, D), np.float32)
    ys = np.zeros((8, 32, D), np.float32)
    sk = np.zeros((1, 8, 32, 4, 128), np.float32)
    sv = np.zeros((1, 8, 32, 4, 128), np.float32)
    sC = np.zeros((1, 8, 4, 128, 128), np.float32)
    sn = np.zeros((1, 8, 4, 128), np.float32)
    sm = np.zeros((1, 8, 4), np.float32)
    sconv = np.zeros((1, 8, 30, D), np.float32)
    for c in range(8):
        r = res[c]
        b, half = c // 2, c % 2
        sl = slice(half * 4096, (half + 1) * 4096)
        yp[b, sl] = r["y_own"]
        pk[0, b, sl] = r["o_pk"].reshape(4096, 4, 128)
        pv[0, b, sl] = r["o_pv"].reshape(4096, 4, 128)
        if half == 1:
            pxk[:, b] = r["o_pxk"].reshape(2, 256, 4, 128)
            pxv[:, b] = r["o_pxv"].reshape(2, 256, 4, 128)
            pC[0, b] = r["o_pC"]
            pn[0, b] = r["o_pn"]
            pm[0, b] = r["o_pm"].reshape(4)
            pconv[0, b] = r["o_pconv"]
        ys[c] = r["y_smp"]
        sk[0, c] = r["o_sk"].reshape(32, 4, 128)
        sv[0, c] = r["o_sv"].reshape(32, 4, 128)
        sC[0, c] = r["o_sC"]
        sn[0, c] = r["o_sn"]
        sm[0, c] = r["o_sm"].reshape(4)
        sconv[0, c] = r["o_sconv"]
    return (yp, ys, pxk, pxv, pk, pv, pC, pn, pm, pconv, sk, sv, sC, sn, sm, sconv)
```

```python
import math
import numpy as np
import concourse.bass as bass
import concourse.mybir as mybir
from concourse.bass_utils import run_bass_kernel_spmd

F32 = mybir.dt.float32
BF16 = mybir.dt.bfloat16
AF = mybir.ActivationFunctionType
ALU = mybir.AluOpType
AX = mybir.AxisListType

D = 1024
EPS = 1e-6
SLOPES = [2.0 ** (-8.0 * (h + 1) / 4) for h in range(4)]
NSUBH = [2, 1, 1, 1]
NDELTA = 72
DOFF = 3
PMASK = -200.0
LAM_INIT0 = 0.8 - 0.6 * math.exp(0.0)


class Buf:
    __slots__ = ("name", "last_w", "readers", "excl")

    def __init__(self, name="", excl=False):
        self.name = name
        self.last_w = None
        self.readers = []
        self.excl = excl


class Op:
    __slots__ = ("eng", "fn", "deps", "signal", "is_dma", "sem", "semval", "vc", "gi", "desc")


class Sched:
    def __init__(self, nc, n_dma_sems=10):
        self.nc = nc
        self.ops = {e: [] for e in ("pe", "act", "dve", "pool", "sp")}
        self.all = []
        self.n_dma_sems = n_dma_sems
        self.dma_rr = {q: 0 for q in ("sp", "act", "pool")}
        self.dma_last = {}

    def op(self, eng, fn, reads=(), writes=(), dma=False):
        import os as _os
        if len(self.all) >= int(_os.environ.get("MAXOPS", "100000000")):
            return None
        o = Op()
        o.eng = eng
        o.fn = fn
        o.is_dma = dma
        o.signal = False
        o.sem = None
        o.semval = 0
        o.gi = len(self.all)
        deps = set()
        reads = list(reads)
        writes = list(writes)
        for b in reads:
            if b.excl and b not in writes:
                writes.append(b)
        for b in reads:
            if b.last_w is not None:
                deps.add(b.last_w)
        for b in writes:
            if b.last_w is not None:
                deps.add(b.last_w)
            for r in b.readers:
                deps.add(r)
        if dma:
            key = (eng, self.dma_rr[eng] % self.n_dma_sems)
            self.dma_rr[eng] += 1
            prev = self.dma_last.get(key)
            if prev is not None:
                deps.add(prev)
            self.dma_last[key] = o
            o.sem = key
        deps.discard(o)
        o.deps = deps
        for b in reads:
            b.readers.append(o)
        for b in writes:
            b.last_w = o
            b.readers = []
        self.ops[eng].append(o)
        self.all.append(o)
        return o

    def emit(self, final_waits=()):
        nc = self.nc

        def pe_pe(o, d):
            return (not d.is_dma) and d.eng == "pe" and o.eng == "pe" and (not o.is_dma)

        for o in self.all:
            for d in o.deps:
                if d.is_dma or pe_pe(o, d):
                    continue
                d.signal = True
        for o in final_waits:
            if not o.is_dma:
                o.signal = True
        cnt = {}
        for o in self.all:
            if o.is_dma:
                cnt[o.sem] = cnt.get(o.sem, 0) + 16
                o.semval = cnt[o.sem]
            elif o.signal:
                o.sem = ("c", o.eng)
                cnt[o.sem] = cnt.get(o.sem, 0) + 1
                o.semval = cnt[o.sem]
        known = {e: {} for e in self.ops}
        plans = {}
        for o in self.all:
            kn = known[o.eng]
            waits = {}
            for d in o.deps:
                if pe_pe(o, d):
                    continue
                if kn.get(d.sem, 0) >= d.semval:
                    continue
                if waits.get(d.sem, 0) < d.semval:
                    waits[d.sem] = d.semval
            for d in o.deps:
                if pe_pe(o, d):
                    continue
                for s, v in d.vc.items():
                    if kn.get(s, 0) < v:
                        kn[s] = v
            for s in list(waits):
                v = waits[s]
                for d in o.deps:
                    if pe_pe(o, d) or d.sem == s or d.sem not in waits:
                        continue
                    if waits[d.sem] >= d.semval and d.vc.get(s, 0) >= v:
                        del waits[s]
                        break
            plans[o.gi] = waits
            vc = dict(kn)
            if o.sem is not None and vc.get(o.sem, 0) < o.semval:
                vc[o.sem] = o.semval
            o.vc = vc
        sems = {}
        for key in cnt:
            sems[key] = nc.alloc_semaphore("s_" + "_".join(str(k) for k in key))
        fw = [(o.sem, o.semval) for o in final_waits]
        self.n_waits = sum(len(p) for p in plans.values())
        self.plans = plans

        def run_engine(ename):
            def body(eng):
                for o in self.ops[ename]:
                    wl = list(plans[o.gi].items())
                    for s, v in wl[:-1]:
                        eng.wait_ge(sems[s], v)
                    ins = o.fn(eng)
                    if wl:
                        ins = ins._wait_ge(sems[wl[-1][0]], wl[-1][1])
                    if o.is_dma:
                        ins.then_inc(sems[o.sem], 16)
                    elif o.signal:
                        ins.then_inc(sems[o.sem], 1)
                if ename == "sp":
                    done = {}
                    for s, v in fw:
                        done[s] = max(done.get(s, 0), v)
                    for s, v in done.items():
                        eng.wait_ge(sems[s], v)
            return body

        with nc.Block() as block:
            block.tensor(run_engine("pe"))
            block.scalar(run_engine("act"))
            block.vector(run_engine("dve"))
            block.gpsimd(run_engine("pool"))
            block.sync(run_engine("sp"))


class Tl:
    def __init__(self, t, name):
        self.t = t
        self.b = Buf(name)

    def __getitem__(self, k):
        return self.t[k]


class Prog:
    def __init__(self, cfg):
        self.cfg = cfg
        self.nc = bass.Bass("TRN2", target_bir_lowering=False)
        self.S = Sched(self.nc)
        self.outs = []
        self.uid = 0
        self.din = {}
        self.dout = {}

    def sb(self, name, shape, dt=F32):
        return Tl(self.nc.alloc_sbuf_tensor(name, list(shape), dt), name)

    def ps(self, name, shape, dt=F32):
        t = Tl(self.nc.alloc_psum_tensor(name, list(shape), dt), name)
        t.b.excl = True
        return t

    def inp(self, name, shape):
        a = self.nc.dram_tensor(name, list(shape), F32, kind="ExternalInput").ap()
        self.din[name] = a
        return a

    def outp(self, name, shape):
        a = self.nc.dram_tensor(name, list(shape), F32, kind="ExternalOutput").ap()
        self.dout[name] = a
        return a

    @staticmethod
    def _b(xs):
        return [x.b if hasattr(x, "b") else x for x in xs]

    def I(self, eng, meth, *a, r=(), w=(), **kw):
        o = self.S.op(eng, lambda e: getattr(e, meth)(*a, **kw), self._b(r), self._b(w))
        if o is not None:
            o.desc = (eng, meth, str(kw.get("out", a[0] if a else ""))[:90], str(kw.get("func", kw.get("op", kw.get("op0", "")))))
        return o

    def dma(self, q, out, in_, r=(), w=(), final=False, **kw):
        o = self.S.op(q, lambda e: e.dma_start(out=out, in_=in_, **kw), self._b(r), self._b(w), dma=True)
        if o is not None:
            o.desc = (q, "dma", str(out)[:90], "")
        if final and o is not None:
            self.outs.append(o)
        return o

    def mm(self, out, lhsT, rhs, start, stop, r, w, **kw):
        return self.I("pe", "matmul", out, lhsT=lhsT, rhs=rhs, start=start, stop=stop, r=r, w=w, **kw)

    def tr(self, out, in_, ident, r, w):
        return self.I("pe", "transpose", out=out, in_=in_, identity=ident, r=r, w=w)


def bc(ap, shape):
    return ap.broadcast_to(list(shape))


def build(cfg):
    P = Prog(cfg)
    nc = P.nc
    I, dma, mm, tr = P.I, P.dma, P.mm, P.tr
    N_OWN = 128 * sum(cfg["own"])
    N_PRE = 128 * sum(n for n, _ in cfg["pre"])
    NB_OWN = N_OWN // 128
    NB_PRE = N_PRE // 128
    do_smp = cfg.get("sample", True)
    do_l1 = cfg.get("l1", True)

    x_own = P.inp("x_own", [N_OWN, D])
    x_pre = P.inp("x_pre", [max(N_PRE, 128), D])
    x_smp = P.inp("x_smp", [128, D])
    mem = P.inp("mem", [256, D])
    c_xk = P.inp("c_xk", [2, 256, 512])
    c_xv = P.inp("c_xv", [2, 256, 512])
    c_k = P.inp("c_k", [2048, 512])
    c_v = P.inp("c_v", [2048, 512])
    st_C = P.inp("st_C", [4, 128, 128])
    st_n = P.inp("st_n", [4, 128])
    st_m = P.inp("st_m", [4, 1])
    st_conv = P.inp("st_conv", [30, D])
    norm_g = P.inp("norm_g", [2, D])
    w_in_a = P.inp("w_in_a", [D, 5640])
    b_ig = P.inp("b_ig", [4, 1])
    b_fg = P.inp("b_fg", [4, 1])
    mlstm_g = P.inp("mlstm_g", [512])
    qn_g = P.inp("qn_g", [512])
    kn_g = P.inp("kn_g", [512])
    lamv = P.inp("lamv", [4, 64])
    subln_g = P.inp("subln_g", [512])
    w_out_a = P.inp("w_out_a", [1536, D])
    w_in_c = P.inp("w_in_c", [D, 4096])
    conv_wT = P.inp("conv_wT", [D, 31])
    conv_b = P.inp("conv_b", [D])
    ln_g = P.inp("ln_g", [D])
    ln_b = P.inp("ln_b", [D])
    w_out_c = P.inp("w_out_c", [1536, D])
    mem_norm_g = P.inp("mem_norm_g", [2, D])
    w_mem_kv = P.inp("w_mem_kv", [2, D, 1024])
    xq_g = P.inp("xq_g", [2, 512])
    xk_g = P.inp("xk_g", [2, 512])
    c_ident = P.inp("c_ident", [128, 128])
    c_bias = P.inp("c_bias", [128, 2, 4 * NDELTA])
    c_dmask = P.inp("c_dmask", [128, 2, 4, 128])
    c_amask = P.inp("c_amask", [128, 128])
    c_hsel = P.inp("c_hsel", [4, 4, 128])
    c_flag = P.inp("c_flag", [4, 1])

    y_own = P.outp("y_own", [N_OWN, D])
    y_smp = P.outp("y_smp", [32, D])
    o_pxk = P.outp("o_pxk", [2, 256, 512])
    o_pxv = P.outp("o_pxv", [2, 256, 512])
    o_pk = P.outp("o_pk", [N_OWN, 512])
    o_pv = P.outp("o_pv", [N_OWN, 512])
    o_pC = P.outp("o_pC", [4, 128, 128])
    o_pn = P.outp("o_pn", [4, 128])
    o_pm = P.outp("o_pm", [4, 1])
    o_pconv = P.outp("o_pconv", [30, D])
    o_sk = P.outp("o_sk", [32, 512])
    o_sv = P.outp("o_sv", [32, 512])
    o_sC = P.outp("o_sC", [4, 128, 128])
    o_sn = P.outp("o_sn", [4, 128])
    o_sm = P.outp("o_sm", [4, 1])
    o_sconv = P.outp("o_sconv", [30, D])

    wsrc32 = {"w_in_a": (w_in_a, [D, 5640]), "w_out_a": (w_out_a, [1536, D]), "w_in_c": (w_in_c, [D, 4096]),
              "w_out_c": (w_out_c, [1536, D]), "w_mem0": (w_mem_kv[0], [D, 1024]), "w_mem1": (w_mem_kv[1], [D, 1024])}
    wbf = {}
    wbf_b = {}
    for nm, (src32, shp) in wsrc32.items():
        wbf[nm] = nc.dram_tensor("bf_" + nm, shp, BF16, kind="Internal").ap()
        wbf_b[nm] = [Buf(f"wb_{nm}{i}") for i in range(shp[0] // 128)]

    def convert_weights(names):
        for nm in names:
            src32, shp = wsrc32[nm]
            for i in range(shp[0] // 128):
                dma("pool", wbf[nm][i * 128:(i + 1) * 128, :], src32[i * 128:(i + 1) * 128, :], w=[wbf_b[nm][i]])

    NBLK = max(NB_PRE + NB_OWN, 16)
    NHALF = (NBLK + 15) // 16
    scrK = nc.dram_tensor("scrK", [4, NHALF, 128, 2048], BF16, kind="Internal").ap()
    scrV = nc.dram_tensor("scrV", [4, NHALF, 128, 16 * 130], BF16, kind="Internal").ap()
    scr_b = [Buf(f"scr{i}") for i in range(NHALF * 16)]

    pb = [P.ps(f"pb{i}", [128, 512], F32) for i in range(7)]
    ptb_all = nc.alloc_psum_tensor("ptb_all", [128, 1024], BF16)

    class PV_:
        def __init__(self, name, ap):
            self.ap = ap
            self.b = Buf(name)

        def __getitem__(self, k):
            return self.ap[k]
    ptb = [PV_("ptb0", ptb_all[:, 0:512]), PV_("ptb1", pb[5][:, 0:256].bitcast(BF16))]
    ptb[0].b.excl = True
    ptb[1].b = pb[5].b
    identf = P.sb("identf", [128, 128])
    identb = P.sb("identb", [128, 128], BF16)
    onesf = P.sb("onesf", [128, 128])
    epsc = P.sb("epsc", [128, 1])
    lnk = P.sb("lnk", [128, 1])
    ng = P.sb("ng", [128, 2, 8])
    mng = P.sb("mng", [128, 2, 8])
    g_mlstm = P.sb("g_mlstm", [128, 512])
    g_qn = P.sb("g_qn", [128, 512])
    g_kn = P.sb("g_kn", [128, 512])
    g_subln = P.sb("g_subln", [128, 512])
    g_xq = P.sb("g_xq", [128, 2, 512])
    g_xk = P.sb("g_xk", [128, 2, 512])
    lam_t = P.sb("lam_t", [128, 4, 64])
    lam_s = P.sb("lam_s", [128, 4])
    nlam = P.sb("nlam", [128, 1])
    big = P.sb("big", [4, 1])
    bfg = P.sb("bfg", [4, 1])
    flag = P.sb("flag", [4, 1])
    biasT = P.sb("biasT", [128, 2, 4 * NDELTA])
    dmask = P.sb("dmask", [128, 2, 4, 128])
    amask = P.sb("amask", [128, 128])
    hsel = P.sb("hsel", [4, 4, 128])
    wgate = P.sb("wgate", [128, 8, 8], BF16)
    cw = P.sb("cw", [128, 8, 31])
    cb = P.sb("cb", [128, 8])
    lg = P.sb("lg", [128, 8])
    lb = P.sb("lb", [128, 8])
    dummy = P.sb("dummyt", [128, 2])

    NW = 4
    wring = [P.sb(f"wring{i}", [128, 4096], BF16) for i in range(NW)]
    wctr = [0]
    NXS = 3
    xs = [P.sb(f"xs{i}", [128, D]) for i in range(NXS)]
    y0 = P.sb("y0", [128, 4, D])
    y0b = [Buf(f"y0_{j}") for j in range(4)]
    xnb = P.sb("xnb", [128, 4, D], BF16)
    xnT = P.sb("xnT", [128, 8, 512], BF16)
    sqs = P.sb("sqs", [128, D])
    st1 = P.sb("st1", [128, 8])
    st2 = P.sb("st2", [128, 8])
    tk0 = P.sb("tk0", [128, 512])
    tk1 = P.sb("tk1", [128, 512])
    tk2 = P.sb("tk2", [128, 512])
    tkrot = [tk0, tk1, tk2]
    tkrc = [0]
    tkb = P.sb("tkb", [128, 512], BF16)
    tkbs = [tkb, P.sb("tkb2", [128, 512], BF16)]
    tkc = [0]
    mixT = P.sb("mixT", [128, 12, 512], BF16)
    PT = [[P.sb(f"PT{i}{m}", [128, 512], BF16) for m in range(2)] for i in range(2)]
    pctr = [0]
    ob = P.sb("ob", [128, 4, 512])
    xqT = P.sb("xqT", [128, 4, 512], BF16)
    mKT = P.sb("mKT", [128, 2, 4, 256], BF16)
    mV = P.sb("mV", [128, 2, 2, 4, 130], BF16)
    e1 = P.sb("e1", [128, 512])
    Cst = P.sb("Cst", [128, 4, 130])
    Cbf = P.sb("Cbf", [128, 4, 130], BF16)
    mst = P.sb("mst", [4, 1])
    gtok = P.sb("gtok", [128, 4, 8])
    cA = P.sb("cA", [4, 8])
    cB = P.sb("cB", [4, 8])
    cM = P.sb("cM", [4, 9])
    cX = P.sb("cX", [4, 8])
    cD = P.sb("cD", [4, 8])
    cDx = P.sb("cDx", [4, 8, 4])
    decb = P.sb("decb", [128, 8, 4])
    den4 = P.sb("den4", [128, 4])
    rl = P.sb("rl", [128, 2])
    rl2 = [P.sb("rl2a", [128, 2]), P.sb("rl2b", [128, 2])]
    rl8s = [P.sb("rl8a", [128, 8]), P.sb("rl8b", [128, 8])]
    rl8c = [0]
    xoc = [0]
    halo = P.sb("halo", [128, 8, 30])

    ARW = 12288
    arena = nc.alloc_sbuf_tensor("arena", [128, ARW], F32)
    arena_tls = []

    class View:
        def __init__(self, name, ap):
            self.ap = ap
            self.b = Buf(name)
            arena_tls.append(self)

        def __getitem__(self, k):
            return self.ap[k]

    aoff = [0]

    def av(name, words, dt, pat=None, part=None, **kw):
        a = arena[:, aoff[0]:aoff[0] + words] if part is None else arena[0:part, aoff[0]:aoff[0] + words]
        aoff[0] += words
        assert aoff[0] <= ARW, (name, aoff[0])
        if dt == BF16:
            a = a.bitcast(BF16)
        if pat is not None:
            a = a.rearrange(pat, **kw)
        return View(name, a)

    aoff[0] = 0
    aqT = av("aqT", 1024, BF16, "p (h t) -> p h t", h=4)
    akT = av("akT", 1024, BF16, "p (h t) -> p h t", h=4)
    ga = av("ga", 2048, F32, "p (j c) -> p j c", j=4)
    Stil = av("Stil", 256, BF16, "p (h t) -> p h t", h=4)
    hraw = av("hraw", 512, F32)
    kw = av("kw", 1024, BF16, "p (j h d) -> p j h d", j=4, h=4)
    avx = av("avx", 1040, BF16, "p (j h e) -> p j h e", j=4, h=4)
    gI = av("gI", 512, F32, part=4)
    gF = av("gF", 512, F32, part=4)
    gB = av("gB", 512, F32, part=4)
    gA = av("gA", 512, F32, part=4)
    gZ = av("gZ", 512, F32, part=4)
    gWe = av("gWe", 512, F32, part=4)
    gTh = av("gTh", 512, F32, part=4)
    aoff[0] = 0
    KTo = av("KTo", 1024, BF16, "p (h t) -> p h t", h=4)
    Vo = av("Vo", 1040, BF16, "p (j h e) -> p j h e", j=4, h=4)
    assert aoff[0] <= 4864
    bqT = av("bqT", 1024, BF16, "p (h t) -> p h t", h=4)
    rK = [av(f"rK{i}", 1024, BF16) for i in range(2)]
    rV = [av(f"rV{i}", 1040, BF16, "p (b e) -> p b e", b=16) for i in range(2)]
    rctr = [0]
    aoff[0] = 0
    uT = av("uT", 2168, BF16, "p (c t) -> p c t", c=8)
    dgr = [av(f"dg{i}", 64, BF16) for i in range(8)]
    dgc = [0]
    cacc = av("cacc", 4096, F32, "p (c t) -> p c t", c=8)
    szT = av("szT", 2048, BF16, "p (c t) -> p c t", c=8)
    e2 = av("e2", 512, F32)
    lmean = av("lmean", 512, F32)
    lrstd = av("lrstd", 512, F32)

    def switch(phase):
        flush()
        I("pool", "memset", dummy[:, 0:1], 0.0, w=[dummy] + arena_tls)
        if phase in ("A", "AB"):
            I("pool", "memset", avx[:, :, :, 128:130], 1.0, w=[avx])
        if phase in ("B", "AB"):
            I("pool", "memset", Vo[:, :, :, 128:130], 1.0, w=[Vo])
        if phase == "B":
            for r_ in rV:
                I("pool", "memset", r_[:, :, 128:130], 1.0, w=[r_])

    dma("sp", identf[:], c_ident, w=[identf])
    I("dve", "tensor_copy", out=identb[:], in_=identf[:], r=[identf], w=[identb])
    I("pool", "memset", onesf[:], 1.0, w=[onesf])
    I("pool", "memset", epsc[:], EPS, w=[epsc])
    I("pool", "memset", lnk[:], math.log(128.0 ** -0.5), w=[lnk])
    for l in range(2):
        dma("sp", ng[:, l, :], norm_g[l].rearrange("(c p) -> p c", p=128), w=[ng], allow_slow_non_contiguous=True)
        dma("sp", mng[:, l, :], mem_norm_g[l].rearrange("(c p) -> p c", p=128), w=[mng], allow_slow_non_contiguous=True)
        dma("sp", g_xq[:, l, :], xq_g[l].partition_broadcast(128), w=[g_xq])
        dma("sp", g_xk[:, l, :], xk_g[l].partition_broadcast(128), w=[g_xk])
    dma("sp", g_mlstm[:], mlstm_g.partition_broadcast(128), w=[g_mlstm])
    dma("sp", g_qn[:], qn_g.partition_broadcast(128), w=[g_qn])
    dma("sp", g_kn[:], kn_g.partition_broadcast(128), w=[g_kn])
    dma("sp", g_subln[:], subln_g.partition_broadcast(128), w=[g_subln])
    dma("sp", lam_t[:].rearrange("p a d -> p (a d)"), lamv.rearrange("a d -> (a d)").partition_broadcast(128), w=[lam_t])
    dma("sp", big[:], b_ig, w=[big])
    dma("sp", bfg[:], b_fg, w=[bfg])
    dma("sp", flag[:], c_flag, w=[flag])
    dma("sp", biasT[:], c_bias, w=[biasT])
    dma("sp", dmask[:], c_dmask, w=[dmask])
    dma("sp", amask[:], c_amask, w=[amask])
    dma("sp", hsel[:], c_hsel, w=[hsel])
    convert_weights(["w_mem0", "w_in_a", "w_out_a", "w_mem1", "w_in_c", "w_out_c"])
    dma("pool", wgate[:], wbf["w_in_a"].rearrange("(c p) n -> p c n", p=128)[:, :, 2560:2568], r=wbf_b["w_in_a"], w=[wgate])
    dma("sp", cw[:], conv_wT.rearrange("(c p) j -> p c j", p=128), w=[cw])
    dma("sp", cb[:], conv_b.rearrange("(c p) -> p c", p=128), w=[cb], allow_slow_non_contiguous=True)
    dma("sp", lg[:], ln_g.rearrange("(c p) -> p c", p=128), w=[lg], allow_slow_non_contiguous=True)
    dma("sp", lb[:], ln_b.rearrange("(c p) -> p c", p=128), w=[lb], allow_slow_non_contiguous=True)
    I("dve", "tensor_scalar", out=g_subln[:], in0=g_subln[:], scalar1=1.0 - LAM_INIT0, scalar2=None, op0=ALU.mult, r=[g_subln], w=[g_subln])
    I("dve", "tensor_scalar", out=bfg[:], in0=bfg[:], scalar1=-1.0, scalar2=None, op0=ALU.mult, r=[bfg], w=[bfg])
    I("dve", "tensor_tensor", out=lam_t[:, 0, :], in0=lam_t[:, 0, :], in1=lam_t[:, 1, :], op=ALU.mult, r=[lam_t], w=[lam_t])
    I("dve", "tensor_tensor", out=lam_t[:, 2, :], in0=lam_t[:, 2, :], in1=lam_t[:, 3, :], op=ALU.mult, r=[lam_t], w=[lam_t])
    I("dve", "tensor_reduce", out=lam_s[:, 0:1], in_=lam_t[:, 0, :], axis=AX.X, op=ALU.add, r=[lam_t], w=[lam_s])
    I("dve", "tensor_reduce", out=lam_s[:, 1:2], in_=lam_t[:, 2, :], axis=AX.X, op=ALU.add, r=[lam_t], w=[lam_s])
    I("act", "activation", out=lam_s[:, 2:4], in_=lam_s[:, 0:2], func=AF.Exp, r=[lam_s], w=[lam_s])
    I("dve", "tensor_tensor", out=nlam[:], in0=lam_s[:, 3:4], in1=lam_s[:, 2:3], op=ALU.subtract, r=[lam_s], w=[nlam])
    I("dve", "tensor_scalar", out=nlam[:], in0=nlam[:], scalar1=-LAM_INIT0, scalar2=None, op0=ALU.add, r=[nlam], w=[nlam])
    I("pool", "memset", mV[:], 1.0, w=[mV])
    I("pool", "memset", halo[:], 0.0, w=[halo])

    def load_w(nm, c0, ncols, nkc):
        wt = wring[wctr[0] % NW]
        wctr[0] += 1
        v = wt[:, 0:nkc * ncols].rearrange("p (c n) -> p c n", c=nkc)
        dma("pool", v, wbf[nm].rearrange("(c p) n -> p c n", p=128)[:, :, c0:c0 + ncols], r=wbf_b[nm], w=[wt])
        return (wt, v)

    def make_xnT(nsb, gcol, src_fn, stats_done=False):
        flush()
        if not stats_done:
            xn_stats(nsb, src_fn)
        xn_T(nsb, gcol)

    def xn_stats(nsb, src_fn):
        for j in range(nsb):
            sap, stl = src_fn(j)
            I("act", "activation", out=sqs[:], in_=sap, func=AF.Square, accum_out=st1[:, j:j + 1], r=[stl], w=[sqs, st1])
            I("act", "activation", out=st2[:, j:j + 1], in_=st1[:, j:j + 1], func=AF.Ln, scale=1.0 / D, bias=epsc[:], r=[st1, epsc], w=[st2])
            I("act", "activation", out=st2[:, j:j + 1], in_=st2[:, j:j + 1], func=AF.Exp, scale=-0.5, r=[st2], w=[st2])
            I("dve", "tensor_scalar", out=xnb[:, j, :], in0=sap, scalar1=st2[:, j:j + 1], scalar2=None, op0=ALU.mult, r=[stl, st2], w=[xnb])
            if hasattr(src_fn, "release"):
                src_fn.release(j)

    def xn_T(nsb, gcol):
        N = 128 * nsb
        for kc in range(8):
            pt = ptb[kc % 2]
            for j in range(nsb):
                tr(pt[:, j * 128:(j + 1) * 128], xnb[:, j, kc * 128:(kc + 1) * 128], identb[:], r=[xnb, identb], w=[pt])
            if kc % 2 == 0:
                I("dve", "tensor_scalar", out=xnT[:, kc, 0:N], in0=pt[:, 0:N], scalar1=gcol[:, kc:kc + 1], scalar2=None, op0=ALU.mult, r=[pt, ng, mng], w=[xnT])
            else:
                I("act", "activation", out=xnT[:, kc, 0:N], in_=pt[:, 0:N], func=AF.Copy, scale=gcol[:, kc:kc + 1], r=[pt, ng, mng], w=[xnT])

    xplan = []
    for l_ in range(2 if do_l1 else 1):
        xplan += [(mem, 0), (mem, 128)]
    blk_ = 0
    for (n_, m_) in cfg["pre"]:
        xplan += [(x_pre, (blk_ + k_) * 128) for k_ in range(n_)]
        blk_ += n_
    blk_ = 0
    for n_ in cfg["own"]:
        xplan += [(x_own, (blk_ + k_) * 128) for k_ in range(n_)]
        blk_ += n_
    if do_smp:
        xplan += [(x_smp, 0)]
    xstate = {"issued": 0, "next": 0}

    def x_issue():
        i = xstate["issued"]
        if i < len(xplan):
            src_, row_ = xplan[i]
            t = xs[i % NXS]
            dma("sp", t[:], src_[row_:row_ + 128, :], w=[t])
            xstate["issued"] = i + 1

    def dram_src(src, row0):
        def fn(j):
            i = xstate["next"]
            assert xplan[i][0] is src and xplan[i][1] == row0 + j * 128, (i, row0, j)
            while xstate["issued"] <= i:
                x_issue()
            xstate["next"] = i + 1
            t = xs[i % NXS]
            return t[:], t
        fn.release = lambda j: x_issue()
        return fn

    def proj_tok(w, j, ncols, out_ps, nkc=8, coff=0):
        wt, wv = w
        for kc in range(nkc):
            mm(out_ps[:, 0:ncols], xnT[:, kc, j * 128:(j + 1) * 128], wv[:, kc, coff:coff + ncols], kc == 0, kc == nkc - 1, r=[xnT, wt], w=[out_ps])

    def proj_feat(w, c, N, out_ps, nkc=8):
        wt, wv = w
        for kc in range(nkc):
            mm(out_ps[:, 0:N], wv[:, kc, c * 128:(c + 1) * 128], xnT[:, kc, 0:N], kc == 0, kc == nkc - 1, r=[xnT, wt], w=[out_ps])

    def group_norm(src_ap, src_tl, ng_, gs, gain_ap, gain_tl, dst_ap, dst_tl):
        n = ng_ * gs
        I("act", "activation", out=sqs[:, 0:n], in_=src_ap, func=AF.Square, r=[src_tl], w=[sqs])
        I("dve", "tensor_reduce", out=st1[:, 0:ng_], in_=sqs[:, 0:n].rearrange("p (g d) -> p g d", g=ng_), axis=AX.X, op=ALU.add, r=[sqs], w=[st1])
        I("act", "activation", out=st2[:, 0:ng_], in_=st1[:, 0:ng_], func=AF.Ln, scale=1.0 / gs, bias=epsc[:], r=[st1, epsc], w=[st2])
        I("act", "activation", out=st2[:, 0:ng_], in_=st2[:, 0:ng_], func=AF.Exp, scale=-0.5, r=[st2], w=[st2])
        I("dve", "tensor_tensor", out=dst_ap.rearrange("p (g d) -> p g d", g=ng_), in0=src_ap.rearrange("p (g d) -> p g d", g=ng_),
          in1=bc(st2[:, 0:ng_].unsqueeze(2), [128, ng_, gs]), op=ALU.mult, r=[src_tl, st2], w=[dst_tl])
        I("dve", "tensor_tensor", out=dst_ap, in0=dst_ap, in1=gain_ap, op=ALU.mult, r=[dst_tl, gain_tl], w=[dst_tl])

    def silu_gate(ps_tl, ncols, dst_ap, dst_tl, mul_ap, mul_tl):
        I("act", "activation", out=e1[:, 0:ncols], in_=ps_tl[:, 0:ncols], func=AF.Silu, r=[ps_tl], w=[e1])
        I("dve", "tensor_tensor", out=dst_ap, in0=e1[:, 0:ncols], in1=mul_ap, op=ALU.mult, r=[e1, mul_tl], w=[dst_tl])

    pending = []

    def flush(keep=0):
        n = len(pending) - keep
        if n <= 0:
            return
        fs = pending[:n]
        del pending[:n]
        for f_ in fs:
            f_()

    def tok_to_T(src_ap, src_tl, dstT, j, dst_tl):
        flush(keep=1)
        tb = tkbs[tkc[0] % 2]
        tkc[0] += 1
        I("dve", "tensor_copy", out=tb[:], in_=src_ap, r=[src_tl], w=[tb])

        def later():
            pt = ptb[j % 2]
            for c in range(4):
                tr(pt[:, c * 128:(c + 1) * 128], tb[:, c * 128:(c + 1) * 128], identb[:], r=[tb, identb], w=[pt])
            I("act", "activation", out=dstT, in_=pt[:, 0:512].rearrange("p (c t) -> p c t", c=4), func=AF.Copy, r=[pt], w=[dst_tl])
        pending.append(later)

    def to_mixT(src_ap, src_tl, j, c0):
        tok_to_T(src_ap, src_tl, mixT[:, c0:c0 + 4, j * 128:(j + 1) * 128], j, mixT)

    def mem_kv(l):
        make_xnT(2, mng[:, l, :], dram_src(mem, 0))
        wk = load_w(f"w_mem{l}", 0, 512, 8)
        wv = load_w(f"w_mem{l}", 512, 512, 8)
        for j in range(2):
            proj_tok(wk, j, 512, pb[0])
            group_norm(pb[0][:, :], pb[0], 4, 128, g_xk[:, l, :], g_xk, tk0[:], tk0)
            dma("sp", o_pxk[l, j * 128:(j + 1) * 128, :], tk0[:], r=[tk0], final=True)
            tok_to_T(tk0[:], tk0, mKT[:, l, :, j * 128:(j + 1) * 128], j, mKT)
            proj_tok(wv, j, 512, pb[1])
            I("dve", "tensor_copy", out=tk1[:], in_=pb[1][:, :], r=[pb[1]], w=[tk1])
            I("act", "activation", out=mV[:, l, j, :, 0:128], in_=pb[1][:, :].rearrange("p (h d) -> p h d", h=4), func=AF.Copy, r=[pb[1]], w=[mV])
            dma("sp", o_pxv[l, j * 128:(j + 1) * 128, :], tk1[:], r=[tk1], final=True)

    def smp_mem_kv():
        for l in range(2):
            for j in range(2):
                dma("sp", tk0[:], c_xk[l, j * 128:(j + 1) * 128, :], w=[tk0])
                tok_to_T(tk0[:], tk0, mKT[:, l, :, j * 128:(j + 1) * 128], j, mKT)
                dma("pool", mV[:, l, j, :, 0:128], c_xv[l, j * 128:(j + 1) * 128, :].rearrange("p (h d) -> p h d", h=4), w=[mV])

    def x_q(l, nsb, wsrc, cq):
        wq = load_w(wsrc, cq, 512, 8)
        for j in range(nsb):
            tkq = tkrot[tkrc[0] % 3]
            tkrc[0] += 1
            proj_tok(wq, j, 512, pb[j % 4])
            group_norm(pb[j % 4][:, :], pb[j % 4], 4, 128, g_xq[:, l, :], g_xq, tkq[:], tkq)
            tok_to_T(tkq[:], tkq, xqT[:, :, j * 128:(j + 1) * 128], j, xqT)

    def x_branch(l, nsb, wsrc, cq, cz, c0, q_done=False):
        N = 128 * nsb
        if not q_done:
            x_q(l, nsb, wsrc, cq)
        wz = load_w(wsrc, cz, 512, 8)
        flush()
        def x_scores(h):
            pts = PT[pctr[0] % 2]
            pctr[0] += 1
            for mb in range(2):
                sp = pb[2 * (h % 2) + mb]
                mm(sp[:, 0:N], mKT[:, l, h, mb * 128:(mb + 1) * 128], xqT[:, h, 0:N], True, True, r=[mKT, xqT], w=[sp])
                I("act", "activation", out=pts[mb][:, 0:N], in_=sp[:, 0:N], func=AF.Exp, scale=128.0 ** -0.5, r=[sp], w=[pts[mb]])
            return pts

        def x_pv(h, pts):
            for j in range(nsb):
                oa = pb[4 + (xoc[0] % 3)]
                xoc[0] += 1
                for mb in range(2):
                    mm(oa[:, 0:129], pts[mb][:, j * 128:(j + 1) * 128], mV[:, l, mb, h, 0:129], mb == 0, mb == 1, r=[pts[mb], mV], w=[oa])
                rr = rl2[xoc[0] % 2]
                I("dve", "reciprocal", out=rr[:, 0:1], in_=oa[:, 128:129], r=[oa], w=[rr])
                I("dve", "tensor_scalar", out=ob[:, j, h * 128:(h + 1) * 128], in0=oa[:, 0:128], scalar1=rr[:, 0:1], scalar2=None, op0=ALU.mult, r=[oa, rr], w=[ob])

        prev_ = None
        for h in range(4):
            pts_h = x_scores(h)
            if prev_ is not None:
                x_pv(*prev_)
            prev_ = (h, pts_h)
        x_pv(*prev_)
        for j in range(nsb):
            proj_tok(wz, j, 512, pb[j % 4])
            silu_gate(pb[j % 4], 512, tk0[:], tk0, ob[:, j, :], ob)
            to_mixT(tk0[:], tk0, j, c0)

    def a_gates(nsb, Lc, nch_per_sb):
        flush()
        N = 128 * nsb
        nch = nsb * nch_per_sb
        CS = 128 // nch_per_sb
        for kc in range(8):
            mm(pb[0][0:4, 0:N], wgate[:, kc, 0:4], xnT[:, kc, 0:N], kc == 0, kc == 7, r=[xnT, wgate], w=[pb[0]])
        for kc in range(8):
            mm(pb[1][0:4, 0:N], wgate[:, kc, 4:8], xnT[:, kc, 0:N], kc == 0, kc == 7, r=[xnT, wgate], w=[pb[1]])
        I("dve", "tensor_scalar", out=gI[:, 0:N], in0=pb[0][0:4, 0:N], scalar1=big[:, 0:1], scalar2=None, op0=ALU.add, r=[pb[0], big], w=[gI])
        I("act", "activation", out=gF[:, 0:N], in_=pb[1][0:4, 0:N], func=AF.Exp, scale=-1.0, bias=bfg[:, 0:1], r=[pb[1], bfg], w=[gF])
        I("act", "activation", out=gF[:, 0:N], in_=gF[:, 0:N], func=AF.Ln, scale=1.0, bias=onesf[0:4, 0:1], r=[gF, onesf], w=[gF])
        I("pool", "memset", gZ[:, 0:N], 0.0, w=[gZ])
        I("pool", "memset", gB[:, 0:N], 0.0, w=[gB])
        for c in range(nch):
            sl = slice(c * CS, c * CS + Lc)
            I("dve", "tensor_tensor_scan", out=gB[:, sl], data0=gF[:, sl], data1=gZ[:, sl], initial=0.0, op0=ALU.add, op1=ALU.add, r=[gF, gZ], w=[gB])
        I("dve", "tensor_tensor", out=gA[:, 0:N], in0=gI[:, 0:N], in1=gB[:, 0:N], op=ALU.add, r=[gI, gB], w=[gA])
        gA3 = gA[:, 0:N].rearrange("p (c t) -> p c t", t=CS)
        gB3 = gB[:, 0:N].rearrange("p (c t) -> p c t", t=CS)
        I("dve", "tensor_reduce", out=cA[:, 0:nch], in_=gA3[:, :, 0:Lc], axis=AX.X, op=ALU.max, r=[gA], w=[cA])
        I("dve", "tensor_scalar", out=cB[:, 0:nch], in0=gB3[:, :, Lc - 1], scalar1=-1.0, scalar2=None, op0=ALU.mult, r=[gB], w=[cB])
        I("dve", "tensor_copy", out=cM[:, 0:1], in_=mst[:], r=[mst], w=[cM])
        I("dve", "tensor_tensor_scan", out=cM[:, 1:nch + 1], data0=cA[:, 0:nch], data1=cB[:, 0:nch], initial=mst[:, 0:1], op0=ALU.max, op1=ALU.add, r=[cA, cB, mst], w=[cM])
        I("dve", "tensor_copy", out=mst[:], in_=cM[:, nch:nch + 1], r=[cM], w=[mst])
        I("dve", "tensor_tensor", out=cX[:, 0:nch], in0=cM[:, 0:nch], in1=cA[:, 0:nch], op=ALU.max, r=[cM, cA], w=[cX])
        I("dve", "tensor_tensor", out=cD[:, 0:nch], in0=cM[:, 0:nch], in1=cX[:, 0:nch], op=ALU.subtract, r=[cM, cX], w=[cD])
        I("act", "activation", out=cD[:, 0:nch], in_=cD[:, 0:nch], func=AF.Exp, r=[cD], w=[cD])
        Xb = bc(cX[:, 0:nch].unsqueeze(2), [4, nch, CS])
        I("dve", "tensor_tensor", out=gA3, in0=gA3, in1=Xb, op=ALU.subtract, r=[gA, cX], w=[gA])
        I("act", "activation", out=gWe[:, 0:N], in_=gA[:, 0:N], func=AF.Exp, bias=lnk[0:4, 0:1], r=[gA, lnk], w=[gWe])
        I("dve", "tensor_tensor", out=gI[:, 0:N].rearrange("p (c t) -> p c t", t=CS), in0=gB3, in1=Xb, op=ALU.subtract, r=[gB, cX], w=[gI])
        I("act", "activation", out=gTh[:, 0:N], in_=gI[:, 0:N], func=AF.Exp, r=[gI], w=[gTh])
        for j in range(nsb):
            tr(pb[2][:, 0:4], gWe[:, j * 128:(j + 1) * 128], identf[0:4, 0:4], r=[gWe, identf], w=[pb[2]])
            tr(pb[2][:, 4:8], gTh[:, j * 128:(j + 1) * 128], identf[0:4, 0:4], r=[gTh, identf], w=[pb[2]])
            I("dve", "tensor_copy", out=gtok[:, j, :], in_=pb[2][:, 0:8], r=[pb[2]], w=[gtok])
        I("dve", "tensor_tensor", out=cDx[:, 0:nch, :], in0=bc(cD[:, 0:nch].unsqueeze(2), [4, nch, 4]),
          in1=hsel[:, :, 0:nch].rearrange("k h c -> k c h"), op=ALU.mult, r=[cD, hsel], w=[cDx])
        mm(pb[2][:, 0:nch * 4], onesf[0:4, :], cDx[:, 0:nch, :].rearrange("k c h -> k (c h)"), True, True, r=[onesf, cDx], w=[pb[2]])
        I("dve", "tensor_copy", out=decb[:, 0:nch, :], in_=pb[2][:, 0:nch * 4].rearrange("p (c h) -> p c h", h=4), r=[pb[2]], w=[decb])

    def evac(h, out_ap, in_ap, r, w):
        if h % 2:
            I("act", "activation", out=out_ap, in_=in_ap, func=AF.Copy, r=r, w=w)
        else:
            I("dve", "tensor_copy", out=out_ap, in_=in_ap, r=r, w=w)

    def a_branch(nsb, mode, wsrc, Lc=64, nch_per_sb=2, state_later=False):
        N = 128 * nsb
        full = mode != "pre"
        CS = 128 // nch_per_sb
        if full:
            wq = load_w(wsrc, 0, 512, 8)
            for h in range(4):
                proj_feat(wq, h, N, pb[h % 4])
                evac(h, aqT[:, h, 0:N], pb[h % 4][:, 0:N], [pb[h % 4]], [aqT])
        wk = load_w(wsrc, 512, 512, 8)
        a_gates(nsb, Lc, nch_per_sb)
        if full:
            for h in range(4):
                proj_feat(wk, h, N, pb[h % 4])
                evac(h, akT[:, h, 0:N], pb[h % 4][:, 0:N], [pb[h % 4]], [akT])
        for j in range(nsb):
            proj_tok(wk, j, 512, pb[j % 4])
            I("dve", "tensor_tensor", out=kw[:, j, :, :], in0=pb[j % 4][:, :].rearrange("p (h d) -> p h d", h=4),
              in1=bc(gtok[:, j, 0:4].unsqueeze(2), [128, 4, 128]), op=ALU.mult, r=[pb[j % 4], gtok], w=[kw])
        wv = load_w(wsrc, 1024, 512, 8)
        for j in range(nsb):
            proj_tok(wv, j, 512, pb[j % 4])
            I("act", "activation", out=avx[:, j, :, 0:128], in_=pb[j % 4][:, :].rearrange("p (h d) -> p h d", h=4), func=AF.Copy, r=[pb[j % 4]], w=[avx])
        if full:
            wo = load_w(wsrc, 1536, 512, 8)
            for j in range(nsb):
                proj_tok(wo, j, 512, pb[j % 4])
                I("act", "activation", out=ga[:, j, :], in_=pb[j % 4][:, :], func=AF.Sigmoid, r=[pb[j % 4]], w=[ga])
            wz = load_w(wsrc, 2048, 512, 8)
            for j in range(nsb):
                proj_tok(wz, j, 512, pb[j % 4])
                silu_gate(pb[j % 4], 512, ga[:, j, :], ga, ga[:, j, :], ga)
        if state_later:
            def later_state():
                k_ = 0
                for j in range(nsb):
                    for cc in range(nch_per_sb):
                        rows = slice(cc * CS, cc * CS + CS)
                        c = j * nch_per_sb + cc
                        ups = (pb[6], pb[3]) if k_ % 2 == 0 else (pb[4], pb[5])
                        k_ += 1
                        I("dve", "tensor_tensor", out=Cst[:], in0=Cst[:], in1=bc(decb[:, c, :].unsqueeze(2), [128, 4, 130]), op=ALU.mult, r=[Cst, decb], w=[Cst])
                        for h in range(4):
                            up = ups[h // 2]
                            hh = h % 2
                            mm(up[:, hh * 129:(hh + 1) * 129], kw[rows, j, h, :], avx[rows, j, h, 0:129], True, True, r=[kw, avx], w=[up])
                        for hp in range(2):
                            I("dve", "tensor_tensor", out=Cst[:, 2 * hp:2 * hp + 2, 0:129], in0=Cst[:, 2 * hp:2 * hp + 2, 0:129],
                              in1=ups[hp][:, 0:258].rearrange("p (h e) -> p h e", h=2), op=ALU.add, r=[Cst, ups[hp]], w=[Cst])
            return later_state
        for j in range(nsb):
            if full:
                for h in range(4):
                    mm(pb[0][:, h * 128:(h + 1) * 128], akT[:, h, j * 128:(j + 1) * 128], aqT[:, h, j * 128:(j + 1) * 128], True, True, r=[akT, aqT], w=[pb[0]])
                for h in range(4):
                    I("dve", "scalar_tensor_tensor", out=Stil[:, h, :], in0=pb[0][:, h * 128:(h + 1) * 128], scalar=gtok[:, j, h:h + 1], in1=amask[:],
                      op0=ALU.mult, op1=ALU.mult, r=[pb[0], gtok, amask], w=[Stil])
            for cc in range(nch_per_sb):
                rows = slice(cc * CS, cc * CS + CS)
                c = j * nch_per_sb + cc
                t0 = j * 128 + cc * CS
                if full:
                    I("dve", "tensor_tensor", out=Cbf[:], in0=Cst[:], in1=bc(decb[:, c, :].unsqueeze(2), [128, 4, 130]), op=ALU.mult, r=[Cst, decb], w=[Cbf])
                I("dve", "tensor_tensor", out=Cst[:], in0=Cst[:], in1=bc(decb[:, c, :].unsqueeze(2), [128, 4, 130]), op=ALU.mult, r=[Cst, decb], w=[Cst])
                if full:
                    for h in range(4):
                        ac = pb[4] if h < 2 else pb[5]
                        hh = h % 2
                        mm(ac[rows, hh * 129:(hh + 1) * 129], aqT[:, h, t0:t0 + CS], Cbf[:, h, 0:129], True, False, r=[aqT, Cbf], w=[ac])
                        mm(ac[rows, hh * 129:(hh + 1) * 129], Stil[:, h, cc * CS:cc * CS + CS], avx[:, j, h, 0:129], False, True, r=[Stil, avx], w=[ac])
                ups = (pb[6], pb[3]) if c % 2 == 0 else (pb[1], pb[2])
                for h in range(4):
                    up = ups[h // 2]
                    hh = h % 2
                    mm(up[:, hh * 129:(hh + 1) * 129], kw[rows, j, h, :], avx[rows, j, h, 0:129], True, True, r=[kw, avx], w=[up])
                for hp in range(2):
                    I("dve", "tensor_tensor", out=Cst[:, 2 * hp:2 * hp + 2, 0:129], in0=Cst[:, 2 * hp:2 * hp + 2, 0:129],
                      in1=ups[hp][:, 0:258].rearrange("p (h e) -> p h e", h=2), op=ALU.add, r=[Cst, ups[hp]], w=[Cst])
            if full:
                for hp in range(2):
                    ac = pb[4 + hp]
                    I("act", "activation", out=den4[:, 2 * hp:2 * hp + 2], in_=ac[:, 0:258].rearrange("p (h e) -> p h e", h=2)[:, :, 128], func=AF.Abs, r=[ac], w=[den4])
                I("dve", "tensor_tensor", out=den4[:], in0=den4[:], in1=gtok[:, j, 4:8], op=ALU.max, r=[den4, gtok], w=[den4])
                I("dve", "reciprocal", out=den4[:], in_=den4[:], r=[den4], w=[den4])
                for hp in range(2):
                    ac = pb[4 + hp]
                    I("dve", "tensor_tensor", out=hraw[:, hp * 256:(hp + 1) * 256].rearrange("p (h e) -> p h e", h=2),
                      in0=ac[:, 0:258].rearrange("p (h e) -> p h e", h=2)[:, :, 0:128],
                      in1=bc(den4[:, 2 * hp:2 * hp + 2].unsqueeze(2), [128, 2, 128]), op=ALU.mult, r=[ac, den4], w=[hraw])
                group_norm(hraw[:], hraw, 4, 128, g_mlstm[:], g_mlstm, tk0[:], tk0)
                I("dve", "tensor_tensor", out=tk0[:], in0=tk0[:], in1=ga[:, j, :], op=ALU.mult, r=[tk0, ga], w=[tk0])
                to_mixT(tk0[:], tk0, j, 0)

    def scr_write(j, blk):
        flush()
        hf, bi = blk // 16, blk % 16
        dma("sp", scrK[:, hf, :, bi * 128:(bi + 1) * 128].rearrange("h p k -> p h k"), KTo[:, :, j * 128:(j + 1) * 128], r=[KTo], w=[scr_b[blk]])
        dma("sp", scrV[:, hf, :, bi * 130:(bi + 1) * 130].rearrange("h p e -> p h e"), Vo[:, j, :, :], r=[Vo], w=[scr_b[blk]])

    def b_kv(nsb, wsrc, blk0, out_row0, smp=False):
        wk = load_w(wsrc, 3080, 512, 8)
        for j in range(nsb):
            proj_tok(wk, j, 512, pb[j % 4])
            tkr = tk0 if j % 2 == 0 else tk2
            group_norm(pb[j % 4][:, :], pb[j % 4], 8, 64, g_kn[:], g_kn, tkr[:], tkr)
            if out_row0 is not None:
                if smp:
                    dma("sp", o_sk[0:32, :], tkr[0:32, :], r=[tkr], final=True)
                else:
                    dma("sp", o_pk[out_row0 + j * 128:out_row0 + (j + 1) * 128, :], tkr[:], r=[tkr], final=True)
            tok_to_T(tkr[:], tkr, KTo[:, :, j * 128:(j + 1) * 128], j, KTo)
        wv = load_w(wsrc, 3592, 512, 8)
        for j in range(nsb):
            proj_tok(wv, j, 512, pb[j % 4])
            if out_row0 is not None:
                I("dve", "tensor_copy", out=tk1[:], in_=pb[j % 4][:, :], r=[pb[j % 4]], w=[tk1])
                if smp:
                    dma("sp", o_sv[0:32, :], tk1[0:32, :], r=[tk1], final=True)
                else:
                    dma("sp", o_pv[out_row0 + j * 128:out_row0 + (j + 1) * 128, :], tk1[:], r=[tk1], final=True)
            I("act", "activation", out=Vo[:, j, :, 0:128], in_=pb[j % 4][:, :].rearrange("p (h d) -> p h d", h=4), func=AF.Copy, r=[pb[j % 4]], w=[Vo])
        if blk0 is not None:
            for j in range(nsb):
                scr_write(j, blk0 + j)

    def b_attn(nsb, wsrc, tile_blk, n_masked, diag_kind):
        N = 128 * nsb
        wq = load_w(wsrc, 2568, 512, 8)
        for j in range(nsb):
            proj_tok(wq, j, 512, pb[j % 4])
            group_norm(pb[j % 4][:, :], pb[j % 4], 8, 64, g_qn[:], g_qn, tk0[:], tk0)
            tok_to_T(tk0[:], tk0, bqT[:, :, j * 128:(j + 1) * 128], j, bqT)
        wz = load_w(wsrc, 4104, 512, 8)
        flush()
        items = []
        for h in range(4):
            nsub = NSUBH[h] if nsb == 4 else nsb
            gsz = nsb // nsub
            blocks = []
            ring_loads = []
            for hf in range((tile_blk + 15) // 16):
                nb_h = min(16, tile_blk - hf * 16)
                ring_loads.append((hf, nb_h))
            items.append(("head", h, nsub, gsz, ring_loads))

        def emit_qk(it):
            (h, nsub, gsz, tbl, kap, vap, kdep, vdep, eb, dj, bidx, pts, sps) = it
            q0 = 0 if dj is None else dj
            for m in range(2):
                mm(sps[m][:, q0 * 128:N], kap[m * 64:(m + 1) * 64, :], bqT[m * 64:(m + 1) * 64, h, q0 * 128:N], True, True, r=[kdep, bqT], w=[sps[m]])
            if dj is not None:
                for m in range(2):
                    I("dve", "tensor_tensor", out=sps[m][:, dj * 128:(dj + 1) * 128], in0=sps[m][:, dj * 128:(dj + 1) * 128],
                      in1=dmask[:, diag_kind, h, :], op=ALU.add, r=[sps[m], dmask], w=[sps[m]])
            for m in range(2):
                for g in range(nsub):
                    qa, qe = max(g * gsz, q0), (g + 1) * gsz
                    if qe <= qa:
                        continue
                    delta = (tile_blk + g * gsz) - eb
                    col = h * NDELTA + delta + DOFF
                    I("act", "activation", out=pts[m][:, qa * 128:qe * 128], in_=sps[m][:, qa * 128:qe * 128], func=AF.Exp, scale=0.125,
                      bias=biasT[:, tbl, col:col + 1], r=[sps[m], biasT], w=[pts[m]])

        def emit_pv(it):
            (h, nsub, gsz, tbl, kap, vap, kdep, vdep, eb, dj, bidx, pts, sps) = it
            q0 = 0 if dj is None else dj
            for qs in range(q0, nsb):
                for m in range(2):
                    rg = qs * 2 + m
                    oa = pb[4 + rg // 3]
                    o0 = (rg % 3) * 129
                    last = (dj is not None and dj == qs)
                    mm(oa[:, o0:o0 + 129], pts[m][:, qs * 128:(qs + 1) * 128], vap, (bidx == 0 and rg % 3 == 0), last, r=[pts[m], vdep], w=[oa],
                       skip_group_check=True)

        def emit_epilogue(h):
            nreg = 2 * nsb
            rl8 = rl8s[rl8c[0] % 2]
            rl8c[0] += 1
            for bk in range((nreg + 2) // 3):
                nr = min(3, nreg - 3 * bk)
                oa = pb[4 + bk]
                I("dve", "reciprocal", out=rl8[:, 3 * bk:3 * bk + nr], in_=oa[:, 0:nr * 129].rearrange("p (r e) -> p r e", e=129)[:, :, 128], r=[oa], w=[rl8])
            I("dve", "tensor_tensor", out=rl8[:, 0:nreg].rearrange("p (q m) -> p q m", m=2)[:, :, 1], in0=rl8[:, 0:nreg].rearrange("p (q m) -> p q m", m=2)[:, :, 1],
              in1=bc(nlam[:, 0:1], [128, nsb]), op=ALU.mult, r=[rl8, nlam], w=[rl8])
            for qs in range(nsb):
                r0, r1 = qs * 2, qs * 2 + 1
                oa0, oa1 = pb[4 + r0 // 3], pb[4 + r1 // 3]
                a0, a1 = (r0 % 3) * 129, (r1 % 3) * 129
                I("dve", "tensor_scalar", out=ob[:, qs, h * 128:(h + 1) * 128], in0=oa0[:, a0:a0 + 128], scalar1=rl8[:, r0:r0 + 1], scalar2=None, op0=ALU.mult, r=[oa0, rl8], w=[ob])
                I("dve", "scalar_tensor_tensor", out=ob[:, qs, h * 128:(h + 1) * 128], in0=oa1[:, a1:a1 + 128], scalar=rl8[:, r1:r1 + 1], in1=ob[:, qs, h * 128:(h + 1) * 128],
                  op0=ALU.mult, op1=ALU.add, r=[oa1, rl8, ob], w=[ob])

        prev = None
        for (_, h, nsub, gsz, ring_loads) in items:
            blocks = []
            first_eb = 0
            for eb_ in range(tile_blk):
                if SLOPES[h] * (128 * (tile_blk - eb_) - 127) > 56.0:
                    first_eb = eb_ + 1
            for (hf, nb_h) in ring_loads:
                b0 = max(0, first_eb - hf * 16)
                if b0 >= nb_h:
                    continue
                slot = rctr[0] % 2
                rctr[0] += 1
                deps = [scr_b[hf * 16 + bi] for bi in range(b0, nb_h)]
                dma("sp", rK[slot][:, b0 * 128:nb_h * 128], scrK[h, hf, :, b0 * 128:nb_h * 128], r=deps, w=[rK[slot]])
                dma("sp", rV[slot][:, b0:nb_h, :], scrV[h, hf, :, b0 * 130:nb_h * 130].rearrange("p (b e) -> p b e", e=130), r=deps, w=[rV[slot]])
                for bi in range(b0, nb_h):
                    eb = hf * 16 + bi
                    blocks.append((1 if eb < n_masked else 0, rK[slot][:, bi * 128:(bi + 1) * 128], rV[slot][:, bi, 0:129], rK[slot], rV[slot], eb, None))
            for dj in range(nsb):
                blocks.append((0, KTo[:, h, dj * 128:(dj + 1) * 128], Vo[:, dj, h, 0:129], KTo, Vo, tile_blk + dj, dj))
            for bidx, (tbl, kap, vap, kdep, vdep, eb, dj) in enumerate(blocks):
                pts = PT[pctr[0] % 2]
                sps = [pb[(pctr[0] % 2) * 2 + m] for m in range(2)]
                pctr[0] += 1
                it = (h, nsub, gsz, tbl, kap, vap, kdep, vdep, eb, dj, bidx, pts, sps)
                emit_qk(it)
                if prev is not None:
                    emit_pv(prev)
                    if prev[0] != h:
                        emit_epilogue(prev[0])
                prev = it
        emit_pv(prev)
        emit_epilogue(prev[0])
        for j in range(nsb):
            group_norm(ob[:, j, :], ob, 4, 128, g_subln[:], g_subln, tk1[:], tk1)
            proj_tok(wz, j, 512, pb[j % 4])
            silu_gate(pb[j % 4], 512, tk0[:], tk0, tk1[:], tk1)
            to_mixT(tk0[:], tk0, j, 4)

    def out_proj(nsb, wsrc, res_fn, dst_fn):
        flush()
        wos = [[load_w(wsrc, cg * 512 + hh * 256, 256, 12) for hh in range(2)] for cg in range(2)]
        k_ = 0
        for j in range(nsb):
            for cg in range(2):
                woh = wos[cg]
                ps_ = pb[k_ % 4]
                k_ += 1
                rap, rtl = res_fn(j, cg)
                for hh in range(2):
                    wt, wv = woh[hh]
                    for kc in range(12):
                        mm(ps_[:, hh * 256:(hh + 1) * 256], mixT[:, kc, j * 128:(j + 1) * 128], wv[:, kc, :], kc == 0, kc == 11, r=[mixT, wt], w=[ps_])
                dap, dtl = dst_fn(j, cg)
                I("dve", "tensor_tensor", out=dap, in0=ps_[:, :], in1=rap, op=ALU.add, r=[ps_, rtl], w=[dtl])
                dst_done(j, cg)

    dst_done_fn = [lambda j, cg: None]

    def dst_done(j, cg):
        dst_done_fn[0](j, cg)

    def conv_out(dst, t0, t1):
        flush()
        j = (t1 - 1) // 128
        for c in range(8):
            tr(pb[4 + c // 4][:, (c % 4) * 128:(c % 4 + 1) * 128], cacc[:, c, 0:128], identf[:], r=[cacc, identf], w=[pb[4 + c // 4]])
        for hp in range(2):
            I("dve", "tensor_copy", out=sqs[:, hp * 512:(hp + 1) * 512], in_=pb[4 + hp][:, :], r=[pb[4 + hp]], w=[sqs])
        r0 = t0 - j * 128
        dma("sp", dst, sqs[r0:r0 + 30, :], r=[sqs], final=True)

    def layer1(nsb, y_dst, nrows, halo_only=False, smp=False, pconv=False, before_out=None):
        N = 128 * nsb
        make_xnT(nsb, ng[:, 1, :], lambda j: (y0[:, j, :], y0b[j]))
        switch("L")
        I("dve", "tensor_copy", out=uT[:, :, 0:30], in_=halo[:], r=[halo], w=[uT])
        for c2 in range(4):
            wu = load_w("w_in_c", c2 * 256, 256, 8)
            wg = load_w("w_in_c", 1024 + c2 * 256, 256, 8)
            for cc in range(2):
                c = c2 * 2 + cc
                pu, pg = pb[2 * (c % 2)], pb[2 * (c % 2) + 1]
                proj_feat(wu, cc, N, pu)
                proj_feat(wg, cc, N, pg)
                I("act", "activation", out=e2[:, 0:N], in_=pg[:, 0:N], func=AF.Sigmoid, r=[pg], w=[e2])
                I("dve", "tensor_tensor", out=uT[:, c, 30:30 + N], in0=pu[:, 0:N], in1=e2[:, 0:N], op=ALU.mult, r=[pu, e2], w=[uT])
                if smp or pconv:
                    I("dve", "tensor_tensor", out=cacc[:, c, 0:128], in0=pu[:, N - 128:N], in1=e2[:, N - 128:N], op=ALU.mult, r=[pu, e2], w=[cacc])
        if smp:
            conv_out(o_sconv, 2, 32)
        if pconv:
            conv_out(o_pconv, N - 30, N)
        I("dve", "tensor_copy", out=halo[:], in_=uT[:, :, N:N + 30], r=[uT], w=[halo])
        if halo_only:
            return
        x_q(1, nsb, "w_in_c", 3072)
        for c in range(8):
            psc = pb[c % 4]
            for jt in range(31):
                dg = dgr[dgc[0] % 8]
                dgc[0] += 1
                I("dve", "tensor_scalar", out=dg[:, :], in0=identb[:], scalar1=cw[:, c, jt:jt + 1], scalar2=None, op0=ALU.mult, r=[identb, cw], w=[dg])
                mm(psc[:, 0:N], dg[:, :], uT[:, c, jt:jt + N], jt == 0, jt == 30, r=[dg, uT], w=[psc])
            I("act", "activation", out=cacc[:, c, 0:N], in_=psc[:, 0:N], func=AF.Identity, bias=cb[:, c:c + 1], r=[psc, cb], w=[cacc])
        for c in range(8):
            mm(pb[2][:, 0:N], onesf[:], cacc[:, c, 0:N], c == 0, c == 7, r=[onesf, cacc], w=[pb[2]])
        for c in range(8):
            I("act", "activation", out=sqs[:, 0:N], in_=cacc[:, c, 0:N], func=AF.Square, r=[cacc], w=[sqs])
            mm(pb[3][:, 0:N], onesf[:], sqs[:, 0:N], c == 0, c == 7, r=[onesf, sqs], w=[pb[3]])
        I("dve", "tensor_scalar", out=lmean[:, 0:N], in0=pb[2][:, 0:N], scalar1=1.0 / D, scalar2=None, op0=ALU.mult, r=[pb[2]], w=[lmean])
        I("dve", "tensor_tensor", out=e1[:, 0:N], in0=lmean[:, 0:N], in1=lmean[:, 0:N], op=ALU.mult, r=[lmean], w=[e1])
        I("dve", "scalar_tensor_tensor", out=e1[:, 0:N], in0=pb[3][:, 0:N], scalar=1.0 / D, in1=e1[:, 0:N], op0=ALU.mult, op1=ALU.subtract, r=[pb[3], e1], w=[e1])
        I("act", "activation", out=lrstd[:, 0:N], in_=e1[:, 0:N], func=AF.Ln, bias=epsc[:], r=[e1, epsc], w=[lrstd])
        I("act", "activation", out=lrstd[:, 0:N], in_=lrstd[:, 0:N], func=AF.Exp, scale=-0.5, r=[lrstd], w=[lrstd])
        wzs = [load_w("w_in_c", 2048, 512, 8), load_w("w_in_c", 2560, 512, 8)]
        for c in range(8):
            proj_feat(wzs[c // 4], c % 4, N, pb[c % 4])
            I("act", "activation", out=szT[:, c, 0:N], in_=pb[c % 4][:, 0:N], func=AF.Silu, r=[pb[c % 4]], w=[szT])
            I("dve", "tensor_tensor", out=cacc[:, c, 0:N], in0=cacc[:, c, 0:N], in1=lmean[:, 0:N], op=ALU.subtract, r=[cacc, lmean], w=[cacc])
            I("dve", "tensor_tensor", out=cacc[:, c, 0:N], in0=cacc[:, c, 0:N], in1=lrstd[:, 0:N], op=ALU.mult, r=[cacc, lrstd], w=[cacc])
            I("act", "activation", out=e1[:, 0:N], in_=cacc[:, c, 0:N], func=AF.Silu, scale=lg[:, c:c + 1], bias=lb[:, c:c + 1], r=[cacc, lg, lb], w=[e1])
            I("dve", "tensor_tensor", out=mixT[:, c, 0:N], in0=e1[:, 0:N], in1=szT[:, c, 0:N], op=ALU.mult, r=[e1, szT], w=[mixT])
        x_branch(1, nsb, "w_in_c", 3072, 3584, 8, q_done=True)

        cur = [None]

        def dst_fn(j, cg):
            cur[0] = tkrot[tkrc[0] % 3]
            tkrc[0] += 1
            return cur[0][:], cur[0]

        def done(j, cg):
            dma("sp", y_dst[j * 128:j * 128 + nrows, cg * 512:(cg + 1) * 512], cur[0][0:nrows, :], r=[cur[0]], final=True)
        dst_done_fn[0] = done
        if before_out is not None:
            before_out()
        out_proj(nsb, "w_out_c", lambda j, cg: (y0[:, j, cg * 512:(cg + 1) * 512], y0b[j]), dst_fn)
        dst_done_fn[0] = lambda j, cg: None

    class _Stop(Exception):
        pass

    def stage(n):
        if cfg.get("stage") == n:
            raise _Stop()

    try:
        for _ in range(NXS):
            x_issue()
        stage(1)
        mem_kv(0)
        stage(2)
        if do_l1:
            mem_kv(1)
        stage(3)
        I("dve", "memset", Cst[:], 0.0, w=[Cst])
        I("dve", "memset", mst[:], 0.0, w=[mst])

        def res_from(src, row0):
            def fn(j, cg):
                t_ = tkrot[tkrc[0] % 3]
                tkrc[0] += 1
                dma("sp", t_[:], src[row0 + j * 128:row0 + (j + 1) * 128, cg * 512:(cg + 1) * 512], w=[t_])
                return t_[:], t_
            return fn

        def full_tile(src, row0, nsb, tile_blk, n_masked, blk0, out_row0, y_dst, halo_only=False, pconv=False, stats_done=False, before_out=None):
            make_xnT(nsb, ng[:, 0, :], None if stats_done else dram_src(src, row0), stats_done=stats_done)
            switch("A")
            a_branch(nsb, "full", "w_in_a")
            switch("B")
            b_kv(nsb, "w_in_a", blk0, out_row0)
            x_q(0, nsb, "w_in_a", 4616)
            b_attn(nsb, "w_in_a", tile_blk, n_masked, 0)
            x_branch(0, nsb, "w_in_a", 4616, 5128, 8, q_done=True)
            out_proj(nsb, "w_out_a", res_from(src, row0), lambda j, cg: (y0[:, j, cg * 512:(cg + 1) * 512], y0b[j]))
            if do_l1:
                layer1(nsb, y_dst, 128, halo_only=halo_only, pconv=pconv, before_out=before_out)

        blk = 0
        pre_started = [False]
        for (nsb, mode) in cfg["pre"]:
            if mode == "pre":
                make_xnT(nsb, ng[:, 0, :], dram_src(x_pre, blk * 128))
                if not pre_started[0]:
                    switch("AB")
                    pre_started[0] = True
                st_later = a_branch(nsb, "pre", "w_in_a", state_later=True)
                b_kv(nsb, "w_in_a", blk, None)
                st_later()
                stage(4)
            else:
                assert nsb == 1
                full_tile(x_pre, blk * 128, 1, blk, NB_PRE, blk, None, None, halo_only=True)
                stage(5)
            blk += nsb
        if N_PRE > 0:
            I("dve", "tensor_tensor", out=mst[:], in0=mst[:], in1=flag[:], op=ALU.mult, r=[mst, flag], w=[mst])
        ob_ = 0
        nown = len(cfg["own"])
        for ti, nsb in enumerate(cfg["own"]):
            r0 = ob_ * 128
            last = ti == nown - 1
            hook = None
            if (not last) and do_l1:
                nsb2 = cfg["own"][ti + 1]
                r2 = (ob_ + nsb) * 128
                hook = (lambda nsb2=nsb2, r2=r2: xn_stats(nsb2, dram_src(x_own, r2)))
            full_tile(x_own, r0, nsb, NB_PRE + ob_, NB_PRE, (NB_PRE + ob_) if not last else None, r0, y_own[r0:r0 + 128 * nsb, :], pconv=last,
                      stats_done=(ti > 0 and do_l1), before_out=hook)
            ob_ += nsb
        dma("sp", o_pC.rearrange("h d e -> d h e"), Cst[:, :, 0:128], r=[Cst], final=True)
        dma("sp", o_pn.rearrange("h d -> d h"), Cst[:, :, 128], r=[Cst], final=True, allow_slow_non_contiguous=True)
        dma("sp", o_pm, mst[:], r=[mst], final=True)
        stage(6)

        if do_smp:
            smp_mem_kv()
            switch("B")
            for kb in range(16):
                tks = tkrot[kb % 3]
                dma("act", tks[:], c_k[kb * 128:(kb + 1) * 128, :], w=[tks])
                tok_to_T(tks[:], tks, KTo[:, :, (kb % 4) * 128:(kb % 4 + 1) * 128], kb, KTo)
                dma("pool", Vo[:, kb % 4, :, 0:128], c_v[kb * 128:(kb + 1) * 128, :].rearrange("p (h d) -> p h d", h=4), w=[Vo])
                scr_write(kb % 4, kb)
            dma("sp", Cst[:, :, 0:128], st_C.rearrange("h d e -> d h e"), w=[Cst])
            dma("sp", Cst[:, :, 128], st_n.rearrange("h d -> d h"), w=[Cst], allow_slow_non_contiguous=True)
            dma("sp", mst[:], st_m, w=[mst])
            I("pool", "memset", sqs[:], 0.0, w=[sqs])
            dma("sp", sqs[0:30, :], st_conv, w=[sqs])
            for c in range(8):
                tr(pb[4 + c // 4][:, (c % 4) * 128:(c % 4 + 1) * 128], sqs[:, c * 128:(c + 1) * 128], identf[:], r=[sqs, identf], w=[pb[4 + c // 4]])
            for hp in range(2):
                I("dve", "tensor_copy", out=halo[:, hp * 4:(hp + 1) * 4, :], in_=pb[4 + hp][:, :].rearrange("p (c t) -> p c t", c=4)[:, :, 0:30], r=[pb[4 + hp]], w=[halo])
            make_xnT(1, ng[:, 0, :], dram_src(x_smp, 0))
            switch("A")
            a_branch(1, "full", "w_in_a", Lc=32, nch_per_sb=1)
            dma("sp", o_sC.rearrange("h d e -> d h e"), Cst[:, :, 0:128], r=[Cst], final=True)
            dma("sp", o_sn.rearrange("h d -> d h"), Cst[:, :, 128], r=[Cst], final=True, allow_slow_non_contiguous=True)
            dma("sp", o_sm, mst[:], r=[mst], final=True)
            switch("B")
            b_kv(1, "w_in_a", None, 0, smp=True)
            b_attn(1, "w_in_a", 16, 0, 1)
            x_branch(0, 1, "w_in_a", 4616, 5128, 8)
            out_proj(1, "w_out_a", res_from(x_smp, 0), lambda j, cg: (y0[:, j, cg * 512:(cg + 1) * 512], y0b[j]))
            if do_l1:
                layer1(1, y_smp, 32, smp=True)

    except _Stop:
        pass
    flush()
    P.S.emit(final_waits=P.outs)
    return P


def host_consts(has_prefix):
    ident = np.eye(128, dtype=np.float32)
    a = np.arange(128, dtype=np.float64)
    bias = np.zeros((128, 2, 4 * NDELTA), np.float32)
    for h in range(4):
        for dl in range(-DOFF, NDELTA - DOFF):
            v = SLOPES[h] * (a - 128.0 * dl)
            bias[:, 0, h * NDELTA + dl + DOFF] = v
            bias[:, 1, h * NDELTA + dl + DOFF] = v + (0.0 if has_prefix else PMASK)
    ka = np.arange(128)[:, None]
    qc = np.arange(128)[None, :]
    dm = np.zeros((128, 2, 4, 128), np.float32)
    for h in range(4):
        corr = np.where(ka > qc, -16.0 * SLOPES[h] * (ka - qc), 0.0)
        vis = (ka // 64) <= (qc // 64)
        dm[:, 0, h, :] = np.where(vis, corr, -1.0e5)
        vis_s = (ka < 32) & (qc < 128)
        dm[:, 1, h, :] = np.where(vis_s, corr, -1.0e5)
    am = ((ka <= qc) & ((ka // 64) == (qc // 64))).astype(np.float32)
    hsel = np.zeros((4, 4, 128), np.float32)
    for h in range(4):
        hsel[h, h, :] = 1.0
    flag = np.full((4, 1), 1.0 if has_prefix else 0.0, np.float32)
    return dict(c_ident=ident, c_bias=bias, c_dmask=dm, c_amask=am, c_hsel=hsel, c_flag=flag)


FULL_CFG = dict(own=[4] * 8, pre=[(4, "pre")] * 7 + [(3, "pre"), (1, "halo")], sample=True, l1=True)
_CACHE = {}


def core_inputs(c, cfg, inp):
    f = lambda a: np.ascontiguousarray(a, dtype=np.float32)
    b, half = c // 2, c % 2
    n_own = 128 * sum(cfg["own"])
    n_pre = 128 * sum(n for n, _ in cfg["pre"])
    xp = inp["x_prompt"][b]
    d = {}
    d["x_own"] = f(xp[half * n_own:(half + 1) * n_own])
    if half == 1:
        d["x_pre"] = f(xp[n_own - n_pre:n_own])
    else:
        d["x_pre"] = np.zeros((max(n_pre, 128), D), np.float32)
    xs = np.zeros((128, D), np.float32)
    xs[:32] = inp["x_sample"][c]
    d["x_smp"] = xs
    d["mem"] = f(inp["mem_prompt"][b])
    d["c_xk"] = f(inp["cache_xk"][:, c].reshape(2, 256, 512))
    d["c_xv"] = f(inp["cache_xv"][:, c].reshape(2, 256, 512))
    d["c_k"] = f(inp["cache_k"][0, c].reshape(2048, 512))
    d["c_v"] = f(inp["cache_v"][0, c].reshape(2048, 512))
    d["st_C"] = f(inp["state_C"][0, c])
    d["st_n"] = f(inp["state_n"][0, c])
    d["st_m"] = f(inp["state_m"][0, c].reshape(4, 1))
    d["st_conv"] = f(inp["state_conv"][0, c])
    d["norm_g"] = f(inp["norm_g"])
    d["w_in_a"] = f(inp["w_in_a"][0])
    d["b_ig"] = f(inp["b_ig"][0].reshape(4, 1))
    d["b_fg"] = f(inp["b_fg"][0].reshape(4, 1))
    d["mlstm_g"] = f(inp["mlstm_norm_g"][0].reshape(512))
    d["qn_g"] = f(np.tile(inp["qn_g"][0], 8))
    d["kn_g"] = f(np.tile(inp["kn_g"][0], 8))
    d["lamv"] = f(np.stack([inp["lam_q1"][0], inp["lam_k1"][0], inp["lam_q2"][0], inp["lam_k2"][0]]))
    d["subln_g"] = f(np.tile(inp["subln_g"][0], 4))
    d["w_out_a"] = f(inp["w_out_a"][0])
    d["w_in_c"] = f(inp["w_in_c"][0])
    d["conv_wT"] = f(inp["conv_w"][0].T)
    d["conv_b"] = f(inp["conv_b"][0])
    d["ln_g"] = f(inp["conv_ln_g"][0])
    d["ln_b"] = f(inp["conv_ln_b"][0])
    d["w_out_c"] = f(inp["w_out_c"][0])
    d["mem_norm_g"] = f(inp["mem_norm_g"])
    d["w_mem_kv"] = f(inp["w_mem_kv"])
    d["xq_g"] = f(np.stack([np.tile(inp["xq_norm_g"][l], 4) for l in range(2)]))
    d["xk_g"] = f(np.stack([np.tile(inp["xk_norm_g"][l], 4) for l in range(2)]))
    d.update(host_consts(half == 1))
    return d


def kernel(**inp):
    inp = {k: np.asarray(v) for k, v in inp.items()}
    cfg = FULL_CFG
    if "prog" not in _CACHE:
        _CACHE["prog"] = build(cfg)
    P = _CACHE["prog"]
    in_maps = [core_inputs(c, cfg, inp) for c in range(8)]
    res = run_bass_kernel_spmd(P.nc, in_maps, core_ids=list(range(8))).results
    B, SEQ = 4, 8192
    yp = np.zeros((B, SEQ, D), np.float32)
    pk = np.zeros((1, B, SEQ, 4, 128), np.float32)
    pv = np.zeros((1, B, SEQ, 4, 128), np.float32)
    pxk = np.zeros((2, B, 256, 4, 128), np.float32)
    pxv = np.zeros((2, B, 256, 4, 128), np.float32)
    pC = np.zeros((1, B, 4, 128, 128), np.float32)
    pn = np.zeros((1, B, 4, 128), np.float32)
    pm = np.zeros((1, B, 4), np.float32)
    pconv = np.zeros((1, B, 30, D), np.float32)
    ys = np.zeros((8, 32, D), np.float32)
    sk = np.zeros((1, 8, 32, 4, 128), np.float32)
    sv = np.zeros((1, 8, 32, 4, 128), np.float32)
    sC = np.zeros((1, 8, 4, 128, 128), np.float32)
    sn = np.zeros((1, 8, 4, 128), np.float32)
    sm = np.zeros((1, 8, 4), np.float32)
    sconv = np.zeros((1, 8, 30, D), np.float32)
    for c in range(8):
        r = res[c]
        b, half = c // 2, c % 2
        sl = slice(half * 4096, (half + 1) * 4096)
        yp[b, sl] = r["y_own"]
        pk[0, b, sl] = r["o_pk"].reshape(4096, 4, 128)
        pv[0, b, sl] = r["o_pv"].reshape(4096, 4, 128)
        if half == 1:
            pxk[:, b] = r["o_pxk"].reshape(2, 256, 4, 128)
            pxv[:, b] = r["o_pxv"].reshape(2, 256, 4, 128)
            pC[0, b] = r["o_pC"]
            pn[0, b] = r["o_pn"]
            pm[0, b] = r["o_pm"].reshape(4)
            pconv[0, b] = r["o_pconv"]
        ys[c] = r["y_smp"]
        sk[0, c] = r["o_sk"].reshape(32, 4, 128)
        sv[0, c] = r["o_sv"].reshape(32, 4, 128)
        sC[0, c] = r["o_sC"]
        sn[0, c] = r["o_sn"]
        sm[0, c] = r["o_sm"].reshape(4)
        sconv[0, c] = r["o_sconv"]
    return (yp, ys, pxk, pxv, pk, pv, pC, pn, pm, pconv, sk, sv, sC, sn, sm, sconv)
```

```python
import math
import numpy as np
import concourse.bass as bass
import concourse.mybir as mybir
from concourse.bass_utils import run_bass_kernel_spmd

F32 = mybir.dt.float32
BF16 = mybir.dt.bfloat16
AF = mybir.ActivationFunctionType
ALU = mybir.AluOpType
AX = mybir.AxisListType

D = 1024
EPS = 1e-6
SLOPES = [2.0 ** (-8.0 * (h + 1) / 4) for h in range(4)]
NSUBH = [2, 1, 1, 1]
NDELTA = 72
DOFF = 3
PMASK = -200.0
LAM_INIT0 = 0.8 - 0.6 * math.exp(0.0)


class Buf:
    __slots__ = ("name", "last_w", "readers", "excl")

    def __init__(self, name="", excl=False):
        self.name = name
        self.last_w = None
        self.readers = []
        self.excl = excl


class Op:
    __slots__ = ("eng", "fn", "deps", "signal", "is_dma", "sem", "semval", "vc", "gi", "desc")


class Sched:
    def __init__(self, nc, n_dma_sems=10):
        self.nc = nc
        self.ops = {e: [] for e in ("pe", "act", "dve", "pool", "sp")}
        self.all = []
        self.n_dma_sems = n_dma_sems
        self.dma_rr = {q: 0 for q in ("sp", "act", "pool")}
        self.dma_last = {}

    def op(self, eng, fn, reads=(), writes=(), dma=False):
        import os as _os
        if len(self.all) >= int(_os.environ.get("MAXOPS", "100000000")):
            return None
        o = Op()
        o.eng = eng
        o.fn = fn
        o.is_dma = dma
        o.signal = False
        o.sem = None
        o.semval = 0
        o.gi = len(self.all)
        deps = set()
        reads = list(reads)
        writes = list(writes)
        for b in reads:
            if b.excl and b not in writes:
                writes.append(b)
        for b in reads:
            if b.last_w is not None:
                deps.add(b.last_w)
        for b in writes:
            if b.last_w is not None:
                deps.add(b.last_w)
            for r in b.readers:
                deps.add(r)
        if dma:
            key = (eng, self.dma_rr[eng] % self.n_dma_sems)
            self.dma_rr[eng] += 1
            prev = self.dma_last.get(key)
            if prev is not None:
                deps.add(prev)
            self.dma_last[key] = o
            o.sem = key
        deps.discard(o)
        o.deps = deps
        for b in reads:
            b.readers.append(o)
        for b in writes:
            b.last_w = o
            b.readers = []
        self.ops[eng].append(o)
        self.all.append(o)
        return o

    def emit(self, final_waits=()):
        nc = self.nc

        def pe_pe(o, d):
            return (not d.is_dma) and d.eng == "pe" and o.eng == "pe" and (not o.is_dma)

        for o in self.all:
            for d in o.deps:
                if d.is_dma or pe_pe(o, d):
                    continue
                d.signal = True
        for o in final_waits:
            if not o.is_dma:
                o.signal = True
        cnt = {}
        for o in self.all:
            if o.is_dma:
                cnt[o.sem] = cnt.get(o.sem, 0) + 16
                o.semval = cnt[o.sem]
            elif o.signal:
                o.sem = ("c", o.eng)
                cnt[o.sem] = cnt.get(o.sem, 0) + 1
                o.semval = cnt[o.sem]
        known = {e: {} for e in self.ops}
        plans = {}
        for o in self.all:
            kn = known[o.eng]
            waits = {}
            for d in o.deps:
                if pe_pe(o, d):
                    continue
                if kn.get(d.sem, 0) >= d.semval:
                    continue
                if waits.get(d.sem, 0) < d.semval:
                    waits[d.sem] = d.semval
            for d in o.deps:
                if pe_pe(o, d):
                    continue
                for s, v in d.vc.items():
                    if kn.get(s, 0) < v:
                        kn[s] = v
            for s in list(waits):
                v = waits[s]
                for d in o.deps:
                    if pe_pe(o, d) or d.sem == s or d.sem not in waits:
                        continue
                    if waits[d.sem] >= d.semval and d.vc.get(s, 0) >= v:
                        del waits[s]
                        break
            plans[o.gi] = waits
            vc = dict(kn)
            if o.sem is not None and vc.get(o.sem, 0) < o.semval:
                vc[o.sem] = o.semval
            o.vc = vc
        sems = {}
        for key in cnt:
            sems[key] = nc.alloc_semaphore("s_" + "_".join(str(k) for k in key))
        fw = [(o.sem, o.semval) for o in final_waits]
        self.n_waits = sum(len(p) for p in plans.values())
        self.plans = plans

        def run_engine(ename):
            def body(eng):
                for o in self.ops[ename]:
                    wl = list(plans[o.gi].items())
                    for s, v in wl[:-1]:
                        eng.wait_ge(sems[s], v)
                    ins = o.fn(eng)
                    if wl:
                        ins = ins._wait_ge(sems[wl[-1][0]], wl[-1][1])
                    if o.is_dma:
                        ins.then_inc(sems[o.sem], 16)
                    elif o.signal:
                        ins.then_inc(sems[o.sem], 1)
                if ename == "sp":
                    done = {}
                    for s, v in fw:
                        done[s] = max(done.get(s, 0), v)
                    for s, v in done.items():
                        eng.wait_ge(sems[s], v)
            return body

        with nc.Block() as block:
            block.tensor(run_engine("pe"))
            block.scalar(run_engine("act"))
            block.vector(run_engine("dve"))
            block.gpsimd(run_engine("pool"))
            block.sync(run_engine("sp"))


class Tl:
    def __init__(self, t, name):
        self.t = t
        self.b = Buf(name)

    def __getitem__(self, k):
        return self.t[k]


class Prog:
    def __init__(self, cfg):
        self.cfg = cfg
        self.nc = bass.Bass("TRN2", target_bir_lowering=False)
        self.S = Sched(self.nc)
        self.outs = []
        self.uid = 0
        self.din = {}
        self.dout = {}

    def sb(self, name, shape, dt=F32):
        return Tl(self.nc.alloc_sbuf_tensor(name, list(shape), dt), name)

    def ps(self, name, shape, dt=F32):
        t = Tl(self.nc.alloc_psum_tensor(name, list(shape), dt), name)
        t.b.excl = True
        return t

    def inp(self, name, shape):
        a = self.nc.dram_tensor(name, list(shape), F32, kind="ExternalInput").ap()
        self.din[name] = a
        return a

    def outp(self, name, shape):
        a = self.nc.dram_tensor(name, list(shape), F32, kind="ExternalOutput").ap()
        self.dout[name] = a
        return a

    @staticmethod
    def _b(xs):
        return [x.b if hasattr(x, "b") else x for x in xs]

    def I(self, eng, meth, *a, r=(), w=(), **kw):
        o = self.S.op(eng, lambda e: getattr(e, meth)(*a, **kw), self._b(r), self._b(w))
        if o is not None:
            o.desc = (eng, meth, str(kw.get("out", a[0] if a else ""))[:90], str(kw.get("func", kw.get("op", kw.get("op0", "")))))
        return o

    def dma(self, q, out, in_, r=(), w=(), final=False, **kw):
        o = self.S.op(q, lambda e: e.dma_start(out=out, in_=in_, **kw), self._b(r), self._b(w), dma=True)
        if o is not None:
            o.desc = (q, "dma", str(out)[:90], "")
        if final and o is not None:
            self.outs.append(o)
        return o

    def mm(self, out, lhsT, rhs, start, stop, r, w, **kw):
        return self.I("pe", "matmul", out, lhsT=lhsT, rhs=rhs, start=start, stop=stop, r=r, w=w, **kw)

    def tr(self, out, in_, ident, r, w):
        return self.I("pe", "transpose", out=out, in_=in_, identity=ident, r=r, w=w)


def bc(ap, shape):
    return ap.broadcast_to(list(shape))


def build(cfg):
    P = Prog(cfg)
    nc = P.nc
    I, dma, mm, tr = P.I, P.dma, P.mm, P.tr
    N_OWN = 128 * sum(cfg["own"])
    N_PRE = 128 * sum(n for n, _ in cfg["pre"])
    NB_OWN = N_OWN // 128
    NB_PRE = N_PRE // 128
    do_smp = cfg.get("sample", True)
    do_l1 = cfg.get("l1", True)

    x_own = P.inp("x_own", [N_OWN, D])
    x_pre = P.inp("x_pre", [max(N_PRE, 128), D])
    x_smp = P.inp("x_smp", [128, D])
    mem = P.inp("mem", [256, D])
    c_xk = P.inp("c_xk", [2, 256, 512])
    c_xv = P.inp("c_xv", [2, 256, 512])
    c_k = P.inp("c_k", [2048, 512])
    c_v = P.inp("c_v", [2048, 512])
    st_C = P.inp("st_C", [4, 128, 128])
    st_n = P.inp("st_n", [4, 128])
    st_m = P.inp("st_m", [4, 1])
    st_conv = P.inp("st_conv", [30, D])
    norm_g = P.inp("norm_g", [2, D])
    w_in_a = P.inp("w_in_a", [D, 5640])
    b_ig = P.inp("b_ig", [4, 1])
    b_fg = P.inp("b_fg", [4, 1])
    mlstm_g = P.inp("mlstm_g", [512])
    qn_g = P.inp("qn_g", [512])
    kn_g = P.inp("kn_g", [512])
    lamv = P.inp("lamv", [4, 64])
    subln_g = P.inp("subln_g", [512])
    w_out_a = P.inp("w_out_a", [1536, D])
    w_in_c = P.inp("w_in_c", [D, 4096])
    conv_wT = P.inp("conv_wT", [D, 31])
    conv_b = P.inp("conv_b", [D])
    ln_g = P.inp("ln_g", [D])
    ln_b = P.inp("ln_b", [D])
    w_out_c = P.inp("w_out_c", [1536, D])
    mem_norm_g = P.inp("mem_norm_g", [2, D])
    w_mem_kv = P.inp("w_mem_kv", [2, D, 1024])
    xq_g = P.inp("xq_g", [2, 512])
    xk_g = P.inp("xk_g", [2, 512])
    c_ident = P.inp("c_ident", [128, 128])
    c_bias = P.inp("c_bias", [128, 2, 4 * NDELTA])
    c_dmask = P.inp("c_dmask", [128, 2, 4, 128])
    c_amask = P.inp("c_amask", [128, 128])
    c_hsel = P.inp("c_hsel", [4, 4, 128])
    c_flag = P.inp("c_flag", [4, 1])

    y_own = P.outp("y_own", [N_OWN, D])
    y_smp = P.outp("y_smp", [32, D])
    o_pxk = P.outp("o_pxk", [2, 256, 512])
    o_pxv = P.outp("o_pxv", [2, 256, 512])
    o_pk = P.outp("o_pk", [N_OWN, 512])
    o_pv = P.outp("o_pv", [N_OWN, 512])
    o_pC = P.outp("o_pC", [4, 128, 128])
    o_pn = P.outp("o_pn", [4, 128])
    o_pm = P.outp("o_pm", [4, 1])
    o_pconv = P.outp("o_pconv", [30, D])
    o_sk = P.outp("o_sk", [32, 512])
    o_sv = P.outp("o_sv", [32, 512])
    o_sC = P.outp("o_sC", [4, 128, 128])
    o_sn = P.outp("o_sn", [4, 128])
    o_sm = P.outp("o_sm", [4, 1])
    o_sconv = P.outp("o_sconv", [30, D])

    wsrc32 = {"w_in_a": (w_in_a, [D, 5640]), "w_out_a": (w_out_a, [1536, D]), "w_in_c": (w_in_c, [D, 4096]),
              "w_out_c": (w_out_c, [1536, D]), "w_mem0": (w_mem_kv[0], [D, 1024]), "w_mem1": (w_mem_kv[1], [D, 1024])}
    wbf = {}
    wbf_b = {}
    for nm, (src32, shp) in wsrc32.items():
        wbf[nm] = nc.dram_tensor("bf_" + nm, shp, BF16, kind="Internal").ap()
        wbf_b[nm] = [Buf(f"wb_{nm}{i}") for i in range(shp[0] // 128)]

    WA_GROUPS = [(512, 1536), (2560, 2568), (3080, 4104), (0, 512), (1536, 2560), (2568, 3080), (4104, 5640)]
    wa_b = [[Buf(f"wa_{g}_{i}") for i in range(8)] for g in range(len(WA_GROUPS))]

    def convert_weights(names):
        for nm in names:
            src32, shp = wsrc32[nm]
            if nm == "w_in_a":
                continue
            for i in range(shp[0] // 128):
                dma("pool", wbf[nm][i * 128:(i + 1) * 128, :], src32[i * 128:(i + 1) * 128, :], w=[wbf_b[nm][i]])

    def convert_wa(groups):
        src32 = wsrc32["w_in_a"][0]
        for g in groups:
            c0, c1 = WA_GROUPS[g]
            for i in range(8):
                dma("pool", wbf["w_in_a"][i * 128:(i + 1) * 128, c0:c1], src32[i * 128:(i + 1) * 128, c0:c1], w=[wa_b[g][i]])

    def wdeps(nm, c0, c1):
        if nm != "w_in_a":
            return wbf_b[nm]
        out = []
        for g, (g0, g1) in enumerate(WA_GROUPS):
            if g0 < c1 and c0 < g1:
                out += wa_b[g]
        return out

    NBLK = max(NB_PRE + NB_OWN, 16)
    NHALF = (NBLK + 15) // 16
    scrK = nc.dram_tensor("scrK", [4, NHALF, 128, 2048], BF16, kind="Internal").ap()
    scrV = nc.dram_tensor("scrV", [4, NHALF, 128, 16 * 130], BF16, kind="Internal").ap()
    scr_b = [Buf(f"scr{i}") for i in range(NHALF * 16)]

    pb = [P.ps(f"pb{i}", [128, 512], F32) for i in range(7)]
    ptb_all = nc.alloc_psum_tensor("ptb_all", [128, 1024], BF16)

    class PV_:
        def __init__(self, name, ap):
            self.ap = ap
            self.b = Buf(name)

        def __getitem__(self, k):
            return self.ap[k]
    ptb = [PV_("ptb0", ptb_all[:, 0:512]), PV_("ptb1", pb[5][:, 0:256].bitcast(BF16))]
    ptb[0].b.excl = True
    ptb[1].b = pb[5].b
    identf = P.sb("identf", [128, 128])
    identb = P.sb("identb", [128, 128], BF16)
    onesf = P.sb("onesf", [128, 128])
    epsc = P.sb("epsc", [128, 1])
    lnk = P.sb("lnk", [128, 1])
    ng = P.sb("ng", [128, 2, 8])
    mng = P.sb("mng", [128, 2, 8])
    g_mlstm = P.sb("g_mlstm", [128, 512])
    g_qn = P.sb("g_qn", [128, 512])
    g_kn = P.sb("g_kn", [128, 512])
    g_subln = P.sb("g_subln", [128, 512])
    g_xq = P.sb("g_xq", [128, 2, 512])
    g_xk = P.sb("g_xk", [128, 2, 512])
    lam_t = P.sb("lam_t", [128, 4, 64])
    lam_s = P.sb("lam_s", [128, 4])
    nlam = P.sb("nlam", [128, 1])
    big = P.sb("big", [4, 1])
    bfg = P.sb("bfg", [4, 1])
    flag = P.sb("flag", [4, 1])
    biasT = P.sb("biasT", [128, 2, 4 * NDELTA])
    dmask = P.sb("dmask", [128, 2, 4, 128])
    amask = P.sb("amask", [128, 128])
    hsel = P.sb("hsel", [4, 4, 128])
    wgate = P.sb("wgate", [128, 8, 8], BF16)
    cw = P.sb("cw", [128, 8, 31])
    cb = P.sb("cb", [128, 8])
    lg = P.sb("lg", [128, 8])
    lb = P.sb("lb", [128, 8])
    dummy = P.sb("dummyt", [128, 2])

    NW = 4
    wring = [P.sb(f"wring{i}", [128, 4096], BF16) for i in range(NW)]
    wctr = [0]
    NXS = 3
    xs = [P.sb(f"xs{i}", [128, D]) for i in range(NXS)]
    y0 = P.sb("y0", [128, 4, D])
    y0b = [Buf(f"y0_{j}") for j in range(4)]
    xnb = P.sb("xnb", [128, 4, D], BF16)
    xnT = P.sb("xnT", [128, 8, 512], BF16)
    sqs = P.sb("sqs", [128, D])
    st1 = P.sb("st1", [128, 8])
    st2 = P.sb("st2", [128, 8])
    tk0 = P.sb("tk0", [128, 512])
    tk1 = P.sb("tk1", [128, 512])
    tk2 = P.sb("tk2", [128, 512])
    tkrot = [tk0, tk1, tk2]
    tkrc = [0]
    tkb = P.sb("tkb", [128, 512], BF16)
    tkbs = [tkb, P.sb("tkb2", [128, 512], BF16)]
    tkc = [0]
    mixT = P.sb("mixT", [128, 12, 512], BF16)
    PT = [[P.sb(f"PT{i}{m}", [128, 512], BF16) for m in range(2)] for i in range(2)]
    pctr = [0]
    ob = P.sb("ob", [128, 4, 512])
    xqT = P.sb("xqT", [128, 4, 512], BF16)
    mKT = P.sb("mKT", [128, 2, 4, 256], BF16)
    mV = P.sb("mV", [128, 2, 2, 4, 130], BF16)
    e1 = P.sb("e1", [128, 512])
    Cst = P.sb("Cst", [128, 4, 130])
    Cbf = P.sb("Cbf", [128, 4, 130], BF16)
    mst = P.sb("mst", [4, 1])
    gtok = P.sb("gtok", [128, 4, 8])
    cA = P.sb("cA", [4, 8])
    cB = P.sb("cB", [4, 8])
    cM = P.sb("cM", [4, 9])
    cX = P.sb("cX", [4, 8])
    cD = P.sb("cD", [4, 8])
    cDx = P.sb("cDx", [4, 8, 4])
    decb = P.sb("decb", [128, 8, 4])
    den4 = P.sb("den4", [128, 4])
    den4s = [den4, P.sb("den4b", [128, 4])]
    rl = P.sb("rl", [128, 2])
    rl2 = [P.sb("rl2a", [128, 2]), P.sb("rl2b", [128, 2])]
    rl8s = [P.sb("rl8a", [128, 8]), P.sb("rl8b", [128, 8])]
    rl8c = [0]
    xoc = [0]
    halo = P.sb("halo", [128, 8, 30])

    ARW = 12288
    arena = nc.alloc_sbuf_tensor("arena", [128, ARW], F32)
    arena_tls = []

    class View:
        def __init__(self, name, ap):
            self.ap = ap
            self.b = Buf(name)
            arena_tls.append(self)

        def __getitem__(self, k):
            return self.ap[k]

    aoff = [0]

    def av(name, words, dt, pat=None, part=None, **kw):
        a = arena[:, aoff[0]:aoff[0] + words] if part is None else arena[0:part, aoff[0]:aoff[0] + words]
        aoff[0] += words
        assert aoff[0] <= ARW, (name, aoff[0])
        if dt == BF16:
            a = a.bitcast(BF16)
        if pat is not None:
            a = a.rearrange(pat, **kw)
        return View(name, a)

    aoff[0] = 0
    aqT = av("aqT", 1024, BF16, "p (h t) -> p h t", h=4)
    akT = av("akT", 1024, BF16, "p (h t) -> p h t", h=4)
    ga = av("ga", 2048, F32, "p (j c) -> p j c", j=4)
    Stil = av("Stil", 256, BF16, "p (h t) -> p h t", h=4)
    hraw = av("hraw", 512, F32)
    kw = av("kw", 1024, BF16, "p (j h d) -> p j h d", j=4, h=4)
    avx = av("avx", 1040, BF16, "p (j h e) -> p j h e", j=4, h=4)
    gI = av("gI", 512, F32, part=4)
    gF = av("gF", 512, F32, part=4)
    gB = av("gB", 512, F32, part=4)
    gA = av("gA", 512, F32, part=4)
    gZ = av("gZ", 512, F32, part=4)
    gWe = av("gWe", 512, F32, part=4)
    gTh = av("gTh", 512, F32, part=4)
    hraws = [hraw, av("hraw2", 512, F32)]
    aoff[0] = 0
    KTo = av("KTo", 1024, BF16, "p (h t) -> p h t", h=4)
    Vo = av("Vo", 1040, BF16, "p (j h e) -> p j h e", j=4, h=4)
    assert aoff[0] <= 4864
    bqT = av("bqT", 1024, BF16, "p (h t) -> p h t", h=4)
    rK = [av(f"rK{i}", 1024, BF16) for i in range(2)]
    rV = [av(f"rV{i}", 1040, BF16, "p (b e) -> p b e", b=16) for i in range(2)]
    rctr = [0]
    aoff[0] = 0
    uT = av("uT", 2168, BF16, "p (c t) -> p c t", c=8)
    dgr = [av(f"dg{i}", 64, BF16) for i in range(8)]
    dgc = [0]
    cacc = av("cacc", 4096, F32, "p (c t) -> p c t", c=8)
    szT = av("szT", 2048, BF16, "p (c t) -> p c t", c=8)
    e2 = av("e2", 512, F32)
    lmean = av("lmean", 512, F32)
    lrstd = av("lrstd", 512, F32)

    def switch(phase):
        flush()
        I("pool", "memset", dummy[:, 0:1], 0.0, w=[dummy] + arena_tls)
        if phase in ("A", "AB"):
            I("pool", "memset", avx[:, :, :, 128:130], 1.0, w=[avx])
        if phase in ("B", "AB"):
            I("pool", "memset", Vo[:, :, :, 128:130], 1.0, w=[Vo])
        if phase == "B":
            for r_ in rV:
                I("pool", "memset", r_[:, :, 128:130], 1.0, w=[r_])

    dma("sp", identf[:], c_ident, w=[identf])
    I("dve", "tensor_copy", out=identb[:], in_=identf[:], r=[identf], w=[identb])
    I("pool", "memset", onesf[:], 1.0, w=[onesf])
    I("pool", "memset", epsc[:], EPS, w=[epsc])
    I("pool", "memset", lnk[:], math.log(128.0 ** -0.5), w=[lnk])
    for l in range(2):
        dma("sp", ng[:, l, :], norm_g[l].rearrange("(c p) -> p c", p=128), w=[ng], allow_slow_non_contiguous=True)
        dma("sp", mng[:, l, :], mem_norm_g[l].rearrange("(c p) -> p c", p=128), w=[mng], allow_slow_non_contiguous=True)
        dma("sp", g_xq[:, l, :], xq_g[l].partition_broadcast(128), w=[g_xq])
        dma("sp", g_xk[:, l, :], xk_g[l].partition_broadcast(128), w=[g_xk])
    dma("sp", g_mlstm[:], mlstm_g.partition_broadcast(128), w=[g_mlstm])
    dma("sp", g_qn[:], qn_g.partition_broadcast(128), w=[g_qn])
    dma("sp", g_kn[:], kn_g.partition_broadcast(128), w=[g_kn])
    dma("sp", g_subln[:], subln_g.partition_broadcast(128), w=[g_subln])
    dma("sp", lam_t[:].rearrange("p a d -> p (a d)"), lamv.rearrange("a d -> (a d)").partition_broadcast(128), w=[lam_t])
    dma("sp", big[:], b_ig, w=[big])
    dma("sp", bfg[:], b_fg, w=[bfg])
    dma("sp", flag[:], c_flag, w=[flag])
    dma("sp", biasT[:], c_bias, w=[biasT])
    dma("sp", dmask[:], c_dmask, w=[dmask])
    dma("sp", amask[:], c_amask, w=[amask])
    dma("sp", hsel[:], c_hsel, w=[hsel])
    convert_weights(["w_mem0", "w_mem1"])
    convert_wa([0, 1, 2])
    convert_wa([3, 4, 5, 6])
    convert_weights(["w_out_a", "w_in_c", "w_out_c"])
    dma("pool", wgate[:], wbf["w_in_a"].rearrange("(c p) n -> p c n", p=128)[:, :, 2560:2568], r=wdeps("w_in_a", 2560, 2568), w=[wgate])
    dma("sp", cw[:], conv_wT.rearrange("(c p) j -> p c j", p=128), w=[cw])
    dma("sp", cb[:], conv_b.rearrange("(c p) -> p c", p=128), w=[cb], allow_slow_non_contiguous=True)
    dma("sp", lg[:], ln_g.rearrange("(c p) -> p c", p=128), w=[lg], allow_slow_non_contiguous=True)
    dma("sp", lb[:], ln_b.rearrange("(c p) -> p c", p=128), w=[lb], allow_slow_non_contiguous=True)
    I("dve", "tensor_scalar", out=g_subln[:], in0=g_subln[:], scalar1=1.0 - LAM_INIT0, scalar2=None, op0=ALU.mult, r=[g_subln], w=[g_subln])
    I("dve", "tensor_scalar", out=bfg[:], in0=bfg[:], scalar1=-1.0, scalar2=None, op0=ALU.mult, r=[bfg], w=[bfg])
    I("dve", "tensor_tensor", out=lam_t[:, 0, :], in0=lam_t[:, 0, :], in1=lam_t[:, 1, :], op=ALU.mult, r=[lam_t], w=[lam_t])
    I("dve", "tensor_tensor", out=lam_t[:, 2, :], in0=lam_t[:, 2, :], in1=lam_t[:, 3, :], op=ALU.mult, r=[lam_t], w=[lam_t])
    I("dve", "tensor_reduce", out=lam_s[:, 0:1], in_=lam_t[:, 0, :], axis=AX.X, op=ALU.add, r=[lam_t], w=[lam_s])
    I("dve", "tensor_reduce", out=lam_s[:, 1:2], in_=lam_t[:, 2, :], axis=AX.X, op=ALU.add, r=[lam_t], w=[lam_s])
    I("act", "activation", out=lam_s[:, 2:4], in_=lam_s[:, 0:2], func=AF.Exp, r=[lam_s], w=[lam_s])
    I("dve", "tensor_tensor", out=nlam[:], in0=lam_s[:, 3:4], in1=lam_s[:, 2:3], op=ALU.subtract, r=[lam_s], w=[nlam])
    I("dve", "tensor_scalar", out=nlam[:], in0=nlam[:], scalar1=-LAM_INIT0, scalar2=None, op0=ALU.add, r=[nlam], w=[nlam])
    I("pool", "memset", mV[:], 1.0, w=[mV])
    I("pool", "memset", halo[:], 0.0, w=[halo])

    def load_w(nm, c0, ncols, nkc):
        wt = wring[wctr[0] % NW]
        wctr[0] += 1
        v = wt[:, 0:nkc * ncols].rearrange("p (c n) -> p c n", c=nkc)
        dma("pool", v, wbf[nm].rearrange("(c p) n -> p c n", p=128)[:, :, c0:c0 + ncols], r=wdeps(nm, c0, c0 + ncols), w=[wt])
        return (wt, v)

    def make_xnT(nsb, gcol, src_fn, stats_done=False):
        flush()
        if not stats_done:
            xn_stats(nsb, src_fn)
        xn_T(nsb, gcol)

    def xn_stats(nsb, src_fn):
        for j in range(nsb):
            sap, stl = src_fn(j)
            I("act", "activation", out=sqs[:], in_=sap, func=AF.Square, accum_out=st1[:, j:j + 1], r=[stl], w=[sqs, st1])
            I("act", "activation", out=st2[:, j:j + 1], in_=st1[:, j:j + 1], func=AF.Ln, scale=1.0 / D, bias=epsc[:], r=[st1, epsc], w=[st2])
            I("act", "activation", out=st2[:, j:j + 1], in_=st2[:, j:j + 1], func=AF.Exp, scale=-0.5, r=[st2], w=[st2])
            I("dve", "tensor_scalar", out=xnb[:, j, :], in0=sap, scalar1=st2[:, j:j + 1], scalar2=None, op0=ALU.mult, r=[stl, st2], w=[xnb])
            if hasattr(src_fn, "release"):
                src_fn.release(j)

    def xn_T(nsb, gcol):
        N = 128 * nsb
        for kc in range(8):
            pt = ptb[kc % 2]
            for j in range(nsb):
                tr(pt[:, j * 128:(j + 1) * 128], xnb[:, j, kc * 128:(kc + 1) * 128], identb[:], r=[xnb, identb], w=[pt])
            if kc % 2 == 0:
                I("dve", "tensor_scalar", out=xnT[:, kc, 0:N], in0=pt[:, 0:N], scalar1=gcol[:, kc:kc + 1], scalar2=None, op0=ALU.mult, r=[pt, ng, mng], w=[xnT])
            else:
                I("act", "activation", out=xnT[:, kc, 0:N], in_=pt[:, 0:N], func=AF.Copy, scale=gcol[:, kc:kc + 1], r=[pt, ng, mng], w=[xnT])

    xplan = []
    for l_ in range(2 if do_l1 else 1):
        xplan += [(mem, 0), (mem, 128)]
    blk_ = 0
    for (n_, m_) in cfg["pre"]:
        xplan += [(x_pre, (blk_ + k_) * 128) for k_ in range(n_)]
        blk_ += n_
    blk_ = 0
    for n_ in cfg["own"]:
        xplan += [(x_own, (blk_ + k_) * 128) for k_ in range(n_)]
        blk_ += n_
    if do_smp:
        xplan += [(x_smp, 0)]
    xstate = {"issued": 0, "next": 0}

    def x_issue():
        i = xstate["issued"]
        if i < len(xplan):
            src_, row_ = xplan[i]
            t = xs[i % NXS]
            dma("sp", t[:], src_[row_:row_ + 128, :], w=[t])
            xstate["issued"] = i + 1

    def dram_src(src, row0):
        def fn(j):
            i = xstate["next"]
            assert xplan[i][0] is src and xplan[i][1] == row0 + j * 128, (i, row0, j)
            while xstate["issued"] <= i:
                x_issue()
            xstate["next"] = i + 1
            t = xs[i % NXS]
            return t[:], t
        fn.release = lambda j: x_issue()
        return fn

    def proj_tok(w, j, ncols, out_ps, nkc=8, coff=0):
        wt, wv = w
        for kc in range(nkc):
            mm(out_ps[:, 0:ncols], xnT[:, kc, j * 128:(j + 1) * 128], wv[:, kc, coff:coff + ncols], kc == 0, kc == nkc - 1, r=[xnT, wt], w=[out_ps])

    def proj_feat(w, c, N, out_ps, nkc=8):
        wt, wv = w
        for kc in range(nkc):
            mm(out_ps[:, 0:N], wv[:, kc, c * 128:(c + 1) * 128], xnT[:, kc, 0:N], kc == 0, kc == nkc - 1, r=[xnT, wt], w=[out_ps])

    def group_norm(src_ap, src_tl, ng_, gs, gain_ap, gain_tl, dst_ap, dst_tl):
        n = ng_ * gs
        if ng_ <= 4:
            for g_ in range(ng_):
                I("act", "activation", out=sqs[:, g_ * gs:(g_ + 1) * gs], in_=src_ap[:, g_ * gs:(g_ + 1) * gs], func=AF.Square, accum_out=st1[:, g_:g_ + 1],
                  r=[src_tl], w=[sqs, st1])
        else:
            I("act", "activation", out=sqs[:, 0:n], in_=src_ap, func=AF.Square, r=[src_tl], w=[sqs])
            I("dve", "tensor_reduce", out=st1[:, 0:ng_], in_=sqs[:, 0:n].rearrange("p (g d) -> p g d", g=ng_), axis=AX.X, op=ALU.add, r=[sqs], w=[st1])
        I("act", "activation", out=st2[:, 0:ng_], in_=st1[:, 0:ng_], func=AF.Ln, scale=1.0 / gs, bias=epsc[:], r=[st1, epsc], w=[st2])
        I("act", "activation", out=st2[:, 0:ng_], in_=st2[:, 0:ng_], func=AF.Exp, scale=-0.5, r=[st2], w=[st2])
        I("dve", "tensor_tensor", out=dst_ap.rearrange("p (g d) -> p g d", g=ng_), in0=src_ap.rearrange("p (g d) -> p g d", g=ng_),
          in1=bc(st2[:, 0:ng_].unsqueeze(2), [128, ng_, gs]), op=ALU.mult, r=[src_tl, st2], w=[dst_tl])
        I("dve", "tensor_tensor", out=dst_ap, in0=dst_ap, in1=gain_ap, op=ALU.mult, r=[dst_tl, gain_tl], w=[dst_tl])

    def silu_gate(ps_tl, ncols, dst_ap, dst_tl, mul_ap, mul_tl):
        I("act", "activation", out=e1[:, 0:ncols], in_=ps_tl[:, 0:ncols], func=AF.Silu, r=[ps_tl], w=[e1])
        I("dve", "tensor_tensor", out=dst_ap, in0=e1[:, 0:ncols], in1=mul_ap, op=ALU.mult, r=[e1, mul_tl], w=[dst_tl])

    pending = []

    def flush(keep=0):
        n = len(pending) - keep
        if n <= 0:
            return
        fs = pending[:n]
        del pending[:n]
        for f_ in fs:
            f_()

    def tok_to_T(src_ap, src_tl, dstT, j, dst_tl):
        flush(keep=1)
        tb = tkbs[tkc[0] % 2]
        tkc[0] += 1
        I("dve", "tensor_copy", out=tb[:], in_=src_ap, r=[src_tl], w=[tb])

        def later():
            pt = ptb[j % 2]
            for c in range(4):
                tr(pt[:, c * 128:(c + 1) * 128], tb[:, c * 128:(c + 1) * 128], identb[:], r=[tb, identb], w=[pt])
            I("act", "activation", out=dstT, in_=pt[:, 0:512].rearrange("p (c t) -> p c t", c=4), func=AF.Copy, r=[pt], w=[dst_tl])
        pending.append(later)

    def to_mixT(src_ap, src_tl, j, c0):
        tok_to_T(src_ap, src_tl, mixT[:, c0:c0 + 4, j * 128:(j + 1) * 128], j, mixT)

    def mem_kv(l):
        make_xnT(2, mng[:, l, :], dram_src(mem, 0))
        wk = load_w(f"w_mem{l}", 0, 512, 8)
        wv = load_w(f"w_mem{l}", 512, 512, 8)
        for j in range(2):
            proj_tok(wk, j, 512, pb[0])
            group_norm(pb[0][:, :], pb[0], 4, 128, g_xk[:, l, :], g_xk, tk0[:], tk0)
            dma("sp", o_pxk[l, j * 128:(j + 1) * 128, :], tk0[:], r=[tk0], final=True)
            tok_to_T(tk0[:], tk0, mKT[:, l, :, j * 128:(j + 1) * 128], j, mKT)
            proj_tok(wv, j, 512, pb[1])
            I("dve", "tensor_copy", out=tk1[:], in_=pb[1][:, :], r=[pb[1]], w=[tk1])
            I("act", "activation", out=mV[:, l, j, :, 0:128], in_=pb[1][:, :].rearrange("p (h d) -> p h d", h=4), func=AF.Copy, r=[pb[1]], w=[mV])
            dma("sp", o_pxv[l, j * 128:(j + 1) * 128, :], tk1[:], r=[tk1], final=True)

    def smp_mem_kv():
        for l in range(2):
            for j in range(2):
                dma("sp", tk0[:], c_xk[l, j * 128:(j + 1) * 128, :], w=[tk0])
                tok_to_T(tk0[:], tk0, mKT[:, l, :, j * 128:(j + 1) * 128], j, mKT)
                dma("pool", mV[:, l, j, :, 0:128], c_xv[l, j * 128:(j + 1) * 128, :].rearrange("p (h d) -> p h d", h=4), w=[mV])

    def x_q(l, nsb, wsrc, cq):
        wq = load_w(wsrc, cq, 512, 8)
        for j in range(nsb):
            tkq = tkrot[tkrc[0] % 3]
            tkrc[0] += 1
            proj_tok(wq, j, 512, pb[j % 4])
            group_norm(pb[j % 4][:, :], pb[j % 4], 4, 128, g_xq[:, l, :], g_xq, tkq[:], tkq)
            tok_to_T(tkq[:], tkq, xqT[:, :, j * 128:(j + 1) * 128], j, xqT)

    def x_branch(l, nsb, wsrc, cq, cz, c0, q_done=False):
        N = 128 * nsb
        if not q_done:
            x_q(l, nsb, wsrc, cq)
        wz = load_w(wsrc, cz, 512, 8)
        flush()
        def x_scores(h):
            pts = PT[pctr[0] % 2]
            pctr[0] += 1
            for mb in range(2):
                sp = pb[2 * (h % 2) + mb]
                mm(sp[:, 0:N], mKT[:, l, h, mb * 128:(mb + 1) * 128], xqT[:, h, 0:N], True, True, r=[mKT, xqT], w=[sp])
                I("act", "activation", out=pts[mb][:, 0:N], in_=sp[:, 0:N], func=AF.Exp, scale=128.0 ** -0.5, r=[sp], w=[pts[mb]])
            return pts

        def x_pv(h, pts):
            for j in range(nsb):
                oa = pb[4 + (xoc[0] % 3)]
                xoc[0] += 1
                for mb in range(2):
                    mm(oa[:, 0:129], pts[mb][:, j * 128:(j + 1) * 128], mV[:, l, mb, h, 0:129], mb == 0, mb == 1, r=[pts[mb], mV], w=[oa])
                rr = rl2[xoc[0] % 2]
                I("dve", "reciprocal", out=rr[:, 0:1], in_=oa[:, 128:129], r=[oa], w=[rr])
                I("dve", "tensor_scalar", out=ob[:, j, h * 128:(h + 1) * 128], in0=oa[:, 0:128], scalar1=rr[:, 0:1], scalar2=None, op0=ALU.mult, r=[oa, rr], w=[ob])

        prev_ = None
        for h in range(4):
            pts_h = x_scores(h)
            if prev_ is not None:
                x_pv(*prev_)
            prev_ = (h, pts_h)
        x_pv(*prev_)
        for j in range(nsb):
            proj_tok(wz, j, 512, pb[j % 4])
            silu_gate(pb[j % 4], 512, tk0[:], tk0, ob[:, j, :], ob)
            to_mixT(tk0[:], tk0, j, c0)

    def a_gates(nsb, Lc, nch_per_sb):
        flush()
        N = 128 * nsb
        nch = nsb * nch_per_sb
        CS = 128 // nch_per_sb
        for kc in range(8):
            mm(pb[0][0:4, 0:N], wgate[:, kc, 0:4], xnT[:, kc, 0:N], kc == 0, kc == 7, r=[xnT, wgate], w=[pb[0]])
        for kc in range(8):
            mm(pb[1][0:4, 0:N], wgate[:, kc, 4:8], xnT[:, kc, 0:N], kc == 0, kc == 7, r=[xnT, wgate], w=[pb[1]])
        I("dve", "tensor_scalar", out=gI[:, 0:N], in0=pb[0][0:4, 0:N], scalar1=big[:, 0:1], scalar2=None, op0=ALU.add, r=[pb[0], big], w=[gI])
        I("act", "activation", out=gF[:, 0:N], in_=pb[1][0:4, 0:N], func=AF.Exp, scale=-1.0, bias=bfg[:, 0:1], r=[pb[1], bfg], w=[gF])
        I("act", "activation", out=gF[:, 0:N], in_=gF[:, 0:N], func=AF.Ln, scale=1.0, bias=onesf[0:4, 0:1], r=[gF, onesf], w=[gF])
        I("pool", "memset", gZ[:, 0:N], 0.0, w=[gZ])
        I("pool", "memset", gB[:, 0:N], 0.0, w=[gB])
        for c in range(nch):
            sl = slice(c * CS, c * CS + Lc)
            I("dve", "tensor_tensor_scan", out=gB[:, sl], data0=gF[:, sl], data1=gZ[:, sl], initial=0.0, op0=ALU.add, op1=ALU.add, r=[gF, gZ], w=[gB])
        I("dve", "tensor_tensor", out=gA[:, 0:N], in0=gI[:, 0:N], in1=gB[:, 0:N], op=ALU.add, r=[gI, gB], w=[gA])
        gA3 = gA[:, 0:N].rearrange("p (c t) -> p c t", t=CS)
        gB3 = gB[:, 0:N].rearrange("p (c t) -> p c t", t=CS)
        I("dve", "tensor_reduce", out=cA[:, 0:nch], in_=gA3[:, :, 0:Lc], axis=AX.X, op=ALU.max, r=[gA], w=[cA])
        I("dve", "tensor_scalar", out=cB[:, 0:nch], in0=gB3[:, :, Lc - 1], scalar1=-1.0, scalar2=None, op0=ALU.mult, r=[gB], w=[cB])
        I("dve", "tensor_copy", out=cM[:, 0:1], in_=mst[:], r=[mst], w=[cM])
        I("dve", "tensor_tensor_scan", out=cM[:, 1:nch + 1], data0=cA[:, 0:nch], data1=cB[:, 0:nch], initial=mst[:, 0:1], op0=ALU.max, op1=ALU.add, r=[cA, cB, mst], w=[cM])
        I("dve", "tensor_copy", out=mst[:], in_=cM[:, nch:nch + 1], r=[cM], w=[mst])
        I("dve", "tensor_tensor", out=cX[:, 0:nch], in0=cM[:, 0:nch], in1=cA[:, 0:nch], op=ALU.max, r=[cM, cA], w=[cX])
        I("dve", "tensor_tensor", out=cD[:, 0:nch], in0=cM[:, 0:nch], in1=cX[:, 0:nch], op=ALU.subtract, r=[cM, cX], w=[cD])
        I("act", "activation", out=cD[:, 0:nch], in_=cD[:, 0:nch], func=AF.Exp, r=[cD], w=[cD])
        Xb = bc(cX[:, 0:nch].unsqueeze(2), [4, nch, CS])
        I("dve", "tensor_tensor", out=gA3, in0=gA3, in1=Xb, op=ALU.subtract, r=[gA, cX], w=[gA])
        I("act", "activation", out=gWe[:, 0:N], in_=gA[:, 0:N], func=AF.Exp, bias=lnk[0:4, 0:1], r=[gA, lnk], w=[gWe])
        I("dve", "tensor_tensor", out=gI[:, 0:N].rearrange("p (c t) -> p c t", t=CS), in0=gB3, in1=Xb, op=ALU.subtract, r=[gB, cX], w=[gI])
        I("act", "activation", out=gTh[:, 0:N], in_=gI[:, 0:N], func=AF.Exp, r=[gI], w=[gTh])
        for j in range(nsb):
            tr(pb[2][:, 0:4], gWe[:, j * 128:(j + 1) * 128], identf[0:4, 0:4], r=[gWe, identf], w=[pb[2]])
            tr(pb[2][:, 4:8], gTh[:, j * 128:(j + 1) * 128], identf[0:4, 0:4], r=[gTh, identf], w=[pb[2]])
            I("dve", "tensor_copy", out=gtok[:, j, :], in_=pb[2][:, 0:8], r=[pb[2]], w=[gtok])
        I("dve", "tensor_tensor", out=cDx[:, 0:nch, :], in0=bc(cD[:, 0:nch].unsqueeze(2), [4, nch, 4]),
          in1=hsel[:, :, 0:nch].rearrange("k h c -> k c h"), op=ALU.mult, r=[cD, hsel], w=[cDx])
        mm(pb[2][:, 0:nch * 4], onesf[0:4, :], cDx[:, 0:nch, :].rearrange("k c h -> k (c h)"), True, True, r=[onesf, cDx], w=[pb[2]])
        I("dve", "tensor_copy", out=decb[:, 0:nch, :], in_=pb[2][:, 0:nch * 4].rearrange("p (c h) -> p c h", h=4), r=[pb[2]], w=[decb])

    def evac(h, out_ap, in_ap, r, w):
        if h % 2:
            I("act", "activation", out=out_ap, in_=in_ap, func=AF.Copy, r=r, w=w)
        else:
            I("dve", "tensor_copy", out=out_ap, in_=in_ap, r=r, w=w)

    def a_branch(nsb, mode, wsrc, Lc=64, nch_per_sb=2, state_later=False):
        N = 128 * nsb
        full = mode != "pre"
        CS = 128 // nch_per_sb
        if full:
            wq = load_w(wsrc, 0, 512, 8)
            for h in range(4):
                proj_feat(wq, h, N, pb[h % 4])
                evac(h, aqT[:, h, 0:N], pb[h % 4][:, 0:N], [pb[h % 4]], [aqT])
        wk = load_w(wsrc, 512, 512, 8)
        wv = load_w(wsrc, 1024, 512, 8)
        a_gates(nsb, Lc, nch_per_sb)
        for j in range(nsb):
            proj_tok(wv, j, 512, pb[j % 4])
            I("act", "activation", out=avx[:, j, :, 0:128], in_=pb[j % 4][:, :].rearrange("p (h d) -> p h d", h=4), func=AF.Copy, r=[pb[j % 4]], w=[avx])
        if full:
            for h in range(4):
                proj_feat(wk, h, N, pb[h % 4])
                evac(h, akT[:, h, 0:N], pb[h % 4][:, 0:N], [pb[h % 4]], [akT])
        for j in range(nsb):
            proj_tok(wk, j, 512, pb[j % 4])
            I("dve", "tensor_tensor", out=kw[:, j, :, :], in0=pb[j % 4][:, :].rearrange("p (h d) -> p h d", h=4),
              in1=bc(gtok[:, j, 0:4].unsqueeze(2), [128, 4, 128]), op=ALU.mult, r=[pb[j % 4], gtok], w=[kw])
        if full:
            wo = load_w(wsrc, 1536, 512, 8)
            for j in range(nsb):
                proj_tok(wo, j, 512, pb[j % 4])
                I("act", "activation", out=ga[:, j, :], in_=pb[j % 4][:, :], func=AF.Sigmoid, r=[pb[j % 4]], w=[ga])
            wz = load_w(wsrc, 2048, 512, 8)
            for j in range(nsb):
                proj_tok(wz, j, 512, pb[j % 4])
                silu_gate(pb[j % 4], 512, ga[:, j, :], ga, ga[:, j, :], ga)
        if state_later:
            def later_state():
                k_ = 0
                for j in range(nsb):
                    for cc in range(nch_per_sb):
                        rows = slice(cc * CS, cc * CS + CS)
                        c = j * nch_per_sb + cc
                        ups = (pb[6], pb[3]) if k_ % 2 == 0 else (pb[4], pb[5])
                        k_ += 1
                        I("dve", "tensor_tensor", out=Cst[:], in0=Cst[:], in1=bc(decb[:, c, :].unsqueeze(2), [128, 4, 130]), op=ALU.mult, r=[Cst, decb], w=[Cst])
                        for h in range(4):
                            up = ups[h // 2]
                            hh = h % 2
                            mm(up[:, hh * 129:(hh + 1) * 129], kw[rows, j, h, :], avx[rows, j, h, 0:129], True, True, r=[kw, avx], w=[up])
                        for hp in range(2):
                            I("dve", "tensor_tensor", out=Cst[:, 2 * hp:2 * hp + 2, 0:129], in0=Cst[:, 2 * hp:2 * hp + 2, 0:129],
                              in1=ups[hp][:, 0:258].rearrange("p (h e) -> p h e", h=2), op=ALU.add, r=[Cst, ups[hp]], w=[Cst])
            return later_state
        prev_epi = [None]
        for j in range(nsb):
            accs = (pb[4], pb[5]) if j % 2 == 0 else (pb[1], pb[2])
            if full:
                for h in range(4):
                    mm(pb[0][:, h * 128:(h + 1) * 128], akT[:, h, j * 128:(j + 1) * 128], aqT[:, h, j * 128:(j + 1) * 128], True, True, r=[akT, aqT], w=[pb[0]])
                for h in range(4):
                    I("dve", "scalar_tensor_tensor", out=Stil[:, h, :], in0=pb[0][:, h * 128:(h + 1) * 128], scalar=gtok[:, j, h:h + 1], in1=amask[:],
                      op0=ALU.mult, op1=ALU.mult, r=[pb[0], gtok, amask], w=[Stil])
            for cc in range(nch_per_sb):
                rows = slice(cc * CS, cc * CS + CS)
                c = j * nch_per_sb + cc
                t0 = j * 128 + cc * CS
                if full:
                    I("dve", "tensor_tensor", out=Cbf[:], in0=Cst[:], in1=bc(decb[:, c, :].unsqueeze(2), [128, 4, 130]), op=ALU.mult, r=[Cst, decb], w=[Cbf])
                I("dve", "tensor_tensor", out=Cst[:], in0=Cst[:], in1=bc(decb[:, c, :].unsqueeze(2), [128, 4, 130]), op=ALU.mult, r=[Cst, decb], w=[Cst])
                if full:
                    for h in range(4):
                        ac = accs[h // 2]
                        hh = h % 2
                        mm(ac[rows, hh * 129:(hh + 1) * 129], aqT[:, h, t0:t0 + CS], Cbf[:, h, 0:129], True, False, r=[aqT, Cbf], w=[ac])
                        mm(ac[rows, hh * 129:(hh + 1) * 129], Stil[:, h, cc * CS:cc * CS + CS], avx[:, j, h, 0:129], False, True, r=[Stil, avx], w=[ac])
                ups = (pb[6], pb[3])
                for h in range(4):
                    up = ups[h // 2]
                    hh = h % 2
                    mm(up[:, hh * 129:(hh + 1) * 129], kw[rows, j, h, :], avx[rows, j, h, 0:129], True, True, r=[kw, avx], w=[up])
                for hp in range(2):
                    I("dve", "tensor_tensor", out=Cst[:, 2 * hp:2 * hp + 2, 0:129], in0=Cst[:, 2 * hp:2 * hp + 2, 0:129],
                      in1=ups[hp][:, 0:258].rearrange("p (h e) -> p h e", h=2), op=ALU.add, r=[Cst, ups[hp]], w=[Cst])
            if full:
                d4 = den4s[j % 2]
                hr = hraws[j % 2]
                for hp in range(2):
                    ac = accs[hp]
                    I("act", "activation", out=d4[:, 2 * hp:2 * hp + 2], in_=ac[:, 0:258].rearrange("p (h e) -> p h e", h=2)[:, :, 128], func=AF.Abs, r=[ac], w=[d4])

                def epi(j=j, d4=d4, hr=hr, accs=accs):
                    I("dve", "tensor_tensor", out=d4[:], in0=d4[:], in1=gtok[:, j, 4:8], op=ALU.max, r=[d4, gtok], w=[d4])
                    I("dve", "reciprocal", out=d4[:], in_=d4[:], r=[d4], w=[d4])
                    for hp in range(2):
                        ac = accs[hp]
                        I("dve", "tensor_tensor", out=hr[:, hp * 256:(hp + 1) * 256].rearrange("p (h e) -> p h e", h=2),
                          in0=ac[:, 0:258].rearrange("p (h e) -> p h e", h=2)[:, :, 0:128],
                          in1=bc(d4[:, 2 * hp:2 * hp + 2].unsqueeze(2), [128, 2, 128]), op=ALU.mult, r=[ac, d4], w=[hr])
                    group_norm(hr[:], hr, 4, 128, g_mlstm[:], g_mlstm, tk0[:], tk0)
                    I("dve", "tensor_tensor", out=tk0[:], in0=tk0[:], in1=ga[:, j, :], op=ALU.mult, r=[tk0, ga], w=[tk0])
                    to_mixT(tk0[:], tk0, j, 0)
                if prev_epi[0] is not None:
                    prev_epi[0]()
                prev_epi[0] = epi
        if full and prev_epi[0] is not None:
            prev_epi[0]()
            prev_epi[0] = None

    def scr_write(j, blk):
        flush()
        hf, bi = blk // 16, blk % 16
        dma("sp", scrK[:, hf, :, bi * 128:(bi + 1) * 128].rearrange("h p k -> p h k"), KTo[:, :, j * 128:(j + 1) * 128], r=[KTo], w=[scr_b[blk]])
        dma("sp", scrV[:, hf, :, bi * 130:(bi + 1) * 130].rearrange("h p e -> p h e"), Vo[:, j, :, :], r=[Vo], w=[scr_b[blk]])

    def b_kv(nsb, wsrc, blk0, out_row0, smp=False):
        wk = load_w(wsrc, 3080, 512, 8)
        for j in range(nsb):
            proj_tok(wk, j, 512, pb[j % 4])
            tkr = tk0 if j % 2 == 0 else tk2
            group_norm(pb[j % 4][:, :], pb[j % 4], 8, 64, g_kn[:], g_kn, tkr[:], tkr)
            if out_row0 is not None:
                if smp:
                    dma("sp", o_sk[0:32, :], tkr[0:32, :], r=[tkr], final=True)
                else:
                    dma("sp", o_pk[out_row0 + j * 128:out_row0 + (j + 1) * 128, :], tkr[:], r=[tkr], final=True)
            tok_to_T(tkr[:], tkr, KTo[:, :, j * 128:(j + 1) * 128], j, KTo)
        wv = load_w(wsrc, 3592, 512, 8)
        for j in range(nsb):
            proj_tok(wv, j, 512, pb[j % 4])
            if out_row0 is not None:
                I("dve", "tensor_copy", out=tk1[:], in_=pb[j % 4][:, :], r=[pb[j % 4]], w=[tk1])
                if smp:
                    dma("sp", o_sv[0:32, :], tk1[0:32, :], r=[tk1], final=True)
                else:
                    dma("sp", o_pv[out_row0 + j * 128:out_row0 + (j + 1) * 128, :], tk1[:], r=[tk1], final=True)
            I("act", "activation", out=Vo[:, j, :, 0:128], in_=pb[j % 4][:, :].rearrange("p (h d) -> p h d", h=4), func=AF.Copy, r=[pb[j % 4]], w=[Vo])
        if blk0 is not None:
            for j in range(nsb):
                scr_write(j, blk0 + j)

    def b_attn(nsb, wsrc, tile_blk, n_masked, diag_kind):
        N = 128 * nsb
        wq = load_w(wsrc, 2568, 512, 8)
        for j in range(nsb):
            proj_tok(wq, j, 512, pb[j % 4])
            group_norm(pb[j % 4][:, :], pb[j % 4], 8, 64, g_qn[:], g_qn, tk0[:], tk0)
            tok_to_T(tk0[:], tk0, bqT[:, :, j * 128:(j + 1) * 128], j, bqT)
        wz = load_w(wsrc, 4104, 512, 8)
        flush()
        items = []
        for h in range(4):
            nsub = NSUBH[h] if nsb == 4 else nsb
            gsz = nsb // nsub
            blocks = []
            ring_loads = []
            for hf in range((tile_blk + 15) // 16):
                nb_h = min(16, tile_blk - hf * 16)
                ring_loads.append((hf, nb_h))
            items.append(("head", h, nsub, gsz, ring_loads))

        def emit_qk(it):
            (h, nsub, gsz, tbl, kap, vap, kdep, vdep, eb, dj, bidx, pts, sps) = it
            q0 = 0 if dj is None else dj
            for m in range(2):
                mm(sps[m][:, q0 * 128:N], kap[m * 64:(m + 1) * 64, :], bqT[m * 64:(m + 1) * 64, h, q0 * 128:N], True, True, r=[kdep, bqT], w=[sps[m]])
            if dj is not None:
                for m in range(2):
                    I("dve", "tensor_tensor", out=sps[m][:, dj * 128:(dj + 1) * 128], in0=sps[m][:, dj * 128:(dj + 1) * 128],
                      in1=dmask[:, diag_kind, h, :], op=ALU.add, r=[sps[m], dmask], w=[sps[m]])
            for m in range(2):
                for g in range(nsub):
                    qa, qe = max(g * gsz, q0), (g + 1) * gsz
                    if qe <= qa:
                        continue
                    delta = (tile_blk + g * gsz) - eb
                    col = h * NDELTA + delta + DOFF
                    I("act", "activation", out=pts[m][:, qa * 128:qe * 128], in_=sps[m][:, qa * 128:qe * 128], func=AF.Exp, scale=0.125,
                      bias=biasT[:, tbl, col:col + 1], r=[sps[m], biasT], w=[pts[m]])

        def emit_pv(it):
            (h, nsub, gsz, tbl, kap, vap, kdep, vdep, eb, dj, bidx, pts, sps) = it
            q0 = 0 if dj is None else dj
            for qs in range(q0, nsb):
                for m in range(2):
                    rg = qs * 2 + m
                    oa = pb[4 + rg // 3]
                    o0 = (rg % 3) * 129
                    last = (dj is not None and dj == qs)
                    mm(oa[:, o0:o0 + 129], pts[m][:, qs * 128:(qs + 1) * 128], vap, (bidx == 0 and rg % 3 == 0), last, r=[pts[m], vdep], w=[oa],
                       skip_group_check=True)

        def emit_epilogue(h):
            nreg = 2 * nsb
            rl8 = rl8s[rl8c[0] % 2]
            rl8c[0] += 1
            for bk in range((nreg + 2) // 3):
                nr = min(3, nreg - 3 * bk)
                oa = pb[4 + bk]
                I("dve", "reciprocal", out=rl8[:, 3 * bk:3 * bk + nr], in_=oa[:, 0:nr * 129].rearrange("p (r e) -> p r e", e=129)[:, :, 128], r=[oa], w=[rl8])
            I("dve", "tensor_tensor", out=rl8[:, 0:nreg].rearrange("p (q m) -> p q m", m=2)[:, :, 1], in0=rl8[:, 0:nreg].rearrange("p (q m) -> p q m", m=2)[:, :, 1],
              in1=bc(nlam[:, 0:1], [128, nsb]), op=ALU.mult, r=[rl8, nlam], w=[rl8])
            for qs in range(nsb):
                r0, r1 = qs * 2, qs * 2 + 1
                oa0, oa1 = pb[4 + r0 // 3], pb[4 + r1 // 3]
                a0, a1 = (r0 % 3) * 129, (r1 % 3) * 129
                I("dve", "tensor_scalar", out=ob[:, qs, h * 128:(h + 1) * 128], in0=oa0[:, a0:a0 + 128], scalar1=rl8[:, r0:r0 + 1], scalar2=None, op0=ALU.mult, r=[oa0, rl8], w=[ob])
                I("dve", "scalar_tensor_tensor", out=ob[:, qs, h * 128:(h + 1) * 128], in0=oa1[:, a1:a1 + 128], scalar=rl8[:, r1:r1 + 1], in1=ob[:, qs, h * 128:(h + 1) * 128],
                  op0=ALU.mult, op1=ALU.add, r=[oa1, rl8, ob], w=[ob])

        prev = None
        for (_, h, nsub, gsz, ring_loads) in items:
            blocks = []
            first_eb = 0
            for eb_ in range(tile_blk):
                if SLOPES[h] * (128 * (tile_blk - eb_) - 127) > 56.0:
                    first_eb = eb_ + 1
            for (hf, nb_h) in ring_loads:
                b0 = max(0, first_eb - hf * 16)
                if b0 >= nb_h:
                    continue
                slot = rctr[0] % 2
                rctr[0] += 1
                deps = [scr_b[hf * 16 + bi] for bi in range(b0, nb_h)]
                dma("sp", rK[slot][:, b0 * 128:nb_h * 128], scrK[h, hf, :, b0 * 128:nb_h * 128], r=deps, w=[rK[slot]])
                dma("sp", rV[slot][:, b0:nb_h, :], scrV[h, hf, :, b0 * 130:nb_h * 130].rearrange("p (b e) -> p b e", e=130), r=deps, w=[rV[slot]])
                for bi in range(b0, nb_h):
                    eb = hf * 16 + bi
                    blocks.append((1 if eb < n_masked else 0, rK[slot][:, bi * 128:(bi + 1) * 128], rV[slot][:, bi, 0:129], rK[slot], rV[slot], eb, None))
            for dj in range(nsb):
                blocks.append((0, KTo[:, h, dj * 128:(dj + 1) * 128], Vo[:, dj, h, 0:129], KTo, Vo, tile_blk + dj, dj))
            for bidx, (tbl, kap, vap, kdep, vdep, eb, dj) in enumerate(blocks):
                pts = PT[pctr[0] % 2]
                sps = [pb[(pctr[0] % 2) * 2 + m] for m in range(2)]
                pctr[0] += 1
                it = (h, nsub, gsz, tbl, kap, vap, kdep, vdep, eb, dj, bidx, pts, sps)
                emit_qk(it)
                if prev is not None:
                    emit_pv(prev)
                    if prev[0] != h:
                        emit_epilogue(prev[0])
                prev = it
        emit_pv(prev)
        emit_epilogue(prev[0])
        for j in range(nsb):
            group_norm(ob[:, j, :], ob, 4, 128, g_subln[:], g_subln, tk1[:], tk1)
            proj_tok(wz, j, 512, pb[j % 4])
            silu_gate(pb[j % 4], 512, tk0[:], tk0, tk1[:], tk1)
            to_mixT(tk0[:], tk0, j, 4)

    def out_proj(nsb, wsrc, res_fn, dst_fn):
        flush()
        wos = [[load_w(wsrc, cg * 512 + hh * 256, 256, 12) for hh in range(2)] for cg in range(2)]
        k_ = 0
        for j in range(nsb):
            for cg in range(2):
                woh = wos[cg]
                ps_ = pb[k_ % 4]
                k_ += 1
                rap, rtl = res_fn(j, cg)
                for hh in range(2):
                    wt, wv = woh[hh]
                    for kc in range(12):
                        mm(ps_[:, hh * 256:(hh + 1) * 256], mixT[:, kc, j * 128:(j + 1) * 128], wv[:, kc, :], kc == 0, kc == 11, r=[mixT, wt], w=[ps_])
                dap, dtl = dst_fn(j, cg)
                I("dve", "tensor_tensor", out=dap, in0=ps_[:, :], in1=rap, op=ALU.add, r=[ps_, rtl], w=[dtl])
                dst_done(j, cg)

    dst_done_fn = [lambda j, cg: None]

    def dst_done(j, cg):
        dst_done_fn[0](j, cg)

    def conv_out(dst, t0, t1):
        flush()
        j = (t1 - 1) // 128
        for c in range(8):
            tr(pb[4 + c // 4][:, (c % 4) * 128:(c % 4 + 1) * 128], cacc[:, c, 0:128], identf[:], r=[cacc, identf], w=[pb[4 + c // 4]])
        for hp in range(2):
            I("dve", "tensor_copy", out=sqs[:, hp * 512:(hp + 1) * 512], in_=pb[4 + hp][:, :], r=[pb[4 + hp]], w=[sqs])
        r0 = t0 - j * 128
        dma("sp", dst, sqs[r0:r0 + 30, :], r=[sqs], final=True)

    def layer1(nsb, y_dst, nrows, halo_only=False, smp=False, pconv=False, before_out=None):
        N = 128 * nsb
        make_xnT(nsb, ng[:, 1, :], lambda j: (y0[:, j, :], y0b[j]))
        switch("L")
        I("dve", "tensor_copy", out=uT[:, :, 0:30], in_=halo[:], r=[halo], w=[uT])
        for c2 in range(4):
            wu = load_w("w_in_c", c2 * 256, 256, 8)
            wg = load_w("w_in_c", 1024 + c2 * 256, 256, 8)
            for cc in range(2):
                c = c2 * 2 + cc
                pu, pg = pb[2 * (c % 2)], pb[2 * (c % 2) + 1]
                proj_feat(wu, cc, N, pu)
                proj_feat(wg, cc, N, pg)
                I("act", "activation", out=e2[:, 0:N], in_=pg[:, 0:N], func=AF.Sigmoid, r=[pg], w=[e2])
                I("dve", "tensor_tensor", out=uT[:, c, 30:30 + N], in0=pu[:, 0:N], in1=e2[:, 0:N], op=ALU.mult, r=[pu, e2], w=[uT])
                if smp or pconv:
                    I("dve", "tensor_tensor", out=cacc[:, c, 0:128], in0=pu[:, N - 128:N], in1=e2[:, N - 128:N], op=ALU.mult, r=[pu, e2], w=[cacc])
        if smp:
            conv_out(o_sconv, 2, 32)
        if pconv:
            conv_out(o_pconv, N - 30, N)
        I("dve", "tensor_copy", out=halo[:], in_=uT[:, :, N:N + 30], r=[uT], w=[halo])
        if halo_only:
            return
        x_q(1, nsb, "w_in_c", 3072)
        for c in range(8):
            psc = pb[c % 4]
            for jt in range(31):
                dg = dgr[dgc[0] % 8]
                dgc[0] += 1
                I("dve", "tensor_scalar", out=dg[:, :], in0=identb[:], scalar1=cw[:, c, jt:jt + 1], scalar2=None, op0=ALU.mult, r=[identb, cw], w=[dg])
                mm(psc[:, 0:N], dg[:, :], uT[:, c, jt:jt + N], jt == 0, jt == 30, r=[dg, uT], w=[psc])
            I("act", "activation", out=cacc[:, c, 0:N], in_=psc[:, 0:N], func=AF.Identity, bias=cb[:, c:c + 1], r=[psc, cb], w=[cacc])
        for c in range(8):
            mm(pb[2][:, 0:N], onesf[:], cacc[:, c, 0:N], c == 0, c == 7, r=[onesf, cacc], w=[pb[2]])
        for c in range(8):
            I("act", "activation", out=sqs[:, 0:N], in_=cacc[:, c, 0:N], func=AF.Square, r=[cacc], w=[sqs])
            mm(pb[3][:, 0:N], onesf[:], sqs[:, 0:N], c == 0, c == 7, r=[onesf, sqs], w=[pb[3]])
        I("dve", "tensor_scalar", out=lmean[:, 0:N], in0=pb[2][:, 0:N], scalar1=1.0 / D, scalar2=None, op0=ALU.mult, r=[pb[2]], w=[lmean])
        I("dve", "tensor_tensor", out=e1[:, 0:N], in0=lmean[:, 0:N], in1=lmean[:, 0:N], op=ALU.mult, r=[lmean], w=[e1])
        I("dve", "scalar_tensor_tensor", out=e1[:, 0:N], in0=pb[3][:, 0:N], scalar=1.0 / D, in1=e1[:, 0:N], op0=ALU.mult, op1=ALU.subtract, r=[pb[3], e1], w=[e1])
        I("act", "activation", out=lrstd[:, 0:N], in_=e1[:, 0:N], func=AF.Ln, bias=epsc[:], r=[e1, epsc], w=[lrstd])
        I("act", "activation", out=lrstd[:, 0:N], in_=lrstd[:, 0:N], func=AF.Exp, scale=-0.5, r=[lrstd], w=[lrstd])
        wzs = [load_w("w_in_c", 2048, 512, 8), load_w("w_in_c", 2560, 512, 8)]
        for c in range(8):
            proj_feat(wzs[c // 4], c % 4, N, pb[c % 4])
            I("act", "activation", out=szT[:, c, 0:N], in_=pb[c % 4][:, 0:N], func=AF.Silu, r=[pb[c % 4]], w=[szT])
            I("dve", "tensor_tensor", out=cacc[:, c, 0:N], in0=cacc[:, c, 0:N], in1=lmean[:, 0:N], op=ALU.subtract, r=[cacc, lmean], w=[cacc])
            I("dve", "tensor_tensor", out=cacc[:, c, 0:N], in0=cacc[:, c, 0:N], in1=lrstd[:, 0:N], op=ALU.mult, r=[cacc, lrstd], w=[cacc])
            I("act", "activation", out=e1[:, 0:N], in_=cacc[:, c, 0:N], func=AF.Silu, scale=lg[:, c:c + 1], bias=lb[:, c:c + 1], r=[cacc, lg, lb], w=[e1])
            I("dve", "tensor_tensor", out=mixT[:, c, 0:N], in0=e1[:, 0:N], in1=szT[:, c, 0:N], op=ALU.mult, r=[e1, szT], w=[mixT])
        x_branch(1, nsb, "w_in_c", 3072, 3584, 8, q_done=True)

        cur = [None]

        def dst_fn(j, cg):
            cur[0] = tkrot[tkrc[0] % 3]
            tkrc[0] += 1
            return cur[0][:], cur[0]

        def done(j, cg):
            dma("sp", y_dst[j * 128:j * 128 + nrows, cg * 512:(cg + 1) * 512], cur[0][0:nrows, :], r=[cur[0]], final=True)
        dst_done_fn[0] = done
        if before_out is not None:
            before_out()
        out_proj(nsb, "w_out_c", lambda j, cg: (y0[:, j, cg * 512:(cg + 1) * 512], y0b[j]), dst_fn)
        dst_done_fn[0] = lambda j, cg: None

    class _Stop(Exception):
        pass

    def stage(n):
        if cfg.get("stage") == n:
            raise _Stop()

    try:
        for _ in range(NXS):
            x_issue()
        stage(1)
        mem_kv(0)
        stage(2)
        if do_l1:
            mem_kv(1)
        stage(3)
        I("dve", "memset", Cst[:], 0.0, w=[Cst])
        I("dve", "memset", mst[:], 0.0, w=[mst])

        def res_from(src, row0):
            def fn(j, cg):
                t_ = tkrot[tkrc[0] % 3]
                tkrc[0] += 1
                dma("sp", t_[:], src[row0 + j * 128:row0 + (j + 1) * 128, cg * 512:(cg + 1) * 512], w=[t_])
                return t_[:], t_
            return fn

        def full_tile(src, row0, nsb, tile_blk, n_masked, blk0, out_row0, y_dst, halo_only=False, pconv=False, stats_done=False, before_out=None):
            make_xnT(nsb, ng[:, 0, :], None if stats_done else dram_src(src, row0), stats_done=stats_done)
            switch("A")
            a_branch(nsb, "full", "w_in_a")
            switch("B")
            b_kv(nsb, "w_in_a", blk0, out_row0)
            x_q(0, nsb, "w_in_a", 4616)
            b_attn(nsb, "w_in_a", tile_blk, n_masked, 0)
            x_branch(0, nsb, "w_in_a", 4616, 5128, 8, q_done=True)
            out_proj(nsb, "w_out_a", res_from(src, row0), lambda j, cg: (y0[:, j, cg * 512:(cg + 1) * 512], y0b[j]))
            if do_l1:
                layer1(nsb, y_dst, 128, halo_only=halo_only, pconv=pconv, before_out=before_out)

        blk = 0
        pre_started = [False]
        for (nsb, mode) in cfg["pre"]:
            if mode == "pre":
                make_xnT(nsb, ng[:, 0, :], dram_src(x_pre, blk * 128))
                if not pre_started[0]:
                    switch("AB")
                    pre_started[0] = True
                st_later = a_branch(nsb, "pre", "w_in_a", state_later=True)
                b_kv(nsb, "w_in_a", blk, None)
                st_later()
                stage(4)
            else:
                assert nsb == 1
                full_tile(x_pre, blk * 128, 1, blk, NB_PRE, blk, None, None, halo_only=True)
                stage(5)
            blk += nsb
        if N_PRE > 0:
            I("dve", "tensor_tensor", out=mst[:], in0=mst[:], in1=flag[:], op=ALU.mult, r=[mst, flag], w=[mst])
        ob_ = 0
        nown = len(cfg["own"])
        for ti, nsb in enumerate(cfg["own"]):
            r0 = ob_ * 128
            last = ti == nown - 1
            hook = None
            if (not last) and do_l1:
                nsb2 = cfg["own"][ti + 1]
                r2 = (ob_ + nsb) * 128
                hook = (lambda nsb2=nsb2, r2=r2: xn_stats(nsb2, dram_src(x_own, r2)))
            full_tile(x_own, r0, nsb, NB_PRE + ob_, NB_PRE, (NB_PRE + ob_) if not last else None, r0, y_own[r0:r0 + 128 * nsb, :], pconv=last,
                      stats_done=(ti > 0 and do_l1), before_out=hook)
            ob_ += nsb
        dma("sp", o_pC.rearrange("h d e -> d h e"), Cst[:, :, 0:128], r=[Cst], final=True)
        dma("sp", o_pn.rearrange("h d -> d h"), Cst[:, :, 128], r=[Cst], final=True, allow_slow_non_contiguous=True)
        dma("sp", o_pm, mst[:], r=[mst], final=True)
        stage(6)

        if do_smp:
            smp_mem_kv()
            switch("B")
            for kb in range(16):
                tks = tkrot[kb % 3]
                dma("act", tks[:], c_k[kb * 128:(kb + 1) * 128, :], w=[tks])
                tok_to_T(tks[:], tks, KTo[:, :, (kb % 4) * 128:(kb % 4 + 1) * 128], kb, KTo)
                dma("pool", Vo[:, kb % 4, :, 0:128], c_v[kb * 128:(kb + 1) * 128, :].rearrange("p (h d) -> p h d", h=4), w=[Vo])
                scr_write(kb % 4, kb)
            dma("sp", Cst[:, :, 0:128], st_C.rearrange("h d e -> d h e"), w=[Cst])
            dma("sp", Cst[:, :, 128], st_n.rearrange("h d -> d h"), w=[Cst], allow_slow_non_contiguous=True)
            dma("sp", mst[:], st_m, w=[mst])
            I("pool", "memset", sqs[:], 0.0, w=[sqs])
            dma("sp", sqs[0:30, :], st_conv, w=[sqs])
            for c in range(8):
                tr(pb[4 + c // 4][:, (c % 4) * 128:(c % 4 + 1) * 128], sqs[:, c * 128:(c + 1) * 128], identf[:], r=[sqs, identf], w=[pb[4 + c // 4]])
            for hp in range(2):
                I("dve", "tensor_copy", out=halo[:, hp * 4:(hp + 1) * 4, :], in_=pb[4 + hp][:, :].rearrange("p (c t) -> p c t", c=4)[:, :, 0:30], r=[pb[4 + hp]], w=[halo])
            make_xnT(1, ng[:, 0, :], dram_src(x_smp, 0))
            switch("A")
            a_branch(1, "full", "w_in_a", Lc=32, nch_per_sb=1)
            dma("sp", o_sC.rearrange("h d e -> d h e"), Cst[:, :, 0:128], r=[Cst], final=True)
            dma("sp", o_sn.rearrange("h d -> d h"), Cst[:, :, 128], r=[Cst], final=True, allow_slow_non_contiguous=True)
            dma("sp", o_sm, mst[:], r=[mst], final=True)
            switch("B")
            b_kv(1, "w_in_a", None, 0, smp=True)
            b_attn(1, "w_in_a", 16, 0, 1)
            x_branch(0, 1, "w_in_a", 4616, 5128, 8)
            out_proj(1, "w_out_a", res_from(x_smp, 0), lambda j, cg: (y0[:, j, cg * 512:(cg + 1) * 512], y0b[j]))
            if do_l1:
                layer1(1, y_smp, 32, smp=True)

    except _Stop:
        pass
    flush()
    P.S.emit(final_waits=P.outs)
    return P


def host_consts(has_prefix):
    ident = np.eye(128, dtype=np.float32)
    a = np.arange(128, dtype=np.float64)
    bias = np.zeros((128, 2, 4 * NDELTA), np.float32)
    for h in range(4):
        for dl in range(-DOFF, NDELTA - DOFF):
            v = SLOPES[h] * (a - 128.0 * dl)
            bias[:, 0, h * NDELTA + dl + DOFF] = v
            bias[:, 1, h * NDELTA + dl + DOFF] = v + (0.0 if has_prefix else PMASK)
    ka = np.arange(128)[:, None]
    qc = np.arange(128)[None, :]
    dm = np.zeros((128, 2, 4, 128), np.float32)
    for h in range(4):
        corr = np.where(ka > qc, -16.0 * SLOPES[h] * (ka - qc), 0.0)
        vis = (ka // 64) <= (qc // 64)
        dm[:, 0, h, :] = np.where(vis, corr, -1.0e5)
        vis_s = (ka < 32) & (qc < 128)
        dm[:, 1, h, :] = np.where(vis_s, corr, -1.0e5)
    am = ((ka <= qc) & ((ka // 64) == (qc // 64))).astype(np.float32)
    hsel = np.zeros((4, 4, 128), np.float32)
    for h in range(4):
        hsel[h, h, :] = 1.0
    flag = np.full((4, 1), 1.0 if has_prefix else 0.0, np.float32)
    return dict(c_ident=ident, c_bias=bias, c_dmask=dm, c_amask=am, c_hsel=hsel, c_flag=flag)


FULL_CFG = dict(own=[4] * 8, pre=[(4, "pre")] * 7 + [(3, "pre"), (1, "halo")], sample=True, l1=True)
_CACHE = {}


def core_inputs(c, cfg, inp):
    f = lambda a: np.ascontiguousarray(a, dtype=np.float32)
    b, half = c // 2, c % 2
    n_own = 128 * sum(cfg["own"])
    n_pre = 128 * sum(n for n, _ in cfg["pre"])
    xp = inp["x_prompt"][b]
    d = {}
    d["x_own"] = f(xp[half * n_own:(half + 1) * n_own])
    if half == 1:
        d["x_pre"] = f(xp[n_own - n_pre:n_own])
    else:
        d["x_pre"] = np.zeros((max(n_pre, 128), D), np.float32)
    xs = np.zeros((128, D), np.float32)
    xs[:32] = inp["x_sample"][c]
    d["x_smp"] = xs
    d["mem"] = f(inp["mem_prompt"][b])
    d["c_xk"] = f(inp["cache_xk"][:, c].reshape(2, 256, 512))
    d["c_xv"] = f(inp["cache_xv"][:, c].reshape(2, 256, 512))
    d["c_k"] = f(inp["cache_k"][0, c].reshape(2048, 512))
    d["c_v"] = f(inp["cache_v"][0, c].reshape(2048, 512))
    d["st_C"] = f(inp["state_C"][0, c])
    d["st_n"] = f(inp["state_n"][0, c])
    d["st_m"] = f(inp["state_m"][0, c].reshape(4, 1))
    d["st_conv"] = f(inp["state_conv"][0, c])
    d["norm_g"] = f(inp["norm_g"])
    d["w_in_a"] = f(inp["w_in_a"][0])
    d["b_ig"] = f(inp["b_ig"][0].reshape(4, 1))
    d["b_fg"] = f(inp["b_fg"][0].reshape(4, 1))
    d["mlstm_g"] = f(inp["mlstm_norm_g"][0].reshape(512))
    d["qn_g"] = f(np.tile(inp["qn_g"][0], 8))
    d["kn_g"] = f(np.tile(inp["kn_g"][0], 8))
    d["lamv"] = f(np.stack([inp["lam_q1"][0], inp["lam_k1"][0], inp["lam_q2"][0], inp["lam_k2"][0]]))
    d["subln_g"] = f(np.tile(inp["subln_g"][0], 4))
    d["w_out_a"] = f(inp["w_out_a"][0])
    d["w_in_c"] = f(inp["w_in_c"][0])
    d["conv_wT"] = f(inp["conv_w"][0].T)
    d["conv_b"] = f(inp["conv_b"][0])
    d["ln_g"] = f(inp["conv_ln_g"][0])
    d["ln_b"] = f(inp["conv_ln_b"][0])
    d["w_out_c"] = f(inp["w_out_c"][0])
    d["mem_norm_g"] = f(inp["mem_norm_g"])
    d["w_mem_kv"] = f(inp["w_mem_kv"])
    d["xq_g"] = f(np.stack([np.tile(inp["xq_norm_g"][l], 4) for l in range(2)]))
    d["xk_g"] = f(np.stack([np.tile(inp["xk_norm_g"][l], 4) for l in range(2)]))
    d.update(host_consts(half == 1))
    return d


def kernel(**inp):
    inp = {k: np.asarray(v) for k, v in inp.items()}
    cfg = FULL_CFG
    if "prog" not in _CACHE:
        _CACHE["prog"] = build(cfg)
    P = _CACHE["prog"]
    in_maps = [core_inputs(c, cfg, inp) for c in range(8)]
    res = run_bass_kernel_spmd(P.nc, in_maps, core_ids=list(range(8))).results
    B, SEQ = 4, 8192
    yp = np.zeros((B, SEQ, D), np.float32)
    pk = np.zeros((1, B, SEQ, 4, 128), np.float32)
    pv = np.zeros((1, B, SEQ, 4, 128), np.float32)
    pxk = np.zeros((2, B, 256, 4, 128), np.float32)
    pxv = np.zeros((2, B, 256, 4, 128), np.float32)
    pC = np.zeros((1, B, 4, 128, 128), np.float32)
    pn = np.zeros((1, B, 4, 128), np.float32)
    pm = np.zeros((1, B, 4), np.float32)
    pconv = np.zeros((1, B, 30, D), np.float32)
    ys = np.zeros((8, 32, D), np.float32)
    sk = np.zeros((1, 8, 32, 4, 128), np.float32)
    sv = np.zeros((1, 8, 32, 4, 128), np.float32)
    sC = np.zeros((1, 8, 4, 128, 128), np.float32)
    sn = np.zeros((1, 8, 4, 128), np.float32)
    sm = np.zeros((1, 8, 4), np.float32)
    sconv = np.zeros((1, 8, 30, D), np.float32)
    for c in range(8):
        r = res[c]
        b, half = c // 2, c % 2
        sl = slice(half * 4096, (half + 1) * 4096)
        yp[b, sl] = r["y_own"]
        pk[0, b, sl] = r["o_pk"].reshape(4096, 4, 128)
        pv[0, b, sl] = r["o_pv"].reshape(4096, 4, 128)
        if half == 1:
            pxk[:, b] = r["o_pxk"].reshape(2, 256, 4, 128)
            pxv[:, b] = r["o_pxv"].reshape(2, 256, 4, 128)
            pC[0, b] = r["o_pC"]
            pn[0, b] = r["o_pn"]
            pm[0, b] = r["o_pm"].reshape(4)
            pconv[0, b] = r["o_pconv"]
        ys[c] = r["y_smp"]
        sk[0, c] = r["o_sk"].reshape(32, 4, 128)
        sv[0, c] = r["o_sv"].reshape(32, 4, 128)
        sC[0, c] = r["o_sC"]
        sn[0, c] = r["o_sn"]
        sm[0, c] = r["o_sm"].reshape(4)
        sconv[0, c] = r["o_sconv"]
    return (yp, ys, pxk, pxv, pk, pv, pC, pn, pm, pconv, sk, sv, sC, sn, sm, sconv)
```

```python
import math
import numpy as np
import concourse.bass as bass
import concourse.mybir as mybir
from concourse.bass_utils import run_bass_kernel_spmd

F32 = mybir.dt.float32
BF16 = mybir.dt.bfloat16
AF = mybir.ActivationFunctionType
ALU = mybir.AluOpType
AX = mybir.AxisListType

D = 1024
EPS = 1e-6
SLOPES = [2.0 ** (-8.0 * (h + 1) / 4) for h in range(4)]
NSUBH = [2, 1, 1, 1]
NDELTA = 72
DOFF = 3
PMASK = -200.0
LAM_INIT0 = 0.8 - 0.6 * math.exp(0.0)


class Buf:
    __slots__ = ("name", "last_w", "readers", "excl")

    def __init__(self, name="", excl=False):
        self.name = name
        self.last_w = None
        self.readers = []
        self.excl = excl


class Op:
    __slots__ = ("eng", "fn", "deps", "signal", "is_dma", "sem", "semval", "vc", "gi", "desc")


class Sched:
    def __init__(self, nc, n_dma_sems=10):
        self.nc = nc
        self.ops = {e: [] for e in ("pe", "act", "dve", "pool", "sp")}
        self.all = []
        self.n_dma_sems = n_dma_sems
        self.dma_rr = {q: 0 for q in ("sp", "act", "pool")}
        self.dma_last = {}

    def op(self, eng, fn, reads=(), writes=(), dma=False):
        import os as _os
        if len(self.all) >= int(_os.environ.get("MAXOPS", "100000000")):
            return None
        o = Op()
        o.eng = eng
        o.fn = fn
        o.is_dma = dma
        o.signal = False
        o.sem = None
        o.semval = 0
        o.gi = len(self.all)
        deps = set()
        reads = list(reads)
        writes = list(writes)
        for b in reads:
            if b.excl and b not in writes:
                writes.append(b)
        for b in reads:
            if b.last_w is not None:
                deps.add(b.last_w)
        for b in writes:
            if b.last_w is not None:
                deps.add(b.last_w)
            for r in b.readers:
                deps.add(r)
        if dma:
            key = (eng, self.dma_rr[eng] % self.n_dma_sems)
            self.dma_rr[eng] += 1
            prev = self.dma_last.get(key)
            if prev is not None:
                deps.add(prev)
            self.dma_last[key] = o
            o.sem = key
        deps.discard(o)
        o.deps = deps
        for b in reads:
            b.readers.append(o)
        for b in writes:
            b.last_w = o
            b.readers = []
        self.ops[eng].append(o)
        self.all.append(o)
        return o

    def emit(self, final_waits=()):
        nc = self.nc

        def pe_pe(o, d):
            return (not d.is_dma) and d.eng == "pe" and o.eng == "pe" and (not o.is_dma)

        for o in self.all:
            for d in o.deps:
                if d.is_dma or pe_pe(o, d):
                    continue
                d.signal = True
        for o in final_waits:
            if not o.is_dma:
                o.signal = True
        cnt = {}
        for o in self.all:
            if o.is_dma:
                cnt[o.sem] = cnt.get(o.sem, 0) + 16
                o.semval = cnt[o.sem]
            elif o.signal:
                o.sem = ("c", o.eng)
                cnt[o.sem] = cnt.get(o.sem, 0) + 1
                o.semval = cnt[o.sem]
        known = {e: {} for e in self.ops}
        plans = {}
        for o in self.all:
            kn = known[o.eng]
            waits = {}
            for d in o.deps:
                if pe_pe(o, d):
                    continue
                if kn.get(d.sem, 0) >= d.semval:
                    continue
                if waits.get(d.sem, 0) < d.semval:
                    waits[d.sem] = d.semval
            for d in o.deps:
                if pe_pe(o, d):
                    continue
                for s, v in d.vc.items():
                    if kn.get(s, 0) < v:
                        kn[s] = v
            for s in list(waits):
                v = waits[s]
                for d in o.deps:
                    if pe_pe(o, d) or d.sem == s or d.sem not in waits:
                        continue
                    if waits[d.sem] >= d.semval and d.vc.get(s, 0) >= v:
                        del waits[s]
                        break
            plans[o.gi] = waits
            vc = dict(kn)
            if o.sem is not None and vc.get(o.sem, 0) < o.semval:
                vc[o.sem] = o.semval
            o.vc = vc
        sems = {}
        for key in cnt:
            sems[key] = nc.alloc_semaphore("s_" + "_".join(str(k) for k in key))
        fw = [(o.sem, o.semval) for o in final_waits]
        self.n_waits = sum(len(p) for p in plans.values())
        self.plans = plans

        def run_engine(ename):
            def body(eng):
                for o in self.ops[ename]:
                    wl = list(plans[o.gi].items())
                    for s, v in wl[:-1]:
                        eng.wait_ge(sems[s], v)
                    ins = o.fn(eng)
                    if wl:
                        ins = ins._wait_ge(sems[wl[-1][0]], wl[-1][1])
                    if o.is_dma:
                        ins.then_inc(sems[o.sem], 16)
                    elif o.signal:
                        ins.then_inc(sems[o.sem], 1)
                if ename == "sp":
                    done = {}
                    for s, v in fw:
                        done[s] = max(done.get(s, 0), v)
                    for s, v in done.items():
                        eng.wait_ge(sems[s], v)
            return body

        with nc.Block() as block:
            block.tensor(run_engine("pe"))
            block.scalar(run_engine("act"))
            block.vector(run_engine("dve"))
            block.gpsimd(run_engine("pool"))
            block.sync(run_engine("sp"))


class Tl:
    def __init__(self, t, name):
        self.t = t
        self.b = Buf(name)

    def __getitem__(self, k):
        return self.t[k]


class Prog:
    def __init__(self, cfg):
        self.cfg = cfg
        self.nc = bass.Bass("TRN2", target_bir_lowering=False)
        self.S = Sched(self.nc)
        self.outs = []
        self.uid = 0
        self.din = {}
        self.dout = {}

    def sb(self, name, shape, dt=F32):
        return Tl(self.nc.alloc_sbuf_tensor(name, list(shape), dt), name)

    def ps(self, name, shape, dt=F32):
        t = Tl(self.nc.alloc_psum_tensor(name, list(shape), dt), name)
        t.b.excl = True
        return t

    def inp(self, name, shape):
        a = self.nc.dram_tensor(name, list(shape), F32, kind="ExternalInput").ap()
        self.din[name] = a
        return a

    def outp(self, name, shape):
        a = self.nc.dram_tensor(name, list(shape), F32, kind="ExternalOutput").ap()
        self.dout[name] = a
        return a

    @staticmethod
    def _b(xs):
        return [x.b if hasattr(x, "b") else x for x in xs]

    def I(self, eng, meth, *a, r=(), w=(), **kw):
        o = self.S.op(eng, lambda e: getattr(e, meth)(*a, **kw), self._b(r), self._b(w))
        if o is not None:
            o.desc = (eng, meth, str(kw.get("out", a[0] if a else ""))[:90], str(kw.get("func", kw.get("op", kw.get("op0", "")))))
        return o

    def dma(self, q, out, in_, r=(), w=(), final=False, **kw):
        o = self.S.op(q, lambda e: e.dma_start(out=out, in_=in_, **kw), self._b(r), self._b(w), dma=True)
        if o is not None:
            o.desc = (q, "dma", str(out)[:90], "")
        if final and o is not None:
            self.outs.append(o)
        return o

    def mm(self, out, lhsT, rhs, start, stop, r, w, **kw):
        return self.I("pe", "matmul", out, lhsT=lhsT, rhs=rhs, start=start, stop=stop, r=r, w=w, **kw)

    def tr(self, out, in_, ident, r, w):
        return self.I("pe", "transpose", out=out, in_=in_, identity=ident, r=r, w=w)


def bc(ap, shape):
    return ap.broadcast_to(list(shape))


def build(cfg):
    P = Prog(cfg)
    nc = P.nc
    I, dma, mm, tr = P.I, P.dma, P.mm, P.tr
    N_OWN = 128 * sum(cfg["own"])
    N_PRE = 128 * sum(n for n, _ in cfg["pre"])
    NB_OWN = N_OWN // 128
    NB_PRE = N_PRE // 128
    do_smp = cfg.get("sample", True)
    do_l1 = cfg.get("l1", True)

    x_own = P.inp("x_own", [N_OWN, D])
    x_pre = P.inp("x_pre", [max(N_PRE, 128), D])
    x_smp = P.inp("x_smp", [128, D])
    mem = P.inp("mem", [256, D])
    c_xk = P.inp("c_xk", [2, 256, 512])
    c_xv = P.inp("c_xv", [2, 256, 512])
    c_k = P.inp("c_k", [2048, 512])
    c_v = P.inp("c_v", [2048, 512])
    st_C = P.inp("st_C", [4, 128, 128])
    st_n = P.inp("st_n", [4, 128])
    st_m = P.inp("st_m", [4, 1])
    st_conv = P.inp("st_conv", [30, D])
    norm_g = P.inp("norm_g", [2, D])
    w_in_a = P.inp("w_in_a", [D, 5640])
    b_ig = P.inp("b_ig", [4, 1])
    b_fg = P.inp("b_fg", [4, 1])
    mlstm_g = P.inp("mlstm_g", [512])
    qn_g = P.inp("qn_g", [512])
    kn_g = P.inp("kn_g", [512])
    lamv = P.inp("lamv", [4, 64])
    subln_g = P.inp("subln_g", [512])
    w_out_a = P.inp("w_out_a", [1536, D])
    w_in_c = P.inp("w_in_c", [D, 4096])
    conv_wT = P.inp("conv_wT", [D, 31])
    conv_b = P.inp("conv_b", [D])
    ln_g = P.inp("ln_g", [D])
    ln_b = P.inp("ln_b", [D])
    w_out_c = P.inp("w_out_c", [1536, D])
    mem_norm_g = P.inp("mem_norm_g", [2, D])
    w_mem_kv = P.inp("w_mem_kv", [2, D, 1024])
    xq_g = P.inp("xq_g", [2, 512])
    xk_g = P.inp("xk_g", [2, 512])
    c_ident = P.inp("c_ident", [128, 128])
    c_bias = P.inp("c_bias", [128, 2, 4 * NDELTA])
    c_dmask = P.inp("c_dmask", [128, 2, 4, 128])
    c_amask = P.inp("c_amask", [128, 128])
    c_hsel = P.inp("c_hsel", [4, 4, 128])
    c_flag = P.inp("c_flag", [4, 1])

    y_own = P.outp("y_own", [N_OWN, D])
    y_smp = P.outp("y_smp", [32, D])
    o_pxk = P.outp("o_pxk", [2, 256, 512])
    o_pxv = P.outp("o_pxv", [2, 256, 512])
    o_pk = P.outp("o_pk", [N_OWN, 512])
    o_pv = P.outp("o_pv", [N_OWN, 512])
    o_pC = P.outp("o_pC", [4, 128, 128])
    o_pn = P.outp("o_pn", [4, 128])
    o_pm = P.outp("o_pm", [4, 1])
    o_pconv = P.outp("o_pconv", [30, D])
    o_sk = P.outp("o_sk", [32, 512])
    o_sv = P.outp("o_sv", [32, 512])
    o_sC = P.outp("o_sC", [4, 128, 128])
    o_sn = P.outp("o_sn", [4, 128])
    o_sm = P.outp("o_sm", [4, 1])
    o_sconv = P.outp("o_sconv", [30, D])

    wsrc32 = {"w_in_a": (w_in_a, [D, 5640]), "w_out_a": (w_out_a, [1536, D]), "w_in_c": (w_in_c, [D, 4096]),
              "w_out_c": (w_out_c, [1536, D]), "w_mem0": (w_mem_kv[0], [D, 1024]), "w_mem1": (w_mem_kv[1], [D, 1024])}
    wbf = {}
    wbf_b = {}
    for nm, (src32, shp) in wsrc32.items():
        wbf[nm] = nc.dram_tensor("bf_" + nm, shp, BF16, kind="Internal").ap()
        wbf_b[nm] = [Buf(f"wb_{nm}{i}") for i in range(shp[0] // 128)]

    def convert_weights(names):
        for nm in names:
            src32, shp = wsrc32[nm]
            for i in range(shp[0] // 128):
                dma("pool", wbf[nm][i * 128:(i + 1) * 128, :], src32[i * 128:(i + 1) * 128, :], w=[wbf_b[nm][i]])

    NBLK = max(NB_PRE + NB_OWN, 16)
    NHALF = (NBLK + 15) // 16
    scrK = nc.dram_tensor("scrK", [4, NHALF, 128, 2048], BF16, kind="Internal").ap()
    scrV = nc.dram_tensor("scrV", [4, NHALF, 128, 16 * 130], BF16, kind="Internal").ap()
    scr_b = [Buf(f"scr{i}") for i in range(NHALF * 16)]

    class PV_:
        def __init__(self, name, ap, excl=False):
            self.ap = ap
            self.b = Buf(name, excl)

        def __getitem__(self, k):
            return self.ap[k]
    pbS = [nc.alloc_psum_tensor(f"pbS{i}", [128, 1024], F32) for i in range(2)]
    pb = [PV_(f"pb{i}", pbS[i // 2][:, (i % 2) * 512:(i % 2 + 1) * 512], excl=True) for i in range(4)]
    pb += [P.ps(f"pb{i}", [128, 512], F32) for i in range(4, 7)]
    ptb_all = nc.alloc_psum_tensor("ptb_all", [128, 1024], BF16)
    ptb = [PV_("ptb0", ptb_all[:, 0:512]), PV_("ptb1", pb[5][:, 0:256].bitcast(BF16))]
    ptb[0].b.excl = True
    ptb[1].b = pb[5].b
    identf = P.sb("identf", [128, 128])
    identb = P.sb("identb", [128, 128], BF16)
    onesf = P.sb("onesf", [128, 128])
    epsc = P.sb("epsc", [128, 1])
    lnk = P.sb("lnk", [128, 1])
    ng = P.sb("ng", [128, 2, 8])
    mng = P.sb("mng", [128, 2, 8])
    g_mlstm = P.sb("g_mlstm", [128, 512])
    g_qn = P.sb("g_qn", [128, 512])
    g_kn = P.sb("g_kn", [128, 512])
    g_subln = P.sb("g_subln", [128, 512])
    g_xq = P.sb("g_xq", [128, 2, 512])
    g_xk = P.sb("g_xk", [128, 2, 512])
    lam_t = P.sb("lam_t", [128, 4, 64])
    lam_s = P.sb("lam_s", [128, 4])
    nlam = P.sb("nlam", [128, 1])
    big = P.sb("big", [4, 1])
    bfg = P.sb("bfg", [4, 1])
    flag = P.sb("flag", [4, 1])
    biasT = P.sb("biasT", [128, 2, 4 * NDELTA])
    dmask = P.sb("dmask", [128, 2, 4, 128])
    amask = P.sb("amask", [128, 128])
    hsel = P.sb("hsel", [4, 4, 128])
    wgate = P.sb("wgate", [128, 8, 8], BF16)
    cw = P.sb("cw", [128, 8, 31])
    cb = P.sb("cb", [128, 8])
    lg = P.sb("lg", [128, 8])
    lb = P.sb("lb", [128, 8])
    dummy = P.sb("dummyt", [128, 2])

    NW = 4
    wring = [P.sb(f"wring{i}", [128, 4096], BF16) for i in range(NW)]
    wctr = [0]
    NXS = 3
    xs = [P.sb(f"xs{i}", [128, D]) for i in range(NXS)]
    y0 = P.sb("y0", [128, 4, D])
    y0b = [Buf(f"y0_{j}") for j in range(4)]
    xnb = P.sb("xnb", [128, 4, D], BF16)
    xnT = P.sb("xnT", [128, 8, 512], BF16)
    sqs = P.sb("sqs", [128, D])
    st1 = P.sb("st1", [128, 8])
    st2 = P.sb("st2", [128, 8])
    tk0 = P.sb("tk0", [128, 512])
    tk1 = P.sb("tk1", [128, 512])
    tk2 = P.sb("tk2", [128, 512])
    tkrot = [tk0, tk1, tk2]
    tkrc = [0]
    tkb = P.sb("tkb", [128, 512], BF16)
    tkbs = [tkb, P.sb("tkb2", [128, 512], BF16)]
    tkc = [0]
    mixT = P.sb("mixT", [128, 12, 512], BF16)
    PTt = [nc.alloc_sbuf_tensor(f"PTt{i}", [128, 1024], BF16) for i in range(2)]
    PT = [[PV_(f"PT{i}{m}", PTt[i][:, m * 512:(m + 1) * 512]) for m in range(2)] for i in range(2)]
    pctr = [0]
    ob = P.sb("ob", [128, 4, 512])
    xqT = P.sb("xqT", [128, 4, 512], BF16)
    mKT = P.sb("mKT", [128, 2, 4, 256], BF16)
    mV = P.sb("mV", [128, 2, 2, 4, 130], BF16)
    e1 = P.sb("e1", [128, 512])
    Cst = P.sb("Cst", [128, 4, 130])
    Cbf = P.sb("Cbf", [128, 4, 130], BF16)
    mst = P.sb("mst", [4, 1])
    gtok = P.sb("gtok", [128, 4, 8])
    cA = P.sb("cA", [4, 8])
    cB = P.sb("cB", [4, 8])
    cM = P.sb("cM", [4, 9])
    cX = P.sb("cX", [4, 8])
    cD = P.sb("cD", [4, 8])
    cDx = P.sb("cDx", [4, 8, 4])
    decb = P.sb("decb", [128, 8, 4])
    den4 = P.sb("den4", [128, 4])
    rl = P.sb("rl", [128, 2])
    rl2 = [P.sb("rl2a", [128, 2]), P.sb("rl2b", [128, 2])]
    rl8s = [P.sb("rl8a", [128, 8]), P.sb("rl8b", [128, 8])]
    rl8c = [0]
    xoc = [0]
    halo = P.sb("halo", [128, 8, 30])

    ARW = 12288
    arena = nc.alloc_sbuf_tensor("arena", [128, ARW], F32)
    arena_tls = []

    class View:
        def __init__(self, name, ap):
            self.ap = ap
            self.b = Buf(name)
            arena_tls.append(self)

        def __getitem__(self, k):
            return self.ap[k]

    aoff = [0]

    def av(name, words, dt, pat=None, part=None, **kw):
        a = arena[:, aoff[0]:aoff[0] + words] if part is None else arena[0:part, aoff[0]:aoff[0] + words]
        aoff[0] += words
        assert aoff[0] <= ARW, (name, aoff[0])
        if dt == BF16:
            a = a.bitcast(BF16)
        if pat is not None:
            a = a.rearrange(pat, **kw)
        return View(name, a)

    aoff[0] = 0
    aqT = av("aqT", 1024, BF16, "p (h t) -> p h t", h=4)
    akT = av("akT", 1024, BF16, "p (h t) -> p h t", h=4)
    ga = av("ga", 2048, F32, "p (j c) -> p j c", j=4)
    Stil = av("Stil", 256, BF16, "p (h t) -> p h t", h=4)
    hraw = av("hraw", 512, F32)
    kw = av("kw", 1024, BF16, "p (j h d) -> p j h d", j=4, h=4)
    avx = av("avx", 1040, BF16, "p (j h e) -> p j h e", j=4, h=4)
    gI = av("gI", 512, F32, part=4)
    gF = av("gF", 512, F32, part=4)
    gB = av("gB", 512, F32, part=4)
    gA = av("gA", 512, F32, part=4)
    gZ = av("gZ", 512, F32, part=4)
    gWe = av("gWe", 512, F32, part=4)
    gTh = av("gTh", 512, F32, part=4)
    aoff[0] = 0
    KTo = av("KTo", 1024, BF16, "p (h t) -> p h t", h=4)
    Vo = av("Vo", 1040, BF16, "p (j h e) -> p j h e", j=4, h=4)
    assert aoff[0] <= 4864
    bqT = av("bqT", 1024, BF16, "p (h t) -> p h t", h=4)
    rK = [av(f"rK{i}", 1024, BF16) for i in range(2)]
    rV = [av(f"rV{i}", 1040, BF16, "p (b e) -> p b e", b=16) for i in range(2)]
    rctr = [0]
    aoff[0] = 0
    uT = av("uT", 2168, BF16, "p (c t) -> p c t", c=8)
    dgr = [av(f"dg{i}", 64, BF16) for i in range(8)]
    dgc = [0]
    cacc = av("cacc", 4096, F32, "p (c t) -> p c t", c=8)
    szT = av("szT", 2048, BF16, "p (c t) -> p c t", c=8)
    e2 = av("e2", 512, F32)
    lmean = av("lmean", 512, F32)
    lrstd = av("lrstd", 512, F32)

    def switch(phase):
        flush()
        I("pool", "memset", dummy[:, 0:1], 0.0, w=[dummy] + arena_tls)
        if phase in ("A", "AB"):
            I("pool", "memset", avx[:, :, :, 128:130], 1.0, w=[avx])
        if phase in ("B", "AB"):
            I("pool", "memset", Vo[:, :, :, 128:130], 1.0, w=[Vo])
        if phase == "B":
            for r_ in rV:
                I("pool", "memset", r_[:, :, 128:130], 1.0, w=[r_])

    dma("sp", identf[:], c_ident, w=[identf])
    I("dve", "tensor_copy", out=identb[:], in_=identf[:], r=[identf], w=[identb])
    I("pool", "memset", onesf[:], 1.0, w=[onesf])
    I("pool", "memset", epsc[:], EPS, w=[epsc])
    I("pool", "memset", lnk[:], math.log(128.0 ** -0.5), w=[lnk])
    for l in range(2):
        dma("sp", ng[:, l, :], norm_g[l].rearrange("(c p) -> p c", p=128), w=[ng], allow_slow_non_contiguous=True)
        dma("sp", mng[:, l, :], mem_norm_g[l].rearrange("(c p) -> p c", p=128), w=[mng], allow_slow_non_contiguous=True)
        dma("sp", g_xq[:, l, :], xq_g[l].partition_broadcast(128), w=[g_xq])
        dma("sp", g_xk[:, l, :], xk_g[l].partition_broadcast(128), w=[g_xk])
    dma("sp", g_mlstm[:], mlstm_g.partition_broadcast(128), w=[g_mlstm])
    dma("sp", g_qn[:], qn_g.partition_broadcast(128), w=[g_qn])
    dma("sp", g_kn[:], kn_g.partition_broadcast(128), w=[g_kn])
    dma("sp", g_subln[:], subln_g.partition_broadcast(128), w=[g_subln])
    dma("sp", lam_t[:].rearrange("p a d -> p (a d)"), lamv.rearrange("a d -> (a d)").partition_broadcast(128), w=[lam_t])
    dma("sp", big[:], b_ig, w=[big])
    dma("sp", bfg[:], b_fg, w=[bfg])
    dma("sp", flag[:], c_flag, w=[flag])
    dma("sp", biasT[:], c_bias, w=[biasT])
    dma("sp", dmask[:], c_dmask, w=[dmask])
    dma("sp", amask[:], c_amask, w=[amask])
    dma("sp", hsel[:], c_hsel, w=[hsel])
    convert_weights(["w_mem0", "w_in_a", "w_out_a", "w_mem1", "w_in_c", "w_out_c"])
    dma("pool", wgate[:], wbf["w_in_a"].rearrange("(c p) n -> p c n", p=128)[:, :, 2560:2568], r=wbf_b["w_in_a"], w=[wgate])
    dma("sp", cw[:], conv_wT.rearrange("(c p) j -> p c j", p=128), w=[cw])
    dma("sp", cb[:], conv_b.rearrange("(c p) -> p c", p=128), w=[cb], allow_slow_non_contiguous=True)
    dma("sp", lg[:], ln_g.rearrange("(c p) -> p c", p=128), w=[lg], allow_slow_non_contiguous=True)
    dma("sp", lb[:], ln_b.rearrange("(c p) -> p c", p=128), w=[lb], allow_slow_non_contiguous=True)
    I("dve", "tensor_scalar", out=g_subln[:], in0=g_subln[:], scalar1=1.0 - LAM_INIT0, scalar2=None, op0=ALU.mult, r=[g_subln], w=[g_subln])
    I("dve", "tensor_scalar", out=bfg[:], in0=bfg[:], scalar1=-1.0, scalar2=None, op0=ALU.mult, r=[bfg], w=[bfg])
    I("dve", "tensor_tensor", out=lam_t[:, 0, :], in0=lam_t[:, 0, :], in1=lam_t[:, 1, :], op=ALU.mult, r=[lam_t], w=[lam_t])
    I("dve", "tensor_tensor", out=lam_t[:, 2, :], in0=lam_t[:, 2, :], in1=lam_t[:, 3, :], op=ALU.mult, r=[lam_t], w=[lam_t])
    I("dve", "tensor_reduce", out=lam_s[:, 0:1], in_=lam_t[:, 0, :], axis=AX.X, op=ALU.add, r=[lam_t], w=[lam_s])
    I("dve", "tensor_reduce", out=lam_s[:, 1:2], in_=lam_t[:, 2, :], axis=AX.X, op=ALU.add, r=[lam_t], w=[lam_s])
    I("act", "activation", out=lam_s[:, 2:4], in_=lam_s[:, 0:2], func=AF.Exp, r=[lam_s], w=[lam_s])
    I("dve", "tensor_tensor", out=nlam[:], in0=lam_s[:, 3:4], in1=lam_s[:, 2:3], op=ALU.subtract, r=[lam_s], w=[nlam])
    I("dve", "tensor_scalar", out=nlam[:], in0=nlam[:], scalar1=-LAM_INIT0, scalar2=None, op0=ALU.add, r=[nlam], w=[nlam])
    I("pool", "memset", mV[:], 1.0, w=[mV])
    I("pool", "memset", halo[:], 0.0, w=[halo])

    def load_w(nm, c0, ncols, nkc):
        wt = wring[wctr[0] % NW]
        wctr[0] += 1
        v = wt[:, 0:nkc * ncols].rearrange("p (c n) -> p c n", c=nkc)
        dma("pool", v, wbf[nm].rearrange("(c p) n -> p c n", p=128)[:, :, c0:c0 + ncols], r=wbf_b[nm], w=[wt])
        return (wt, v)

    def make_xnT(nsb, gcol, src_fn, stats_done=False):
        flush()
        if not stats_done:
            xn_stats(nsb, src_fn)
        xn_T(nsb, gcol)

    def xn_stats(nsb, src_fn):
        for j in range(nsb):
            sap, stl = src_fn(j)
            I("act", "activation", out=sqs[:], in_=sap, func=AF.Square, accum_out=st1[:, j:j + 1], r=[stl], w=[sqs, st1])
            I("act", "activation", out=st2[:, j:j + 1], in_=st1[:, j:j + 1], func=AF.Ln, scale=1.0 / D, bias=epsc[:], r=[st1, epsc], w=[st2])
            I("act", "activation", out=st2[:, j:j + 1], in_=st2[:, j:j + 1], func=AF.Exp, scale=-0.5, r=[st2], w=[st2])
            I("dve", "tensor_scalar", out=xnb[:, j, :], in0=sap, scalar1=st2[:, j:j + 1], scalar2=None, op0=ALU.mult, r=[stl, st2], w=[xnb])
            if hasattr(src_fn, "release"):
                src_fn.release(j)

    def xn_T(nsb, gcol):
        N = 128 * nsb
        for kc in range(8):
            pt = ptb[kc % 2]
            for j in range(nsb):
                tr(pt[:, j * 128:(j + 1) * 128], xnb[:, j, kc * 128:(kc + 1) * 128], identb[:], r=[xnb, identb], w=[pt])
            if kc % 2 == 0:
                I("dve", "tensor_scalar", out=xnT[:, kc, 0:N], in0=pt[:, 0:N], scalar1=gcol[:, kc:kc + 1], scalar2=None, op0=ALU.mult, r=[pt, ng, mng], w=[xnT])
            else:
                I("act", "activation", out=xnT[:, kc, 0:N], in_=pt[:, 0:N], func=AF.Copy, scale=gcol[:, kc:kc + 1], r=[pt, ng, mng], w=[xnT])

    xplan = []
    for l_ in range(2 if do_l1 else 1):
        xplan += [(mem, 0), (mem, 128)]
    blk_ = 0
    for (n_, m_) in cfg["pre"]:
        xplan += [(x_pre, (blk_ + k_) * 128) for k_ in range(n_)]
        blk_ += n_
    blk_ = 0
    for n_ in cfg["own"]:
        xplan += [(x_own, (blk_ + k_) * 128) for k_ in range(n_)]
        blk_ += n_
    if do_smp:
        xplan += [(x_smp, 0)]
    xstate = {"issued": 0, "next": 0}

    def x_issue():
        i = xstate["issued"]
        if i < len(xplan):
            src_, row_ = xplan[i]
            t = xs[i % NXS]
            dma("sp", t[:], src_[row_:row_ + 128, :], w=[t])
            xstate["issued"] = i + 1

    def dram_src(src, row0):
        def fn(j):
            i = xstate["next"]
            assert xplan[i][0] is src and xplan[i][1] == row0 + j * 128, (i, row0, j)
            while xstate["issued"] <= i:
                x_issue()
            xstate["next"] = i + 1
            t = xs[i % NXS]
            return t[:], t
        fn.release = lambda j: x_issue()
        return fn

    def proj_tok(w, j, ncols, out_ps, nkc=8, coff=0):
        wt, wv = w
        for kc in range(nkc):
            mm(out_ps[:, 0:ncols], xnT[:, kc, j * 128:(j + 1) * 128], wv[:, kc, coff:coff + ncols], kc == 0, kc == nkc - 1, r=[xnT, wt], w=[out_ps])

    def proj_feat(w, c, N, out_ps, nkc=8):
        wt, wv = w
        for kc in range(nkc):
            mm(out_ps[:, 0:N], wv[:, kc, c * 128:(c + 1) * 128], xnT[:, kc, 0:N], kc == 0, kc == nkc - 1, r=[xnT, wt], w=[out_ps])

    def group_norm(src_ap, src_tl, ng_, gs, gain_ap, gain_tl, dst_ap, dst_tl):
        n = ng_ * gs
        I("act", "activation", out=sqs[:, 0:n], in_=src_ap, func=AF.Square, r=[src_tl], w=[sqs])
        I("dve", "tensor_reduce", out=st1[:, 0:ng_], in_=sqs[:, 0:n].rearrange("p (g d) -> p g d", g=ng_), axis=AX.X, op=ALU.add, r=[sqs], w=[st1])
        I("act", "activation", out=st2[:, 0:ng_], in_=st1[:, 0:ng_], func=AF.Ln, scale=1.0 / gs, bias=epsc[:], r=[st1, epsc], w=[st2])
        I("act", "activation", out=st2[:, 0:ng_], in_=st2[:, 0:ng_], func=AF.Exp, scale=-0.5, r=[st2], w=[st2])
        I("dve", "tensor_tensor", out=dst_ap.rearrange("p (g d) -> p g d", g=ng_), in0=src_ap.rearrange("p (g d) -> p g d", g=ng_),
          in1=bc(st2[:, 0:ng_].unsqueeze(2), [128, ng_, gs]), op=ALU.mult, r=[src_tl, st2], w=[dst_tl])
        I("dve", "tensor_tensor", out=dst_ap, in0=dst_ap, in1=gain_ap, op=ALU.mult, r=[dst_tl, gain_tl], w=[dst_tl])

    def silu_gate(ps_tl, ncols, dst_ap, dst_tl, mul_ap, mul_tl):
        I("act", "activation", out=e1[:, 0:ncols], in_=ps_tl[:, 0:ncols], func=AF.Silu, r=[ps_tl], w=[e1])
        I("dve", "tensor_tensor", out=dst_ap, in0=e1[:, 0:ncols], in1=mul_ap, op=ALU.mult, r=[e1, mul_tl], w=[dst_tl])

    pending = []

    def flush(keep=0):
        n = len(pending) - keep
        if n <= 0:
            return
        fs = pending[:n]
        del pending[:n]
        for f_ in fs:
            f_()

    def tok_to_T(src_ap, src_tl, dstT, j, dst_tl):
        flush(keep=1)
        tb = tkbs[tkc[0] % 2]
        tkc[0] += 1
        I("dve", "tensor_copy", out=tb[:], in_=src_ap, r=[src_tl], w=[tb])

        def later():
            pt = ptb[j % 2]
            for c in range(4):
                tr(pt[:, c * 128:(c + 1) * 128], tb[:, c * 128:(c + 1) * 128], identb[:], r=[tb, identb], w=[pt])
            I("act", "activation", out=dstT, in_=pt[:, 0:512].rearrange("p (c t) -> p c t", c=4), func=AF.Copy, r=[pt], w=[dst_tl])
        pending.append(later)

    def to_mixT(src_ap, src_tl, j, c0):
        tok_to_T(src_ap, src_tl, mixT[:, c0:c0 + 4, j * 128:(j + 1) * 128], j, mixT)

    def mem_kv(l):
        make_xnT(2, mng[:, l, :], dram_src(mem, 0))
        wk = load_w(f"w_mem{l}", 0, 512, 8)
        wv = load_w(f"w_mem{l}", 512, 512, 8)
        for j in range(2):
            proj_tok(wk, j, 512, pb[0])
            group_norm(pb[0][:, :], pb[0], 4, 128, g_xk[:, l, :], g_xk, tk0[:], tk0)
            dma("sp", o_pxk[l, j * 128:(j + 1) * 128, :], tk0[:], r=[tk0], final=True)
            tok_to_T(tk0[:], tk0, mKT[:, l, :, j * 128:(j + 1) * 128], j, mKT)
            proj_tok(wv, j, 512, pb[1])
            I("dve", "tensor_copy", out=tk1[:], in_=pb[1][:, :], r=[pb[1]], w=[tk1])
            I("act", "activation", out=mV[:, l, j, :, 0:128], in_=pb[1][:, :].rearrange("p (h d) -> p h d", h=4), func=AF.Copy, r=[pb[1]], w=[mV])
            dma("sp", o_pxv[l, j * 128:(j + 1) * 128, :], tk1[:], r=[tk1], final=True)

    def smp_mem_kv():
        for l in range(2):
            for j in range(2):
                dma("sp", tk0[:], c_xk[l, j * 128:(j + 1) * 128, :], w=[tk0])
                tok_to_T(tk0[:], tk0, mKT[:, l, :, j * 128:(j + 1) * 128], j, mKT)
                dma("pool", mV[:, l, j, :, 0:128], c_xv[l, j * 128:(j + 1) * 128, :].rearrange("p (h d) -> p h d", h=4), w=[mV])

    def x_q(l, nsb, wsrc, cq):
        wq = load_w(wsrc, cq, 512, 8)
        for j in range(nsb):
            tkq = tkrot[tkrc[0] % 3]
            tkrc[0] += 1
            proj_tok(wq, j, 512, pb[j % 4])
            group_norm(pb[j % 4][:, :], pb[j % 4], 4, 128, g_xq[:, l, :], g_xq, tkq[:], tkq)
            tok_to_T(tkq[:], tkq, xqT[:, :, j * 128:(j + 1) * 128], j, xqT)

    def x_branch(l, nsb, wsrc, cq, cz, c0, q_done=False):
        N = 128 * nsb
        if not q_done:
            x_q(l, nsb, wsrc, cq)
        wz = load_w(wsrc, cz, 512, 8)
        flush()
        def x_scores(h):
            pts = PT[pctr[0] % 2]
            pctr[0] += 1
            for mb in range(2):
                sp = pb[2 * (h % 2) + mb]
                mm(sp[:, 0:N], mKT[:, l, h, mb * 128:(mb + 1) * 128], xqT[:, h, 0:N], True, True, r=[mKT, xqT], w=[sp])
                I("act", "activation", out=pts[mb][:, 0:N], in_=sp[:, 0:N], func=AF.Exp, scale=128.0 ** -0.5, r=[sp], w=[pts[mb]])
            return pts

        def x_pv(h, pts):
            for j in range(nsb):
                oa = pb[4 + (xoc[0] % 3)]
                xoc[0] += 1
                for mb in range(2):
                    mm(oa[:, 0:129], pts[mb][:, j * 128:(j + 1) * 128], mV[:, l, mb, h, 0:129], mb == 0, mb == 1, r=[pts[mb], mV], w=[oa])
                rr = rl2[xoc[0] % 2]
                I("dve", "reciprocal", out=rr[:, 0:1], in_=oa[:, 128:129], r=[oa], w=[rr])
                I("dve", "tensor_scalar", out=ob[:, j, h * 128:(h + 1) * 128], in0=oa[:, 0:128], scalar1=rr[:, 0:1], scalar2=None, op0=ALU.mult, r=[oa, rr], w=[ob])

        prev_ = None
        for h in range(4):
            pts_h = x_scores(h)
            if prev_ is not None:
                x_pv(*prev_)
            prev_ = (h, pts_h)
        x_pv(*prev_)
        for j in range(nsb):
            proj_tok(wz, j, 512, pb[j % 4])
            silu_gate(pb[j % 4], 512, tk0[:], tk0, ob[:, j, :], ob)
            to_mixT(tk0[:], tk0, j, c0)

    def a_gates(nsb, Lc, nch_per_sb):
        flush()
        N = 128 * nsb
        nch = nsb * nch_per_sb
        CS = 128 // nch_per_sb
        for kc in range(8):
            mm(pb[0][0:4, 0:N], wgate[:, kc, 0:4], xnT[:, kc, 0:N], kc == 0, kc == 7, r=[xnT, wgate], w=[pb[0]])
        for kc in range(8):
            mm(pb[1][0:4, 0:N], wgate[:, kc, 4:8], xnT[:, kc, 0:N], kc == 0, kc == 7, r=[xnT, wgate], w=[pb[1]])
        I("dve", "tensor_scalar", out=gI[:, 0:N], in0=pb[0][0:4, 0:N], scalar1=big[:, 0:1], scalar2=None, op0=ALU.add, r=[pb[0], big], w=[gI])
        I("act", "activation", out=gF[:, 0:N], in_=pb[1][0:4, 0:N], func=AF.Exp, scale=-1.0, bias=bfg[:, 0:1], r=[pb[1], bfg], w=[gF])
        I("act", "activation", out=gF[:, 0:N], in_=gF[:, 0:N], func=AF.Ln, scale=1.0, bias=onesf[0:4, 0:1], r=[gF, onesf], w=[gF])
        I("pool", "memset", gZ[:, 0:N], 0.0, w=[gZ])
        I("pool", "memset", gB[:, 0:N], 0.0, w=[gB])
        for c in range(nch):
            sl = slice(c * CS, c * CS + Lc)
            I("dve", "tensor_tensor_scan", out=gB[:, sl], data0=gF[:, sl], data1=gZ[:, sl], initial=0.0, op0=ALU.add, op1=ALU.add, r=[gF, gZ], w=[gB])
        I("dve", "tensor_tensor", out=gA[:, 0:N], in0=gI[:, 0:N], in1=gB[:, 0:N], op=ALU.add, r=[gI, gB], w=[gA])
        gA3 = gA[:, 0:N].rearrange("p (c t) -> p c t", t=CS)
        gB3 = gB[:, 0:N].rearrange("p (c t) -> p c t", t=CS)
        I("dve", "tensor_reduce", out=cA[:, 0:nch], in_=gA3[:, :, 0:Lc], axis=AX.X, op=ALU.max, r=[gA], w=[cA])
        I("dve", "tensor_scalar", out=cB[:, 0:nch], in0=gB3[:, :, Lc - 1], scalar1=-1.0, scalar2=None, op0=ALU.mult, r=[gB], w=[cB])
        I("dve", "tensor_copy", out=cM[:, 0:1], in_=mst[:], r=[mst], w=[cM])
        I("dve", "tensor_tensor_scan", out=cM[:, 1:nch + 1], data0=cA[:, 0:nch], data1=cB[:, 0:nch], initial=mst[:, 0:1], op0=ALU.max, op1=ALU.add, r=[cA, cB, mst], w=[cM])
        I("dve", "tensor_copy", out=mst[:], in_=cM[:, nch:nch + 1], r=[cM], w=[mst])
        I("dve", "tensor_tensor", out=cX[:, 0:nch], in0=cM[:, 0:nch], in1=cA[:, 0:nch], op=ALU.max, r=[cM, cA], w=[cX])
        I("dve", "tensor_tensor", out=cD[:, 0:nch], in0=cM[:, 0:nch], in1=cX[:, 0:nch], op=ALU.subtract, r=[cM, cX], w=[cD])
        I("act", "activation", out=cD[:, 0:nch], in_=cD[:, 0:nch], func=AF.Exp, r=[cD], w=[cD])
        Xb = bc(cX[:, 0:nch].unsqueeze(2), [4, nch, CS])
        I("dve", "tensor_tensor", out=gA3, in0=gA3, in1=Xb, op=ALU.subtract, r=[gA, cX], w=[gA])
        I("act", "activation", out=gWe[:, 0:N], in_=gA[:, 0:N], func=AF.Exp, bias=lnk[0:4, 0:1], r=[gA, lnk], w=[gWe])
        I("dve", "tensor_tensor", out=gI[:, 0:N].rearrange("p (c t) -> p c t", t=CS), in0=gB3, in1=Xb, op=ALU.subtract, r=[gB, cX], w=[gI])
        I("act", "activation", out=gTh[:, 0:N], in_=gI[:, 0:N], func=AF.Exp, r=[gI], w=[gTh])
        for j in range(nsb):
            tr(pb[2][:, 0:4], gWe[:, j * 128:(j + 1) * 128], identf[0:4, 0:4], r=[gWe, identf], w=[pb[2]])
            tr(pb[2][:, 4:8], gTh[:, j * 128:(j + 1) * 128], identf[0:4, 0:4], r=[gTh, identf], w=[pb[2]])
            I("dve", "tensor_copy", out=gtok[:, j, :], in_=pb[2][:, 0:8], r=[pb[2]], w=[gtok])
        I("dve", "tensor_tensor", out=cDx[:, 0:nch, :], in0=bc(cD[:, 0:nch].unsqueeze(2), [4, nch, 4]),
          in1=hsel[:, :, 0:nch].rearrange("k h c -> k c h"), op=ALU.mult, r=[cD, hsel], w=[cDx])
        mm(pb[2][:, 0:nch * 4], onesf[0:4, :], cDx[:, 0:nch, :].rearrange("k c h -> k (c h)"), True, True, r=[onesf, cDx], w=[pb[2]])
        I("dve", "tensor_copy", out=decb[:, 0:nch, :], in_=pb[2][:, 0:nch * 4].rearrange("p (c h) -> p c h", h=4), r=[pb[2]], w=[decb])

    def evac(h, out_ap, in_ap, r, w):
        if h % 2:
            I("act", "activation", out=out_ap, in_=in_ap, func=AF.Copy, r=r, w=w)
        else:
            I("dve", "tensor_copy", out=out_ap, in_=in_ap, r=r, w=w)

    def a_branch(nsb, mode, wsrc, Lc=64, nch_per_sb=2, state_later=False):
        N = 128 * nsb
        full = mode != "pre"
        CS = 128 // nch_per_sb
        if full:
            wq = load_w(wsrc, 0, 512, 8)
            for h in range(4):
                proj_feat(wq, h, N, pb[h % 4])
                evac(h, aqT[:, h, 0:N], pb[h % 4][:, 0:N], [pb[h % 4]], [aqT])
        wk = load_w(wsrc, 512, 512, 8)
        a_gates(nsb, Lc, nch_per_sb)
        if full:
            for h in range(4):
                proj_feat(wk, h, N, pb[h % 4])
                evac(h, akT[:, h, 0:N], pb[h % 4][:, 0:N], [pb[h % 4]], [akT])
        for j in range(nsb):
            proj_tok(wk, j, 512, pb[j % 4])
            I("dve", "tensor_tensor", out=kw[:, j, :, :], in0=pb[j % 4][:, :].rearrange("p (h d) -> p h d", h=4),
              in1=bc(gtok[:, j, 0:4].unsqueeze(2), [128, 4, 128]), op=ALU.mult, r=[pb[j % 4], gtok], w=[kw])
        wv = load_w(wsrc, 1024, 512, 8)
        for j in range(nsb):
            proj_tok(wv, j, 512, pb[j % 4])
            I("act", "activation", out=avx[:, j, :, 0:128], in_=pb[j % 4][:, :].rearrange("p (h d) -> p h d", h=4), func=AF.Copy, r=[pb[j % 4]], w=[avx])
        if full:
            wo = load_w(wsrc, 1536, 512, 8)
            for j in range(nsb):
                proj_tok(wo, j, 512, pb[j % 4])
                I("act", "activation", out=ga[:, j, :], in_=pb[j % 4][:, :], func=AF.Sigmoid, r=[pb[j % 4]], w=[ga])
            wz = load_w(wsrc, 2048, 512, 8)
            for j in range(nsb):
                proj_tok(wz, j, 512, pb[j % 4])
                silu_gate(pb[j % 4], 512, ga[:, j, :], ga, ga[:, j, :], ga)
        if state_later:
            def later_state():
                k_ = 0
                for j in range(nsb):
                    for cc in range(nch_per_sb):
                        rows = slice(cc * CS, cc * CS + CS)
                        c = j * nch_per_sb + cc
                        ups = (pb[6], pb[3]) if k_ % 2 == 0 else (pb[4], pb[5])
                        k_ += 1
                        I("dve", "tensor_tensor", out=Cst[:], in0=Cst[:], in1=bc(decb[:, c, :].unsqueeze(2), [128, 4, 130]), op=ALU.mult, r=[Cst, decb], w=[Cst])
                        for h in range(4):
                            up = ups[h // 2]
                            hh = h % 2
                            mm(up[:, hh * 129:(hh + 1) * 129], kw[rows, j, h, :], avx[rows, j, h, 0:129], True, True, r=[kw, avx], w=[up])
                        for hp in range(2):
                            I("dve", "tensor_tensor", out=Cst[:, 2 * hp:2 * hp + 2, 0:129], in0=Cst[:, 2 * hp:2 * hp + 2, 0:129],
                              in1=ups[hp][:, 0:258].rearrange("p (h e) -> p h e", h=2), op=ALU.add, r=[Cst, ups[hp]], w=[Cst])
            return later_state
        for j in range(nsb):
            if full:
                for h in range(4):
                    mm(pb[0][:, h * 128:(h + 1) * 128], akT[:, h, j * 128:(j + 1) * 128], aqT[:, h, j * 128:(j + 1) * 128], True, True, r=[akT, aqT], w=[pb[0]])
                for h in range(4):
                    I("dve", "scalar_tensor_tensor", out=Stil[:, h, :], in0=pb[0][:, h * 128:(h + 1) * 128], scalar=gtok[:, j, h:h + 1], in1=amask[:],
                      op0=ALU.mult, op1=ALU.mult, r=[pb[0], gtok, amask], w=[Stil])
            for cc in range(nch_per_sb):
                rows = slice(cc * CS, cc * CS + CS)
                c = j * nch_per_sb + cc
                t0 = j * 128 + cc * CS
                if full:
                    I("dve", "tensor_tensor", out=Cbf[:], in0=Cst[:], in1=bc(decb[:, c, :].unsqueeze(2), [128, 4, 130]), op=ALU.mult, r=[Cst, decb], w=[Cbf])
                I("dve", "tensor_tensor", out=Cst[:], in0=Cst[:], in1=bc(decb[:, c, :].unsqueeze(2), [128, 4, 130]), op=ALU.mult, r=[Cst, decb], w=[Cst])
                if full:
                    for h in range(4):
                        ac = pb[4] if h < 2 else pb[5]
                        hh = h % 2
                        mm(ac[rows, hh * 129:(hh + 1) * 129], aqT[:, h, t0:t0 + CS], Cbf[:, h, 0:129], True, False, r=[aqT, Cbf], w=[ac])
                        mm(ac[rows, hh * 129:(hh + 1) * 129], Stil[:, h, cc * CS:cc * CS + CS], avx[:, j, h, 0:129], False, True, r=[Stil, avx], w=[ac])
                ups = (pb[6], pb[3]) if c % 2 == 0 else (pb[1], pb[2])
                for h in range(4):
                    up = ups[h // 2]
                    hh = h % 2
                    mm(up[:, hh * 129:(hh + 1) * 129], kw[rows, j, h, :], avx[rows, j, h, 0:129], True, True, r=[kw, avx], w=[up])
                for hp in range(2):
                    I("dve", "tensor_tensor", out=Cst[:, 2 * hp:2 * hp + 2, 0:129], in0=Cst[:, 2 * hp:2 * hp + 2, 0:129],
                      in1=ups[hp][:, 0:258].rearrange("p (h e) -> p h e", h=2), op=ALU.add, r=[Cst, ups[hp]], w=[Cst])
            if full:
                for hp in range(2):
                    ac = pb[4 + hp]
                    I("act", "activation", out=den4[:, 2 * hp:2 * hp + 2], in_=ac[:, 0:258].rearrange("p (h e) -> p h e", h=2)[:, :, 128], func=AF.Abs, r=[ac], w=[den4])
                I("dve", "tensor_tensor", out=den4[:], in0=den4[:], in1=gtok[:, j, 4:8], op=ALU.max, r=[den4, gtok], w=[den4])
                I("dve", "reciprocal", out=den4[:], in_=den4[:], r=[den4], w=[den4])
                for hp in range(2):
                    ac = pb[4 + hp]
                    I("dve", "tensor_tensor", out=hraw[:, hp * 256:(hp + 1) * 256].rearrange("p (h e) -> p h e", h=2),
                      in0=ac[:, 0:258].rearrange("p (h e) -> p h e", h=2)[:, :, 0:128],
                      in1=bc(den4[:, 2 * hp:2 * hp + 2].unsqueeze(2), [128, 2, 128]), op=ALU.mult, r=[ac, den4], w=[hraw])
                group_norm(hraw[:], hraw, 4, 128, g_mlstm[:], g_mlstm, tk0[:], tk0)
                I("dve", "tensor_tensor", out=tk0[:], in0=tk0[:], in1=ga[:, j, :], op=ALU.mult, r=[tk0, ga], w=[tk0])
                to_mixT(tk0[:], tk0, j, 0)

    def scr_write(j, blk):
        flush()
        hf, bi = blk // 16, blk % 16
        dma("sp", scrK[:, hf, :, bi * 128:(bi + 1) * 128].rearrange("h p k -> p h k"), KTo[:, :, j * 128:(j + 1) * 128], r=[KTo], w=[scr_b[blk]])
        dma("sp", scrV[:, hf, :, bi * 130:(bi + 1) * 130].rearrange("h p e -> p h e"), Vo[:, j, :, :], r=[Vo], w=[scr_b[blk]])

    def b_kv(nsb, wsrc, blk0, out_row0, smp=False):
        wk = load_w(wsrc, 3080, 512, 8)
        for j in range(nsb):
            proj_tok(wk, j, 512, pb[j % 4])
            tkr = tk0 if j % 2 == 0 else tk2
            group_norm(pb[j % 4][:, :], pb[j % 4], 8, 64, g_kn[:], g_kn, tkr[:], tkr)
            if out_row0 is not None:
                if smp:
                    dma("sp", o_sk[0:32, :], tkr[0:32, :], r=[tkr], final=True)
                else:
                    dma("sp", o_pk[out_row0 + j * 128:out_row0 + (j + 1) * 128, :], tkr[:], r=[tkr], final=True)
            tok_to_T(tkr[:], tkr, KTo[:, :, j * 128:(j + 1) * 128], j, KTo)
        wv = load_w(wsrc, 3592, 512, 8)
        for j in range(nsb):
            proj_tok(wv, j, 512, pb[j % 4])
            if out_row0 is not None:
                I("dve", "tensor_copy", out=tk1[:], in_=pb[j % 4][:, :], r=[pb[j % 4]], w=[tk1])
                if smp:
                    dma("sp", o_sv[0:32, :], tk1[0:32, :], r=[tk1], final=True)
                else:
                    dma("sp", o_pv[out_row0 + j * 128:out_row0 + (j + 1) * 128, :], tk1[:], r=[tk1], final=True)
            I("act", "activation", out=Vo[:, j, :, 0:128], in_=pb[j % 4][:, :].rearrange("p (h d) -> p h d", h=4), func=AF.Copy, r=[pb[j % 4]], w=[Vo])
        if blk0 is not None:
            for j in range(nsb):
                scr_write(j, blk0 + j)

    def b_attn(nsb, wsrc, tile_blk, n_masked, diag_kind):
        N = 128 * nsb
        wq = load_w(wsrc, 2568, 512, 8)
        for j in range(nsb):
            proj_tok(wq, j, 512, pb[j % 4])
            group_norm(pb[j % 4][:, :], pb[j % 4], 8, 64, g_qn[:], g_qn, tk0[:], tk0)
            tok_to_T(tk0[:], tk0, bqT[:, :, j * 128:(j + 1) * 128], j, bqT)
        wz = load_w(wsrc, 4104, 512, 8)
        flush()
        items = []
        for h in range(4):
            nsub = NSUBH[h] if nsb == 4 else nsb
            gsz = nsb // nsub
            blocks = []
            ring_loads = []
            for hf in range((tile_blk + 15) // 16):
                nb_h = min(16, tile_blk - hf * 16)
                ring_loads.append((hf, nb_h))
            items.append(("head", h, nsub, gsz, ring_loads))

        def emit_qk(it):
            (h, nsub, gsz, tbl, kap, vap, kdep, vdep, eb, dj, bidx, pts, sps, bufidx) = it
            q0 = 0 if dj is None else dj
            for m in range(2):
                mm(sps[m][:, q0 * 128:N], kap[m * 64:(m + 1) * 64, :], bqT[m * 64:(m + 1) * 64, h, q0 * 128:N], True, True, r=[kdep, bqT], w=[sps[m]])
            if dj is not None:
                for m in range(2):
                    I("dve", "tensor_tensor", out=sps[m][:, dj * 128:(dj + 1) * 128], in0=sps[m][:, dj * 128:(dj + 1) * 128],
                      in1=dmask[:, diag_kind, h, :], op=ALU.add, r=[sps[m], dmask], w=[sps[m]])
            pi_ = bufidx
            for g in range(nsub):
                qa, qe = max(g * gsz, q0), (g + 1) * gsz
                if qe <= qa:
                    continue
                delta = (tile_blk + g * gsz) - eb
                col = h * NDELTA + delta + DOFF
                I("act", "activation", out=PTt[pi_][:, :].rearrange("p (m c) -> p m c", m=2)[:, :, qa * 128:qe * 128],
                  in_=pbS[pi_][:, :].rearrange("p (m c) -> p m c", m=2)[:, :, qa * 128:qe * 128], func=AF.Exp, scale=0.125,
                  bias=biasT[:, tbl, col:col + 1], r=[sps[0], sps[1], biasT], w=[pts[0], pts[1]])

        def emit_pv(it):
            (h, nsub, gsz, tbl, kap, vap, kdep, vdep, eb, dj, bidx, pts, sps, bufidx) = it
            q0 = 0 if dj is None else dj
            for qs in range(q0, nsb):
                for m in range(2):
                    rg = qs * 2 + m
                    oa = pb[4 + rg // 3]
                    o0 = (rg % 3) * 129
                    last = (dj is not None and dj == qs)
                    mm(oa[:, o0:o0 + 129], pts[m][:, qs * 128:(qs + 1) * 128], vap, (bidx == 0 and rg % 3 == 0), last, r=[pts[m], vdep], w=[oa],
                       skip_group_check=True)

        def emit_epilogue(h):
            nreg = 2 * nsb
            rl8 = rl8s[rl8c[0] % 2]
            rl8c[0] += 1
            for bk in range((nreg + 2) // 3):
                nr = min(3, nreg - 3 * bk)
                oa = pb[4 + bk]
                I("dve", "reciprocal", out=rl8[:, 3 * bk:3 * bk + nr], in_=oa[:, 0:nr * 129].rearrange("p (r e) -> p r e", e=129)[:, :, 128], r=[oa], w=[rl8])
            I("dve", "tensor_tensor", out=rl8[:, 0:nreg].rearrange("p (q m) -> p q m", m=2)[:, :, 1], in0=rl8[:, 0:nreg].rearrange("p (q m) -> p q m", m=2)[:, :, 1],
              in1=bc(nlam[:, 0:1], [128, nsb]), op=ALU.mult, r=[rl8, nlam], w=[rl8])
            for qs in range(nsb):
                r0, r1 = qs * 2, qs * 2 + 1
                oa0, oa1 = pb[4 + r0 // 3], pb[4 + r1 // 3]
                a0, a1 = (r0 % 3) * 129, (r1 % 3) * 129
                I("dve", "tensor_scalar", out=ob[:, qs, h * 128:(h + 1) * 128], in0=oa0[:, a0:a0 + 128], scalar1=rl8[:, r0:r0 + 1], scalar2=None, op0=ALU.mult, r=[oa0, rl8], w=[ob])
                I("dve", "scalar_tensor_tensor", out=ob[:, qs, h * 128:(h + 1) * 128], in0=oa1[:, a1:a1 + 128], scalar=rl8[:, r1:r1 + 1], in1=ob[:, qs, h * 128:(h + 1) * 128],
                  op0=ALU.mult, op1=ALU.add, r=[oa1, rl8, ob], w=[ob])

        prev = None
        for (_, h, nsub, gsz, ring_loads) in items:
            blocks = []
            first_eb = 0
            for eb_ in range(tile_blk):
                if SLOPES[h] * (128 * (tile_blk - eb_) - 127) > 56.0:
                    first_eb = eb_ + 1
            for (hf, nb_h) in ring_loads:
                b0 = max(0, first_eb - hf * 16)
                if b0 >= nb_h:
                    continue
                slot = rctr[0] % 2
                rctr[0] += 1
                deps = [scr_b[hf * 16 + bi] for bi in range(b0, nb_h)]
                dma("sp", rK[slot][:, b0 * 128:nb_h * 128], scrK[h, hf, :, b0 * 128:nb_h * 128], r=deps, w=[rK[slot]])
                dma("sp", rV[slot][:, b0:nb_h, :], scrV[h, hf, :, b0 * 130:nb_h * 130].rearrange("p (b e) -> p b e", e=130), r=deps, w=[rV[slot]])
                for bi in range(b0, nb_h):
                    eb = hf * 16 + bi
                    blocks.append((1 if eb < n_masked else 0, rK[slot][:, bi * 128:(bi + 1) * 128], rV[slot][:, bi, 0:129], rK[slot], rV[slot], eb, None))
            for dj in range(nsb):
                blocks.append((0, KTo[:, h, dj * 128:(dj + 1) * 128], Vo[:, dj, h, 0:129], KTo, Vo, tile_blk + dj, dj))
            for bidx, (tbl, kap, vap, kdep, vdep, eb, dj) in enumerate(blocks):
                bufidx = pctr[0] % 2
                pts = PT[bufidx]
                sps = [pb[bufidx * 2 + m] for m in range(2)]
                pctr[0] += 1
                it = (h, nsub, gsz, tbl, kap, vap, kdep, vdep, eb, dj, bidx, pts, sps, bufidx)
                emit_qk(it)
                if prev is not None:
                    emit_pv(prev)
                    if prev[0] != h:
                        emit_epilogue(prev[0])
                prev = it
        emit_pv(prev)
        emit_epilogue(prev[0])
        for j in range(nsb):
            group_norm(ob[:, j, :], ob, 4, 128, g_subln[:], g_subln, tk1[:], tk1)
            proj_tok(wz, j, 512, pb[j % 4])
            silu_gate(pb[j % 4], 512, tk0[:], tk0, tk1[:], tk1)
            to_mixT(tk0[:], tk0, j, 4)

    def out_proj(nsb, wsrc, res_fn, dst_fn):
        flush()
        wos = [[load_w(wsrc, cg * 512 + hh * 256, 256, 12) for hh in range(2)] for cg in range(2)]
        k_ = 0
        for j in range(nsb):
            for cg in range(2):
                woh = wos[cg]
                ps_ = pb[k_ % 4]
                k_ += 1
                rap, rtl = res_fn(j, cg)
                for hh in range(2):
                    wt, wv = woh[hh]
                    for kc in range(12):
                        mm(ps_[:, hh * 256:(hh + 1) * 256], mixT[:, kc, j * 128:(j + 1) * 128], wv[:, kc, :], kc == 0, kc == 11, r=[mixT, wt], w=[ps_])
                dap, dtl = dst_fn(j, cg)
                I("dve", "tensor_tensor", out=dap, in0=ps_[:, :], in1=rap, op=ALU.add, r=[ps_, rtl], w=[dtl])
                dst_done(j, cg)

    dst_done_fn = [lambda j, cg: None]

    def dst_done(j, cg):
        dst_done_fn[0](j, cg)

    def conv_out(dst, t0, t1):
        flush()
        j = (t1 - 1) // 128
        for c in range(8):
            tr(pb[4 + c // 4][:, (c % 4) * 128:(c % 4 + 1) * 128], cacc[:, c, 0:128], identf[:], r=[cacc, identf], w=[pb[4 + c // 4]])
        for hp in range(2):
            I("dve", "tensor_copy", out=sqs[:, hp * 512:(hp + 1) * 512], in_=pb[4 + hp][:, :], r=[pb[4 + hp]], w=[sqs])
        r0 = t0 - j * 128
        dma("sp", dst, sqs[r0:r0 + 30, :], r=[sqs], final=True)

    def layer1(nsb, y_dst, nrows, halo_only=False, smp=False, pconv=False, before_out=None):
        N = 128 * nsb
        make_xnT(nsb, ng[:, 1, :], lambda j: (y0[:, j, :], y0b[j]))
        switch("L")
        I("dve", "tensor_copy", out=uT[:, :, 0:30], in_=halo[:], r=[halo], w=[uT])
        for c2 in range(4):
            wu = load_w("w_in_c", c2 * 256, 256, 8)
            wg = load_w("w_in_c", 1024 + c2 * 256, 256, 8)
            for cc in range(2):
                c = c2 * 2 + cc
                pu, pg = pb[2 * (c % 2)], pb[2 * (c % 2) + 1]
                proj_feat(wu, cc, N, pu)
                proj_feat(wg, cc, N, pg)
                I("act", "activation", out=e2[:, 0:N], in_=pg[:, 0:N], func=AF.Sigmoid, r=[pg], w=[e2])
                I("dve", "tensor_tensor", out=uT[:, c, 30:30 + N], in0=pu[:, 0:N], in1=e2[:, 0:N], op=ALU.mult, r=[pu, e2], w=[uT])
                if smp or pconv:
                    I("dve", "tensor_tensor", out=cacc[:, c, 0:128], in0=pu[:, N - 128:N], in1=e2[:, N - 128:N], op=ALU.mult, r=[pu, e2], w=[cacc])
        if smp:
            conv_out(o_sconv, 2, 32)
        if pconv:
            conv_out(o_pconv, N - 30, N)
        I("dve", "tensor_copy", out=halo[:], in_=uT[:, :, N:N + 30], r=[uT], w=[halo])
        if halo_only:
            return
        x_q(1, nsb, "w_in_c", 3072)
        for c in range(8):
            psc = pb[c % 4]
            for jt in range(31):
                dg = dgr[dgc[0] % 8]
                dgc[0] += 1
                I("dve", "tensor_scalar", out=dg[:, :], in0=identb[:], scalar1=cw[:, c, jt:jt + 1], scalar2=None, op0=ALU.mult, r=[identb, cw], w=[dg])
                mm(psc[:, 0:N], dg[:, :], uT[:, c, jt:jt + N], jt == 0, jt == 30, r=[dg, uT], w=[psc])
            I("act", "activation", out=cacc[:, c, 0:N], in_=psc[:, 0:N], func=AF.Identity, bias=cb[:, c:c + 1], r=[psc, cb], w=[cacc])
        for c in range(8):
            mm(pb[2][:, 0:N], onesf[:], cacc[:, c, 0:N], c == 0, c == 7, r=[onesf, cacc], w=[pb[2]])
        for c in range(8):
            I("act", "activation", out=sqs[:, 0:N], in_=cacc[:, c, 0:N], func=AF.Square, r=[cacc], w=[sqs])
            mm(pb[3][:, 0:N], onesf[:], sqs[:, 0:N], c == 0, c == 7, r=[onesf, sqs], w=[pb[3]])
        I("dve", "tensor_scalar", out=lmean[:, 0:N], in0=pb[2][:, 0:N], scalar1=1.0 / D, scalar2=None, op0=ALU.mult, r=[pb[2]], w=[lmean])
        I("dve", "tensor_tensor", out=e1[:, 0:N], in0=lmean[:, 0:N], in1=lmean[:, 0:N], op=ALU.mult, r=[lmean], w=[e1])
        I("dve", "scalar_tensor_tensor", out=e1[:, 0:N], in0=pb[3][:, 0:N], scalar=1.0 / D, in1=e1[:, 0:N], op0=ALU.mult, op1=ALU.subtract, r=[pb[3], e1], w=[e1])
        I("act", "activation", out=lrstd[:, 0:N], in_=e1[:, 0:N], func=AF.Ln, bias=epsc[:], r=[e1, epsc], w=[lrstd])
        I("act", "activation", out=lrstd[:, 0:N], in_=lrstd[:, 0:N], func=AF.Exp, scale=-0.5, r=[lrstd], w=[lrstd])
        wzs = [load_w("w_in_c", 2048, 512, 8), load_w("w_in_c", 2560, 512, 8)]
        for c in range(8):
            proj_feat(wzs[c // 4], c % 4, N, pb[c % 4])
            I("act", "activation", out=szT[:, c, 0:N], in_=pb[c % 4][:, 0:N], func=AF.Silu, r=[pb[c % 4]], w=[szT])
            I("dve", "tensor_tensor", out=cacc[:, c, 0:N], in0=cacc[:, c, 0:N], in1=lmean[:, 0:N], op=ALU.subtract, r=[cacc, lmean], w=[cacc])
            I("dve", "tensor_tensor", out=cacc[:, c, 0:N], in0=cacc[:, c, 0:N], in1=lrstd[:, 0:N], op=ALU.mult, r=[cacc, lrstd], w=[cacc])
            I("act", "activation", out=e1[:, 0:N], in_=cacc[:, c, 0:N], func=AF.Silu, scale=lg[:, c:c + 1], bias=lb[:, c:c + 1], r=[cacc, lg, lb], w=[e1])
            I("dve", "tensor_tensor", out=mixT[:, c, 0:N], in0=e1[:, 0:N], in1=szT[:, c, 0:N], op=ALU.mult, r=[e1, szT], w=[mixT])
        x_branch(1, nsb, "w_in_c", 3072, 3584, 8, q_done=True)

        cur = [None]

        def dst_fn(j, cg):
            cur[0] = tkrot[tkrc[0] % 3]
            tkrc[0] += 1
            return cur[0][:], cur[0]

        def done(j, cg):
            dma("sp", y_dst[j * 128:j * 128 + nrows, cg * 512:(cg + 1) * 512], cur[0][0:nrows, :], r=[cur[0]], final=True)
        dst_done_fn[0] = done
        if before_out is not None:
            before_out()
        out_proj(nsb, "w_out_c", lambda j, cg: (y0[:, j, cg * 512:(cg + 1) * 512], y0b[j]), dst_fn)
        dst_done_fn[0] = lambda j, cg: None

    class _Stop(Exception):
        pass

    def stage(n):
        if cfg.get("stage") == n:
            raise _Stop()

    try:
        for _ in range(NXS):
            x_issue()
        stage(1)
        mem_kv(0)
        stage(2)
        if do_l1:
            mem_kv(1)
        stage(3)
        I("dve", "memset", Cst[:], 0.0, w=[Cst])
        I("dve", "memset", mst[:], 0.0, w=[mst])

        def res_from(src, row0):
            def fn(j, cg):
                t_ = tkrot[tkrc[0] % 3]
                tkrc[0] += 1
                dma("sp", t_[:], src[row0 + j * 128:row0 + (j + 1) * 128, cg * 512:(cg + 1) * 512], w=[t_])
                return t_[:], t_
            return fn

        def full_tile(src, row0, nsb, tile_blk, n_masked, blk0, out_row0, y_dst, halo_only=False, pconv=False, stats_done=False, before_out=None):
            make_xnT(nsb, ng[:, 0, :], None if stats_done else dram_src(src, row0), stats_done=stats_done)
            switch("A")
            a_branch(nsb, "full", "w_in_a")
            switch("B")
            b_kv(nsb, "w_in_a", blk0, out_row0)
            x_q(0, nsb, "w_in_a", 4616)
            b_attn(nsb, "w_in_a", tile_blk, n_masked, 0)
            x_branch(0, nsb, "w_in_a", 4616, 5128, 8, q_done=True)
            out_proj(nsb, "w_out_a", res_from(src, row0), lambda j, cg: (y0[:, j, cg * 512:(cg + 1) * 512], y0b[j]))
            if do_l1:
                layer1(nsb, y_dst, 128, halo_only=halo_only, pconv=pconv, before_out=before_out)

        blk = 0
        pre_started = [False]
        for (nsb, mode) in cfg["pre"]:
            if mode == "pre":
                make_xnT(nsb, ng[:, 0, :], dram_src(x_pre, blk * 128))
                if not pre_started[0]:
                    switch("AB")
                    pre_started[0] = True
                st_later = a_branch(nsb, "pre", "w_in_a", state_later=True)
                b_kv(nsb, "w_in_a", blk, None)
                st_later()
                stage(4)
            else:
                assert nsb == 1
                full_tile(x_pre, blk * 128, 1, blk, NB_PRE, blk, None, None, halo_only=True)
                stage(5)
            blk += nsb
        if N_PRE > 0:
            I("dve", "tensor_tensor", out=mst[:], in0=mst[:], in1=flag[:], op=ALU.mult, r=[mst, flag], w=[mst])
        ob_ = 0
        nown = len(cfg["own"])
        for ti, nsb in enumerate(cfg["own"]):
            r0 = ob_ * 128
            last = ti == nown - 1
            hook = None
            if (not last) and do_l1:
                nsb2 = cfg["own"][ti + 1]
                r2 = (ob_ + nsb) * 128
                hook = (lambda nsb2=nsb2, r2=r2: xn_stats(nsb2, dram_src(x_own, r2)))
            full_tile(x_own, r0, nsb, NB_PRE + ob_, NB_PRE, (NB_PRE + ob_) if not last else None, r0, y_own[r0:r0 + 128 * nsb, :], pconv=last,
                      stats_done=(ti > 0 and do_l1), before_out=hook)
            ob_ += nsb
        dma("sp", o_pC.rearrange("h d e -> d h e"), Cst[:, :, 0:128], r=[Cst], final=True)
        dma("sp", o_pn.rearrange("h d -> d h"), Cst[:, :, 128], r=[Cst], final=True, allow_slow_non_contiguous=True)
        dma("sp", o_pm, mst[:], r=[mst], final=True)
        stage(6)

        if do_smp:
            smp_mem_kv()
            switch("B")
            for kb in range(16):
                tks = tkrot[kb % 3]
                dma("act", tks[:], c_k[kb * 128:(kb + 1) * 128, :], w=[tks])
                tok_to_T(tks[:], tks, KTo[:, :, (kb % 4) * 128:(kb % 4 + 1) * 128], kb, KTo)
                dma("pool", Vo[:, kb % 4, :, 0:128], c_v[kb * 128:(kb + 1) * 128, :].rearrange("p (h d) -> p h d", h=4), w=[Vo])
                scr_write(kb % 4, kb)
            dma("sp", Cst[:, :, 0:128], st_C.rearrange("h d e -> d h e"), w=[Cst])
            dma("sp", Cst[:, :, 128], st_n.rearrange("h d -> d h"), w=[Cst], allow_slow_non_contiguous=True)
            dma("sp", mst[:], st_m, w=[mst])
            I("pool", "memset", sqs[:], 0.0, w=[sqs])
            dma("sp", sqs[0:30, :], st_conv, w=[sqs])
            for c in range(8):
                tr(pb[4 + c // 4][:, (c % 4) * 128:(c % 4 + 1) * 128], sqs[:, c * 128:(c + 1) * 128], identf[:], r=[sqs, identf], w=[pb[4 + c // 4]])
            for hp in range(2):
                I("dve", "tensor_copy", out=halo[:, hp * 4:(hp + 1) * 4, :], in_=pb[4 + hp][:, :].rearrange("p (c t) -> p c t", c=4)[:, :, 0:30], r=[pb[4 + hp]], w=[halo])
            make_xnT(1, ng[:, 0, :], dram_src(x_smp, 0))
            switch("A")
            a_branch(1, "full", "w_in_a", Lc=32, nch_per_sb=1)
            dma("sp", o_sC.rearrange("h d e -> d h e"), Cst[:, :, 0:128], r=[Cst], final=True)
            dma("sp", o_sn.rearrange("h d -> d h"), Cst[:, :, 128], r=[Cst], final=True, allow_slow_non_contiguous=True)
            dma("sp", o_sm, mst[:], r=[mst], final=True)
            switch("B")
            b_kv(1, "w_in_a", None, 0, smp=True)
            b_attn(1, "w_in_a", 16, 0, 1)
            x_branch(0, 1, "w_in_a", 4616, 5128, 8)
            out_proj(1, "w_out_a", res_from(x_smp, 0), lambda j, cg: (y0[:, j, cg * 512:(cg + 1) * 512], y0b[j]))
            if do_l1:
                layer1(1, y_smp, 32, smp=True)

    except _Stop:
        pass
    flush()
    P.S.emit(final_waits=P.outs)
    return P


def host_consts(has_prefix):
    ident = np.eye(128, dtype=np.float32)
    a = np.arange(128, dtype=np.float64)
    bias = np.zeros((128, 2, 4 * NDELTA), np.float32)
    for h in range(4):
        for dl in range(-DOFF, NDELTA - DOFF):
            v = SLOPES[h] * (a - 128.0 * dl)
            bias[:, 0, h * NDELTA + dl + DOFF] = v
            bias[:, 1, h * NDELTA + dl + DOFF] = v + (0.0 if has_prefix else PMASK)
    ka = np.arange(128)[:, None]
    qc = np.arange(128)[None, :]
    dm = np.zeros((128, 2, 4, 128), np.float32)
    for h in range(4):
        corr = np.where(ka > qc, -16.0 * SLOPES[h] * (ka - qc), 0.0)
        vis = (ka // 64) <= (qc // 64)
        dm[:, 0, h, :] = np.where(vis, corr, -1.0e5)
        vis_s = (ka < 32) & (qc < 128)
        dm[:, 1, h, :] = np.where(vis_s, corr, -1.0e5)
    am = ((ka <= qc) & ((ka // 64) == (qc // 64))).astype(np.float32)
    hsel = np.zeros((4, 4, 128), np.float32)
    for h in range(4):
        hsel[h, h, :] = 1.0
    flag = np.full((4, 1), 1.0 if has_prefix else 0.0, np.float32)
    return dict(c_ident=ident, c_bias=bias, c_dmask=dm, c_amask=am, c_hsel=hsel, c_flag=flag)


FULL_CFG = dict(own=[4] * 8, pre=[(4, "pre")] * 7 + [(3, "pre"), (1, "halo")], sample=True, l1=True)
_CACHE = {}


def core_inputs(c, cfg, inp):
    f = lambda a: np.ascontiguousarray(a, dtype=np.float32)
    b, half = c // 2, c % 2
    n_own = 128 * sum(cfg["own"])
    n_pre = 128 * sum(n for n, _ in cfg["pre"])
    xp = inp["x_prompt"][b]
    d = {}
    d["x_own"] = f(xp[half * n_own:(half + 1) * n_own])
    if half == 1:
        d["x_pre"] = f(xp[n_own - n_pre:n_own])
    else:
        d["x_pre"] = np.zeros((max(n_pre, 128), D), np.float32)
    xs = np.zeros((128, D), np.float32)
    xs[:32] = inp["x_sample"][c]
    d["x_smp"] = xs
    d["mem"] = f(inp["mem_prompt"][b])
    d["c_xk"] = f(inp["cache_xk"][:, c].reshape(2, 256, 512))
    d["c_xv"] = f(inp["cache_xv"][:, c].reshape(2, 256, 512))
    d["c_k"] = f(inp["cache_k"][0, c].reshape(2048, 512))
    d["c_v"] = f(inp["cache_v"][0, c].reshape(2048, 512))
    d["st_C"] = f(inp["state_C"][0, c])
    d["st_n"] = f(inp["state_n"][0, c])
    d["st_m"] = f(inp["state_m"][0, c].reshape(4, 1))
    d["st_conv"] = f(inp["state_conv"][0, c])
    d["norm_g"] = f(inp["norm_g"])
    d["w_in_a"] = f(inp["w_in_a"][0])
    d["b_ig"] = f(inp["b_ig"][0].reshape(4, 1))
    d["b_fg"] = f(inp["b_fg"][0].reshape(4, 1))
    d["mlstm_g"] = f(inp["mlstm_norm_g"][0].reshape(512))
    d["qn_g"] = f(np.tile(inp["qn_g"][0], 8))
    d["kn_g"] = f(np.tile(inp["kn_g"][0], 8))
    d["lamv"] = f(np.stack([inp["lam_q1"][0], inp["lam_k1"][0], inp["lam_q2"][0], inp["lam_k2"][0]]))
    d["subln_g"] = f(np.tile(inp["subln_g"][0], 4))
    d["w_out_a"] = f(inp["w_out_a"][0])
    d["w_in_c"] = f(inp["w_in_c"][0])
    d["conv_wT"] = f(inp["conv_w"][0].T)
    d["conv_b"] = f(inp["conv_b"][0])
    d["ln_g"] = f(inp["conv_ln_g"][0])
    d["ln_b"] = f(inp["conv_ln_b"][0])
    d["w_out_c"] = f(inp["w_out_c"][0])
    d["mem_norm_g"] = f(inp["mem_norm_g"])
    d["w_mem_kv"] = f(inp["w_mem_kv"])
    d["xq_g"] = f(np.stack([np.tile(inp["xq_norm_g"][l], 4) for l in range(2)]))
    d["xk_g"] = f(np.stack([np.tile(inp["xk_norm_g"][l], 4) for l in range(2)]))
    d.update(host_consts(half == 1))
    return d


def kernel(**inp):
    inp = {k: np.asarray(v) for k, v in inp.items()}
    cfg = FULL_CFG
    if "prog" not in _CACHE:
        _CACHE["prog"] = build(cfg)
    P = _CACHE["prog"]
    in_maps = [core_inputs(c, cfg, inp) for c in range(8)]
    res = run_bass_kernel_spmd(P.nc, in_maps, core_ids=list(range(8))).results
    B, SEQ = 4, 8192
    yp = np.zeros((B, SEQ, D), np.float32)
    pk = np.zeros((1, B, SEQ, 4, 128), np.float32)
    pv = np.zeros((1, B, SEQ, 4, 128), np.float32)
    pxk = np.zeros((2, B, 256, 4, 128), np.float32)
    pxv = np.zeros((2, B, 256, 4, 128), np.float32)
    pC = np.zeros((1, B, 4, 128, 128), np.float32)
    pn = np.zeros((1, B, 4, 128), np.float32)
    pm = np.zeros((1, B, 4), np.float32)
    pconv = np.zeros((1, B, 30, D), np.float32)
    ys = np.zeros((8, 32, D), np.float32)
    sk = np.zeros((1, 8, 32, 4, 128), np.float32)
    sv = np.zeros((1, 8, 32, 4, 128), np.float32)
    sC = np.zeros((1, 8, 4, 128, 128), np.float32)
    sn = np.zeros((1, 8, 4, 128), np.float32)
    sm = np.zeros((1, 8, 4), np.float32)
    sconv = np.zeros((1, 8, 30, D), np.float32)
    for c in range(8):
        r = res[c]
        b, half = c // 2, c % 2
        sl = slice(half * 4096, (half + 1) * 4096)
        yp[b, sl] = r["y_own"]
        pk[0, b, sl] = r["o_pk"].reshape(4096, 4, 128)
        pv[0, b, sl] = r["o_pv"].reshape(4096, 4, 128)
        if half == 1:
            pxk[:, b] = r["o_pxk"].reshape(2, 256, 4, 128)
            pxv[:, b] = r["o_pxv"].reshape(2, 256, 4, 128)
            pC[0, b] = r["o_pC"]
            pn[0, b] = r["o_pn"]
            pm[0, b] = r["o_pm"].reshape(4)
            pconv[0, b] = r["o_pconv"]
        ys[c] = r["y_smp"]
        sk[0, c] = r["o_sk"].reshape(32, 4, 128)
        sv[0, c] = r["o_sv"].reshape(32, 4, 128)
        sC[0, c] = r["o_sC"]
        sn[0, c] = r["o_sn"]
        sm[0, c] = r["o_sm"].reshape(4)
        sconv[0, c] = r["o_sconv"]
    return (yp, ys, pxk, pxv, pk, pv, pC, pn, pm, pconv, sk, sv, sC, sn, sm, sconv)
```
